# Optimizing a Trainium2 kernel written in Bass

```python
import math
import jax, jax.numpy as jnp
from jax import lax
import numpy as np

D_MODEL = 1024
BATCH = 4
SEQ = 4096
DEPTH = 4

GRID_W = 64
CTX_LEN = 256
HEAD_DIM = 64
A_HEADS = 6
A_QK = HEAD_DIM // 2
A_W = A_HEADS * HEAD_DIM
Q_BLOCK = 128
ROPE_THETA = 10000.0
B_HEADS = 6
B_W = B_HEADS * HEAD_DIM
CONV_W = 5
GDN_CHUNK = 64
C_HEADS = 4
C_W = C_HEADS * HEAD_DIM
WIN_ROWS = 8
WIN_COLS = 16
MIX_W = A_W + B_W + C_W
PROJ_SIZES = (3 * A_W, 3 * B_W, B_W, 2 * B_HEADS, 2 * B_HEADS, 3 * C_W)
D_IN = sum(PROJ_SIZES)
D_FF = -(-8 * D_MODEL // (3 * 256)) * 256
EPS = 1e-6

kernel_name = 'hybrid_diffattn_gdn_natten_dit'


def rmsnorm(x, w):
    xf = x.astype(jnp.float32)
    y = xf * lax.rsqrt(jnp.mean(xf * xf, axis=-1, keepdims=True) + EPS)
    return (y * w.astype(jnp.float32)).astype(x.dtype)


def l2norm(x):
    return x * lax.rsqrt(jnp.sum(x * x, axis=-1, keepdims=True) + EPS)


def modulate(h, shift, scale):
    return h * (1.0 + scale) + shift


def swiglu(h, w_in, w_out):
    g, u = jnp.split(h @ w_in, 2, axis=-1)
    return (jax.nn.silu(g) * u) @ w_out


def axial_rope_tables(n_tok):
    half = A_QK // 2
    inv_freq = 1.0 / (ROPE_THETA ** (jnp.arange(0, half, 2, dtype=jnp.float32) / half))
    t = jnp.arange(n_tok, dtype=jnp.int32)
    ang_r = (t // GRID_W).astype(jnp.float32)[:, None] * inv_freq
    ang_c = (t % GRID_W).astype(jnp.float32)[:, None] * inv_freq
    ang = jnp.concatenate([ang_r, ang_r, ang_c, ang_c], axis=-1)
    return jnp.cos(ang), jnp.sin(ang)


def apply_axial_rope(x, cos, sin):
    r1, r2, c1, c2 = jnp.split(x, 4, axis=-1)
    rot = jnp.concatenate([-r2, r1, -c2, c1], axis=-1)
    cos = cos[:, None, None, :].astype(x.dtype)
    sin = sin[:, None, None, :].astype(x.dtype)
    return x * cos + rot * sin


def diff_attention(qkv_l, qkv_c, lam, lam_init, norm_w, cos, sin, with_ctx):
    nb, t, _ = qkv_l.shape
    qk_heads = lambda z: z.reshape(z.shape[0], z.shape[1], A_HEADS, 2, A_QK)
    v_heads = lambda z: z.reshape(z.shape[0], z.shape[1], A_HEADS, HEAD_DIM)
    q_l, k_l, v_l = jnp.split(qkv_l, 3, axis=-1)
    q_c, k_c, v_c = jnp.split(qkv_c, 3, axis=-1)
    q_l = apply_axial_rope(qk_heads(q_l), cos, sin)
    k_l = apply_axial_rope(qk_heads(k_l), cos, sin)
    k_c, v_c = qk_heads(k_c), v_heads(v_c)
    keys = jnp.concatenate([k_c, k_l], axis=1)
    vals = jnp.concatenate([v_c, v_heads(v_l)], axis=1)
    scale = A_QK ** -0.5

    def attend(q, k, v):
        s = jnp.einsum('bqhmd,bkhmd->bhmqk', q, k).astype(jnp.float32) * scale
        p = jax.nn.softmax(s, axis=-1)
        w = (p[:, :, 0] - lam * p[:, :, 1]).astype(v.dtype)
        return jnp.einsum('bhqk,bkhd->bqhd', w, v)

    def post(o):
        o = rmsnorm(o, norm_w) * (1.0 - lam_init)
        return o.reshape(o.shape[0], o.shape[1], A_W)

    n_blk = t // Q_BLOCK
    q_blocks = q_l.reshape(nb, n_blk, Q_BLOCK, A_HEADS, 2, A_QK).transpose(1, 0, 2, 3, 4, 5)
    o_l = lax.map(lambda qb: attend(qb, keys, vals), q_blocks)
    o_l = o_l.transpose(1, 0, 2, 3, 4).reshape(nb, t, A_HEADS, HEAD_DIM)
    y_c = post(attend(qk_heads(q_c), k_c, v_c)) if with_ctx else None
    return post(o_l), y_c


def short_conv(x, w):
    y = lax.conv_general_dilated(
        x, w[:, None, :].astype(x.dtype), window_strides=(1,),
        padding=[(CONV_W // 2, CONV_W // 2)], dimension_numbers=('NWC', 'WIO', 'NWC'),
        feature_group_count=x.shape[-1])
    return jax.nn.silu(y)


def chunk_gated_delta(q, k, v, g, beta, s0):
    nb, nh, t, dk = q.shape
    dv = v.shape[-1]
    c = GDN_CHUNK
    n = t // c
    ch = lambda a: a.reshape(nb, nh, n, c, *a.shape[3:])
    q, k, v, g, beta = ch(q), ch(k), ch(v), ch(g), ch(beta)
    gc = jnp.cumsum(g, axis=-1)
    idx = jnp.arange(c)
    incl = idx[:, None] >= idx[None, :]
    strict = idx[:, None] > idx[None, :]
    decay = jnp.exp(jnp.where(incl, gc[..., :, None] - gc[..., None, :], -jnp.inf))
    kb = k * beta[..., None]
    lmat = jnp.where(strict, jnp.einsum('bhnid,bhnjd->bhnij', kb, k) * decay, 0.0)
    eye = jnp.eye(c, dtype=q.dtype)
    t_inv = lax.linalg.triangular_solve(eye + lmat, jnp.broadcast_to(eye, lmat.shape),
                                        left_side=True, lower=True, unit_diagonal=True)
    u = t_inv @ (v * beta[..., None])
    w = t_inv @ (kb * jnp.exp(gc)[..., None])
    a_qk = jnp.einsum('bhnid,bhnjd->bhnij', q, k) * decay
    g_last = gc[..., -1]
    k_tail = k * jnp.exp(g_last[..., None] - gc)[..., None]
    q_head = q * jnp.exp(gc)[..., None]
    xs = tuple(jnp.moveaxis(a, 2, 0) for a in (q_head, k_tail, u, w, a_qk, g_last))

    def step(s, inp):
        qh, kt, ui, wi, ai, gl = inp
        v_new = ui - wi @ s
        o = qh @ s + ai @ v_new
        s = s * jnp.exp(gl)[..., None, None] + jnp.swapaxes(kt, -1, -2) @ v_new
        return s, o

    s, o = lax.scan(step, s0, xs)
    return jnp.moveaxis(o, 0, 2).reshape(nb, nh, t, dv), s


def gdn_prepare(qkv, alpha, beta, conv_w, a_log, dt_bias):
    nb, t = qkv.shape[:2]
    q, k, v = jnp.split(short_conv(qkv, conv_w), 3, axis=-1)
    heads = lambda z: z.reshape(nb, t, B_HEADS, HEAD_DIM).transpose(0, 2, 1, 3).astype(jnp.float32)
    q = l2norm(heads(q)) * (HEAD_DIM ** -0.5)
    k = l2norm(heads(k))
    v = heads(v)
    alpha = alpha.reshape(nb, t, 2, B_HEADS).astype(jnp.float32)
    beta = beta.reshape(nb, t, 2, B_HEADS).astype(jnp.float32)
    g = -jnp.exp(a_log.astype(jnp.float32)) * jax.nn.softplus(alpha + dt_bias.astype(jnp.float32))
    b = jax.nn.sigmoid(beta)
    return q, k, v, g.transpose(2, 0, 3, 1), b.transpose(2, 0, 3, 1)


def bidir_scan(q, k, v, g, b, s_fwd, s_bwd):
    flip = lambda z: jnp.flip(z, axis=2)
    o_f, sf = chunk_gated_delta(q, k, v, g[0], b[0], s_fwd)
    o_b, sb = chunk_gated_delta(flip(q), flip(k), flip(v), flip(g[1]), flip(b[1]), s_bwd)
    return o_f + flip(o_b), sf, sb


def gated_deltanet(qkv_l, gate_l, alpha_l, beta_l, qkv_c, gate_c, alpha_c, beta_c,
                   conv_w, a_log, dt_bias, norm_w, with_ctx):
    lat = gdn_prepare(qkv_l, alpha_l, beta_l, conv_w, a_log, dt_bias)
    cx = gdn_prepare(qkv_c, alpha_c, beta_c, conv_w, a_log, dt_bias)
    s0 = jnp.zeros((qkv_c.shape[0], B_HEADS, HEAD_DIM, HEAD_DIM), jnp.float32)
    o_c, s_f, s_b = bidir_scan(*cx, s0, s0)
    o_l, _, _ = bidir_scan(*lat, s_f, s_b)

    def out(o, gate):
        nb, _, t, _ = o.shape
        gt = jax.nn.silu(gate.reshape(nb, t, B_HEADS, HEAD_DIM).astype(jnp.float32))
        o = rmsnorm(o.transpose(0, 2, 1, 3), norm_w) * gt
        return o.reshape(nb, t, B_W).astype(gate.dtype)

    return out(o_l, gate_l), (out(o_c, gate_c) if with_ctx else None)


def neighborhood_attention(qkv_l, qkv_c, rel_bias, with_ctx):
    nb, t, _ = qkv_l.shape
    rows = t // GRID_W
    wr = min(WIN_ROWS, rows)
    nk = wr * WIN_COLS
    heads = lambda z: z.reshape(z.shape[0], z.shape[1], C_HEADS, HEAD_DIM).transpose(0, 2, 1, 3)
    q_l, k_l, v_l = (heads(z) for z in jnp.split(qkv_l, 3, axis=-1))
    q_c, k_c, v_c = (heads(z) for z in jnp.split(qkv_c, 3, axis=-1))
    scale = HEAD_DIM ** -0.5
    cq = jnp.arange(GRID_W)
    kcols = jnp.clip(cq - WIN_COLS // 2, 0, GRID_W - WIN_COLS)[:, None] + jnp.arange(WIN_COLS)
    dcol = kcols - cq[:, None] + (WIN_COLS - 1)
    q_rows = q_l.reshape(nb, C_HEADS, rows, GRID_W, HEAD_DIM)

    def row_block(r):
        krows = jnp.clip(r - wr // 2, 0, rows - wr) + jnp.arange(wr)
        drow = krows - r + (WIN_ROWS - 1)
        idx = (krows[None, :, None] * GRID_W + kcols[:, None, :]).reshape(GRID_W, nk)
        kn = jnp.take(k_l, idx, axis=2)
        vn = jnp.take(v_l, idx, axis=2)
        bias = rel_bias[:, drow[None, :, None], dcol[:, None, :]].reshape(C_HEADS, GRID_W, nk)
        qr = lax.dynamic_index_in_dim(q_rows, r, axis=2, keepdims=False)
        s_n = jnp.einsum('bhqd,bhqkd->bhqk', qr, kn).astype(jnp.float32) * scale + bias.astype(jnp.float32)
        s_c = jnp.einsum('bhqd,bhkd->bhqk', qr, k_c).astype(jnp.float32) * scale
        p = jax.nn.softmax(jnp.concatenate([s_n, s_c], axis=-1), axis=-1).astype(v_l.dtype)
        return (jnp.einsum('bhqk,bhqkd->bhqd', p[..., :nk], vn)
                + jnp.einsum('bhqk,bhkd->bhqd', p[..., nk:], v_c))

    o = lax.map(row_block, jnp.arange(rows))
    y_l = o.transpose(1, 0, 3, 2, 4).reshape(nb, t, C_W)
    y_c = None
    if with_ctx:
        s = jnp.einsum('bhqd,bhkd->bhqk', q_c, k_c).astype(jnp.float32) * scale
        p = jax.nn.softmax(s, axis=-1).astype(v_c.dtype)
        oc = jnp.einsum('bhqk,bhkd->bhqd', p, v_c)
        y_c = oc.transpose(0, 2, 1, 3).reshape(oc.shape[0], oc.shape[2], C_W)
    return y_l, y_c


def setup_inputs(seed: int = 0) -> dict:
    key = jax.random.key(seed)
    ks = jax.random.split(key, 24)
    nrm = lambda k, shape: jax.random.normal(k, shape, jnp.float32)
    D = D_MODEL
    x = nrm(ks[0], (BATCH, SEQ, D))
    c = nrm(ks[1], (BATCH, D))
    ctx = nrm(ks[2], (BATCH, CTX_LEN, D))
    c_ctx = nrm(ks[3], (D,))
    w_mod = nrm(ks[4], (DEPTH, D, 6 * D)) * (0.5 * D ** -0.5)
    b_mod = 0.02 * nrm(ks[5], (DEPTH, 6 * D))
    norm1_w = 1.0 + 0.05 * nrm(ks[6], (DEPTH, D))
    norm2_w = 1.0 + 0.05 * nrm(ks[7], (DEPTH, D))
    w_in = nrm(ks[8], (DEPTH, D, D_IN)) * D ** -0.5
    w_out = nrm(ks[9], (DEPTH, MIX_W, D)) * MIX_W ** -0.5
    lambda_q1 = 0.1 * nrm(ks[10], (DEPTH, A_QK))
    lambda_k1 = 0.1 * nrm(ks[11], (DEPTH, A_QK))
    lambda_q2 = 0.1 * nrm(ks[12], (DEPTH, A_QK))
    lambda_k2 = 0.1 * nrm(ks[13], (DEPTH, A_QK))
    diff_norm_w = 1.0 + 0.05 * nrm(ks[14], (DEPTH, HEAD_DIM))
    conv_w = nrm(ks[15], (DEPTH, CONV_W, 3 * B_W)) * CONV_W ** -0.5
    a_log = jnp.log(jax.random.uniform(ks[16], (DEPTH, 2, B_HEADS), jnp.float32, 1.0, 16.0))
    dt = jnp.exp(jax.random.uniform(ks[17], (DEPTH, 2, B_HEADS), jnp.float32,
                                    math.log(1e-3), math.log(1e-1)))
    dt_bias = dt + jnp.log(-jnp.expm1(-dt))
    gdn_norm_w = 1.0 + 0.05 * nrm(ks[18], (DEPTH, HEAD_DIM))
    na_bias = 0.1 * nrm(ks[19], (DEPTH, C_HEADS, 2 * WIN_ROWS - 1, 2 * WIN_COLS - 1))
    w_ffn_in = nrm(ks[20], (DEPTH, D, 2 * D_FF)) * D ** -0.5
    w_ffn_out = nrm(ks[21], (DEPTH, D_FF, D)) * D_FF ** -0.5
    final_norm_w = 1.0 + 0.05 * nrm(ks[22], (D,))
    return {'x': x, 'c': c, 'ctx': ctx, 'c_ctx': c_ctx, 'w_mod': w_mod, 'b_mod': b_mod,
            'norm1_w': norm1_w, 'norm2_w': norm2_w, 'w_in': w_in, 'w_out': w_out,
            'lambda_q1': lambda_q1, 'lambda_k1': lambda_k1, 'lambda_q2': lambda_q2,
            'lambda_k2': lambda_k2, 'diff_norm_w': diff_norm_w, 'conv_w': conv_w,
            'a_log': a_log, 'dt_bias': dt_bias, 'gdn_norm_w': gdn_norm_w, 'na_bias': na_bias,
            'w_ffn_in': w_ffn_in, 'w_ffn_out': w_ffn_out, 'final_norm_w': final_norm_w}


def reference(x, c, ctx, c_ctx, w_mod, b_mod, norm1_w, norm2_w, w_in, w_out,
              lambda_q1, lambda_k1, lambda_q2, lambda_k2, diff_norm_w,
              conv_w, a_log, dt_bias, gdn_norm_w, na_bias,
              w_ffn_in, w_ffn_out, final_norm_w):
    cos, sin = axial_rope_tables(x.shape[1])
    split_at = [int(s) for s in np.cumsum(PROJ_SIZES)[:-1]]
    silu_c = jax.nn.silu(c)
    silu_cc = jax.nn.silu(c_ctx)
    xl, xc = x, ctx
    for l in range(DEPTH):
        with_ctx = l < DEPTH - 1
        mod_l = jnp.split((silu_c @ w_mod[l] + b_mod[l])[:, None, :], 6, axis=-1)
        mod_c = jnp.split(silu_cc @ w_mod[l] + b_mod[l], 6, axis=-1)
        hl = modulate(rmsnorm(xl, norm1_w[l]), mod_l[0], mod_l[1])
        hc = modulate(rmsnorm(xc, norm1_w[l]), mod_c[0], mod_c[1])
        qkv_a_l, qkv_b_l, gate_b_l, alpha_l, beta_l, qkv_c_l = jnp.split(hl @ w_in[l], split_at, axis=-1)
        qkv_a_c, qkv_b_c, gate_b_c, alpha_c, beta_c, qkv_c_c = jnp.split(hc @ w_in[l], split_at, axis=-1)

        lam_init = 0.8 - 0.6 * math.exp(-0.3 * l)
        lam = (jnp.exp(jnp.sum(lambda_q1[l].astype(jnp.float32) * lambda_k1[l].astype(jnp.float32)))
               - jnp.exp(jnp.sum(lambda_q2[l].astype(jnp.float32) * lambda_k2[l].astype(jnp.float32)))
               + lam_init)
        ya_l, ya_c = diff_attention(qkv_a_l, qkv_a_c, lam, lam_init, diff_norm_w[l], cos, sin, with_ctx)
        yb_l, yb_c = gated_deltanet(qkv_b_l, gate_b_l, alpha_l, beta_l, qkv_b_c, gate_b_c, alpha_c, beta_c,
                                    conv_w[l], a_log[l], dt_bias[l], gdn_norm_w[l], with_ctx)
        yc_l, yc_c = neighborhood_attention(qkv_c_l, qkv_c_c, na_bias[l], with_ctx)

        xl = xl + mod_l[2] * (jnp.concatenate([ya_l, yb_l, yc_l], axis=-1) @ w_out[l])
        hl = modulate(rmsnorm(xl, norm2_w[l]), mod_l[3], mod_l[4])
        xl = xl + mod_l[5] * swiglu(hl, w_ffn_in[l], w_ffn_out[l])
        if with_ctx:
            xc = xc + mod_c[2] * (jnp.concatenate([ya_c, yb_c, yc_c], axis=-1) @ w_out[l])
            hc = modulate(rmsnorm(xc, norm2_w[l]), mod_c[3], mod_c[4])
            xc = xc + mod_c[5] * swiglu(hc, w_ffn_in[l], w_ffn_out[l])
    return rmsnorm(xl, final_norm_w)
```

```python
import contextlib
import math
import numpy as np
import concourse.bass as bass
import concourse.mybir as mybir
from concourse.bass_utils import run_bass_kernel_spmd

F32 = mybir.dt.float32
BF16 = mybir.dt.bfloat16
AF = mybir.ActivationFunctionType
ALU = mybir.AluOpType
AX = mybir.AxisListType

NT, NCTX, TL, D, L = 4352, 256, 4096, 1024, 4
NTILE = NT // 128
DFF = 2816
EPS = 1e-6
C_QA, C_QAP, C_KA, C_KAP, C_B, C_QC, C_KC = 0, 512, 1024, 1536, 2048, 3200, 3456
NFM = 3712
C_VA, C_VC, C_G = 3712, 4096, 4352
NW = 4760
GROUPS = [(0, 256)] + [(256 + 512 * i, 512) for i in range(8)]


class Buf:
    __slots__ = ("w", "r", "g")

    def __init__(self):
        self.w = None
        self.r = {}
        self.g = None


class T:
    __slots__ = ("ap", "b")

    def __init__(self, ap):
        self.ap = ap
        self.b = Buf()


class Prog:
    ENG = ("pe", "act", "dve", "pool", "sp")

    def __init__(self, nc, stack, n_dma=48, same=True):
        self.nc = nc
        self.ops = {e: [] for e in self.ENG}
        self.cnt = {e: 0 for e in self.ENG}
        self.seen = {e: {} for e in self.ENG}
        self.esem = {e: stack.enter_context(nc.semaphore("s_" + e)) for e in self.ENG}
        self.dsem = [stack.enter_context(nc.semaphore("d%d" % i)) for i in range(n_dma)]
        self.dval = [0] * n_dma
        self.dnext = 0
        self.same = same

    def _wait(self, eng, tok):
        if tok is None:
            return
        key, val = tok
        if key == eng and (not self.same or eng == "pe"):
            return
        if self.seen[eng].get(key, 0) >= val:
            return
        self.seen[eng][key] = val
        sem = self.esem[key] if isinstance(key, str) else self.dsem[key]
        self.ops[eng].append(lambda e: e.wait_ge(sem, val))

    def _deps(self, eng, reads, writes):
        for b in reads:
            self._wait(eng, b.w)
        for b in writes:
            self._wait(eng, b.w)
            for t in list(b.r.values()):
                self._wait(eng, t)

    def _commit(self, tok, reads, writes):
        for b in reads:
            b.r[tok[0]] = tok
        for b in writes:
            b.w = tok
            b.r = {}

    def op(self, eng, fn, r=(), w=()):
        self._deps(eng, r, w)
        guards = [b.g for b in r if b.g is not None] if eng in ("act", "dve") else ()
        for g in guards:
            if g[0] is not None and g[1] != eng:
                self._wait(eng, g[0])
        self.cnt[eng] += 1
        tok = (eng, self.cnt[eng])
        sem = self.esem[eng]
        self.ops[eng].append(lambda e: fn(e).then_inc(sem, 1))
        self._commit(tok, r, w)
        for g in guards:
            g[0], g[1] = tok, eng

    def dma(self, out, in_, r=(), w=(), eng="sp"):
        k = self.dnext
        self.dnext = (self.dnext + 1) % len(self.dsem)
        if self.dval[k]:
            self._wait(eng, (k, self.dval[k]))
        self._deps(eng, r, w)
        self.dval[k] += 16
        tok = (k, self.dval[k])
        sem = self.dsem[k]
        self.ops[eng].append(lambda e: e.dma_start(out=out, in_=in_).then_inc(sem, 16))
        self._commit(tok, r, w)

    def mm(self, out, lhsT, rhs, start, stop, r, w):
        self.op("pe", lambda e: e.matmul(out, lhsT=lhsT, rhs=rhs, start=start, stop=stop), r, w)

    def tr(self, out, in_, ident, r, w):
        self.op("pe", lambda e: e.transpose(out, in_, ident), r, w)

    def act(self, out, in_, func, r, w, bias=None, scale=None, accum=None):
        kw = {}
        if bias is not None:
            kw["bias"] = bias
        if scale is not None:
            kw["scale"] = scale
        if accum is not None:
            kw["accum_out"] = accum
        self.op("act", lambda e: e.activation(out=out, in_=in_, func=func, **kw), r, w)

    def tt(self, eng, out, in0, in1, op, r, w):
        self.op(eng, lambda e: e.tensor_tensor(out=out, in0=in0, in1=in1, op=op), r, w)

    def ts(self, eng, out, in0, s1, op0, r, w, s2=None, op1=None):
        if op1 is None:
            self.op(eng, lambda e: e.tensor_scalar(out=out, in0=in0, scalar1=s1, scalar2=None, op0=op0), r, w)
        else:
            self.op(eng, lambda e: e.tensor_scalar(out=out, in0=in0, scalar1=s1, scalar2=s2, op0=op0, op1=op1), r, w)

    def stt(self, eng, out, in0, scalar, in1, op0, op1, r, w):
        self.op(eng, lambda e: e.scalar_tensor_tensor(out=out, in0=in0, scalar=scalar, in1=in1, op0=op0, op1=op1), r, w)

    def copy(self, eng, out, in_, r, w):
        if eng == "act":
            self.op("act", lambda e: e.activation(out=out, in_=in_, func=AF.Copy), r, w)
        else:
            self.op(eng, lambda e: e.tensor_copy(out=out, in_=in_), r, w)

    def memset(self, eng, out, val, w):
        self.op(eng, lambda e: e.memset(out, val), (), w)

    def barrier(self):
        for e in self.ENG:
            for f in self.ENG:
                if f != e and self.cnt[f]:
                    self._wait(e, (f, self.cnt[f]))
            for k, v in enumerate(self.dval):
                if v:
                    self._wait(e, (k, v))

    def finish(self):
        self.barrier()
        ops = self.ops
        with self.nc.Block() as block:
            @block.tensor
            def _(e):
                for f in ops["pe"]:
                    f(e)

            @block.scalar
            def _(e):
                for f in ops["act"]:
                    f(e)

            @block.vector
            def _(e):
                for f in ops["dve"]:
                    f(e)

            @block.gpsimd
            def _(e):
                for f in ops["pool"]:
                    f(e)

            @block.sync
            def _(e):
                for f in ops["sp"]:
                    f(e)


class Arena:
    def __init__(self, ap, n):
        self.ap, self.n, self.top = ap, n, 0

    def f32(self, n, shape=None):
        a = self.top
        self.top += n
        assert self.top <= self.n, ("SBUF arena overflow", self.top, self.n)
        ap = self.ap[:, a:a + n]
        return T(ap)

    def bf16(self, n):
        t = self.f32((n + 1) // 2)
        t.ap = t.ap.bitcast(BF16)
        return t

    def mark(self):
        return self.top

    def release(self, m):
        self.top = m


def sub(bank, ap):
    t = T(ap)
    t.b = bank.b
    return t


def v3(ap, a):
    return ap.rearrange("p (a b) -> p a b", a=a)


class K:
    def __init__(self, debug=False, nlayers=L, phases=None):
        self.debug = debug
        self.nlayers = nlayers
        self.phases = phases
        nc = self.nc = bass.Bass("TRN2", target_bir_lowering=False)
        self.st = contextlib.ExitStack()
        self.P = Prog(nc, self.st)
        ein = lambda n, s, dt=F32: nc.dram_tensor(n, list(s), dt, kind="ExternalInput").ap()
        self.xT = ein("xT", (D, NT))
        self.cT = ein("cT", (128, 16))
        self.w_mod = ein("w_mod", (L, D, 6 * D))
        self.bmodT = ein("bmodT", (128, L * 48))
        self.n1T = ein("n1T", (128, L * 8))
        self.n2T = ein("n2T", (128, L * 8))
        self.fnT = ein("fnT", (128, 8))
        self.w_in = ein("w_in", (L, D, NW))
        self.rope = ein("rope", (128, 2, TL))
        self.lamp = ein("lamp", (1, L * 128))
        self.dnT = ein("dnT", (128, L))
        self.nab = ein("nab", (L, 4, 128, 3200))
        self.convT = ein("convT", (128, L * 45))
        self.gpar = ein("gpar", (1, L * 24 + L * 64))
        self.gmask = ein("gmask", (128, 2048))
        self.w_out = ein("w_out", (L, D, D))
        self.w_f1 = ein("w_f1", (L, D, 2 * DFF))
        self.w_f2 = ein("w_f2", (L, DFF, D))
        sk = "ExternalOutput" if debug else "Internal"
        scr = lambda n, s, dt=F32: nc.dram_tensor(n, list(s), dt, kind=sk).ap()
        self.xs = scr("xs", (D, NT))
        self.qa = scr("qa", (512, NT), BF16)
        self.ka = scr("ka", (512, NT), BF16)
        self.va = scr("va", (NT, 384), BF16)
        self.qkvb = scr("qkvb", (1152, NT))
        self.gab = scr("gab", (NT, 408))
        self.qc = scr("qc", (256, NT), BF16)
        self.kc = scr("kc", (256, NT), BF16)
        self.vc = scr("vc", (NT, 256), BF16)
        self.yT = scr("yT", (D, NT), BF16)
        self.qkn = scr("qkn", (1152, NT))
        self.of = scr("of", (NT, 384))
        self.outT = nc.dram_tensor("outT", [D, TL], F32, kind="ExternalOutput").ap()
        self.dbufs = {}
        arena_ap = self.st.enter_context(nc.sbuf_tensor("arena", [128, 53200], F32))
        self.A = Arena(arena_ap, 53200)
        self.ps = [T(self.st.enter_context(nc.psum_tensor("ps%d" % i, [128, 512], F32))[:]) for i in range(8)]
        for t in self.ps:
            t.b.g = [None, None]

    def db(self, name):
        return Buf()

    def consts(self):
        P, A = self.P, self.A
        self.ones = A.f32(128)
        P.memset("pool", self.ones.ap, 1.0, [self.ones.b])
        self.cTt = A.f32(16)
        P.dma(self.cTt.ap, self.cT, w=[self.cTt.b])
        self.sc = A.f32(16)
        P.act(self.sc.ap, self.cTt.ap, AF.Silu, [self.cTt.b], [self.sc.b])
        self.mod = A.f32(L * 48 * 2)
        self.bm = A.f32(L * 48)
        P.dma(self.bm.ap, self.bmodT, w=[self.bm.b])
        self.n1 = A.f32(L * 8)
        self.n2 = A.f32(L * 8)
        self.fn = A.f32(8)
        P.dma(self.n1.ap, self.n1T, w=[self.n1.b])
        P.dma(self.n2.ap, self.n2T, w=[self.n2.b])
        P.dma(self.fn.ap, self.fnT, w=[self.fn.b])
        self.cw = A.f32(L * 45)
        P.dma(self.cw.ap, self.convT, w=[self.cw.b])
        self.gm = A.f32(2048)
        P.dma(self.gm.ap, self.gmask, w=[self.gm.b])
        gp = self.gp = A.f32(L * 24 + L * 64)
        P.dma(gp.ap, self.gpar.partition_broadcast(128), w=[gp.b])
        self.nea = A.f32(L * 12)
        P.act(self.nea.ap, gp.ap[:, 0:L * 12], AF.Exp, [gp.b], [self.nea.b])
        P.ts("dve", self.nea.ap, self.nea.ap, -1.0, ALU.mult, [self.nea.b], [self.nea.b])
        self.nlam = A.f32(L)
        self.dnl = A.f32(L)
        self.a1 = A.f32(L * 16)
        self.a2 = A.f32(L * 16)
        mtmp = A.mark()
        lp = A.f32(L * 128)
        P.dma(lp.ap, self.lamp.partition_broadcast(128), w=[lp.b])
        lpv = lp.ap.rearrange("p (l f d) -> p l f d", l=L, f=4)
        pr_ = A.f32(L * 64)
        prv = pr_.ap.rearrange("p (l f d) -> p l f d", l=L, f=2)
        P.tt("dve", prv, lpv[:, :, 0:4:2, :], lpv[:, :, 1:4:2, :], ALU.mult, [lp.b], [pr_.b])
        ee = A.f32(L * 2)
        P.op("dve", lambda e: e.tensor_reduce(out=ee.ap, in_=pr_.ap.rearrange("p (g d) -> p g d", d=32), axis=AX.X, op=ALU.add), [pr_.b], [ee.b])
        P.act(ee.ap, ee.ap, AF.Exp, [ee.b], [ee.b])
        dn = A.f32(L)
        P.dma(dn.ap, self.dnT, w=[dn.b])
        for l in range(L):
            li = 0.8 - 0.6 * math.exp(-0.3 * l)
            P.stt("dve", self.nlam.ap[:, l:l + 1], ee.ap[:, 2 * l + 1:2 * l + 2], -li, ee.ap[:, 2 * l:2 * l + 1], ALU.add, ALU.subtract,
                  [ee.b], [self.nlam.b])
            P.ts("dve", self.dnl.ap[:, l:l + 1], dn.ap[:, l:l + 1], 1.0 - li, ALU.mult, [dn.b], [self.dnl.b])
        P.barrier()
        A.release(mtmp)

    def phase_mod(self):
        P, A = self.P, self.A
        m0 = A.mark()
        wm = [A.f32(8 * 768), A.f32(8 * 768)]
        ps = self.ps[0]
        it = 0
        for l in range(self.nlayers):
            wl = self.w_mod[l].rearrange("(kc p) n -> p kc n", p=128)
            for j in range(8):
                w = wm[it % 2]
                it += 1
                wv = v3(w.ap, 8)
                P.dma(wv, wl[:, :, j * 768:(j + 1) * 768], w=[w.b])
                for ct in range(6):
                    for kc in range(8):
                        P.mm(ps.ap[:, ct * 2:ct * 2 + 2], wv[:, kc, ct * 128:(ct + 1) * 128],
                             self.sc.ap[:, kc * 2:kc * 2 + 2], kc == 0, kc == 7, [w.b, self.sc.b], [ps.b])
                o = (l * 48 + j * 6) * 2
                P.tt("dve", v3(self.mod.ap[:, o:o + 12], 6), v3(ps.ap[:, 0:12], 6),
                     self.bm.ap[:, l * 48 + j * 6:l * 48 + j * 6 + 6].unsqueeze(2).to_broadcast([128, 6, 2]),
                     ALU.add, [ps.b, self.bm.b], [self.mod.b])
        for l in range(self.nlayers):
            for (a, n, which) in ((self.a1, self.n1, 1), (self.a2, self.n2, 4)):
                o = (l * 48 + which * 8) * 2
                P.stt("dve", v3(a.ap[:, l * 16:(l + 1) * 16], 8), v3(self.mod.ap[:, o:o + 16], 8), 1.0,
                      n.ap[:, l * 8:(l + 1) * 8].unsqueeze(2).to_broadcast([128, 8, 2]),
                      ALU.add, ALU.mult, [self.mod.b, n.b], [a.b])
        P.barrier()
        A.release(m0)

    def modv(self, l, which, kc, s):
        o = ((l * 48 + which * 8 + kc) * 2) + s
        return self.mod.ap[:, o:o + 1]

    def load_w_bf16(self, dst, src3, nk, ncols, stg, piece=512):
        P = self.P
        dv = v3(dst.ap, nk)
        i = 0
        for c0 in range(0, ncols, piece):
            c1 = min(ncols, c0 + piece)
            s = stg[i % 2]
            sv = v3(s.ap[:, 0:nk * (c1 - c0)], nk)
            P.dma(sv, src3[:, :, c0:c1], w=[s.b])
            P.copy("pool" if i % 2 else "act", dv[:, :, c0:c1], sv, [s.b], [dst.b])
            i += 1

    def norm_mod(self, xg, N, l, a, which_shift, s, hb, sq, rstd, ps_ss):
        P = self.P
        xv = v3(xg.ap, 8)
        hv = v3(hb.ap, 8)
        for kc in range(8):
            P.act(sq.ap[:, 0:N], xv[:, kc, :], AF.Square, [xg.b], [sq.b])
            P.mm(ps_ss.ap[:, 0:N], self.ones.ap, sq.ap[:, 0:N], kc == 0, kc == 7, [self.ones.b, sq.b], [ps_ss.b])
        P.act(rstd.ap[:, 0:N], ps_ss.ap[:, 0:N], AF.Ln, [ps_ss.b], [rstd.b], bias=EPS, scale=1.0 / D)
        P.act(rstd.ap[:, 0:N], rstd.ap[:, 0:N], AF.Exp, [rstd.b], [rstd.b], scale=-0.5)
        for kc in range(8):
            ao = l * 16 + kc * 2 + s
            P.stt("dve", sq.ap[:, 0:N], xv[:, kc, :], a.ap[:, ao:ao + 1], rstd.ap[:, 0:N], ALU.mult, ALU.mult,
                  [xg.b, a.b, rstd.b], [sq.b])
            P.act(hv[:, kc, :], sq.ap[:, 0:N], AF.Identity, [sq.b, self.mod.b], [hb.b], bias=self.modv(l, which_shift, kc, s), scale=1.0)

    def phase_in(self, l):
        P, A = self.P, self.A
        m0 = A.mark()
        Wb = A.bf16(8 * NW)
        stg = [A.f32(8 * 256), A.f32(8 * 256)]
        self.load_w_bf16(Wb, self.w_in[l].rearrange("(kc p) n -> p kc n", p=128), 8, NW, stg, piece=256)
        Wv = v3(Wb.ap, 8)
        xg = [A.f32(8 * 512), A.f32(8 * 512)]
        hb = [A.bf16(8 * 512), A.bf16(8 * 512)]
        sq = A.f32(512)
        rstd = A.f32(512)
        rp = [A.f32(1024), A.f32(1024)]
        ofm = [A.f32(512) for _ in range(3)]
        otm = [A.f32(408) for _ in range(3)]
        t1 = A.f32(512)
        t2 = A.f32(512)
        xsrc = self.xT if l == 0 else self.xs
        xsrc3 = xsrc.rearrange("(kc p) n -> p kc n", p=128)
        bx = self.db("xs")
        ps_ss = self.ps[0]
        psf = self.ps[1:5]
        pst = self.ps[5:8]
        nf = 0
        ntm = 0

        def load(gi):
            n0, N = GROUPS[gi]
            x = xg[gi % 2]
            P.dma(v3(x.ap[:, 0:8 * N], 8), xsrc3[:, :, n0:n0 + N], r=[bx], w=[x.b])
            if gi > 0:
                r_ = rp[gi % 2]
                P.dma(v3(r_.ap, 2), self.rope[:, :, n0 - NCTX:n0 - NCTX + 512], w=[r_.b])

        load(0)
        for gi, (n0, N) in enumerate(GROUPS):
            if gi + 1 < len(GROUPS):
                load(gi + 1)
            x = xg[gi % 2]
            h = hb[gi % 2]
            s = 1 if gi == 0 else 0
            xin = T(x.ap[:, 0:8 * N]); xin.b = x.b
            hin = T(h.ap[:, 0:8 * N]); hin.b = h.b
            self.norm_mod(xin, N, l, self.a1, 0, s, hin, sq, rstd, ps_ss)
            hv = v3(h.ap[:, 0:8 * N], 8)
            rv = v3(rp[gi % 2].ap, 2)

            def fm(ft, ps):
                for kc in range(8):
                    P.mm(ps.ap[:, 0:N], Wv[:, kc, ft * 128:(ft + 1) * 128], hv[:, kc, :], kc == 0, kc == 7, [Wb.b, h.b], [ps.b])

            for (c0, dst, nm) in ((C_QA, self.qa, "qa"), (C_KA, self.ka, "ka")):
                for j in range(4):
                    o = ofm[nf % 3]
                    ob = o.ap.bitcast(BF16)[:, 0:N]
                    p1 = psf[nf % 4]
                    nf += 1
                    fm(c0 // 128 + j, p1)
                    if gi == 0:
                        P.copy("dve", ob, p1.ap[:, 0:N], [p1.b], [o.b])
                    else:
                        p2 = psf[nf % 4]
                        nf += 1
                        fm(c0 // 128 + 4 + j, p2)
                        P.tt("dve", t1.ap, p1.ap, rv[:, 0, :], ALU.mult, [p1.b, rp[gi % 2].b], [t1.b])
                        P.tt("dve", t2.ap, p2.ap, rv[:, 1, :], ALU.mult, [p2.b, rp[gi % 2].b], [t2.b])
                        P.tt("pool", ob, t1.ap, t2.ap, ALU.add, [t1.b, t2.b], [o.b])
                    P.dma(dst[j * 128:(j + 1) * 128, n0:n0 + N], ob, r=[o.b], w=[self.db(nm)])
            for j in range(9):
                o = ofm[nf % 3]
                p1 = psf[nf % 4]
                nf += 1
                fm(C_B // 128 + j, p1)
                P.copy("act", o.ap[:, 0:N], p1.ap[:, 0:N], [p1.b], [o.b])
                P.dma(self.qkvb[j * 128:(j + 1) * 128, n0:n0 + N], o.ap[:, 0:N], r=[o.b], w=[self.db("qkvb")])
            for (c0, dst, nm) in ((C_QC, self.qc, "qc"), (C_KC, self.kc, "kc")):
                for j in range(2):
                    o = ofm[nf % 3]
                    ob = o.ap.bitcast(BF16)[:, 0:N]
                    p1 = psf[nf % 4]
                    nf += 1
                    fm(c0 // 128 + j, p1)
                    P.copy("dve", ob, p1.ap[:, 0:N], [p1.b], [o.b])
                    P.dma(dst[j * 128:(j + 1) * 128, n0:n0 + N], ob, r=[o.b], w=[self.db(nm)])
            for tt in range(N // 128):
                tok0 = n0 + tt * 128
                for (c0, nc_, dst, nm, isbf) in ((C_VA, 384, self.va, "va", True), (C_VC, 256, self.vc, "vc", True),
                                                 (C_G, 408, self.gab, "gab", False)):
                    ps = pst[ntm % 3]
                    o = otm[ntm % 3]
                    ntm += 1
                    for kc in range(8):
                        P.mm(ps.ap[:, 0:nc_], hv[:, kc, tt * 128:(tt + 1) * 128], Wv[:, kc, c0:c0 + nc_], kc == 0, kc == 7,
                             [Wb.b, h.b], [ps.b])
                    oa = o.ap.bitcast(BF16)[:, 0:nc_] if isbf else o.ap[:, 0:nc_]
                    P.copy("act" if ntm % 2 else "dve", oa, ps.ap[:, 0:nc_], [ps.b], [o.b])
                    P.dma(dst[tok0:tok0 + 128, :], oa, r=[o.b], w=[self.db(nm)])
        P.barrier()
        A.release(m0)


    def phase_a(self, l):
        P, A = self.P, self.A
        m0 = A.mark()
        KAt = A.bf16(4 * NT)
        P.dma(v3(KAt.ap, 4), self.ka.rearrange("(t p) n -> p t n", p=128), w=[KAt.b])
        Kv = v3(KAt.ap, 4)
        VA = A.bf16(NTILE * 6 * 128)
        Vv = VA.ap.rearrange("p (t h d) -> p t h d", t=NTILE, h=6)
        P.memset("pool", VA.ap, 1.0, [VA.b])
        vsrc = self.va.rearrange("(t p) (h d) -> p t h d", p=128, h=6)
        for t0 in range(NTILE):
            P.dma(Vv[:, t0, :, 0:64], vsrc[:, t0, :, :], w=[VA.b])
        Qt = [A.bf16(4 * 512), A.bf16(4 * 512)]
        pT = [A.bf16(512) for _ in range(3)]
        o12 = [A.f32(512), A.f32(512)]
        rz = A.f32(512)
        sq = A.f32(512)
        rs = A.f32(512)
        yb = [A.bf16(512), A.bf16(512)]
        psS = self.ps[0:3]
        psO = self.ps[3:5]
        psN = self.ps[5]
        scale = 32.0 ** -0.5
        qsrc = self.qa.rearrange("(t p) n -> p t n", p=128)
        ny = 0

        def loadq(gi):
            n0, N = GROUPS[gi]
            q = Qt[gi % 2]
            P.dma(v3(q.ap, 4)[:, :, 0:N], qsrc[:, :, n0:n0 + N], w=[q.b])

        loadq(0)
        for gi, (n0, N) in enumerate(GROUPS):
            if gi + 1 < len(GROUPS):
                loadq(gi + 1)
            q = Qt[gi % 2]
            qv = v3(q.ap, 4)
            kts = [0, 1] if gi == 0 else list(range(NTILE))
            items = [(h, m, kt) for h in range(6) for m in range(2) for kt in kts]

            def S(i):
                h, m, kt = items[i]
                hm = 2 * h + m
                t, pr = hm // 3, (hm % 3) * 32
                ps = psS[i % 3]
                P.mm(ps.ap[:, 0:N], Kv[pr:pr + 32, t, kt * 128:(kt + 1) * 128], qv[pr:pr + 32, t, 0:N], True, True, [KAt.b, q.b], [ps.b])

            S(0)
            for i, (h, m, kt) in enumerate(items):
                if i + 1 < len(items):
                    S(i + 1)
                ps = psS[i % 3]
                p = pT[i % 3]
                P.act(p.ap[:, 0:N], ps.ap[:, 0:N], AF.Exp, [ps.b], [p.b], scale=scale)
                po = psO[m]
                P.mm(po.ap[:, 0:N], Vv[:, kt, h, :], p.ap[:, 0:N], kt == kts[0], kt == kts[-1], [VA.b, p.b], [po.b])
                if kt == kts[-1]:
                    o = o12[m]
                    P.op("dve", lambda e, po=po: e.reciprocal(out=rz.ap[0:64, 0:N], in_=po.ap[64:128, 0:N]), [po.b], [rz.b])
                    P.tt("dve", o.ap[0:64, 0:N], po.ap[0:64, 0:N], rz.ap[0:64, 0:N], ALU.mult, [po.b, rz.b], [o.b])
                    if m == 1:
                        o1, o2 = o12
                        P.stt("dve", o1.ap[0:64, 0:N], o2.ap[0:64, 0:N], self.nlam.ap[0:64, l:l + 1], o1.ap[0:64, 0:N], ALU.mult, ALU.add,
                              [o1.b, o2.b, self.nlam.b], [o1.b])
                        P.act(sq.ap[0:64, 0:N], o1.ap[0:64, 0:N], AF.Square, [o1.b], [sq.b])
                        P.mm(psN.ap[0:64, 0:N], self.ones.ap[0:64, 0:64], sq.ap[0:64, 0:N], True, True, [self.ones.b, sq.b], [psN.b])
                        P.act(rs.ap[0:64, 0:N], psN.ap[0:64, 0:N], AF.Ln, [psN.b], [rs.b], bias=EPS, scale=1.0 / 64)
                        P.act(rs.ap[0:64, 0:N], rs.ap[0:64, 0:N], AF.Exp, [rs.b], [rs.b], scale=-0.5)
                        y = yb[ny % 2]
                        ny += 1
                        P.stt("dve", y.ap[0:64, 0:N], o1.ap[0:64, 0:N], self.dnl.ap[0:64, l:l + 1], rs.ap[0:64, 0:N], ALU.mult, ALU.mult,
                              [o1.b, rs.b, self.dnl.b], [y.b])
                        P.dma(self.yT[h * 64:(h + 1) * 64, n0:n0 + N], y.ap[0:64, 0:N], r=[y.b])
        P.barrier()
        A.release(m0)

    def phase_c(self, l):
        P, A = self.P, self.A
        m0 = A.mark()
        KCt = A.bf16(2 * NT)
        QCt = A.bf16(2 * NT)
        P.dma(v3(KCt.ap, 2), self.kc.rearrange("(t p) n -> p t n", p=128), w=[KCt.b])
        P.dma(v3(QCt.ap, 2), self.qc.rearrange("(t p) n -> p t n", p=128), w=[QCt.b])
        Kv, Qv = v3(KCt.ap, 2), v3(QCt.ap, 2)
        VC = A.bf16(NTILE * 4 * 128)
        Vv = VC.ap.rearrange("p (t h d) -> p t h d", t=NTILE, h=4)
        P.memset("pool", VC.ap, 1.0, [VC.b])
        vsrc = self.vc.rearrange("(t p) (h d) -> p t h d", p=128, h=4)
        for t0 in range(NTILE):
            P.dma(Vv[:, t0, :, 0:64], vsrc[:, t0, :, :], w=[VC.b])
        NB = A.f32(4 * 3200)
        NBv = NB.ap.rearrange("p (h a j q) -> p h a j q", h=4, a=5, j=5)
        for h in range(4):
            P.dma(NB.ap[:, h * 3200:(h + 1) * 3200], self.nab[l, h], w=[NB.b])
        sA = [A.f32(640), A.f32(640)]
        pA = [A.bf16(896), A.bf16(896)]
        rz = A.f32(128)
        ys = [A.bf16(256), A.bf16(256)]
        psA = self.ps[0:4]
        psO = self.ps[4:6]
        scale = 64.0 ** -0.5
        it = 0
        for qt in range(NTILE):
            yst = ys[qt % 2]
            ysv = v3(yst.ap, 2)
            if qt < 2:
                lat, pat = [], 0
            else:
                r0 = 2 * (qt - 2)
                base = min(max(r0 - 4, 0), 54)
                pat = (r0 - base) // 2
                lat = [2 + base // 2 + j for j in range(5)]
            kts = lat + [0, 1]
            for h in range(4):
                t, pr = h // 2, (h % 2) * 64
                pa, pb = psA[2 * (it % 2)], psA[2 * (it % 2) + 1]
                s_ = sA[it % 2]
                p_ = pA[it % 2]
                po = psO[it % 2]
                it += 1
                qop = Qv[pr:pr + 64, t, qt * 128:(qt + 1) * 128]
                for j, kt in enumerate(kts):
                    dst = pa.ap[:, j * 128:(j + 1) * 128] if j < 4 else pb.ap[:, (j - 4) * 128:(j - 3) * 128]
                    P.mm(dst, Kv[pr:pr + 64, t, kt * 128:(kt + 1) * 128], qop, True, True, [KCt.b, QCt.b], [pa.b if j < 4 else pb.b])
                nk = len(kts)
                if lat:
                    P.stt("dve", s_.ap[:, 0:512], pa.ap[:, 0:512], scale, NBv[:, h, pat, 0:4, :].rearrange("p j q -> p (j q)"), ALU.mult, ALU.add,
                          [pa.b, NB.b], [s_.b])
                    P.stt("dve", s_.ap[:, 512:640], pb.ap[:, 0:128], scale, NBv[:, h, pat, 4, :], ALU.mult, ALU.add,
                          [pb.b, NB.b], [s_.b])
                    P.act(p_.ap[:, 0:640], s_.ap[:, 0:640], AF.Exp, [s_.b], [p_.b])
                    P.act(p_.ap[:, 640:896], pb.ap[:, 128:384], AF.Exp, [pb.b], [p_.b], scale=scale)
                else:
                    P.act(p_.ap[:, 0:256], pa.ap[:, 0:256], AF.Exp, [pa.b], [p_.b], scale=scale)
                for j, kt in enumerate(kts):
                    P.mm(po.ap[:, 0:128], Vv[:, kt, h, :], p_.ap[:, j * 128:(j + 1) * 128], j == 0, j == nk - 1, [VC.b, p_.b], [po.b])
                P.op("dve", lambda e, po=po: e.reciprocal(out=rz.ap[0:64, :], in_=po.ap[64:128, 0:128]), [po.b], [rz.b])
                P.tt("dve", ysv[pr:pr + 64, t, :], po.ap[0:64, 0:128], rz.ap[0:64, :], ALU.mult, [po.b, rz.b], [yst.b])
            P.dma(self.yT[768:1024, qt * 128:(qt + 1) * 128].rearrange("(t p) n -> p t n", p=128), ysv, r=[yst.b])
        P.barrier()
        A.release(m0)

    def gmk(self, i, ncol=128):
        return self.gm.ap[:, i * 128:i * 128 + ncol]

    def phase_b1(self, l):
        P, A = self.P, self.A
        m0 = A.mark()
        xb = [A.f32(516) for _ in range(3)]
        acc = [A.f32(512) for _ in range(2)]
        sl = [A.f32(512) for _ in range(2)]
        sq = A.f32(512)
        rs = A.f32(512)
        ob = [A.f32(512) for _ in range(2)]
        ps = self.ps[0:2]
        onesblk = self.gmk(1)
        it = 0
        for ft in range(9):
            for (n0, N) in GROUPS:
                s0, s1 = (0, NCTX) if n0 < NCTX else (NCTX, NT)
                x = xb[it % 3]
                a = acc[it % 2]
                sv = sl[it % 2]
                o = ob[it % 2]
                p_ = ps[it % 2]
                eng = "dve"
                it += 1
                lo, hi = max(n0 - 2, s0), min(n0 + N + 2, s1)
                if lo > n0 - 2:
                    P.memset("pool", x.ap[:, 0:2], 0.0, [x.b])
                if hi < n0 + N + 2:
                    P.memset("pool", x.ap[:, N + 2:N + 4], 0.0, [x.b])
                P.dma(x.ap[:, lo - (n0 - 2):hi - (n0 - 2)], self.qkvb[ft * 128:(ft + 1) * 128, lo:hi], w=[x.b])
                co = l * 45 + ft * 5
                P.ts(eng, a.ap[:, 0:N], x.ap[:, 0:N], self.cw.ap[:, co:co + 1], ALU.mult, [x.b, self.cw.b], [a.b])
                for j in range(1, 5):
                    P.stt(eng, a.ap[:, 0:N], x.ap[:, j:j + N], self.cw.ap[:, co + j:co + j + 1], a.ap[:, 0:N], ALU.mult, ALU.add,
                          [x.b, self.cw.b, a.b], [a.b])
                if ft < 6:
                    P.act(sv.ap[:, 0:N], a.ap[:, 0:N], AF.Silu, [a.b], [sv.b])
                    P.tt("pool", sq.ap[:, 0:N], sv.ap[:, 0:N], sv.ap[:, 0:N], ALU.mult, [sv.b], [sq.b])
                    P.mm(p_.ap[:, 0:N], onesblk, sq.ap[:, 0:N], True, True, [self.gm.b, sq.b], [p_.b])
                    P.act(rs.ap[:, 0:N], p_.ap[:, 0:N], AF.Ln, [p_.b], [rs.b], bias=EPS, scale=1.0)
                    P.act(rs.ap[:, 0:N], rs.ap[:, 0:N], AF.Exp, [rs.b], [rs.b], scale=-0.5)
                    P.stt("dve", o.ap[:, 0:N], sv.ap[:, 0:N], 0.125 if ft < 3 else 1.0, rs.ap[:, 0:N], ALU.mult, ALU.mult, [sv.b, rs.b], [o.b])
                else:
                    P.act(o.ap[:, 0:N], a.ap[:, 0:N], AF.Silu, [a.b], [o.b])
                P.dma(self.qkn[ft * 128:(ft + 1) * 128, n0:n0 + N], o.ap[:, 0:N], r=[o.b])
        P.barrier()
        A.release(m0)

    def phase_b2(self, l, d):
        P, A = self.P, self.A
        m0 = A.mark()
        ident = self.gmk(0)
        A_d, B_d, MS_d, MIT_d, SelEnd = self.gmk(2 + d), self.gmk(4 + d), self.gmk(6 + d), self.gmk(8 + d), self.gmk(10 + d)
        gmb = self.gm.b
        f = A.f32
        gabt = [f(408), f(408)]
        fmt = [f(9 * 128), f(9 * 128)]
        oft = [f(384), f(384)]
        z, e_, g, eb, beta, nbeta, gc, eg, dgl, ek, bg = [f(6) for _ in range(11)]
        egl = f(12)
        ktm, vtm, ktail, kbg, vb = [f(384) for _ in range(5)]
        Ag, Bg, dg, Es, EiT, aqkT, TT, wT, qhT = [[f(384), f(384)] for _ in range(9)]
        Pm = [[f(384), f(384)], [f(384), f(384)]]
        PTm = [[f(384), f(384)], [f(384), f(384)]]
        u = [f(192), f(192)]
        vnew = [f(192), f(192)]
        S = [f(192), f(192)]
        otile = [f(384), f(384)]
        sq = f(384)
        ss = f(6)
        rs = f(6)
        sg = f(384)
        ybf = A.bf16(384)
        for hb in range(2):
            P.memset("pool", S[hb].ap[0:64, :], 0.0, [S[hb].b])
        X0, X1, X2, X3 = self.ps[0:4]
        b4, b5, b6, b7 = self.ps[4:8]
        psU = sub(b4, b4.ap[:, 0:192]); psW = b5; psE = b6
        psSv = sub(b4, b4.ap[:, 192:384]); psSo = sub(b7, b7.ap[:, 0:192]); psSs = sub(b7, b7.ap[:, 192:384])
        psK = X0; psV = X1
        psSo2 = sub(X2, X2.ap[:, 0:192])
        tmpo = f(192)
        psSc = sub(b4, b4.ap[:, 384:512])
        dtb = self.gp.ap[:, L * 12 + l * 12 + d * 6:L * 12 + l * 12 + d * 6 + 6]
        nea = self.nea.ap[:, l * 12 + d * 6:l * 12 + d * 6 + 6]
        gw = self.gp.ap[:, L * 24 + l * 64:L * 24 + (l + 1) * 64]
        qkn3 = self.qkn.rearrange("(f p) n -> p f n", p=128)
        order = list(range(NTILE)) if d == 0 else [1, 0] + list(range(NTILE - 1, 1, -1))
        chunks = (0, 1) if d == 0 else (1, 0)
        stage = getattr(self, "b2_stage", 99)
        if getattr(self, "b2_tiles", None):
            order = order[:self.b2_tiles]

        def load(i):
            tt = order[i]
            tok0 = tt * 128
            P.dma(gabt[i % 2].ap, self.gab[tok0:tok0 + 128, :], w=[gabt[i % 2].b])
            P.dma(v3(fmt[i % 2].ap, 9), qkn3[:, :, tok0:tok0 + 128], w=[fmt[i % 2].b])
            if d == 1:
                P.dma(oft[i % 2].ap, self.of[tok0:tok0 + 128, :], w=[oft[i % 2].b])

        bc3 = lambda ap, n: ap.unsqueeze(2).to_broadcast([128, ap.shape[1], n])
        mb3 = lambda ap, h: ap.unsqueeze(1).to_broadcast([128, h, 128])
        load(0)
        for i, tt in enumerate(order):
            if i + 1 < len(order):
                load(i + 1)
            tok0 = tt * 128
            ga = gabt[i % 2]
            fm = fmt[i % 2]
            fmv = v3(fm.ap, 9)
            ot = otile[i % 2]
            P.tt("dve", z.ap, ga.ap[:, 384 + 6 * d:390 + 6 * d], dtb, ALU.add, [ga.b, self.gp.b], [z.b])
            P.act(e_.ap, z.ap, AF.Exp, [z.b], [e_.b])
            P.act(e_.ap, e_.ap, AF.Ln, [e_.b], [e_.b], bias=1.0)
            P.tt("dve", g.ap, e_.ap, nea, ALU.mult, [e_.b, self.nea.b], [g.b])
            P.act(eb.ap, ga.ap[:, 396 + 6 * d:402 + 6 * d], AF.Exp, [ga.b], [eb.b], scale=-1.0)
            P.ts("dve", eb.ap, eb.ap, 1.0, ALU.add, [eb.b], [eb.b])
            P.op("dve", lambda e: e.reciprocal(out=beta.ap, in_=eb.ap), [eb.b], [beta.b])
            P.ts("dve", nbeta.ap, beta.ap, -1.0, ALU.mult, [beta.b], [nbeta.b])
            P.mm(psSc.ap[:, 0:6], A_d, g.ap, True, True, [gmb, g.b], [psSc.b])
            P.copy("dve", gc.ap, psSc.ap[:, 0:6], [psSc.b], [gc.b])
            P.act(eg.ap, gc.ap, AF.Exp, [gc.b], [eg.b])
            P.mm(psSc.ap[:, 8:14], SelEnd, gc.ap, True, True, [gmb, gc.b], [psSc.b])
            P.tt("dve", dgl.ap, psSc.ap[:, 8:14], gc.ap, ALU.subtract, [psSc.b, gc.b], [dgl.b])
            P.act(ek.ap, dgl.ap, AF.Exp, [dgl.b], [ek.b])
            P.tt("dve", bg.ap, beta.ap, eg.ap, ALU.mult, [beta.b, eg.b], [bg.b])
            for c in range(2):
                P.mm(psSc.ap[0:64, 16 + c * 6:22 + c * 6], self.gmk(12 + 2 * d + c, 64), gc.ap, True, True, [gmb, gc.b], [psSc.b])
            P.act(egl.ap[0:64, :], psSc.ap[0:64, 16:28], AF.Exp, [psSc.b], [egl.b])
            if stage < 2:
                continue
            for j in range(3):
                P.tr(psK.ap[:, j * 128:(j + 1) * 128], fmv[:, 3 + j, :], ident, [fm.b, gmb], [psK.b])
            P.copy("act", ktm.ap, psK.ap[:, 0:384], [psK.b], [ktm.b])
            for j in range(3):
                P.tr(psV.ap[:, j * 128:(j + 1) * 128], fmv[:, 6 + j, :], ident, [fm.b, gmb], [psV.b])
            P.copy("dve", vtm.ap, psV.ap[:, 0:384], [psV.b], [vtm.b])
            k3, v3_ = v3(ktm.ap, 6), v3(vtm.ap, 6)
            if stage < 3:
                continue
            P.tt("pool", v3(ktail.ap, 6), k3, bc3(ek.ap, 64), ALU.mult, [ktm.b, ek.b], [ktail.b])
            P.tt("pool", v3(kbg.ap, 6), k3, bc3(bg.ap, 64), ALU.mult, [ktm.b, bg.b], [kbg.b])
            P.tt("pool", v3(vb.ap, 6), v3_, bc3(beta.ap, 64), ALU.mult, [vtm.b, beta.b], [vb.b])
            for hb in range(2):
                if stage < 4:
                    continue
                H = [3 * hb, 3 * hb + 1, 3 * hb + 2]
                hs = slice(3 * hb, 3 * hb + 3)
                ag, bgm, dgm, es, eit, aq, tt_, w_, qh = Ag[hb], Bg[hb], dg[hb], Es[hb], EiT[hb], aqkT[hb], TT[hb], wT[hb], qhT[hb]
                P.tt("pool", v3(ag.ap, 3), mb3(A_d, 3), bc3(g.ap[:, hs], 128), ALU.mult, [gmb, g.b], [ag.b])
                P.tt("pool", v3(bgm.ap, 3), mb3(B_d, 3), bc3(g.ap[:, hs], 128), ALU.mult, [gmb, g.b], [bgm.b])
                P.tt("pool", v3(dgm.ap, 3), mb3(ident, 3), bc3(eg.ap[:, hs], 128), ALU.mult, [gmb, eg.b], [dgm.b])
                for k, h in enumerate(H):
                    hp, hr = h // 2, (h % 2) * 64
                    kT = fmv[hr:hr + 64, 3 + hp, :]
                    qT = fmv[hr:hr + 64, hp, :]
                    ks = slice(k * 128, (k + 1) * 128)
                    P.mm(X0.ap[:, ks], kT, kT, True, True, [fm.b], [X0.b])
                    P.mm(X1.ap[:, ks], kT, qT, True, True, [fm.b], [X1.b])
                    P.mm(X2.ap[:, ks], ag.ap[:, ks], B_d, True, False, [ag.b, gmb], [X2.b])
                    P.mm(X2.ap[:, ks], ident, MS_d, False, True, [gmb], [X2.b])
                    P.mm(X3.ap[:, ks], bgm.ap[:, ks], A_d, True, False, [bgm.b, gmb], [X3.b])
                    P.mm(X3.ap[:, ks], ident, MIT_d, False, True, [gmb], [X3.b])
                if stage < 5:
                    continue
                P.act(es.ap, X2.ap[:, 0:384], AF.Exp, [X2.b], [es.b])
                P.act(eit.ap, X3.ap[:, 0:384], AF.Exp, [X3.b], [eit.b])
                p0, pt0 = Pm[hb][0], PTm[hb][0]
                if stage == 5 and getattr(self, "b2_sub", 9) < 2:
                    continue
                for k, h in enumerate(H):
                    ks = slice(k * 128, (k + 1) * 128)
                    P.stt("dve", p0.ap[:, ks], X0.ap[:, ks], nbeta.ap[:, h:h + 1], es.ap[:, ks], ALU.mult, ALU.mult, [X0.b, es.b, nbeta.b], [p0.b])
                P.tt("dve", aq.ap, X1.ap[:, 0:384], eit.ap, ALU.mult, [X1.b, eit.b], [aq.b])
                if stage == 5 and getattr(self, "b2_sub", 9) < 3:
                    continue
                for k in range(3):
                    ks = slice(k * 128, (k + 1) * 128)
                    P.tr(X0.ap[:, ks], p0.ap[:, ks], ident, [p0.b, gmb], [X0.b])
                if stage == 5 and getattr(self, "b2_sub", 9) < 4:
                    continue
                P.copy("act", pt0.ap, X0.ap[:, 0:384], [X0.b], [pt0.b])
                for k in range(3):
                    ks = slice(k * 128, (k + 1) * 128)
                    P.tt("dve", tt_.ap[:, ks], X0.ap[:, ks], ident, ALU.add, [X0.b, gmb], [tt_.b])
                pc, ptc = p0, pt0
                if stage < 6:
                    continue
                for lvl in range(5):
                    pn, ptn = Pm[hb][(lvl + 1) % 2], PTm[hb][(lvl + 1) % 2]
                    for k in range(3):
                        ks = slice(k * 128, (k + 1) * 128)
                        P.mm(X1.ap[:, ks], ptc.ap[:, ks], pc.ap[:, ks], True, True, [ptc.b, pc.b], [X1.b])
                    if lvl < 4:
                        for k in range(3):
                            ks = slice(k * 128, (k + 1) * 128)
                            P.mm(X2.ap[:, ks], pc.ap[:, ks], ptc.ap[:, ks], True, True, [ptc.b, pc.b], [X2.b])
                    P.copy("act", pn.ap, X1.ap[:, 0:384], [X1.b], [pn.b])
                    if lvl < 4:
                        P.copy("dve", ptn.ap, X2.ap[:, 0:384], [X2.b], [ptn.b])
                    for k in range(3):
                        ks = slice(k * 128, (k + 1) * 128)
                        P.mm(X3.ap[:, ks], pn.ap[:, ks], tt_.ap[:, ks], True, True, [pn.b, tt_.b], [X3.b])
                    P.tt("dve", tt_.ap, tt_.ap, X3.ap[:, 0:384], ALU.add, [tt_.b, X3.b], [tt_.b])
                    pc, ptc = pn, ptn
                if stage < 7:
                    continue
                for k, h in enumerate(H):
                    hp, hr = h // 2, (h % 2) * 64
                    ks = slice(k * 128, (k + 1) * 128)
                    P.mm(psU.ap[:, k * 64:(k + 1) * 64], tt_.ap[:, ks], vb.ap[:, h * 64:(h + 1) * 64], True, True, [tt_.b, vb.b], [psU.b])
                    P.mm(psW.ap[0:64, ks], kbg.ap[:, h * 64:(h + 1) * 64], tt_.ap[:, ks], True, True, [tt_.b, kbg.b], [psW.b])
                    P.mm(psE.ap[0:64, ks], self.ones.ap[:, 0:64], dgm.ap[:, ks], True, True, [self.ones.b, dgm.b], [psE.b])
                P.copy("act", u[hb].ap, psU.ap[:, 0:192], [psU.b], [u[hb].b])
                P.copy("dve", w_.ap[0:64, :], psW.ap[0:64, 0:384], [psW.b], [w_.b])
                for k, h in enumerate(H):
                    hp, hr = h // 2, (h % 2) * 64
                    ks = slice(k * 128, (k + 1) * 128)
                    P.tt("dve", qh.ap[0:64, ks], fmv[hr:hr + 64, hp, :], psE.ap[0:64, ks], ALU.mult, [fm.b, psE.b], [qh.b])
                if stage < 8:
                    continue
                Sb = S[hb]
                vn = vnew[hb]
                for c in chunks:
                    cs = c * 64
                    for k, h in enumerate(H):
                        ks = slice(k * 128, (k + 1) * 128)
                        P.mm(psSv.ap[:, k * 64:(k + 1) * 64], w_.ap[0:64, ks], Sb.ap[0:64, k * 64:(k + 1) * 64], True, True, [w_.b, Sb.b], [psSv.b])
                    P.tt("dve", vn.ap[cs:cs + 64, :], u[hb].ap[cs:cs + 64, :], psSv.ap[cs:cs + 64, :], ALU.subtract, [u[hb].b, psSv.b], [vn.b])
                    for k, h in enumerate(H):
                        ks = slice(k * 128, (k + 1) * 128)
                        k6 = slice(k * 64, (k + 1) * 64)
                        P.mm(psSo.ap[:, k6], qh.ap[0:64, ks], Sb.ap[0:64, k6], True, True, [qh.b, Sb.b], [psSo.b])
                        P.mm(psSo2.ap[:, k6], aq.ap[cs:cs + 64, ks], vn.ap[cs:cs + 64, k6], True, True, [aq.b, vn.b], [psSo2.b])
                        P.mm(psSs.ap[0:64, k6], ktail.ap[cs:cs + 64, h * 64:(h + 1) * 64], vn.ap[cs:cs + 64, k6], True, True, [ktail.b, vn.b], [psSs.b])
                    P.copy("act", tmpo.ap[cs:cs + 64, :], psSo2.ap[cs:cs + 64, :], [psSo2.b], [tmpo.b])
                    P.tt("dve", ot.ap[cs:cs + 64, hb * 192:(hb + 1) * 192], psSo.ap[cs:cs + 64, :], tmpo.ap[cs:cs + 64, :], ALU.add,
                         [psSo.b, tmpo.b], [ot.b])
                    for k, h in enumerate(H):
                        k6 = slice(k * 64, (k + 1) * 64)
                        P.stt("dve", Sb.ap[0:64, k6], Sb.ap[0:64, k6], egl.ap[0:64, c * 6 + h:c * 6 + h + 1], psSs.ap[0:64, k6], ALU.mult, ALU.add,
                              [Sb.b, egl.b, psSs.b], [Sb.b])
            if stage < 9:
                continue
            if d == 0:
                P.dma(self.of[tok0:tok0 + 128, :], ot.ap, r=[ot.b])
            else:
                of_ = oft[i % 2]
                P.tt("pool", sg.ap, ot.ap, of_.ap, ALU.add, [ot.b, of_.b], [sg.b])
                P.tt("pool", sq.ap, sg.ap, sg.ap, ALU.mult, [sg.b], [sq.b])
                P.op("dve", lambda e: e.tensor_reduce(out=ss.ap, in_=v3(sq.ap, 6), axis=AX.X, op=ALU.add), [sq.b], [ss.b])
                P.act(rs.ap, ss.ap, AF.Ln, [ss.b], [rs.b], bias=EPS, scale=1.0 / 64)
                P.act(rs.ap, rs.ap, AF.Exp, [rs.b], [rs.b], scale=-0.5)
                P.tt("dve", v3(sq.ap, 6), v3(sg.ap, 6), bc3(rs.ap, 64), ALU.mult, [sg.b, rs.b], [sq.b])
                P.tt("pool", v3(ot.ap, 6), v3(sq.ap, 6), gw.unsqueeze(1).to_broadcast([128, 6, 64]), ALU.mult, [sq.b, self.gp.b], [ot.b])
                P.act(sg.ap, ga.ap[:, 0:384], AF.Silu, [ga.b], [sg.b])
                P.tt("dve", sq.ap, ot.ap, sg.ap, ALU.mult, [ot.b, sg.b], [sq.b])
                for j in range(3):
                    P.tr(psK.ap[:, j * 128:(j + 1) * 128], sq.ap[:, j * 128:(j + 1) * 128], ident, [sq.b, gmb], [psK.b])
                P.copy("act", ybf.ap, psK.ap[:, 0:384], [psK.b], [ybf.b])
                P.dma(self.yT[384:768, tok0:tok0 + 128].rearrange("(j p) n -> p j n", p=128), v3(ybf.ap, 3), r=[ybf.b])
        P.barrier()
        A.release(m0)

    def phase_out(self, l):
        P, A = self.P, self.A
        m0 = A.mark()
        last = (l == self.nlayers - 1) and self.nlayers == L
        Wo = A.bf16(8 * D)
        W1 = A.bf16(8 * 2 * DFF)
        W2 = A.bf16(22 * D)
        stg = [A.f32(1408), A.f32(1408)]
        self.load_w_bf16(Wo, self.w_out[l].rearrange("(kc p) n -> p kc n", p=128), 8, D, stg, piece=176)
        self.load_w_bf16(W1, self.w_f1[l].rearrange("(kc p) n -> p kc n", p=128), 8, 2 * DFF, stg, piece=176)
        self.load_w_bf16(W2, self.w_f2[l].rearrange("(kc p) n -> p kc n", p=128), 22, D, stg, piece=64)
        Wov, W1v, W2v = v3(Wo.ap, 8), v3(W1.ap, 8), v3(W2.ap, 22)
        N = 256
        xg = A.f32(8 * N)
        yg = [A.bf16(8 * N), A.bf16(8 * N)]
        hb = A.bf16(8 * N)
        actb = A.bf16(22 * N)
        sq = A.f32(N)
        rstd = A.f32(N)
        sgt = [A.f32(N)]
        xsrc3 = (self.xT if l == 0 else self.xs).rearrange("(kc p) n -> p kc n", p=128)
        xdst3 = self.xs.rearrange("(kc p) n -> p kc n", p=128)
        y3 = self.yT.rearrange("(kc p) n -> p kc n", p=128)
        o3 = self.outT.rearrange("(kc p) n -> p kc n", p=128)
        ps_ss = self.ps[0]
        psr = self.ps[1:4]
        psg = self.ps[4:6]
        psu = self.ps[6:8]
        ngr = NT // N
        xv = v3(xg.ap, 8)
        hv = v3(hb.ap, 8)
        av = v3(actb.ap, 22)
        nr = 0

        def loady(gi):
            P.dma(v3(yg[gi % 2].ap, 8), y3[:, :, gi * N:(gi + 1) * N], w=[yg[gi % 2].b])

        loady(0)
        for gi in range(ngr):
            n0 = gi * N
            s = 1 if gi == 0 else 0
            if gi + 1 < ngr:
                loady(gi + 1)
            P.dma(xv, xsrc3[:, :, n0:n0 + N], w=[xg.b])
            y = yg[gi % 2]
            yv = v3(y.ap, 8)
            for dt in range(8):
                ps = psr[nr % 3]
                nr += 1
                for kc in range(8):
                    P.mm(ps.ap[:, 0:N], Wov[:, kc, dt * 128:(dt + 1) * 128], yv[:, kc, :], kc == 0, kc == 7, [Wo.b, y.b], [ps.b])
                P.stt("dve", xv[:, dt, :], ps.ap[:, 0:N], self.modv(l, 2, dt, s), xv[:, dt, :], ALU.mult, ALU.add, [ps.b, self.mod.b, xg.b], [xg.b])
            self.norm_mod(xg, N, l, self.a2, 3, s, hb, sq, rstd, ps_ss)
            for ft in range(22):
                pg, pu = psg[ft % 2], psu[ft % 2]
                for kc in range(8):
                    P.mm(pg.ap[:, 0:N], W1v[:, kc, ft * 128:(ft + 1) * 128], hv[:, kc, :], kc == 0, kc == 7, [W1.b, hb.b], [pg.b])
                for kc in range(8):
                    P.mm(pu.ap[:, 0:N], W1v[:, kc, DFF + ft * 128:DFF + (ft + 1) * 128], hv[:, kc, :], kc == 0, kc == 7, [W1.b, hb.b], [pu.b])
                sg = sgt[0]
                P.act(sg.ap, pg.ap[:, 0:N], AF.Silu, [pg.b], [sg.b])
                P.tt("dve", av[:, ft, :], pu.ap[:, 0:N], sg.ap, ALU.mult, [pu.b, sg.b], [actb.b])
            for dt in range(8):
                ps = psr[nr % 3]
                nr += 1
                for ft in range(22):
                    P.mm(ps.ap[:, 0:N], W2v[:, ft, dt * 128:(dt + 1) * 128], av[:, ft, :], ft == 0, ft == 21, [W2.b, actb.b], [ps.b])
                P.stt("dve", xv[:, dt, :], ps.ap[:, 0:N], self.modv(l, 5, dt, s), xv[:, dt, :], ALU.mult, ALU.add, [ps.b, self.mod.b, xg.b], [xg.b])
            if not last:
                P.dma(xdst3[:, :, n0:n0 + N], xv, r=[xg.b])
            elif gi >= 1:
                for kc in range(8):
                    P.act(sq.ap, xv[:, kc, :], AF.Square, [xg.b], [sq.b])
                    P.mm(ps_ss.ap[:, 0:N], self.ones.ap, sq.ap, kc == 0, kc == 7, [self.ones.b, sq.b], [ps_ss.b])
                P.act(rstd.ap, ps_ss.ap[:, 0:N], AF.Ln, [ps_ss.b], [rstd.b], bias=EPS, scale=1.0 / D)
                P.act(rstd.ap, rstd.ap, AF.Exp, [rstd.b], [rstd.b], scale=-0.5)
                for kc in range(8):
                    P.stt("dve", xv[:, kc, :], xv[:, kc, :], self.fn.ap[:, kc:kc + 1], rstd.ap, ALU.mult, ALU.mult, [xg.b, self.fn.b, rstd.b], [xg.b])
                P.dma(o3[:, :, n0 - NCTX:n0 - NCTX + N], xv, r=[xg.b])
        P.barrier()
        A.release(m0)

    def build(self):
        self.consts()
        self.phase_mod()
        ph = self.phases
        for l in range(self.nlayers):
            if ph is None or "in" in ph:
                self.phase_in(l)
            if ph is None or "a" in ph:
                self.phase_a(l)
            if ph is None or "c" in ph:
                self.phase_c(l)
            if ph is None or "b1" in ph:
                self.phase_b1(l)
            if ph is None or "b2" in ph:
                self.phase_b2(l, 0)
                self.phase_b2(l, 1)
            if ph is None or "out" in ph:
                self.phase_out(l)
        self.P.finish()
        self.st.close()
        return self.nc


def rope_tables():
    half = 16
    inv_freq = (1.0 / (10000.0 ** (np.arange(0, half, 2, dtype=np.float32) / np.float32(half)))).astype(np.float32)
    t = np.arange(TL, dtype=np.int32)
    ang_r = (t // 64).astype(np.float32)[:, None] * inv_freq
    ang_c = (t % 64).astype(np.float32)[:, None] * inv_freq
    ang = np.concatenate([ang_r, ang_r, ang_c, ang_c], axis=-1)
    cos = np.cos(ang).astype(np.float32)
    sin = np.sin(ang).astype(np.float32)
    sign = np.array([-1.0] * 8 + [1.0] * 8 + [-1.0] * 8 + [1.0] * 8, np.float32)
    tab = np.stack([cos.T, (sin * sign).T], axis=1)
    return np.ascontiguousarray(np.tile(tab, (4, 1, 1)))


def na_bias_tiles(nb):
    out = np.full((L, 4, 5, 5, 128, 128), -30000.0, np.float32)
    qi = np.arange(128)
    ki = np.arange(128)
    for pat, (r0, base) in enumerate(((0, 0), (2, 0), (4, 0), (60, 54), (62, 54))):
        r = r0 + qi // 64
        c = qi % 64
        rs = np.clip(r - 4, 0, 56)
        cs = np.clip(c - 8, 0, 48)
        for j in range(5):
            kr = base + 2 * j + ki // 64
            kc = ki % 64
            valid = ((kr[:, None] >= rs[None, :]) & (kr[:, None] < rs[None, :] + 8) &
                     (kc[:, None] >= cs[None, :]) & (kc[:, None] < cs[None, :] + 16))
            dr = np.clip(kr[:, None] - r[None, :] + 7, 0, 14)
            dc = np.clip(kc[:, None] - c[None, :] + 15, 0, 30)
            g = nb[:, :, dr, dc]
            out[:, :, pat, j] = np.where(valid[None, None], g, np.float32(-30000.0))
    return np.ascontiguousarray(out.transpose(0, 1, 4, 2, 3, 5).reshape(L, 4, 128, 3200))


def gdn_masks():
    m = np.zeros((16, 128, 128), np.float32)
    i = np.arange(128)
    same = (i[:, None] // 64) == (i[None, :] // 64)
    r, c = i[:, None], i[None, :]
    m[0] = np.eye(128)
    m[1] = same
    m[2] = same & (r <= c)
    m[3] = same & (r >= c)
    m[4] = same & (r > c)
    m[5] = same & (r < c)
    m[6] = np.where(same & (r > c), 0.0, -30000.0)
    m[7] = np.where(same & (r < c), 0.0, -30000.0)
    m[8] = np.where(same & (c >= r), 0.0, -30000.0)
    m[9] = np.where(same & (c <= r), 0.0, -30000.0)
    endf = (i // 64) * 64 + 63
    endb = (i // 64) * 64
    m[10] = (r == endf[None, :])
    m[11] = (r == endb[None, :])
    for d_, ends in enumerate(((63, 127), (0, 64))):
        for c_ in range(2):
            m[12 + 2 * d_ + c_][ends[c_], :] = 1.0
    return np.ascontiguousarray(m.transpose(1, 0, 2).reshape(128, 2048))


def host_prep(inp):
    f = lambda a: np.ascontiguousarray(a, dtype=np.float32)
    w_in = inp["w_in"]
    sizes = (1152, 1152, 384, 12, 12, 768)
    offs = np.cumsum((0,) + sizes)
    a0, b0, g0, al0, be0, c0 = offs[:6]
    perm32 = np.concatenate([np.arange(8, 16), np.arange(0, 8), np.arange(24, 32), np.arange(16, 24)])
    pad = lambda m: np.concatenate([m.reshape(4, 96), m.reshape(4, 96)[:, :32]], axis=1).reshape(-1)
    idq = pad(np.arange(384))
    permq = pad((np.arange(384).reshape(12, 32)[:, perm32]).reshape(-1))
    cols = np.concatenate([
        a0 + idq, a0 + permq, a0 + 384 + idq, a0 + 384 + permq,
        b0 + np.arange(1152), c0 + np.arange(256), c0 + 256 + np.arange(256),
        a0 + 768 + np.arange(384), c0 + 512 + np.arange(256), g0 + np.arange(384), al0 + np.arange(12), be0 + np.arange(12)])
    assert cols.shape[0] == NW
    shared = {
        "w_mod": f(inp["w_mod"]),
        "bmodT": f(inp["b_mod"].reshape(L, 48, 128).transpose(2, 0, 1).reshape(128, L * 48)),
        "n1T": f(inp["norm1_w"].reshape(L, 8, 128).transpose(2, 0, 1).reshape(128, L * 8)),
        "n2T": f(inp["norm2_w"].reshape(L, 8, 128).transpose(2, 0, 1).reshape(128, L * 8)),
        "fnT": f(inp["final_norm_w"].reshape(8, 128).T),
        "w_in": f(w_in[:, :, cols]),
        "rope": rope_tables(),
        "lamp": f(np.stack([inp["lambda_q1"], inp["lambda_k1"], inp["lambda_q2"], inp["lambda_k2"]], axis=1).reshape(1, L * 128)),
        "dnT": f(np.tile(inp["diff_norm_w"].T, (2, 1))),
        "nab": na_bias_tiles(inp["na_bias"]),
        "convT": f(inp["conv_w"].reshape(L, 5, 9, 128).transpose(3, 0, 2, 1).reshape(128, L * 45)),
        "gpar": f(np.concatenate([inp["a_log"].reshape(-1), inp["dt_bias"].reshape(-1), inp["gdn_norm_w"].reshape(-1)])[None, :]),
        "gmask": gdn_masks(),
        "w_out": f(inp["w_out"]),
        "w_f1": f(inp["w_ffn_in"]),
        "w_f2": f(inp["w_ffn_out"]),
    }
    per = []
    for b in range(4):
        xT = f(np.concatenate([inp["ctx"][b], inp["x"][b]], axis=0).T)
        cT = f(np.stack([inp["c"][b].reshape(8, 128).T, inp["c_ctx"].reshape(8, 128).T], axis=2).reshape(128, 16))
        m = dict(shared)
        m["xT"] = xT
        m["cT"] = cT
        per.append(m)
    return per


def kernel(**inputs):
    inp = {k: np.asarray(v) for k, v in inputs.items()}
    per = host_prep(inp)
    nc = K().build()
    in_maps = [per[i % 4] for i in range(8)]
    res = run_bass_kernel_spmd(nc, in_maps, core_ids=list(range(8)))
    out = np.stack([np.ascontiguousarray(res.results[b]["outT"].T) for b in range(4)], axis=0)
    return out.astype(np.float32)
```

```python
import contextlib
import math
import numpy as np
import concourse.bass as bass
import concourse.mybir as mybir
from concourse.bass_utils import run_bass_kernel_spmd

F32 = mybir.dt.float32
BF16 = mybir.dt.bfloat16
AF = mybir.ActivationFunctionType
ALU = mybir.AluOpType
AX = mybir.AxisListType

NT, NCTX, TL, D, L = 4352, 256, 4096, 1024, 4
NTILE = NT // 128
DFF = 2816
EPS = 1e-6
C_QA, C_QAP, C_KA, C_KAP, C_B, C_QC, C_KC = 0, 512, 1024, 1536, 2048, 3200, 3456
NFM = 3712
C_VA, C_VC, C_G = 3712, 4096, 4352
NW = 4760
GROUPS = [(0, 256)] + [(256 + 512 * i, 512) for i in range(8)]


class Buf:
    __slots__ = ("w", "r", "g")

    def __init__(self):
        self.w = None
        self.r = {}
        self.g = None


class T:
    __slots__ = ("ap", "b")

    def __init__(self, ap):
        self.ap = ap
        self.b = Buf()


class Prog:
    ENG = ("pe", "act", "dve", "pool", "sp")

    def __init__(self, nc, stack, n_dma=48, same=True):
        self.nc = nc
        self.ops = {e: [] for e in self.ENG}
        self.cnt = {e: 0 for e in self.ENG}
        self.seen = {e: {} for e in self.ENG}
        self.esem = {e: stack.enter_context(nc.semaphore("s_" + e)) for e in self.ENG}
        self.dsem = [stack.enter_context(nc.semaphore("d%d" % i)) for i in range(n_dma)]
        self.dval = [0] * n_dma
        self.dnext = 0
        self.same = same

    def _wait(self, eng, tok):
        if tok is None:
            return
        key, val = tok
        if key == eng and (not self.same or eng == "pe"):
            return
        if self.seen[eng].get(key, 0) >= val:
            return
        self.seen[eng][key] = val
        sem = self.esem[key] if isinstance(key, str) else self.dsem[key]
        self.ops[eng].append(lambda e: e.wait_ge(sem, val))

    def _deps(self, eng, reads, writes):
        for b in reads:
            self._wait(eng, b.w)
        for b in writes:
            self._wait(eng, b.w)
            for t in list(b.r.values()):
                self._wait(eng, t)

    def _commit(self, tok, reads, writes):
        for b in reads:
            b.r[tok[0]] = tok
        for b in writes:
            b.w = tok
            b.r = {}

    def op(self, eng, fn, r=(), w=()):
        self._deps(eng, r, w)
        guards = [b.g for b in r if b.g is not None] if eng in ("act", "dve") else ()
        for g in guards:
            if g[0] is not None and g[1] != eng:
                self._wait(eng, g[0])
        self.cnt[eng] += 1
        tok = (eng, self.cnt[eng])
        sem = self.esem[eng]
        self.ops[eng].append(lambda e: fn(e).then_inc(sem, 1))
        self._commit(tok, r, w)
        for g in guards:
            g[0], g[1] = tok, eng

    def dma(self, out, in_, r=(), w=(), eng="sp"):
        k = self.dnext
        self.dnext = (self.dnext + 1) % len(self.dsem)
        if self.dval[k]:
            self._wait(eng, (k, self.dval[k]))
        self._deps(eng, r, w)
        self.dval[k] += 16
        tok = (k, self.dval[k])
        sem = self.dsem[k]
        self.ops[eng].append(lambda e: e.dma_start(out=out, in_=in_).then_inc(sem, 16))
        self._commit(tok, r, w)

    def mm(self, out, lhsT, rhs, start, stop, r, w):
        self.op("pe", lambda e: e.matmul(out, lhsT=lhsT, rhs=rhs, start=start, stop=stop), r, w)

    def filler(self, n, out, lhsT, rhs):
        for _ in range(n):
            self.op("pe", lambda e: e.matmul(out, lhsT=lhsT, rhs=rhs, start=True, stop=True), (), ())

    def tr(self, out, in_, ident, r, w):
        self.op("pe", lambda e: e.transpose(out, in_, ident), r, w)

    def act(self, out, in_, func, r, w, bias=None, scale=None, accum=None):
        kw = {}
        if bias is not None:
            kw["bias"] = bias
        if scale is not None:
            kw["scale"] = scale
        if accum is not None:
            kw["accum_out"] = accum
        self.op("act", lambda e: e.activation(out=out, in_=in_, func=func, **kw), r, w)

    def tt(self, eng, out, in0, in1, op, r, w):
        self.op(eng, lambda e: e.tensor_tensor(out=out, in0=in0, in1=in1, op=op), r, w)

    def ts(self, eng, out, in0, s1, op0, r, w, s2=None, op1=None):
        if op1 is None:
            self.op(eng, lambda e: e.tensor_scalar(out=out, in0=in0, scalar1=s1, scalar2=None, op0=op0), r, w)
        else:
            self.op(eng, lambda e: e.tensor_scalar(out=out, in0=in0, scalar1=s1, scalar2=s2, op0=op0, op1=op1), r, w)

    def stt(self, eng, out, in0, scalar, in1, op0, op1, r, w):
        self.op(eng, lambda e: e.scalar_tensor_tensor(out=out, in0=in0, scalar=scalar, in1=in1, op0=op0, op1=op1), r, w)

    def copy(self, eng, out, in_, r, w):
        if eng == "act":
            self.op("act", lambda e: e.activation(out=out, in_=in_, func=AF.Copy), r, w)
        else:
            self.op(eng, lambda e: e.tensor_copy(out=out, in_=in_), r, w)

    def memset(self, eng, out, val, w):
        self.op(eng, lambda e: e.memset(out, val), (), w)

    def barrier(self):
        for e in self.ENG:
            for f in self.ENG:
                if f != e and self.cnt[f]:
                    self._wait(e, (f, self.cnt[f]))
            for k, v in enumerate(self.dval):
                if v:
                    self._wait(e, (k, v))

    def finish(self):
        self.barrier()
        ops = self.ops
        with self.nc.Block() as block:
            @block.tensor
            def _(e):
                for f in ops["pe"]:
                    f(e)

            @block.scalar
            def _(e):
                for f in ops["act"]:
                    f(e)

            @block.vector
            def _(e):
                for f in ops["dve"]:
                    f(e)

            @block.gpsimd
            def _(e):
                for f in ops["pool"]:
                    f(e)

            @block.sync
            def _(e):
                for f in ops["sp"]:
                    f(e)


class Arena:
    def __init__(self, ap, n):
        self.ap, self.n, self.top = ap, n, 0

    def f32(self, n, shape=None):
        a = self.top
        self.top += n
        assert self.top <= self.n, ("SBUF arena overflow", self.top, self.n)
        ap = self.ap[:, a:a + n]
        return T(ap)

    def bf16(self, n):
        t = self.f32((n + 1) // 2)
        t.ap = t.ap.bitcast(BF16)
        return t

    def mark(self):
        return self.top

    def release(self, m):
        self.top = m


def sub(bank, ap):
    t = T(ap)
    t.b = bank.b
    return t


def v3(ap, a):
    return ap.rearrange("p (a b) -> p a b", a=a)


class K:
    def __init__(self, debug=False, nlayers=L, phases=None):
        self.debug = debug
        self.nlayers = nlayers
        self.phases = phases
        nc = self.nc = bass.Bass("TRN2", target_bir_lowering=False)
        self.st = contextlib.ExitStack()
        self.P = Prog(nc, self.st)
        ein = lambda n, s, dt=F32: nc.dram_tensor(n, list(s), dt, kind="ExternalInput").ap()
        self.xT = ein("xT", (D, NT))
        self.cT = ein("cT", (128, 16))
        self.w_mod = ein("w_mod", (L, D, 6 * D))
        self.bmodT = ein("bmodT", (128, L * 48))
        self.n1T = ein("n1T", (128, L * 8))
        self.n2T = ein("n2T", (128, L * 8))
        self.fnT = ein("fnT", (128, 8))
        self.w_in = ein("w_in", (L, D, NW))
        self.rope = ein("rope", (128, 2, TL))
        self.lamp = ein("lamp", (1, L * 128))
        self.dnT = ein("dnT", (128, L))
        self.nab = ein("nab", (L, 4, 128, 3200))
        self.convT = ein("convT", (128, L * 45))
        self.gpar = ein("gpar", (1, L * 24 + L * 64))
        self.gmask = ein("gmask", (128, 2048))
        self.w_out = ein("w_out", (L, D, D))
        self.w_f1 = ein("w_f1", (L, D, 2 * DFF))
        self.w_f2 = ein("w_f2", (L, DFF, D))
        sk = "ExternalOutput" if debug else "Internal"
        scr = lambda n, s, dt=F32: nc.dram_tensor(n, list(s), dt, kind=sk).ap()
        self.xs = scr("xs", (D, NT))
        self.qa = scr("qa", (512, NT), BF16)
        self.ka = scr("ka", (512, NT), BF16)
        self.va = scr("va", (NT, 384), BF16)
        self.qkvb = scr("qkvb", (1152, NT))
        self.gab = scr("gab", (NT, 408))
        self.qc = scr("qc", (256, NT), BF16)
        self.kc = scr("kc", (256, NT), BF16)
        self.vc = scr("vc", (NT, 256), BF16)
        self.yT = scr("yT", (D, NT), BF16)
        self.qkn = scr("qkn", (1152, NT))
        self.of = scr("of", (NT, 384))
        self.outT = nc.dram_tensor("outT", [D, TL], F32, kind="ExternalOutput").ap()
        self.dbufs = {}
        arena_ap = self.st.enter_context(nc.sbuf_tensor("arena", [128, 53200], F32))
        self.A = Arena(arena_ap, 53200)
        self.ps = [T(self.st.enter_context(nc.psum_tensor("ps%d" % i, [128, 512], F32))[:]) for i in range(8)]
        for t in self.ps:
            t.b.g = [None, None]

    def db(self, name):
        return Buf()

    def consts(self):
        P, A = self.P, self.A
        self.ones = A.f32(128)
        P.memset("pool", self.ones.ap, 1.0, [self.ones.b])
        self.cTt = A.f32(16)
        P.dma(self.cTt.ap, self.cT, w=[self.cTt.b])
        self.sc = A.f32(16)
        P.act(self.sc.ap, self.cTt.ap, AF.Silu, [self.cTt.b], [self.sc.b])
        self.mod = A.f32(L * 48 * 2)
        self.bm = A.f32(L * 48)
        P.dma(self.bm.ap, self.bmodT, w=[self.bm.b])
        self.n1 = A.f32(L * 8)
        self.n2 = A.f32(L * 8)
        self.fn = A.f32(8)
        P.dma(self.n1.ap, self.n1T, w=[self.n1.b])
        P.dma(self.n2.ap, self.n2T, w=[self.n2.b])
        P.dma(self.fn.ap, self.fnT, w=[self.fn.b])
        self.cw = A.f32(L * 45)
        P.dma(self.cw.ap, self.convT, w=[self.cw.b])
        self.gm = A.f32(2048)
        P.dma(self.gm.ap, self.gmask, w=[self.gm.b])
        gp = self.gp = A.f32(L * 24 + L * 64)
        P.dma(gp.ap, self.gpar.partition_broadcast(128), w=[gp.b])
        self.nea = A.f32(L * 12)
        P.act(self.nea.ap, gp.ap[:, 0:L * 12], AF.Exp, [gp.b], [self.nea.b])
        P.ts("dve", self.nea.ap, self.nea.ap, -1.0, ALU.mult, [self.nea.b], [self.nea.b])
        self.nlam = A.f32(L)
        self.dnl = A.f32(L)
        self.a1 = A.f32(L * 16)
        self.a2 = A.f32(L * 16)
        mtmp = A.mark()
        lp = A.f32(L * 128)
        P.dma(lp.ap, self.lamp.partition_broadcast(128), w=[lp.b])
        lpv = lp.ap.rearrange("p (l f d) -> p l f d", l=L, f=4)
        pr_ = A.f32(L * 64)
        prv = pr_.ap.rearrange("p (l f d) -> p l f d", l=L, f=2)
        P.tt("dve", prv, lpv[:, :, 0:4:2, :], lpv[:, :, 1:4:2, :], ALU.mult, [lp.b], [pr_.b])
        ee = A.f32(L * 2)
        P.op("dve", lambda e: e.tensor_reduce(out=ee.ap, in_=pr_.ap.rearrange("p (g d) -> p g d", d=32), axis=AX.X, op=ALU.add), [pr_.b], [ee.b])
        P.act(ee.ap, ee.ap, AF.Exp, [ee.b], [ee.b])
        dn = A.f32(L)
        P.dma(dn.ap, self.dnT, w=[dn.b])
        for l in range(L):
            li = 0.8 - 0.6 * math.exp(-0.3 * l)
            P.stt("dve", self.nlam.ap[:, l:l + 1], ee.ap[:, 2 * l + 1:2 * l + 2], -li, ee.ap[:, 2 * l:2 * l + 1], ALU.add, ALU.subtract,
                  [ee.b], [self.nlam.b])
            P.ts("dve", self.dnl.ap[:, l:l + 1], dn.ap[:, l:l + 1], 1.0 - li, ALU.mult, [dn.b], [self.dnl.b])
        P.barrier()
        A.release(mtmp)

    def phase_mod(self):
        P, A = self.P, self.A
        m0 = A.mark()
        wm = [A.f32(8 * 768), A.f32(8 * 768)]
        ps = self.ps[0]
        it = 0
        for l in range(self.nlayers):
            wl = self.w_mod[l].rearrange("(kc p) n -> p kc n", p=128)
            for j in range(8):
                w = wm[it % 2]
                it += 1
                wv = v3(w.ap, 8)
                P.dma(wv, wl[:, :, j * 768:(j + 1) * 768], w=[w.b])
                for ct in range(6):
                    for kc in range(8):
                        P.mm(ps.ap[:, ct * 2:ct * 2 + 2], wv[:, kc, ct * 128:(ct + 1) * 128],
                             self.sc.ap[:, kc * 2:kc * 2 + 2], kc == 0, kc == 7, [w.b, self.sc.b], [ps.b])
                o = (l * 48 + j * 6) * 2
                P.tt("dve", v3(self.mod.ap[:, o:o + 12], 6), v3(ps.ap[:, 0:12], 6),
                     self.bm.ap[:, l * 48 + j * 6:l * 48 + j * 6 + 6].unsqueeze(2).to_broadcast([128, 6, 2]),
                     ALU.add, [ps.b, self.bm.b], [self.mod.b])
        for l in range(self.nlayers):
            for (a, n, which) in ((self.a1, self.n1, 1), (self.a2, self.n2, 4)):
                o = (l * 48 + which * 8) * 2
                P.stt("dve", v3(a.ap[:, l * 16:(l + 1) * 16], 8), v3(self.mod.ap[:, o:o + 16], 8), 1.0,
                      n.ap[:, l * 8:(l + 1) * 8].unsqueeze(2).to_broadcast([128, 8, 2]),
                      ALU.add, ALU.mult, [self.mod.b, n.b], [a.b])
        P.barrier()
        A.release(m0)

    def modv(self, l, which, kc, s):
        o = ((l * 48 + which * 8 + kc) * 2) + s
        return self.mod.ap[:, o:o + 1]

    def load_w_bf16(self, dst, src3, nk, ncols, stg, piece=512):
        P = self.P
        dv = v3(dst.ap, nk)
        i = 0
        for c0 in range(0, ncols, piece):
            c1 = min(ncols, c0 + piece)
            s = stg[i % 2]
            sv = v3(s.ap[:, 0:nk * (c1 - c0)], nk)
            P.dma(sv, src3[:, :, c0:c1], w=[s.b])
            P.copy("pool" if i % 2 else "act", dv[:, :, c0:c1], sv, [s.b], [dst.b])
            i += 1

    def norm_mod(self, xg, N, l, a, which_shift, s, hb, sq, rstd, ps_ss):
        P = self.P
        xv = v3(xg.ap, 8)
        hv = v3(hb.ap, 8)
        for kc in range(8):
            P.act(sq.ap[:, 0:N], xv[:, kc, :], AF.Square, [xg.b], [sq.b])
            P.mm(ps_ss.ap[:, 0:N], self.ones.ap, sq.ap[:, 0:N], kc == 0, kc == 7, [self.ones.b, sq.b], [ps_ss.b])
        P.act(rstd.ap[:, 0:N], ps_ss.ap[:, 0:N], AF.Ln, [ps_ss.b], [rstd.b], bias=EPS, scale=1.0 / D)
        P.act(rstd.ap[:, 0:N], rstd.ap[:, 0:N], AF.Exp, [rstd.b], [rstd.b], scale=-0.5)
        for kc in range(8):
            ao = l * 16 + kc * 2 + s
            P.stt("dve", sq.ap[:, 0:N], xv[:, kc, :], a.ap[:, ao:ao + 1], rstd.ap[:, 0:N], ALU.mult, ALU.mult,
                  [xg.b, a.b, rstd.b], [sq.b])
            P.act(hv[:, kc, :], sq.ap[:, 0:N], AF.Identity, [sq.b, self.mod.b], [hb.b], bias=self.modv(l, which_shift, kc, s), scale=1.0)

    def phase_in(self, l):
        P, A = self.P, self.A
        m0 = A.mark()
        Wb = A.bf16(8 * NW)
        stg = [A.f32(8 * 256), A.f32(8 * 256)]
        self.load_w_bf16(Wb, self.w_in[l].rearrange("(kc p) n -> p kc n", p=128), 8, NW, stg, piece=256)
        Wv = v3(Wb.ap, 8)
        xg = [A.f32(8 * 512), A.f32(8 * 512)]
        hb = [A.bf16(8 * 512), A.bf16(8 * 512)]
        sq = A.f32(512)
        rstd = A.f32(512)
        rp = [A.f32(1024), A.f32(1024)]
        ofm = [A.f32(512) for _ in range(3)]
        otm = [A.f32(408) for _ in range(3)]
        t1 = A.f32(512)
        t2 = A.f32(512)
        xsrc = self.xT if l == 0 else self.xs
        xsrc3 = xsrc.rearrange("(kc p) n -> p kc n", p=128)
        bx = self.db("xs")
        ps_ss = self.ps[0]
        psf = self.ps[1:5]
        pst = self.ps[5:8]
        nf = 0
        ntm = 0

        def load(gi):
            n0, N = GROUPS[gi]
            x = xg[gi % 2]
            P.dma(v3(x.ap[:, 0:8 * N], 8), xsrc3[:, :, n0:n0 + N], r=[bx], w=[x.b])
            if gi > 0:
                r_ = rp[gi % 2]
                P.dma(v3(r_.ap, 2), self.rope[:, :, n0 - NCTX:n0 - NCTX + 512], w=[r_.b])

        load(0)
        for gi, (n0, N) in enumerate(GROUPS):
            if gi + 1 < len(GROUPS):
                load(gi + 1)
            x = xg[gi % 2]
            h = hb[gi % 2]
            s = 1 if gi == 0 else 0
            xin = T(x.ap[:, 0:8 * N]); xin.b = x.b
            hin = T(h.ap[:, 0:8 * N]); hin.b = h.b
            self.norm_mod(xin, N, l, self.a1, 0, s, hin, sq, rstd, ps_ss)
            hv = v3(h.ap[:, 0:8 * N], 8)
            rv = v3(rp[gi % 2].ap, 2)

            def fm(ft, ps):
                for kc in range(8):
                    P.mm(ps.ap[:, 0:N], Wv[:, kc, ft * 128:(ft + 1) * 128], hv[:, kc, :], kc == 0, kc == 7, [Wb.b, h.b], [ps.b])

            for (c0, dst, nm) in ((C_QA, self.qa, "qa"), (C_KA, self.ka, "ka")):
                for j in range(4):
                    o = ofm[nf % 3]
                    ob = o.ap.bitcast(BF16)[:, 0:N]
                    p1 = psf[nf % 4]
                    nf += 1
                    fm(c0 // 128 + j, p1)
                    if gi == 0:
                        P.copy("dve", ob, p1.ap[:, 0:N], [p1.b], [o.b])
                    else:
                        p2 = psf[nf % 4]
                        nf += 1
                        fm(c0 // 128 + 4 + j, p2)
                        P.tt("dve", t1.ap, p1.ap, rv[:, 0, :], ALU.mult, [p1.b, rp[gi % 2].b], [t1.b])
                        P.tt("dve", t2.ap, p2.ap, rv[:, 1, :], ALU.mult, [p2.b, rp[gi % 2].b], [t2.b])
                        P.tt("pool", ob, t1.ap, t2.ap, ALU.add, [t1.b, t2.b], [o.b])
                    P.dma(dst[j * 128:(j + 1) * 128, n0:n0 + N], ob, r=[o.b], w=[self.db(nm)])
            for j in range(9):
                o = ofm[nf % 3]
                p1 = psf[nf % 4]
                nf += 1
                fm(C_B // 128 + j, p1)
                P.copy("act", o.ap[:, 0:N], p1.ap[:, 0:N], [p1.b], [o.b])
                P.dma(self.qkvb[j * 128:(j + 1) * 128, n0:n0 + N], o.ap[:, 0:N], r=[o.b], w=[self.db("qkvb")])
            for (c0, dst, nm) in ((C_QC, self.qc, "qc"), (C_KC, self.kc, "kc")):
                for j in range(2):
                    o = ofm[nf % 3]
                    ob = o.ap.bitcast(BF16)[:, 0:N]
                    p1 = psf[nf % 4]
                    nf += 1
                    fm(c0 // 128 + j, p1)
                    P.copy("dve", ob, p1.ap[:, 0:N], [p1.b], [o.b])
                    P.dma(dst[j * 128:(j + 1) * 128, n0:n0 + N], ob, r=[o.b], w=[self.db(nm)])
            for tt in range(N // 128):
                tok0 = n0 + tt * 128
                for (c0, nc_, dst, nm, isbf) in ((C_VA, 384, self.va, "va", True), (C_VC, 256, self.vc, "vc", True),
                                                 (C_G, 408, self.gab, "gab", False)):
                    ps = pst[ntm % 3]
                    o = otm[ntm % 3]
                    ntm += 1
                    for kc in range(8):
                        P.mm(ps.ap[:, 0:nc_], hv[:, kc, tt * 128:(tt + 1) * 128], Wv[:, kc, c0:c0 + nc_], kc == 0, kc == 7,
                             [Wb.b, h.b], [ps.b])
                    oa = o.ap.bitcast(BF16)[:, 0:nc_] if isbf else o.ap[:, 0:nc_]
                    P.copy("act" if ntm % 2 else "dve", oa, ps.ap[:, 0:nc_], [ps.b], [o.b])
                    P.dma(dst[tok0:tok0 + 128, :], oa, r=[o.b], w=[self.db(nm)])
        P.barrier()
        A.release(m0)


    def phase_a(self, l):
        P, A = self.P, self.A
        m0 = A.mark()
        KAt = A.bf16(4 * NT)
        P.dma(v3(KAt.ap, 4), self.ka.rearrange("(t p) n -> p t n", p=128), w=[KAt.b])
        Kv = v3(KAt.ap, 4)
        VA = A.bf16(NTILE * 6 * 128)
        Vv = VA.ap.rearrange("p (t h d) -> p t h d", t=NTILE, h=6)
        P.memset("pool", VA.ap, 1.0, [VA.b])
        vsrc = self.va.rearrange("(t p) (h d) -> p t h d", p=128, h=6)
        for t0 in range(NTILE):
            P.dma(Vv[:, t0, :, 0:64], vsrc[:, t0, :, :], w=[VA.b])
        Qt = [A.bf16(4 * 512), A.bf16(4 * 512)]
        pT = [[A.bf16(512) for _ in range(3)] for _ in range(2)]
        o12 = [[A.f32(512), A.f32(512)], [A.f32(512), A.f32(512)]]
        rz = A.f32(512)
        sq = A.f32(512)
        rs = A.f32(512)
        yb = [A.bf16(512), A.bf16(512)]
        deferred = []
        nh = 0
        psS = [self.ps[0:2], self.ps[2:4]]
        psO = self.ps[4:6]
        psN = self.ps[6]
        psF = self.ps[7]
        fl = A.bf16(128 + 512)
        P.memset("pool", fl.ap, 0.0, [fl.b])
        NFILL = getattr(self, "a_fill", 1)
        scale = 32.0 ** -0.5
        qsrc = self.qa.rearrange("(t p) n -> p t n", p=128)
        ny = 0

        def loadq(gi):
            n0, N = GROUPS[gi]
            q = Qt[gi % 2]
            P.dma(v3(q.ap, 4)[:, :, 0:N], qsrc[:, :, n0:n0 + N], w=[q.b])

        loadq(0)
        for gi, (n0, N) in enumerate(GROUPS):
            if gi + 1 < len(GROUPS):
                loadq(gi + 1)
            q = Qt[gi % 2]
            qv = v3(q.ap, 4)
            kts = [0, 1] if gi == 0 else list(range(NTILE))
            items = [(h, kt) for h in range(6) for kt in kts]

            def S(i):
                h, kt = items[i]
                for m in range(2):
                    hm = 2 * h + m
                    t, pr = hm // 3, (hm % 3) * 32
                    ps = psS[m][i % 2]
                    P.mm(ps.ap[:, 0:N], Kv[pr:pr + 32, t, kt * 128:(kt + 1) * 128], qv[pr:pr + 32, t, 0:N], True, True, [KAt.b, q.b], [ps.b])

            S(0)
            for i, (h, kt) in enumerate(items):
                if i + 1 < len(items):
                    S(i + 1)
                P.filler(NFILL, psF.ap[:, 0:512], fl.ap[:, 0:128], fl.ap[:, 128:640])
                for m in range(2):
                    ps = psS[m][i % 2]
                    p = pT[m][i % 3]
                    P.act(p.ap[:, 0:N], ps.ap[:, 0:N], AF.Exp, [ps.b], [p.b], scale=scale)
                for m in range(2):
                    p = pT[m][i % 3]
                    po = psO[m]
                    P.mm(po.ap[:, 0:N], Vv[:, kt, h, :], p.ap[:, 0:N], kt == kts[0], kt == kts[-1], [VA.b, p.b], [po.b])
                for fn_ in [f for (due, f) in deferred if due <= i]:
                    fn_()
                deferred[:] = [(due, f) for (due, f) in deferred if due > i]
                if kt == kts[-1]:
                    oo = o12[nh % 2]
                    nh += 1
                    for m in range(2):
                        po = psO[m]
                        P.copy("dve", oo[m].ap[:, 0:N], po.ap[:, 0:N], [po.b], [oo[m].b])
                    P.filler(getattr(self, "a_bfill", 4), psF.ap[:, 0:512], fl.ap[:, 0:128], fl.ap[:, 128:640])

                    def fin1(oo=oo, h=h, N=N, n0=n0):
                        for m in range(2):
                            o = oo[m]
                            P.op("dve", lambda e, o=o: e.reciprocal(out=rz.ap[0:64, 0:N], in_=o.ap[64:128, 0:N]), [o.b], [rz.b])
                            P.tt("dve", o.ap[0:64, 0:N], o.ap[0:64, 0:N], rz.ap[0:64, 0:N], ALU.mult, [o.b, rz.b], [o.b])
                        o1, o2 = oo
                        P.stt("dve", o1.ap[0:64, 0:N], o2.ap[0:64, 0:N], self.nlam.ap[0:64, l:l + 1], o1.ap[0:64, 0:N], ALU.mult, ALU.add,
                              [o1.b, o2.b, self.nlam.b], [o1.b])
                        P.tt("pool", sq.ap[0:64, 0:N], o1.ap[0:64, 0:N], o1.ap[0:64, 0:N], ALU.mult, [o1.b], [sq.b])

                    def fin2(oo=oo, h=h, N=N, n0=n0):
                        o1 = oo[0]
                        P.mm(psN.ap[0:64, 0:N], self.ones.ap[0:64, 0:64], sq.ap[0:64, 0:N], True, True, [self.ones.b, sq.b], [psN.b])
                        P.act(rs.ap[0:64, 0:N], psN.ap[0:64, 0:N], AF.Ln, [psN.b], [rs.b], bias=EPS, scale=1.0 / 64)
                        P.act(rs.ap[0:64, 0:N], rs.ap[0:64, 0:N], AF.Exp, [rs.b], [rs.b], scale=-0.5)
                        y = yb[h % 2]
                        P.stt("dve", y.ap[0:64, 0:N], o1.ap[0:64, 0:N], self.dnl.ap[0:64, l:l + 1], rs.ap[0:64, 0:N], ALU.mult, ALU.mult,
                              [o1.b, rs.b, self.dnl.b], [y.b])
                        P.dma(self.yT[h * 64:(h + 1) * 64, n0:n0 + N], y.ap[0:64, 0:N], r=[y.b])

                    fin1()
                    if gi == 0:
                        fin2()
                    else:
                        deferred.append((i + 6, fin2))
            for (_, f) in deferred:
                f()
            deferred[:] = []
        P.barrier()
        A.release(m0)

    def phase_c(self, l):
        P, A = self.P, self.A
        m0 = A.mark()
        KCt = A.bf16(2 * NT)
        QCt = A.bf16(2 * NT)
        P.dma(v3(KCt.ap, 2), self.kc.rearrange("(t p) n -> p t n", p=128), w=[KCt.b])
        P.dma(v3(QCt.ap, 2), self.qc.rearrange("(t p) n -> p t n", p=128), w=[QCt.b])
        Kv, Qv = v3(KCt.ap, 2), v3(QCt.ap, 2)
        VC = A.bf16(NTILE * 4 * 128)
        Vv = VC.ap.rearrange("p (t h d) -> p t h d", t=NTILE, h=4)
        P.memset("pool", VC.ap, 1.0, [VC.b])
        vsrc = self.vc.rearrange("(t p) (h d) -> p t h d", p=128, h=4)
        for t0 in range(NTILE):
            P.dma(Vv[:, t0, :, 0:64], vsrc[:, t0, :, :], w=[VC.b])
        NB = A.f32(4 * 3200)
        NBv = NB.ap.rearrange("p (h a j q) -> p h a j q", h=4, a=5, j=5)
        for h in range(4):
            P.dma(NB.ap[:, h * 3200:(h + 1) * 3200], self.nab[l, h], w=[NB.b])
        sA = [A.f32(640), A.f32(640)]
        pA = [A.bf16(896), A.bf16(896)]
        rz = A.f32(128)
        ys = [A.bf16(256), A.bf16(256)]
        psA = self.ps[0:4]
        psO = self.ps[4:6]
        scale = 64.0 ** -0.5
        it = 0
        for qt in range(NTILE):
            yst = ys[qt % 2]
            ysv = v3(yst.ap, 2)
            if qt < 2:
                lat, pat = [], 0
            else:
                r0 = 2 * (qt - 2)
                base = min(max(r0 - 4, 0), 54)
                pat = (r0 - base) // 2
                lat = [2 + base // 2 + j for j in range(5)]
            kts = lat + [0, 1]
            for h in range(4):
                t, pr = h // 2, (h % 2) * 64
                pa, pb = psA[2 * (it % 2)], psA[2 * (it % 2) + 1]
                s_ = sA[it % 2]
                p_ = pA[it % 2]
                po = psO[it % 2]
                it += 1
                qop = Qv[pr:pr + 64, t, qt * 128:(qt + 1) * 128]
                for j, kt in enumerate(kts):
                    dst = pa.ap[:, j * 128:(j + 1) * 128] if j < 4 else pb.ap[:, (j - 4) * 128:(j - 3) * 128]
                    P.mm(dst, Kv[pr:pr + 64, t, kt * 128:(kt + 1) * 128], qop, True, True, [KCt.b, QCt.b], [pa.b if j < 4 else pb.b])
                nk = len(kts)
                if lat:
                    P.stt("dve", s_.ap[:, 0:512], pa.ap[:, 0:512], scale, NBv[:, h, pat, 0:4, :].rearrange("p j q -> p (j q)"), ALU.mult, ALU.add,
                          [pa.b, NB.b], [s_.b])
                    P.stt("dve", s_.ap[:, 512:640], pb.ap[:, 0:128], scale, NBv[:, h, pat, 4, :], ALU.mult, ALU.add,
                          [pb.b, NB.b], [s_.b])
                    P.act(p_.ap[:, 0:640], s_.ap[:, 0:640], AF.Exp, [s_.b], [p_.b])
                    P.act(p_.ap[:, 640:896], pb.ap[:, 128:384], AF.Exp, [pb.b], [p_.b], scale=scale)
                else:
                    P.act(p_.ap[:, 0:256], pa.ap[:, 0:256], AF.Exp, [pa.b], [p_.b], scale=scale)
                for j, kt in enumerate(kts):
                    P.mm(po.ap[:, 0:128], Vv[:, kt, h, :], p_.ap[:, j * 128:(j + 1) * 128], j == 0, j == nk - 1, [VC.b, p_.b], [po.b])
                P.op("dve", lambda e, po=po: e.reciprocal(out=rz.ap[0:64, :], in_=po.ap[64:128, 0:128]), [po.b], [rz.b])
                P.tt("dve", ysv[pr:pr + 64, t, :], po.ap[0:64, 0:128], rz.ap[0:64, :], ALU.mult, [po.b, rz.b], [yst.b])
            P.dma(self.yT[768:1024, qt * 128:(qt + 1) * 128].rearrange("(t p) n -> p t n", p=128), ysv, r=[yst.b])
        P.barrier()
        A.release(m0)

    def gmk(self, i, ncol=128):
        return self.gm.ap[:, i * 128:i * 128 + ncol]

    def phase_b1(self, l):
        P, A = self.P, self.A
        m0 = A.mark()
        xb = [A.f32(516) for _ in range(3)]
        acc = [A.f32(512) for _ in range(2)]
        sl = [A.f32(512) for _ in range(3)]
        sqs = [A.f32(512) for _ in range(2)]
        rs = A.f32(512)
        ob = [A.f32(512) for _ in range(2)]
        ps = self.ps[0:2]
        onesblk = self.gmk(1)
        work = [(ft, n0, N) for ft in range(9) for (n0, N) in GROUPS]

        def stage1(it):
            ft, n0, N = work[it]
            s0, s1 = (0, NCTX) if n0 < NCTX else (NCTX, NT)
            x, a, sv, sq, p_ = xb[it % 3], acc[it % 2], sl[it % 3], sqs[it % 2], ps[it % 2]
            lo, hi = max(n0 - 2, s0), min(n0 + N + 2, s1)
            if lo > n0 - 2:
                P.memset("pool", x.ap[:, 0:2], 0.0, [x.b])
            if hi < n0 + N + 2:
                P.memset("pool", x.ap[:, N + 2:N + 4], 0.0, [x.b])
            P.dma(x.ap[:, lo - (n0 - 2):hi - (n0 - 2)], self.qkvb[ft * 128:(ft + 1) * 128, lo:hi], w=[x.b])
            co = l * 45 + ft * 5
            P.ts("dve", a.ap[:, 0:N], x.ap[:, 0:N], self.cw.ap[:, co:co + 1], ALU.mult, [x.b, self.cw.b], [a.b])
            for j in range(1, 5):
                P.stt("dve", a.ap[:, 0:N], x.ap[:, j:j + N], self.cw.ap[:, co + j:co + j + 1], a.ap[:, 0:N], ALU.mult, ALU.add,
                      [x.b, self.cw.b, a.b], [a.b])
            P.act(sv.ap[:, 0:N], a.ap[:, 0:N], AF.Silu, [a.b], [sv.b])
            if ft < 6:
                P.tt("pool", sq.ap[:, 0:N], sv.ap[:, 0:N], sv.ap[:, 0:N], ALU.mult, [sv.b], [sq.b])
                P.mm(p_.ap[:, 0:N], onesblk, sq.ap[:, 0:N], True, True, [self.gm.b, sq.b], [p_.b])

        def stage2(it):
            ft, n0, N = work[it]
            sv, p_, o = sl[it % 3], ps[it % 2], ob[it % 2]
            if ft < 6:
                P.act(rs.ap[:, 0:N], p_.ap[:, 0:N], AF.Ln, [p_.b], [rs.b], bias=EPS, scale=1.0)
                P.act(rs.ap[:, 0:N], rs.ap[:, 0:N], AF.Exp, [rs.b], [rs.b], scale=-0.5)
                P.stt("dve", o.ap[:, 0:N], sv.ap[:, 0:N], 0.125 if ft < 3 else 1.0, rs.ap[:, 0:N], ALU.mult, ALU.mult, [sv.b, rs.b], [o.b])
                P.dma(self.qkn[ft * 128:(ft + 1) * 128, n0:n0 + N], o.ap[:, 0:N], r=[o.b])
            else:
                P.dma(self.qkn[ft * 128:(ft + 1) * 128, n0:n0 + N], sv.ap[:, 0:N], r=[sv.b])

        stage1(0)
        for it in range(len(work)):
            if it + 1 < len(work):
                stage1(it + 1)
            stage2(it)
        P.barrier()
        A.release(m0)

    def phase_b2(self, l, d):
        P, A = self.P, self.A
        m0 = A.mark()
        ident = self.gmk(0)
        A_d, B_d, MS_d, MIT_d, SelEnd = self.gmk(2 + d), self.gmk(4 + d), self.gmk(6 + d), self.gmk(8 + d), self.gmk(10 + d)
        gmb = self.gm.b
        f = A.f32
        gabt = [f(408), f(408)]
        fmt = [f(9 * 128), f(9 * 128)]
        oft = [f(384), f(384)]
        z, e_, g, eb, beta, nbeta, gc, eg, dgl, ek, bg = [f(6) for _ in range(11)]
        egl = f(12)
        ktm, vtm, ktail, kbg, vb = [f(384) for _ in range(5)]
        Ag, Bg, dg, Es, EiT, aqkT, TT, wT, qhT = [[f(384), f(384)] for _ in range(9)]
        Pm = [[f(384), f(384)], [f(384), f(384)]]
        PTm = [[f(384), f(384)], [f(384), f(384)]]
        u = [f(192), f(192)]
        vnew = [f(192), f(192)]
        S = [f(192), f(192)]
        otile = [f(384), f(384)]
        sq = f(384)
        ss = f(6)
        rs = f(6)
        sg = f(384)
        ybf = A.bf16(384)
        for hb in range(2):
            P.memset("pool", S[hb].ap[0:64, :], 0.0, [S[hb].b])
        X0, X1, X2, X3 = self.ps[0:4]
        b4, b5, b6, b7 = self.ps[4:8]
        psU = sub(b4, b4.ap[:, 0:192]); psW = b5; psE = b6
        psSv = sub(b4, b4.ap[:, 192:384]); psSo = sub(b7, b7.ap[:, 0:192]); psSs = sub(b7, b7.ap[:, 192:384])
        psK = X0; psV = X1
        psSo2 = sub(X2, X2.ap[:, 0:192])
        tmpo = f(192)
        psSc = sub(b4, b4.ap[:, 384:512])
        dtb = self.gp.ap[:, L * 12 + l * 12 + d * 6:L * 12 + l * 12 + d * 6 + 6]
        nea = self.nea.ap[:, l * 12 + d * 6:l * 12 + d * 6 + 6]
        gw = self.gp.ap[:, L * 24 + l * 64:L * 24 + (l + 1) * 64]
        qkn3 = self.qkn.rearrange("(f p) n -> p f n", p=128)
        order = list(range(NTILE)) if d == 0 else [1, 0] + list(range(NTILE - 1, 1, -1))
        chunks = (0, 1) if d == 0 else (1, 0)
        stage = getattr(self, "b2_stage", 99)
        if getattr(self, "b2_tiles", None):
            order = order[:self.b2_tiles]

        def load(i):
            tt = order[i]
            tok0 = tt * 128
            P.dma(gabt[i % 2].ap, self.gab[tok0:tok0 + 128, :], w=[gabt[i % 2].b])
            P.dma(v3(fmt[i % 2].ap, 9), qkn3[:, :, tok0:tok0 + 128], w=[fmt[i % 2].b])
            if d == 1:
                P.dma(oft[i % 2].ap, self.of[tok0:tok0 + 128, :], w=[oft[i % 2].b])

        bc3 = lambda ap, n: ap.unsqueeze(2).to_broadcast([128, ap.shape[1], n])
        mb3 = lambda ap, h: ap.unsqueeze(1).to_broadcast([128, h, 128])
        load(0)
        for i, tt in enumerate(order):
            if i + 1 < len(order):
                load(i + 1)
            tok0 = tt * 128
            ga = gabt[i % 2]
            fm = fmt[i % 2]
            fmv = v3(fm.ap, 9)
            ot = otile[i % 2]
            P.tt("dve", z.ap, ga.ap[:, 384 + 6 * d:390 + 6 * d], dtb, ALU.add, [ga.b, self.gp.b], [z.b])
            P.act(e_.ap, z.ap, AF.Exp, [z.b], [e_.b])
            P.act(e_.ap, e_.ap, AF.Ln, [e_.b], [e_.b], bias=1.0)
            P.tt("dve", g.ap, e_.ap, nea, ALU.mult, [e_.b, self.nea.b], [g.b])
            P.act(eb.ap, ga.ap[:, 396 + 6 * d:402 + 6 * d], AF.Exp, [ga.b], [eb.b], scale=-1.0)
            P.ts("dve", eb.ap, eb.ap, 1.0, ALU.add, [eb.b], [eb.b])
            P.op("dve", lambda e: e.reciprocal(out=beta.ap, in_=eb.ap), [eb.b], [beta.b])
            P.ts("dve", nbeta.ap, beta.ap, -1.0, ALU.mult, [beta.b], [nbeta.b])
            P.mm(psSc.ap[:, 0:6], A_d, g.ap, True, True, [gmb, g.b], [psSc.b])
            P.copy("dve", gc.ap, psSc.ap[:, 0:6], [psSc.b], [gc.b])
            P.act(eg.ap, gc.ap, AF.Exp, [gc.b], [eg.b])
            P.mm(psSc.ap[:, 8:14], SelEnd, gc.ap, True, True, [gmb, gc.b], [psSc.b])
            P.tt("dve", dgl.ap, psSc.ap[:, 8:14], gc.ap, ALU.subtract, [psSc.b, gc.b], [dgl.b])
            P.act(ek.ap, dgl.ap, AF.Exp, [dgl.b], [ek.b])
            P.tt("dve", bg.ap, beta.ap, eg.ap, ALU.mult, [beta.b, eg.b], [bg.b])
            for c in range(2):
                P.mm(psSc.ap[0:64, 16 + c * 6:22 + c * 6], self.gmk(12 + 2 * d + c, 64), gc.ap, True, True, [gmb, gc.b], [psSc.b])
            P.act(egl.ap[0:64, :], psSc.ap[0:64, 16:28], AF.Exp, [psSc.b], [egl.b])
            if stage < 2:
                continue
            for j in range(3):
                P.tr(psK.ap[:, j * 128:(j + 1) * 128], fmv[:, 3 + j, :], ident, [fm.b, gmb], [psK.b])
            P.copy("act", ktm.ap, psK.ap[:, 0:384], [psK.b], [ktm.b])
            for j in range(3):
                P.tr(psV.ap[:, j * 128:(j + 1) * 128], fmv[:, 6 + j, :], ident, [fm.b, gmb], [psV.b])
            P.copy("dve", vtm.ap, psV.ap[:, 0:384], [psV.b], [vtm.b])
            k3, v3_ = v3(ktm.ap, 6), v3(vtm.ap, 6)
            if stage < 3:
                continue
            P.tt("pool", v3(ktail.ap, 6), k3, bc3(ek.ap, 64), ALU.mult, [ktm.b, ek.b], [ktail.b])
            P.tt("pool", v3(kbg.ap, 6), k3, bc3(bg.ap, 64), ALU.mult, [ktm.b, bg.b], [kbg.b])
            P.tt("pool", v3(vb.ap, 6), v3_, bc3(beta.ap, 64), ALU.mult, [vtm.b, beta.b], [vb.b])
            for hb in range(2):
                if stage < 4:
                    continue
                H = [3 * hb, 3 * hb + 1, 3 * hb + 2]
                hs = slice(3 * hb, 3 * hb + 3)
                ag, bgm, dgm, es, eit, aq, tt_, w_, qh = Ag[hb], Bg[hb], dg[hb], Es[hb], EiT[hb], aqkT[hb], TT[hb], wT[hb], qhT[hb]
                P.tt("pool", v3(ag.ap, 3), mb3(A_d, 3), bc3(g.ap[:, hs], 128), ALU.mult, [gmb, g.b], [ag.b])
                P.tt("pool", v3(bgm.ap, 3), mb3(B_d, 3), bc3(g.ap[:, hs], 128), ALU.mult, [gmb, g.b], [bgm.b])
                P.tt("pool", v3(dgm.ap, 3), mb3(ident, 3), bc3(eg.ap[:, hs], 128), ALU.mult, [gmb, eg.b], [dgm.b])
                for k, h in enumerate(H):
                    hp, hr = h // 2, (h % 2) * 64
                    kT = fmv[hr:hr + 64, 3 + hp, :]
                    qT = fmv[hr:hr + 64, hp, :]
                    ks = slice(k * 128, (k + 1) * 128)
                    P.mm(X0.ap[:, ks], kT, kT, True, True, [fm.b], [X0.b])
                    P.mm(X1.ap[:, ks], kT, qT, True, True, [fm.b], [X1.b])
                    P.mm(X2.ap[:, ks], ag.ap[:, ks], B_d, True, False, [ag.b, gmb], [X2.b])
                    P.mm(X2.ap[:, ks], ident, MS_d, False, True, [gmb], [X2.b])
                    P.mm(X3.ap[:, ks], bgm.ap[:, ks], A_d, True, False, [bgm.b, gmb], [X3.b])
                    P.mm(X3.ap[:, ks], ident, MIT_d, False, True, [gmb], [X3.b])
                if stage < 5:
                    continue
                P.act(es.ap, X2.ap[:, 0:384], AF.Exp, [X2.b], [es.b])
                P.act(eit.ap, X3.ap[:, 0:384], AF.Exp, [X3.b], [eit.b])
                p0, pt0 = Pm[hb][0], PTm[hb][0]
                if stage == 5 and getattr(self, "b2_sub", 9) < 2:
                    continue
                for k, h in enumerate(H):
                    ks = slice(k * 128, (k + 1) * 128)
                    P.stt("dve", p0.ap[:, ks], X0.ap[:, ks], nbeta.ap[:, h:h + 1], es.ap[:, ks], ALU.mult, ALU.mult, [X0.b, es.b, nbeta.b], [p0.b])
                P.tt("dve", aq.ap, X1.ap[:, 0:384], eit.ap, ALU.mult, [X1.b, eit.b], [aq.b])
                if stage == 5 and getattr(self, "b2_sub", 9) < 3:
                    continue
                for k in range(3):
                    ks = slice(k * 128, (k + 1) * 128)
                    P.tr(X0.ap[:, ks], p0.ap[:, ks], ident, [p0.b, gmb], [X0.b])
                if stage == 5 and getattr(self, "b2_sub", 9) < 4:
                    continue
                P.copy("act", pt0.ap, X0.ap[:, 0:384], [X0.b], [pt0.b])
                for k in range(3):
                    ks = slice(k * 128, (k + 1) * 128)
                    P.tt("dve", tt_.ap[:, ks], X0.ap[:, ks], ident, ALU.add, [X0.b, gmb], [tt_.b])
                pc, ptc = p0, pt0
                if stage < 6:
                    continue
                for lvl in range(5):
                    pn, ptn = Pm[hb][(lvl + 1) % 2], PTm[hb][(lvl + 1) % 2]
                    for k in range(3):
                        ks = slice(k * 128, (k + 1) * 128)
                        P.mm(X1.ap[:, ks], ptc.ap[:, ks], pc.ap[:, ks], True, True, [ptc.b, pc.b], [X1.b])
                    if lvl < 4:
                        for k in range(3):
                            ks = slice(k * 128, (k + 1) * 128)
                            P.mm(X2.ap[:, ks], pc.ap[:, ks], ptc.ap[:, ks], True, True, [ptc.b, pc.b], [X2.b])
                    P.copy("act", pn.ap, X1.ap[:, 0:384], [X1.b], [pn.b])
                    if lvl < 4:
                        P.copy("dve", ptn.ap, X2.ap[:, 0:384], [X2.b], [ptn.b])
                    for k in range(3):
                        ks = slice(k * 128, (k + 1) * 128)
                        P.mm(X3.ap[:, ks], pn.ap[:, ks], tt_.ap[:, ks], True, True, [pn.b, tt_.b], [X3.b])
                    P.tt("dve", tt_.ap, tt_.ap, X3.ap[:, 0:384], ALU.add, [tt_.b, X3.b], [tt_.b])
                    pc, ptc = pn, ptn
                if stage < 7:
                    continue
                for k, h in enumerate(H):
                    hp, hr = h // 2, (h % 2) * 64
                    ks = slice(k * 128, (k + 1) * 128)
                    P.mm(psU.ap[:, k * 64:(k + 1) * 64], tt_.ap[:, ks], vb.ap[:, h * 64:(h + 1) * 64], True, True, [tt_.b, vb.b], [psU.b])
                    P.mm(psW.ap[0:64, ks], kbg.ap[:, h * 64:(h + 1) * 64], tt_.ap[:, ks], True, True, [tt_.b, kbg.b], [psW.b])
                    P.mm(psE.ap[0:64, ks], self.ones.ap[:, 0:64], dgm.ap[:, ks], True, True, [self.ones.b, dgm.b], [psE.b])
                P.copy("act", u[hb].ap, psU.ap[:, 0:192], [psU.b], [u[hb].b])
                P.copy("dve", w_.ap[0:64, :], psW.ap[0:64, 0:384], [psW.b], [w_.b])
                for k, h in enumerate(H):
                    hp, hr = h // 2, (h % 2) * 64
                    ks = slice(k * 128, (k + 1) * 128)
                    P.tt("dve", qh.ap[0:64, ks], fmv[hr:hr + 64, hp, :], psE.ap[0:64, ks], ALU.mult, [fm.b, psE.b], [qh.b])
                if stage < 8:
                    continue
                Sb = S[hb]
                vn = vnew[hb]
                for c in chunks:
                    cs = c * 64
                    for k, h in enumerate(H):
                        ks = slice(k * 128, (k + 1) * 128)
                        P.mm(psSv.ap[:, k * 64:(k + 1) * 64], w_.ap[0:64, ks], Sb.ap[0:64, k * 64:(k + 1) * 64], True, True, [w_.b, Sb.b], [psSv.b])
                    P.tt("dve", vn.ap[cs:cs + 64, :], u[hb].ap[cs:cs + 64, :], psSv.ap[cs:cs + 64, :], ALU.subtract, [u[hb].b, psSv.b], [vn.b])
                    for k, h in enumerate(H):
                        ks = slice(k * 128, (k + 1) * 128)
                        k6 = slice(k * 64, (k + 1) * 64)
                        P.mm(psSo.ap[:, k6], qh.ap[0:64, ks], Sb.ap[0:64, k6], True, True, [qh.b, Sb.b], [psSo.b])
                        P.mm(psSo2.ap[:, k6], aq.ap[cs:cs + 64, ks], vn.ap[cs:cs + 64, k6], True, True, [aq.b, vn.b], [psSo2.b])
                        P.mm(psSs.ap[0:64, k6], ktail.ap[cs:cs + 64, h * 64:(h + 1) * 64], vn.ap[cs:cs + 64, k6], True, True, [ktail.b, vn.b], [psSs.b])
                    P.copy("act", tmpo.ap[cs:cs + 64, :], psSo2.ap[cs:cs + 64, :], [psSo2.b], [tmpo.b])
                    P.tt("dve", ot.ap[cs:cs + 64, hb * 192:(hb + 1) * 192], psSo.ap[cs:cs + 64, :], tmpo.ap[cs:cs + 64, :], ALU.add,
                         [psSo.b, tmpo.b], [ot.b])
                    for k, h in enumerate(H):
                        k6 = slice(k * 64, (k + 1) * 64)
                        P.stt("dve", Sb.ap[0:64, k6], Sb.ap[0:64, k6], egl.ap[0:64, c * 6 + h:c * 6 + h + 1], psSs.ap[0:64, k6], ALU.mult, ALU.add,
                              [Sb.b, egl.b, psSs.b], [Sb.b])
            if stage < 9:
                continue
            if d == 0:
                P.dma(self.of[tok0:tok0 + 128, :], ot.ap, r=[ot.b])
            else:
                of_ = oft[i % 2]
                P.tt("pool", sg.ap, ot.ap, of_.ap, ALU.add, [ot.b, of_.b], [sg.b])
                P.tt("pool", sq.ap, sg.ap, sg.ap, ALU.mult, [sg.b], [sq.b])
                P.op("dve", lambda e: e.tensor_reduce(out=ss.ap, in_=v3(sq.ap, 6), axis=AX.X, op=ALU.add), [sq.b], [ss.b])
                P.act(rs.ap, ss.ap, AF.Ln, [ss.b], [rs.b], bias=EPS, scale=1.0 / 64)
                P.act(rs.ap, rs.ap, AF.Exp, [rs.b], [rs.b], scale=-0.5)
                P.tt("dve", v3(sq.ap, 6), v3(sg.ap, 6), bc3(rs.ap, 64), ALU.mult, [sg.b, rs.b], [sq.b])
                P.tt("pool", v3(ot.ap, 6), v3(sq.ap, 6), gw.unsqueeze(1).to_broadcast([128, 6, 64]), ALU.mult, [sq.b, self.gp.b], [ot.b])
                P.act(sg.ap, ga.ap[:, 0:384], AF.Silu, [ga.b], [sg.b])
                P.tt("dve", sq.ap, ot.ap, sg.ap, ALU.mult, [ot.b, sg.b], [sq.b])
                for j in range(3):
                    P.tr(psK.ap[:, j * 128:(j + 1) * 128], sq.ap[:, j * 128:(j + 1) * 128], ident, [sq.b, gmb], [psK.b])
                P.copy("act", ybf.ap, psK.ap[:, 0:384], [psK.b], [ybf.b])
                P.dma(self.yT[384:768, tok0:tok0 + 128].rearrange("(j p) n -> p j n", p=128), v3(ybf.ap, 3), r=[ybf.b])
        P.barrier()
        A.release(m0)

    def phase_out(self, l):
        P, A = self.P, self.A
        m0 = A.mark()
        last = (l == self.nlayers - 1) and self.nlayers == L
        Wo = A.bf16(8 * D)
        W1 = A.bf16(8 * 2 * DFF)
        W2 = A.bf16(22 * D)
        stg = [A.f32(1408), A.f32(1408)]
        self.load_w_bf16(Wo, self.w_out[l].rearrange("(kc p) n -> p kc n", p=128), 8, D, stg, piece=176)
        self.load_w_bf16(W1, self.w_f1[l].rearrange("(kc p) n -> p kc n", p=128), 8, 2 * DFF, stg, piece=176)
        self.load_w_bf16(W2, self.w_f2[l].rearrange("(kc p) n -> p kc n", p=128), 22, D, stg, piece=64)
        Wov, W1v, W2v = v3(Wo.ap, 8), v3(W1.ap, 8), v3(W2.ap, 22)
        N = 256
        xg = A.f32(8 * N)
        yg = [A.bf16(8 * N), A.bf16(8 * N)]
        hb = A.bf16(8 * N)
        actb = A.bf16(22 * N)
        sq = A.f32(N)
        rstd = A.f32(N)
        sgt = [A.f32(N)]
        xsrc3 = (self.xT if l == 0 else self.xs).rearrange("(kc p) n -> p kc n", p=128)
        xdst3 = self.xs.rearrange("(kc p) n -> p kc n", p=128)
        y3 = self.yT.rearrange("(kc p) n -> p kc n", p=128)
        o3 = self.outT.rearrange("(kc p) n -> p kc n", p=128)
        ps_ss = self.ps[0]
        psr = self.ps[1:4]
        psg = self.ps[4:6]
        psu = self.ps[6:8]
        ngr = NT // N
        xv = v3(xg.ap, 8)
        hv = v3(hb.ap, 8)
        av = v3(actb.ap, 22)
        nr = 0

        def loady(gi):
            P.dma(v3(yg[gi % 2].ap, 8), y3[:, :, gi * N:(gi + 1) * N], w=[yg[gi % 2].b])

        loady(0)
        for gi in range(ngr):
            n0 = gi * N
            s = 1 if gi == 0 else 0
            if gi + 1 < ngr:
                loady(gi + 1)
            P.dma(xv, xsrc3[:, :, n0:n0 + N], w=[xg.b])
            y = yg[gi % 2]
            yv = v3(y.ap, 8)
            for dt in range(8):
                ps = psr[nr % 3]
                nr += 1
                for kc in range(8):
                    P.mm(ps.ap[:, 0:N], Wov[:, kc, dt * 128:(dt + 1) * 128], yv[:, kc, :], kc == 0, kc == 7, [Wo.b, y.b], [ps.b])
                P.stt("dve", xv[:, dt, :], ps.ap[:, 0:N], self.modv(l, 2, dt, s), xv[:, dt, :], ALU.mult, ALU.add, [ps.b, self.mod.b, xg.b], [xg.b])
            self.norm_mod(xg, N, l, self.a2, 3, s, hb, sq, rstd, ps_ss)
            for ft in range(22):
                pg, pu = psg[ft % 2], psu[ft % 2]
                for kc in range(8):
                    P.mm(pg.ap[:, 0:N], W1v[:, kc, ft * 128:(ft + 1) * 128], hv[:, kc, :], kc == 0, kc == 7, [W1.b, hb.b], [pg.b])
                for kc in range(8):
                    P.mm(pu.ap[:, 0:N], W1v[:, kc, DFF + ft * 128:DFF + (ft + 1) * 128], hv[:, kc, :], kc == 0, kc == 7, [W1.b, hb.b], [pu.b])
                sg = sgt[0]
                P.act(sg.ap, pg.ap[:, 0:N], AF.Silu, [pg.b], [sg.b])
                P.tt("dve", av[:, ft, :], pu.ap[:, 0:N], sg.ap, ALU.mult, [pu.b, sg.b], [actb.b])
            for dt in range(8):
                ps = psr[nr % 3]
                nr += 1
                for ft in range(22):
                    P.mm(ps.ap[:, 0:N], W2v[:, ft, dt * 128:(dt + 1) * 128], av[:, ft, :], ft == 0, ft == 21, [W2.b, actb.b], [ps.b])
                P.stt("dve", xv[:, dt, :], ps.ap[:, 0:N], self.modv(l, 5, dt, s), xv[:, dt, :], ALU.mult, ALU.add, [ps.b, self.mod.b, xg.b], [xg.b])
            if not last:
                P.dma(xdst3[:, :, n0:n0 + N], xv, r=[xg.b])
            elif gi >= 1:
                for kc in range(8):
                    P.act(sq.ap, xv[:, kc, :], AF.Square, [xg.b], [sq.b])
                    P.mm(ps_ss.ap[:, 0:N], self.ones.ap, sq.ap, kc == 0, kc == 7, [self.ones.b, sq.b], [ps_ss.b])
                P.act(rstd.ap, ps_ss.ap[:, 0:N], AF.Ln, [ps_ss.b], [rstd.b], bias=EPS, scale=1.0 / D)
                P.act(rstd.ap, rstd.ap, AF.Exp, [rstd.b], [rstd.b], scale=-0.5)
                for kc in range(8):
                    P.stt("dve", xv[:, kc, :], xv[:, kc, :], self.fn.ap[:, kc:kc + 1], rstd.ap, ALU.mult, ALU.mult, [xg.b, self.fn.b, rstd.b], [xg.b])
                P.dma(o3[:, :, n0 - NCTX:n0 - NCTX + N], xv, r=[xg.b])
        P.barrier()
        A.release(m0)

    def build(self):
        self.consts()
        self.phase_mod()
        ph = self.phases
        for l in range(self.nlayers):
            if ph is None or "in" in ph:
                self.phase_in(l)
            if ph is None or "a" in ph:
                self.phase_a(l)
            if ph is None or "c" in ph:
                self.phase_c(l)
            if ph is None or "b1" in ph:
                self.phase_b1(l)
            if ph is None or "b2" in ph:
                self.phase_b2(l, 0)
                self.phase_b2(l, 1)
            if ph is None or "out" in ph:
                self.phase_out(l)
        self.P.finish()
        self.st.close()
        return self.nc


def rope_tables():
    half = 16
    inv_freq = (1.0 / (10000.0 ** (np.arange(0, half, 2, dtype=np.float32) / np.float32(half)))).astype(np.float32)
    t = np.arange(TL, dtype=np.int32)
    ang_r = (t // 64).astype(np.float32)[:, None] * inv_freq
    ang_c = (t % 64).astype(np.float32)[:, None] * inv_freq
    ang = np.concatenate([ang_r, ang_r, ang_c, ang_c], axis=-1)
    cos = np.cos(ang).astype(np.float32)
    sin = np.sin(ang).astype(np.float32)
    sign = np.array([-1.0] * 8 + [1.0] * 8 + [-1.0] * 8 + [1.0] * 8, np.float32)
    tab = np.stack([cos.T, (sin * sign).T], axis=1)
    return np.ascontiguousarray(np.tile(tab, (4, 1, 1)))


def na_bias_tiles(nb):
    out = np.full((L, 4, 5, 5, 128, 128), -30000.0, np.float32)
    qi = np.arange(128)
    ki = np.arange(128)
    for pat, (r0, base) in enumerate(((0, 0), (2, 0), (4, 0), (60, 54), (62, 54))):
        r = r0 + qi // 64
        c = qi % 64
        rs = np.clip(r - 4, 0, 56)
        cs = np.clip(c - 8, 0, 48)
        for j in range(5):
            kr = base + 2 * j + ki // 64
            kc = ki % 64
            valid = ((kr[:, None] >= rs[None, :]) & (kr[:, None] < rs[None, :] + 8) &
                     (kc[:, None] >= cs[None, :]) & (kc[:, None] < cs[None, :] + 16))
            dr = np.clip(kr[:, None] - r[None, :] + 7, 0, 14)
            dc = np.clip(kc[:, None] - c[None, :] + 15, 0, 30)
            g = nb[:, :, dr, dc]
            out[:, :, pat, j] = np.where(valid[None, None], g, np.float32(-30000.0))
    return np.ascontiguousarray(out.transpose(0, 1, 4, 2, 3, 5).reshape(L, 4, 128, 3200))


def gdn_masks():
    m = np.zeros((16, 128, 128), np.float32)
    i = np.arange(128)
    same = (i[:, None] // 64) == (i[None, :] // 64)
    r, c = i[:, None], i[None, :]
    m[0] = np.eye(128)
    m[1] = same
    m[2] = same & (r <= c)
    m[3] = same & (r >= c)
    m[4] = same & (r > c)
    m[5] = same & (r < c)
    m[6] = np.where(same & (r > c), 0.0, -30000.0)
    m[7] = np.where(same & (r < c), 0.0, -30000.0)
    m[8] = np.where(same & (c >= r), 0.0, -30000.0)
    m[9] = np.where(same & (c <= r), 0.0, -30000.0)
    endf = (i // 64) * 64 + 63
    endb = (i // 64) * 64
    m[10] = (r == endf[None, :])
    m[11] = (r == endb[None, :])
    for d_, ends in enumerate(((63, 127), (0, 64))):
        for c_ in range(2):
            m[12 + 2 * d_ + c_][ends[c_], :] = 1.0
    return np.ascontiguousarray(m.transpose(1, 0, 2).reshape(128, 2048))


def host_prep(inp):
    f = lambda a: np.ascontiguousarray(a, dtype=np.float32)
    w_in = inp["w_in"]
    sizes = (1152, 1152, 384, 12, 12, 768)
    offs = np.cumsum((0,) + sizes)
    a0, b0, g0, al0, be0, c0 = offs[:6]
    perm32 = np.concatenate([np.arange(8, 16), np.arange(0, 8), np.arange(24, 32), np.arange(16, 24)])
    pad = lambda m: np.concatenate([m.reshape(4, 96), m.reshape(4, 96)[:, :32]], axis=1).reshape(-1)
    idq = pad(np.arange(384))
    permq = pad((np.arange(384).reshape(12, 32)[:, perm32]).reshape(-1))
    cols = np.concatenate([
        a0 + idq, a0 + permq, a0 + 384 + idq, a0 + 384 + permq,
        b0 + np.arange(1152), c0 + np.arange(256), c0 + 256 + np.arange(256),
        a0 + 768 + np.arange(384), c0 + 512 + np.arange(256), g0 + np.arange(384), al0 + np.arange(12), be0 + np.arange(12)])
    assert cols.shape[0] == NW
    shared = {
        "w_mod": f(inp["w_mod"]),
        "bmodT": f(inp["b_mod"].reshape(L, 48, 128).transpose(2, 0, 1).reshape(128, L * 48)),
        "n1T": f(inp["norm1_w"].reshape(L, 8, 128).transpose(2, 0, 1).reshape(128, L * 8)),
        "n2T": f(inp["norm2_w"].reshape(L, 8, 128).transpose(2, 0, 1).reshape(128, L * 8)),
        "fnT": f(inp["final_norm_w"].reshape(8, 128).T),
        "w_in": f(w_in[:, :, cols]),
        "rope": rope_tables(),
        "lamp": f(np.stack([inp["lambda_q1"], inp["lambda_k1"], inp["lambda_q2"], inp["lambda_k2"]], axis=1).reshape(1, L * 128)),
        "dnT": f(np.tile(inp["diff_norm_w"].T, (2, 1))),
        "nab": na_bias_tiles(inp["na_bias"]),
        "convT": f(inp["conv_w"].reshape(L, 5, 9, 128).transpose(3, 0, 2, 1).reshape(128, L * 45)),
        "gpar": f(np.concatenate([inp["a_log"].reshape(-1), inp["dt_bias"].reshape(-1), inp["gdn_norm_w"].reshape(-1)])[None, :]),
        "gmask": gdn_masks(),
        "w_out": f(inp["w_out"]),
        "w_f1": f(inp["w_ffn_in"]),
        "w_f2": f(inp["w_ffn_out"]),
    }
    per = []
    for b in range(4):
        xT = f(np.concatenate([inp["ctx"][b], inp["x"][b]], axis=0).T)
        cT = f(np.stack([inp["c"][b].reshape(8, 128).T, inp["c_ctx"].reshape(8, 128).T], axis=2).reshape(128, 16))
        m = dict(shared)
        m["xT"] = xT
        m["cT"] = cT
        per.append(m)
    return per


def kernel(**inputs):
    inp = {k: np.asarray(v) for k, v in inputs.items()}
    per = host_prep(inp)
    nc = K().build()
    in_maps = [per[i % 4] for i in range(8)]
    res = run_bass_kernel_spmd(nc, in_maps, core_ids=list(range(8)))
    out = np.stack([np.ascontiguousarray(res.results[b]["outT"].T) for b in range(4)], axis=0)
    return out.astype(np.float32)
```

```python
import contextlib
import itertools
import math
import numpy as np
import concourse.bass as bass
import concourse.mybir as mybir
from concourse.bass_utils import run_bass_kernel_spmd

F32 = mybir.dt.float32
BF16 = mybir.dt.bfloat16
AF = mybir.ActivationFunctionType
ALU = mybir.AluOpType
AX = mybir.AxisListType

NT, NCTX, TL, D, L = 4352, 256, 4096, 1024, 4
NTILE = NT // 128
DFF = 2816
EPS = 1e-6
C_QA, C_QAP, C_KA, C_KAP, C_B, C_QC, C_KC = 0, 512, 1024, 1536, 2048, 3200, 3456
NFM = 3712
C_VA, C_VC, C_G = 3712, 4096, 4352
NW = 4760
GROUPS = [(0, 256)] + [(256 + 512 * i, 512) for i in range(8)]


class Buf:
    __slots__ = ("w", "r", "g")

    def __init__(self):
        self.w = None
        self.r = {}
        self.g = None


class T:
    __slots__ = ("ap", "b")

    def __init__(self, ap):
        self.ap = ap
        self.b = Buf()


class Prog:
    ENG = ("pe", "act", "dve", "pool", "sp")

    def __init__(self, nc, stack, n_dma=48, same=True):
        self.nc = nc
        self.ops = {e: [] for e in self.ENG}
        self.cnt = {e: 0 for e in self.ENG}
        self.seen = {e: {} for e in self.ENG}
        self.esem = {e: stack.enter_context(nc.semaphore("s_" + e)) for e in self.ENG}
        self.dsem = [stack.enter_context(nc.semaphore("d%d" % i)) for i in range(n_dma)]
        self.dval = [0] * n_dma
        self.dnext = 0
        self.same = same

    def _wait(self, eng, tok):
        if tok is None:
            return
        key, val = tok
        if key == eng and (not self.same or eng == "pe"):
            return
        if self.seen[eng].get(key, 0) >= val:
            return
        self.seen[eng][key] = val
        sem = self.esem[key] if isinstance(key, str) else self.dsem[key]
        self.ops[eng].append(lambda e: e.wait_ge(sem, val))

    def _deps(self, eng, reads, writes):
        for b in reads:
            self._wait(eng, b.w)
        for b in writes:
            self._wait(eng, b.w)
            for t in list(b.r.values()):
                self._wait(eng, t)

    def _commit(self, tok, reads, writes):
        for b in reads:
            b.r[tok[0]] = tok
        for b in writes:
            b.w = tok
            b.r = {}

    def op(self, eng, fn, r=(), w=()):
        self._deps(eng, r, w)
        guards = [b.g for b in r if b.g is not None] if eng in ("act", "dve") else ()
        for g in guards:
            if g[0] is not None and g[1] != eng:
                self._wait(eng, g[0])
        self.cnt[eng] += 1
        tok = (eng, self.cnt[eng])
        sem = self.esem[eng]
        self.ops[eng].append(lambda e: fn(e).then_inc(sem, 1))
        self._commit(tok, r, w)
        for g in guards:
            g[0], g[1] = tok, eng

    def dma(self, out, in_, r=(), w=(), eng="sp"):
        k = self.dnext
        self.dnext = (self.dnext + 1) % len(self.dsem)
        if self.dval[k]:
            self._wait(eng, (k, self.dval[k]))
        self._deps(eng, r, w)
        self.dval[k] += 16
        tok = (k, self.dval[k])
        sem = self.dsem[k]
        self.ops[eng].append(lambda e: e.dma_start(out=out, in_=in_).then_inc(sem, 16))
        self._commit(tok, r, w)

    def mm(self, out, lhsT, rhs, start, stop, r, w):
        self.op("pe", lambda e: e.matmul(out, lhsT=lhsT, rhs=rhs, start=start, stop=stop), r, w)

    def filler(self, n, out, lhsT, rhs):
        for _ in range(n):
            self.op("pe", lambda e: e.matmul(out, lhsT=lhsT, rhs=rhs, start=True, stop=True), (), ())

    def tr(self, out, in_, ident, r, w):
        self.op("pe", lambda e: e.transpose(out, in_, ident), r, w)

    def act(self, out, in_, func, r, w, bias=None, scale=None, accum=None):
        kw = {}
        if bias is not None:
            kw["bias"] = bias
        if scale is not None:
            kw["scale"] = scale
        if accum is not None:
            kw["accum_out"] = accum
        self.op("act", lambda e: e.activation(out=out, in_=in_, func=func, **kw), r, w)

    def tt(self, eng, out, in0, in1, op, r, w):
        self.op(eng, lambda e: e.tensor_tensor(out=out, in0=in0, in1=in1, op=op), r, w)

    def ts(self, eng, out, in0, s1, op0, r, w, s2=None, op1=None):
        if op1 is None:
            self.op(eng, lambda e: e.tensor_scalar(out=out, in0=in0, scalar1=s1, scalar2=None, op0=op0), r, w)
        else:
            self.op(eng, lambda e: e.tensor_scalar(out=out, in0=in0, scalar1=s1, scalar2=s2, op0=op0, op1=op1), r, w)

    def stt(self, eng, out, in0, scalar, in1, op0, op1, r, w):
        self.op(eng, lambda e: e.scalar_tensor_tensor(out=out, in0=in0, scalar=scalar, in1=in1, op0=op0, op1=op1), r, w)

    def copy(self, eng, out, in_, r, w):
        if eng == "act":
            self.op("act", lambda e: e.activation(out=out, in_=in_, func=AF.Copy), r, w)
        else:
            self.op(eng, lambda e: e.tensor_copy(out=out, in_=in_), r, w)

    def memset(self, eng, out, val, w):
        self.op(eng, lambda e: e.memset(out, val), (), w)

    def barrier(self):
        for e in self.ENG:
            for f in self.ENG:
                if f != e and self.cnt[f]:
                    self._wait(e, (f, self.cnt[f]))
            for k, v in enumerate(self.dval):
                if v:
                    self._wait(e, (k, v))

    def finish(self):
        self.barrier()
        ops = self.ops
        with self.nc.Block() as block:
            @block.tensor
            def _(e):
                for f in ops["pe"]:
                    f(e)

            @block.scalar
            def _(e):
                for f in ops["act"]:
                    f(e)

            @block.vector
            def _(e):
                for f in ops["dve"]:
                    f(e)

            @block.gpsimd
            def _(e):
                for f in ops["pool"]:
                    f(e)

            @block.sync
            def _(e):
                for f in ops["sp"]:
                    f(e)


class Arena:
    def __init__(self, ap, n):
        self.ap, self.n, self.top = ap, n, 0

    def f32(self, n, shape=None):
        a = self.top
        self.top += n
        assert self.top <= self.n, ("SBUF arena overflow", self.top, self.n)
        ap = self.ap[:, a:a + n]
        return T(ap)

    def bf16(self, n):
        t = self.f32((n + 1) // 2)
        t.ap = t.ap.bitcast(BF16)
        return t

    def mark(self):
        return self.top

    def release(self, m):
        self.top = m


def sub(bank, ap):
    t = T(ap)
    t.b = bank.b
    return t


def v3(ap, a):
    return ap.rearrange("p (a b) -> p a b", a=a)


class K:
    def __init__(self, debug=False, nlayers=L, phases=None):
        self.debug = debug
        self.nlayers = nlayers
        self.phases = phases
        nc = self.nc = bass.Bass("TRN2", target_bir_lowering=False)
        self.st = contextlib.ExitStack()
        self.P = Prog(nc, self.st)
        ein = lambda n, s, dt=F32: nc.dram_tensor(n, list(s), dt, kind="ExternalInput").ap()
        self.xT = ein("xT", (D, NT))
        self.cT = ein("cT", (128, 16))
        self.w_mod = ein("w_mod", (L, D, 6 * D))
        self.bmodT = ein("bmodT", (128, L * 48))
        self.n1T = ein("n1T", (128, L * 8))
        self.n2T = ein("n2T", (128, L * 8))
        self.fnT = ein("fnT", (128, 8))
        self.w_in = ein("w_in", (L, D, NW))
        self.rope = ein("rope", (128, 2, TL))
        self.lamp = ein("lamp", (1, L * 128))
        self.dnT = ein("dnT", (128, L))
        self.nab = ein("nab", (L, 4, 128, 3200))
        self.convT = ein("convT", (128, L * 45))
        self.gpar = ein("gpar", (1, L * 24 + L * 64))
        self.gmask = ein("gmask", (128, 2048))
        self.w_out = ein("w_out", (L, D, D))
        self.w_f1 = ein("w_f1", (L, D, 2 * DFF))
        self.w_f2 = ein("w_f2", (L, DFF, D))
        sk = "ExternalOutput" if debug else "Internal"
        scr = lambda n, s, dt=F32: nc.dram_tensor(n, list(s), dt, kind=sk).ap()
        self.xs = scr("xs", (D, NT))
        self.qa = scr("qa", (512, NT), BF16)
        self.ka = scr("ka", (512, NT), BF16)
        self.va = scr("va", (NT, 384), BF16)
        self.qkvb = scr("qkvb", (1152, NT))
        self.gab = scr("gab", (NT, 408))
        self.qc = scr("qc", (256, NT), BF16)
        self.kc = scr("kc", (256, NT), BF16)
        self.vc = scr("vc", (NT, 256), BF16)
        self.yT = scr("yT", (D, NT), BF16)
        self.qkn = scr("qkn", (1152, NT))
        self.of = scr("of", (NT, 384))
        self.outT = nc.dram_tensor("outT", [D, TL], F32, kind="ExternalOutput").ap()
        self.dbufs = {}
        arena_ap = self.st.enter_context(nc.sbuf_tensor("arena", [128, 53200], F32))
        self.A = Arena(arena_ap, 53200)
        self.ps = [T(self.st.enter_context(nc.psum_tensor("ps%d" % i, [128, 512], F32))[:]) for i in range(8)]
        for t in self.ps:
            t.b.g = [None, None]

    def db(self, name):
        return Buf()

    def consts(self):
        P, A = self.P, self.A
        self.ones = A.f32(128)
        P.memset("pool", self.ones.ap, 1.0, [self.ones.b])
        self.cTt = A.f32(16)
        P.dma(self.cTt.ap, self.cT, w=[self.cTt.b])
        self.sc = A.f32(16)
        P.act(self.sc.ap, self.cTt.ap, AF.Silu, [self.cTt.b], [self.sc.b])
        self.mod = A.f32(L * 48 * 2)
        self.bm = A.f32(L * 48)
        P.dma(self.bm.ap, self.bmodT, w=[self.bm.b])
        self.n1 = A.f32(L * 8)
        self.n2 = A.f32(L * 8)
        self.fn = A.f32(8)
        P.dma(self.n1.ap, self.n1T, w=[self.n1.b])
        P.dma(self.n2.ap, self.n2T, w=[self.n2.b])
        P.dma(self.fn.ap, self.fnT, w=[self.fn.b])
        self.cw = A.f32(L * 45)
        P.dma(self.cw.ap, self.convT, w=[self.cw.b])
        self.gm = A.f32(2048)
        P.dma(self.gm.ap, self.gmask, w=[self.gm.b])
        self.gm16 = A.bf16(640)
        P.copy("pool", self.gm16.ap[:, 0:128], self.gm.ap[:, 0:128], [self.gm.b], [self.gm16.b])
        P.copy("pool", self.gm16.ap[:, 128:640], self.gm.ap[:, 768:1280], [self.gm.b], [self.gm16.b])
        gp = self.gp = A.f32(L * 24 + L * 64)
        P.dma(gp.ap, self.gpar.partition_broadcast(128), w=[gp.b])
        self.nea = A.f32(L * 12)
        P.act(self.nea.ap, gp.ap[:, 0:L * 12], AF.Exp, [gp.b], [self.nea.b])
        P.ts("dve", self.nea.ap, self.nea.ap, -1.0, ALU.mult, [self.nea.b], [self.nea.b])
        self.nlam = A.f32(L)
        self.dnl = A.f32(L)
        self.a1 = A.f32(L * 16)
        self.a2 = A.f32(L * 16)
        mtmp = A.mark()
        lp = A.f32(L * 128)
        P.dma(lp.ap, self.lamp.partition_broadcast(128), w=[lp.b])
        lpv = lp.ap.rearrange("p (l f d) -> p l f d", l=L, f=4)
        pr_ = A.f32(L * 64)
        prv = pr_.ap.rearrange("p (l f d) -> p l f d", l=L, f=2)
        P.tt("dve", prv, lpv[:, :, 0:4:2, :], lpv[:, :, 1:4:2, :], ALU.mult, [lp.b], [pr_.b])
        ee = A.f32(L * 2)
        P.op("dve", lambda e: e.tensor_reduce(out=ee.ap, in_=pr_.ap.rearrange("p (g d) -> p g d", d=32), axis=AX.X, op=ALU.add), [pr_.b], [ee.b])
        P.act(ee.ap, ee.ap, AF.Exp, [ee.b], [ee.b])
        dn = A.f32(L)
        P.dma(dn.ap, self.dnT, w=[dn.b])
        for l in range(L):
            li = 0.8 - 0.6 * math.exp(-0.3 * l)
            P.stt("dve", self.nlam.ap[:, l:l + 1], ee.ap[:, 2 * l + 1:2 * l + 2], -li, ee.ap[:, 2 * l:2 * l + 1], ALU.add, ALU.subtract,
                  [ee.b], [self.nlam.b])
            P.ts("dve", self.dnl.ap[:, l:l + 1], dn.ap[:, l:l + 1], 1.0 - li, ALU.mult, [dn.b], [self.dnl.b])
        P.barrier()
        A.release(mtmp)

    def phase_mod(self):
        P, A = self.P, self.A
        m0 = A.mark()
        wm = [A.f32(8 * 768), A.f32(8 * 768)]
        ps = self.ps[0]
        it = 0
        for l in range(self.nlayers):
            wl = self.w_mod[l].rearrange("(kc p) n -> p kc n", p=128)
            for j in range(8):
                w = wm[it % 2]
                it += 1
                wv = v3(w.ap, 8)
                P.dma(wv, wl[:, :, j * 768:(j + 1) * 768], w=[w.b])
                for ct in range(6):
                    for kc in range(8):
                        P.mm(ps.ap[:, ct * 2:ct * 2 + 2], wv[:, kc, ct * 128:(ct + 1) * 128],
                             self.sc.ap[:, kc * 2:kc * 2 + 2], kc == 0, kc == 7, [w.b, self.sc.b], [ps.b])
                o = (l * 48 + j * 6) * 2
                P.tt("dve", v3(self.mod.ap[:, o:o + 12], 6), v3(ps.ap[:, 0:12], 6),
                     self.bm.ap[:, l * 48 + j * 6:l * 48 + j * 6 + 6].unsqueeze(2).to_broadcast([128, 6, 2]),
                     ALU.add, [ps.b, self.bm.b], [self.mod.b])
        for l in range(self.nlayers):
            for (a, n, which) in ((self.a1, self.n1, 1), (self.a2, self.n2, 4)):
                o = (l * 48 + which * 8) * 2
                P.stt("dve", v3(a.ap[:, l * 16:(l + 1) * 16], 8), v3(self.mod.ap[:, o:o + 16], 8), 1.0,
                      n.ap[:, l * 8:(l + 1) * 8].unsqueeze(2).to_broadcast([128, 8, 2]),
                      ALU.add, ALU.mult, [self.mod.b, n.b], [a.b])
        P.barrier()
        A.release(m0)

    def modv(self, l, which, kc, s):
        o = ((l * 48 + which * 8 + kc) * 2) + s
        return self.mod.ap[:, o:o + 1]

    def load_w_bf16(self, dst, src3, nk, ncols, stg, piece=512):
        P = self.P
        dv = v3(dst.ap, nk)
        i = 0
        for c0 in range(0, ncols, piece):
            c1 = min(ncols, c0 + piece)
            s = stg[i % 2]
            sv = v3(s.ap[:, 0:nk * (c1 - c0)], nk)
            P.dma(sv, src3[:, :, c0:c1], w=[s.b])
            P.copy("pool" if i % 2 else "act", dv[:, :, c0:c1], sv, [s.b], [dst.b])
            i += 1

    def norm_mod(self, xg, N, l, a, which_shift, s, hb, sq, rstd, ps_ss):
        P = self.P
        xv = v3(xg.ap, 8)
        hv = v3(hb.ap, 8)
        for kc in range(8):
            P.act(sq.ap[:, 0:N], xv[:, kc, :], AF.Square, [xg.b], [sq.b])
            P.mm(ps_ss.ap[:, 0:N], self.ones.ap, sq.ap[:, 0:N], kc == 0, kc == 7, [self.ones.b, sq.b], [ps_ss.b])
        P.act(rstd.ap[:, 0:N], ps_ss.ap[:, 0:N], AF.Ln, [ps_ss.b], [rstd.b], bias=EPS, scale=1.0 / D)
        P.act(rstd.ap[:, 0:N], rstd.ap[:, 0:N], AF.Exp, [rstd.b], [rstd.b], scale=-0.5)
        for kc in range(8):
            ao = l * 16 + kc * 2 + s
            P.stt("dve", sq.ap[:, 0:N], xv[:, kc, :], a.ap[:, ao:ao + 1], rstd.ap[:, 0:N], ALU.mult, ALU.mult,
                  [xg.b, a.b, rstd.b], [sq.b])
            P.act(hv[:, kc, :], sq.ap[:, 0:N], AF.Identity, [sq.b, self.mod.b], [hb.b], bias=self.modv(l, which_shift, kc, s), scale=1.0)

    def phase_in(self, l):
        P, A = self.P, self.A
        m0 = A.mark()
        Wb = A.bf16(8 * NW)
        stg = [A.f32(8 * 256), A.f32(8 * 256)]
        self.load_w_bf16(Wb, self.w_in[l].rearrange("(kc p) n -> p kc n", p=128), 8, NW, stg, piece=256)
        Wv = v3(Wb.ap, 8)
        xg = [A.f32(8 * 512), A.f32(8 * 512)]
        hb = [A.bf16(8 * 512), A.bf16(8 * 512)]
        sq = A.f32(512)
        rstd = A.f32(512)
        rp = [A.f32(1024), A.f32(1024)]
        ofm = [A.f32(512) for _ in range(3)]
        otm = [A.f32(408) for _ in range(3)]
        t1 = A.f32(512)
        t2 = A.f32(512)
        xsrc = self.xT if l == 0 else self.xs
        xsrc3 = xsrc.rearrange("(kc p) n -> p kc n", p=128)
        bx = self.db("xs")
        ps_ss = self.ps[0]
        psf = self.ps[1:5]
        pst = self.ps[5:8]
        nf = 0
        ntm = 0

        def load(gi):
            n0, N = GROUPS[gi]
            x = xg[gi % 2]
            P.dma(v3(x.ap[:, 0:8 * N], 8), xsrc3[:, :, n0:n0 + N], r=[bx], w=[x.b])
            if gi > 0:
                r_ = rp[gi % 2]
                P.dma(v3(r_.ap, 2), self.rope[:, :, n0 - NCTX:n0 - NCTX + 512], w=[r_.b])

        load(0)
        for gi, (n0, N) in enumerate(GROUPS):
            if gi + 1 < len(GROUPS):
                load(gi + 1)
            x = xg[gi % 2]
            h = hb[gi % 2]
            s = 1 if gi == 0 else 0
            xin = T(x.ap[:, 0:8 * N]); xin.b = x.b
            hin = T(h.ap[:, 0:8 * N]); hin.b = h.b
            self.norm_mod(xin, N, l, self.a1, 0, s, hin, sq, rstd, ps_ss)
            hv = v3(h.ap[:, 0:8 * N], 8)
            rv = v3(rp[gi % 2].ap, 2)

            def fm(ft, ps):
                for kc in range(8):
                    P.mm(ps.ap[:, 0:N], Wv[:, kc, ft * 128:(ft + 1) * 128], hv[:, kc, :], kc == 0, kc == 7, [Wb.b, h.b], [ps.b])

            for (c0, dst, nm) in ((C_QA, self.qa, "qa"), (C_KA, self.ka, "ka")):
                for j in range(4):
                    o = ofm[nf % 3]
                    ob = o.ap.bitcast(BF16)[:, 0:N]
                    p1 = psf[nf % 4]
                    nf += 1
                    fm(c0 // 128 + j, p1)
                    if gi == 0:
                        P.copy("dve", ob, p1.ap[:, 0:N], [p1.b], [o.b])
                    else:
                        p2 = psf[nf % 4]
                        nf += 1
                        fm(c0 // 128 + 4 + j, p2)
                        P.tt("dve", t1.ap, p1.ap, rv[:, 0, :], ALU.mult, [p1.b, rp[gi % 2].b], [t1.b])
                        P.tt("dve", t2.ap, p2.ap, rv[:, 1, :], ALU.mult, [p2.b, rp[gi % 2].b], [t2.b])
                        P.tt("pool", ob, t1.ap, t2.ap, ALU.add, [t1.b, t2.b], [o.b])
                    P.dma(dst[j * 128:(j + 1) * 128, n0:n0 + N], ob, r=[o.b], w=[self.db(nm)])
            for j in range(9):
                o = ofm[nf % 3]
                p1 = psf[nf % 4]
                nf += 1
                fm(C_B // 128 + j, p1)
                P.copy("act", o.ap[:, 0:N], p1.ap[:, 0:N], [p1.b], [o.b])
                P.dma(self.qkvb[j * 128:(j + 1) * 128, n0:n0 + N], o.ap[:, 0:N], r=[o.b], w=[self.db("qkvb")])
            for (c0, dst, nm) in ((C_QC, self.qc, "qc"), (C_KC, self.kc, "kc")):
                for j in range(2):
                    o = ofm[nf % 3]
                    ob = o.ap.bitcast(BF16)[:, 0:N]
                    p1 = psf[nf % 4]
                    nf += 1
                    fm(c0 // 128 + j, p1)
                    P.copy("dve", ob, p1.ap[:, 0:N], [p1.b], [o.b])
                    P.dma(dst[j * 128:(j + 1) * 128, n0:n0 + N], ob, r=[o.b], w=[self.db(nm)])
            for tt in range(N // 128):
                tok0 = n0 + tt * 128
                for (c0, nc_, dst, nm, isbf) in ((C_VA, 384, self.va, "va", True), (C_VC, 256, self.vc, "vc", True),
                                                 (C_G, 408, self.gab, "gab", False)):
                    ps = pst[ntm % 3]
                    o = otm[ntm % 3]
                    ntm += 1
                    for kc in range(8):
                        P.mm(ps.ap[:, 0:nc_], hv[:, kc, tt * 128:(tt + 1) * 128], Wv[:, kc, c0:c0 + nc_], kc == 0, kc == 7,
                             [Wb.b, h.b], [ps.b])
                    oa = o.ap.bitcast(BF16)[:, 0:nc_] if isbf else o.ap[:, 0:nc_]
                    P.copy("act" if ntm % 2 else "dve", oa, ps.ap[:, 0:nc_], [ps.b], [o.b])
                    P.dma(dst[tok0:tok0 + 128, :], oa, r=[o.b], w=[self.db(nm)])
        P.barrier()
        A.release(m0)


    def phase_a(self, l):
        P, A = self.P, self.A
        m0 = A.mark()
        KAt = A.bf16(4 * NT)
        P.dma(v3(KAt.ap, 4), self.ka.rearrange("(t p) n -> p t n", p=128), w=[KAt.b])
        Kv = v3(KAt.ap, 4)
        VA = A.bf16(NTILE * 6 * 128)
        Vv = VA.ap.rearrange("p (t h d) -> p t h d", t=NTILE, h=6)
        P.memset("pool", VA.ap, 1.0, [VA.b])
        vsrc = self.va.rearrange("(t p) (h d) -> p t h d", p=128, h=6)
        for t0 in range(NTILE):
            P.dma(Vv[:, t0, :, 0:64], vsrc[:, t0, :, :], w=[VA.b])
        Qt = [A.bf16(4 * 512), A.bf16(4 * 512)]
        pT = [[A.bf16(512) for _ in range(3)] for _ in range(2)]
        o12 = [[A.f32(512), A.f32(512)], [A.f32(512), A.f32(512)]]
        rz = A.f32(512)
        sq = A.f32(512)
        rs = A.f32(512)
        yb = [A.bf16(512), A.bf16(512)]
        deferred = []
        nh = 0
        psS = [self.ps[0:2], self.ps[2:4]]
        psO = self.ps[4:6]
        psN = self.ps[6]
        psF = self.ps[7]
        fl = A.bf16(128 + 512)
        P.memset("pool", fl.ap, 0.0, [fl.b])
        NFILL = getattr(self, "a_fill", 1)
        scale = 32.0 ** -0.5
        qsrc = self.qa.rearrange("(t p) n -> p t n", p=128)
        ny = 0

        def loadq(gi):
            n0, N = GROUPS[gi]
            q = Qt[gi % 2]
            P.dma(v3(q.ap, 4)[:, :, 0:N], qsrc[:, :, n0:n0 + N], w=[q.b])

        loadq(0)
        for gi, (n0, N) in enumerate(GROUPS):
            if gi + 1 < len(GROUPS):
                loadq(gi + 1)
            q = Qt[gi % 2]
            qv = v3(q.ap, 4)
            kts = [0, 1] if gi == 0 else list(range(NTILE))
            items = [(h, kt) for h in range(6) for kt in kts]

            def S(i):
                h, kt = items[i]
                for m in range(2):
                    hm = 2 * h + m
                    t, pr = hm // 3, (hm % 3) * 32
                    ps = psS[m][i % 2]
                    P.mm(ps.ap[:, 0:N], Kv[pr:pr + 32, t, kt * 128:(kt + 1) * 128], qv[pr:pr + 32, t, 0:N], True, True, [KAt.b, q.b], [ps.b])

            S(0)
            for i, (h, kt) in enumerate(items):
                if i + 1 < len(items):
                    S(i + 1)
                P.filler(NFILL, psF.ap[:, 0:512], fl.ap[:, 0:128], fl.ap[:, 128:640])
                for m in range(2):
                    ps = psS[m][i % 2]
                    p = pT[m][i % 3]
                    P.act(p.ap[:, 0:N], ps.ap[:, 0:N], AF.Exp, [ps.b], [p.b], scale=scale)
                for m in range(2):
                    p = pT[m][i % 3]
                    po = psO[m]
                    P.mm(po.ap[:, 0:N], Vv[:, kt, h, :], p.ap[:, 0:N], kt == kts[0], kt == kts[-1], [VA.b, p.b], [po.b])
                for fn_ in [f for (due, f) in deferred if due <= i]:
                    fn_()
                deferred[:] = [(due, f) for (due, f) in deferred if due > i]
                if kt == kts[-1]:
                    oo = o12[nh % 2]
                    nh += 1
                    for m in range(2):
                        po = psO[m]
                        P.copy("dve", oo[m].ap[:, 0:N], po.ap[:, 0:N], [po.b], [oo[m].b])
                    P.filler(getattr(self, "a_bfill", 4), psF.ap[:, 0:512], fl.ap[:, 0:128], fl.ap[:, 128:640])

                    def fin1(oo=oo, h=h, N=N, n0=n0):
                        for m in range(2):
                            o = oo[m]
                            P.op("dve", lambda e, o=o: e.reciprocal(out=rz.ap[0:64, 0:N], in_=o.ap[64:128, 0:N]), [o.b], [rz.b])
                            P.tt("dve", o.ap[0:64, 0:N], o.ap[0:64, 0:N], rz.ap[0:64, 0:N], ALU.mult, [o.b, rz.b], [o.b])
                        o1, o2 = oo
                        P.stt("dve", o1.ap[0:64, 0:N], o2.ap[0:64, 0:N], self.nlam.ap[0:64, l:l + 1], o1.ap[0:64, 0:N], ALU.mult, ALU.add,
                              [o1.b, o2.b, self.nlam.b], [o1.b])
                        P.tt("pool", sq.ap[0:64, 0:N], o1.ap[0:64, 0:N], o1.ap[0:64, 0:N], ALU.mult, [o1.b], [sq.b])

                    def fin2(oo=oo, h=h, N=N, n0=n0):
                        o1 = oo[0]
                        P.mm(psN.ap[0:64, 0:N], self.ones.ap[0:64, 0:64], sq.ap[0:64, 0:N], True, True, [self.ones.b, sq.b], [psN.b])
                        P.act(rs.ap[0:64, 0:N], psN.ap[0:64, 0:N], AF.Ln, [psN.b], [rs.b], bias=EPS, scale=1.0 / 64)
                        P.act(rs.ap[0:64, 0:N], rs.ap[0:64, 0:N], AF.Exp, [rs.b], [rs.b], scale=-0.5)
                        y = yb[h % 2]
                        P.stt("dve", y.ap[0:64, 0:N], o1.ap[0:64, 0:N], self.dnl.ap[0:64, l:l + 1], rs.ap[0:64, 0:N], ALU.mult, ALU.mult,
                              [o1.b, rs.b, self.dnl.b], [y.b])
                        P.dma(self.yT[h * 64:(h + 1) * 64, n0:n0 + N], y.ap[0:64, 0:N], r=[y.b])

                    fin1()
                    if gi == 0:
                        fin2()
                    else:
                        deferred.append((i + 6, fin2))
            for (_, f) in deferred:
                f()
            deferred[:] = []
        P.barrier()
        A.release(m0)

    def phase_c(self, l):
        P, A = self.P, self.A
        m0 = A.mark()
        KCt = A.bf16(2 * NT)
        QCt = A.bf16(2 * NT)
        P.dma(v3(KCt.ap, 2), self.kc.rearrange("(t p) n -> p t n", p=128), w=[KCt.b])
        P.dma(v3(QCt.ap, 2), self.qc.rearrange("(t p) n -> p t n", p=128), w=[QCt.b])
        Kv, Qv = v3(KCt.ap, 2), v3(QCt.ap, 2)
        VC = A.bf16(NTILE * 4 * 128)
        Vv = VC.ap.rearrange("p (t h d) -> p t h d", t=NTILE, h=4)
        P.memset("pool", VC.ap, 1.0, [VC.b])
        vsrc = self.vc.rearrange("(t p) (h d) -> p t h d", p=128, h=4)
        for t0 in range(NTILE):
            P.dma(Vv[:, t0, :, 0:64], vsrc[:, t0, :, :], w=[VC.b])
        NB = A.f32(4 * 3200)
        NBv = NB.ap.rearrange("p (h a j q) -> p h a j q", h=4, a=5, j=5)
        for h in range(4):
            P.dma(NB.ap[:, h * 3200:(h + 1) * 3200], self.nab[l, h], w=[NB.b])
        sA = [A.f32(640), A.f32(640)]
        pA = [A.bf16(896), A.bf16(896)]
        rz = A.f32(128)
        ys = [A.bf16(256), A.bf16(256)]
        psA = self.ps[0:4]
        psO = self.ps[4:6]
        scale = 64.0 ** -0.5
        it = 0
        for qt in range(NTILE):
            yst = ys[qt % 2]
            ysv = v3(yst.ap, 2)
            if qt < 2:
                lat, pat = [], 0
            else:
                r0 = 2 * (qt - 2)
                base = min(max(r0 - 4, 0), 54)
                pat = (r0 - base) // 2
                lat = [2 + base // 2 + j for j in range(5)]
            kts = lat + [0, 1]
            for h in range(4):
                t, pr = h // 2, (h % 2) * 64
                pa, pb = psA[2 * (it % 2)], psA[2 * (it % 2) + 1]
                s_ = sA[it % 2]
                p_ = pA[it % 2]
                po = psO[it % 2]
                it += 1
                qop = Qv[pr:pr + 64, t, qt * 128:(qt + 1) * 128]
                for j, kt in enumerate(kts):
                    dst = pa.ap[:, j * 128:(j + 1) * 128] if j < 4 else pb.ap[:, (j - 4) * 128:(j - 3) * 128]
                    P.mm(dst, Kv[pr:pr + 64, t, kt * 128:(kt + 1) * 128], qop, True, True, [KCt.b, QCt.b], [pa.b if j < 4 else pb.b])
                nk = len(kts)
                if lat:
                    P.stt("dve", s_.ap[:, 0:512], pa.ap[:, 0:512], scale, NBv[:, h, pat, 0:4, :].rearrange("p j q -> p (j q)"), ALU.mult, ALU.add,
                          [pa.b, NB.b], [s_.b])
                    P.stt("dve", s_.ap[:, 512:640], pb.ap[:, 0:128], scale, NBv[:, h, pat, 4, :], ALU.mult, ALU.add,
                          [pb.b, NB.b], [s_.b])
                    P.act(p_.ap[:, 0:640], s_.ap[:, 0:640], AF.Exp, [s_.b], [p_.b])
                    P.act(p_.ap[:, 640:896], pb.ap[:, 128:384], AF.Exp, [pb.b], [p_.b], scale=scale)
                else:
                    P.act(p_.ap[:, 0:256], pa.ap[:, 0:256], AF.Exp, [pa.b], [p_.b], scale=scale)
                for j, kt in enumerate(kts):
                    P.mm(po.ap[:, 0:128], Vv[:, kt, h, :], p_.ap[:, j * 128:(j + 1) * 128], j == 0, j == nk - 1, [VC.b, p_.b], [po.b])
                P.op("dve", lambda e, po=po: e.reciprocal(out=rz.ap[0:64, :], in_=po.ap[64:128, 0:128]), [po.b], [rz.b])
                P.tt("dve", ysv[pr:pr + 64, t, :], po.ap[0:64, 0:128], rz.ap[0:64, :], ALU.mult, [po.b, rz.b], [yst.b])
            P.dma(self.yT[768:1024, qt * 128:(qt + 1) * 128].rearrange("(t p) n -> p t n", p=128), ysv, r=[yst.b])
        P.barrier()
        A.release(m0)

    def gmk(self, i, ncol=128):
        return self.gm.ap[:, i * 128:i * 128 + ncol]

    def phase_b1(self, l):
        P, A = self.P, self.A
        m0 = A.mark()
        xb = [A.f32(516) for _ in range(3)]
        acc = [A.f32(512) for _ in range(2)]
        sl = [A.f32(512) for _ in range(3)]
        sqs = [A.f32(512) for _ in range(2)]
        rs = A.f32(512)
        ob = [A.f32(512) for _ in range(2)]
        ps = self.ps[0:2]
        onesblk = self.gmk(1)
        work = [(ft, n0, N) for ft in range(9) for (n0, N) in GROUPS]

        def stage1(it):
            ft, n0, N = work[it]
            s0, s1 = (0, NCTX) if n0 < NCTX else (NCTX, NT)
            x, a, sv, sq, p_ = xb[it % 3], acc[it % 2], sl[it % 3], sqs[it % 2], ps[it % 2]
            lo, hi = max(n0 - 2, s0), min(n0 + N + 2, s1)
            if lo > n0 - 2:
                P.memset("pool", x.ap[:, 0:2], 0.0, [x.b])
            if hi < n0 + N + 2:
                P.memset("pool", x.ap[:, N + 2:N + 4], 0.0, [x.b])
            P.dma(x.ap[:, lo - (n0 - 2):hi - (n0 - 2)], self.qkvb[ft * 128:(ft + 1) * 128, lo:hi], w=[x.b])
            co = l * 45 + ft * 5
            P.ts("dve", a.ap[:, 0:N], x.ap[:, 0:N], self.cw.ap[:, co:co + 1], ALU.mult, [x.b, self.cw.b], [a.b])
            for j in range(1, 5):
                P.stt("dve", a.ap[:, 0:N], x.ap[:, j:j + N], self.cw.ap[:, co + j:co + j + 1], a.ap[:, 0:N], ALU.mult, ALU.add,
                      [x.b, self.cw.b, a.b], [a.b])
            P.act(sv.ap[:, 0:N], a.ap[:, 0:N], AF.Silu, [a.b], [sv.b])
            if ft < 6:
                P.tt("pool", sq.ap[:, 0:N], sv.ap[:, 0:N], sv.ap[:, 0:N], ALU.mult, [sv.b], [sq.b])
                P.mm(p_.ap[:, 0:N], onesblk, sq.ap[:, 0:N], True, True, [self.gm.b, sq.b], [p_.b])

        def stage2(it):
            ft, n0, N = work[it]
            sv, p_, o = sl[it % 3], ps[it % 2], ob[it % 2]
            if ft < 6:
                P.act(rs.ap[:, 0:N], p_.ap[:, 0:N], AF.Ln, [p_.b], [rs.b], bias=EPS, scale=1.0)
                P.act(rs.ap[:, 0:N], rs.ap[:, 0:N], AF.Exp, [rs.b], [rs.b], scale=-0.5)
                P.stt("dve", o.ap[:, 0:N], sv.ap[:, 0:N], 0.125 if ft < 3 else 1.0, rs.ap[:, 0:N], ALU.mult, ALU.mult, [sv.b, rs.b], [o.b])
                P.dma(self.qkn[ft * 128:(ft + 1) * 128, n0:n0 + N], o.ap[:, 0:N], r=[o.b])
            else:
                P.dma(self.qkn[ft * 128:(ft + 1) * 128, n0:n0 + N], sv.ap[:, 0:N], r=[sv.b])

        stage1(0)
        for it in range(len(work)):
            if it + 1 < len(work):
                stage1(it + 1)
            stage2(it)
        P.barrier()
        A.release(m0)

    def phase_b2(self, l, d):
        P, A = self.P, self.A
        m0 = A.mark()
        ident = self.gmk(0)
        A_d, B_d, MS_d, MIT_d, SelEnd = self.gmk(2 + d), self.gmk(4 + d), self.gmk(6 + d), self.gmk(8 + d), self.gmk(10 + d)
        gmb = self.gm.b
        f = A.f32
        gabt = [f(408), f(408)]
        fmt = [f(9 * 128), f(9 * 128)]
        oft = [f(384), f(384)]
        SC = [[f(6) for _ in range(11)] + [f(12)] + [f(384) for _ in range(5)] for _ in range(2)]
        Ag, Bg, dg, Es, EiT, aqkT, TT, wT, qhT = [[f(384), f(384)] for _ in range(9)]
        Pm = [[f(384), f(384)], [f(384), f(384)]]
        PTm = [[f(384), f(384)], [f(384), f(384)]]
        u = [f(192), f(192)]
        vnew = [f(192), f(192)]
        S = [f(192), f(192)]
        otile = [f(384), f(384)]
        sq = f(384)
        ss = f(6)
        rs = f(6)
        sg = f(384)
        ybf = A.bf16(384)
        for hb in range(2):
            P.memset("pool", S[hb].ap[0:64, :], 0.0, [S[hb].b])
        QB = [self.ps[0:4], self.ps[4:8]]
        fmbs = [A.bf16(768), A.bf16(768)]
        tmpos = [f(192), f(192)]
        otbs = [[T(o_.ap[:, 0:192]), T(o_.ap[:, 192:384])] for o_ in otile]
        ident16, MS16, MIT16 = self.gm16.ap[:, 0:128], self.gm16.ap[:, (1 + d) * 128:(2 + d) * 128], self.gm16.ap[:, (3 + d) * 128:(4 + d) * 128]
        psK = self.ps[0]
        kcorn = [sub(self.ps[j_], self.ps[j_].ap[:, 384:512]) for j_ in (0, 2, 3)]
        vcorn = [sub(self.ps[j_], self.ps[j_].ap[:, 384:512]) for j_ in (4, 5, 6)]
        psSc = sub(self.ps[1], self.ps[1].ap[:, 384:512])
        dtb = self.gp.ap[:, L * 12 + l * 12 + d * 6:L * 12 + l * 12 + d * 6 + 6]
        nea = self.nea.ap[:, l * 12 + d * 6:l * 12 + d * 6 + 6]
        gw = self.gp.ap[:, L * 24 + l * 64:L * 24 + (l + 1) * 64]
        qkn3 = self.qkn.rearrange("(f p) n -> p f n", p=128)
        order = list(range(NTILE)) if d == 0 else [1, 0] + list(range(NTILE - 1, 1, -1))
        chunks = (0, 1) if d == 0 else (1, 0)
        if getattr(self, "b2_tiles", None):
            order = order[:self.b2_tiles]

        def load(i):
            tt = order[i]
            tok0 = tt * 128
            P.dma(gabt[i % 2].ap, self.gab[tok0:tok0 + 128, :], w=[gabt[i % 2].b])
            P.dma(v3(fmt[i % 2].ap, 9), qkn3[:, :, tok0:tok0 + 128], w=[fmt[i % 2].b])
            if d == 1:
                P.dma(oft[i % 2].ap, self.of[tok0:tok0 + 128, :], w=[oft[i % 2].b])

        bc3 = lambda ap, n: ap.unsqueeze(2).to_broadcast([128, ap.shape[1], n])
        mb3 = lambda ap, h: ap.unsqueeze(1).to_broadcast([128, h, 128])
        def common(i):
            par = i % 2
            z, e_, g, eb, beta, nbeta, gc, eg, dgl, ek, bg, egl, ktm, vtm, ktail, kbg, vb = SC[par]
            ga, fm, ot, fmb = gabt[par], fmt[par], otile[par], fmbs[par]
            return (z, e_, g, eb, beta, nbeta, gc, eg, dgl, ek, bg, egl, ktm, vtm, ktail, kbg, vb, ga, fm, v3(fm.ap, 9), ot, fmb, v3(fmb.ap, 6))

        def prep(i):
            load(i)
            z, e_, g, eb, beta, nbeta, gc, eg, dgl, ek, bg, egl, ktm, vtm, ktail, kbg, vb, ga, fm, fmv, ot, fmb, fbv = common(i)
            P.tt("dve", z.ap, ga.ap[:, 384 + 6 * d:390 + 6 * d], dtb, ALU.add, [ga.b, self.gp.b], [z.b])
            yield
            P.act(e_.ap, z.ap, AF.Exp, [z.b], [e_.b])
            yield
            P.act(e_.ap, e_.ap, AF.Ln, [e_.b], [e_.b], bias=1.0)
            yield
            P.tt("dve", g.ap, e_.ap, nea, ALU.mult, [e_.b, self.nea.b], [g.b])
            yield
            P.act(eb.ap, ga.ap[:, 396 + 6 * d:402 + 6 * d], AF.Exp, [ga.b], [eb.b], scale=-1.0)
            yield
            P.ts("dve", eb.ap, eb.ap, 1.0, ALU.add, [eb.b], [eb.b])
            yield
            P.op("dve", lambda e: e.reciprocal(out=beta.ap, in_=eb.ap), [eb.b], [beta.b])
            yield
            P.ts("dve", nbeta.ap, beta.ap, -1.0, ALU.mult, [beta.b], [nbeta.b])
            yield
            P.mm(psSc.ap[:, 0:6], A_d, g.ap, True, True, [gmb, g.b], [psSc.b])
            yield
            P.copy("dve", gc.ap, psSc.ap[:, 0:6], [psSc.b], [gc.b])
            yield
            P.act(eg.ap, gc.ap, AF.Exp, [gc.b], [eg.b])
            yield
            P.mm(psSc.ap[:, 8:14], SelEnd, gc.ap, True, True, [gmb, gc.b], [psSc.b])
            yield
            P.tt("dve", dgl.ap, psSc.ap[:, 8:14], gc.ap, ALU.subtract, [psSc.b, gc.b], [dgl.b])
            yield
            P.act(ek.ap, dgl.ap, AF.Exp, [dgl.b], [ek.b])
            yield
            P.tt("dve", bg.ap, beta.ap, eg.ap, ALU.mult, [beta.b, eg.b], [bg.b])
            yield
            for c in range(2):
                P.mm(psSc.ap[0:64, 16 + c * 6:22 + c * 6], self.gmk(12 + 2 * d + c, 64), gc.ap, True, True, [gmb, gc.b], [psSc.b])
                yield
            P.act(egl.ap[0:64, :], psSc.ap[0:64, 16:28], AF.Exp, [psSc.b], [egl.b])
            yield
            for j in range(3):
                cn = kcorn[j]
                P.tr(cn.ap, fmv[:, 3 + j, :], ident, [fm.b, gmb], [cn.b])
                yield
                P.copy("act", ktm.ap[:, j * 128:(j + 1) * 128], cn.ap, [cn.b], [ktm.b])
                yield
            for j in range(3):
                cn = vcorn[j]
                P.tr(cn.ap, fmv[:, 6 + j, :], ident, [fm.b, gmb], [cn.b])
                yield
                P.copy("dve", vtm.ap[:, j * 128:(j + 1) * 128], cn.ap, [cn.b], [vtm.b])
                yield
            k3, v3_ = v3(ktm.ap, 6), v3(vtm.ap, 6)
            P.tt("pool", v3(ktail.ap, 6), k3, bc3(ek.ap, 64), ALU.mult, [ktm.b, ek.b], [ktail.b])
            yield
            P.tt("pool", v3(kbg.ap, 6), k3, bc3(bg.ap, 64), ALU.mult, [ktm.b, bg.b], [kbg.b])
            yield
            P.tt("pool", v3(vb.ap, 6), v3_, bc3(beta.ap, 64), ALU.mult, [vtm.b, beta.b], [vb.b])
            yield
            P.copy("pool", fmb.ap, fm.ap[:, 0:768], [fm.b], [fmb.b])
            yield

        def body(i, nxt):
            tt = order[i]
            tok0 = tt * 128
            z, e_, g, eb, beta, nbeta, gc, eg, dgl, ek, bg, egl, ktm, vtm, ktail, kbg, vb, ga, fm, fmv, ot, fmb, fbv = common(i)
            def batch(hb, fm=fm, fmv=fmv, fbv=fbv, fmb=fmb, i=i):
                Q0, Q1, Q2, Q3 = QB[hb]
                H = [3 * hb, 3 * hb + 1, 3 * hb + 2]
                hs = slice(3 * hb, 3 * hb + 3)
                ag, bgm, dgm, es, eit, aq, tt_, w_, qh = Ag[hb], Bg[hb], dg[hb], Es[hb], EiT[hb], aqkT[hb], TT[hb], wT[hb], qhT[hb]
                otb = otbs[i % 2][hb]
                tmpo = tmpos[hb]
                P.tt("pool", v3(ag.ap, 3), mb3(A_d, 3), bc3(g.ap[:, hs], 128), ALU.mult, [gmb, g.b], [ag.b])
                P.tt("pool", v3(bgm.ap, 3), mb3(B_d, 3), bc3(g.ap[:, hs], 128), ALU.mult, [gmb, g.b], [bgm.b])
                P.tt("pool", v3(dgm.ap, 3), mb3(ident, 3), bc3(eg.ap[:, hs], 128), ALU.mult, [gmb, eg.b], [dgm.b])
                for k, h in enumerate(H):
                    hp, hr = h // 2, (h % 2) * 64
                    kTb = fbv[hr:hr + 64, 3 + hp, :]
                    qTb = fbv[hr:hr + 64, hp, :]
                    ks = slice(k * 128, (k + 1) * 128)
                    P.mm(Q0.ap[:, ks], kTb, kTb, True, True, [fmb.b], [Q0.b])
                    P.mm(Q1.ap[:, ks], kTb, qTb, True, True, [fmb.b], [Q1.b])
                    P.mm(Q2.ap[:, ks], ag.ap[:, ks], B_d, True, False, [ag.b, gmb], [Q2.b])
                    P.mm(Q2.ap[:, ks], ident16, MS16, False, True, [self.gm16.b], [Q2.b])
                    P.mm(Q3.ap[:, ks], bgm.ap[:, ks], A_d, True, False, [bgm.b, gmb], [Q3.b])
                    P.mm(Q3.ap[:, ks], ident16, MIT16, False, True, [self.gm16.b], [Q3.b])
                yield
                P.act(es.ap, Q2.ap[:, 0:384], AF.Exp, [Q2.b], [es.b])
                P.act(eit.ap, Q3.ap[:, 0:384], AF.Exp, [Q3.b], [eit.b])
                p0, pt0 = Pm[hb][0], PTm[hb][0]
                for k, h in enumerate(H):
                    ks = slice(k * 128, (k + 1) * 128)
                    P.stt("dve", p0.ap[:, ks], Q0.ap[:, ks], nbeta.ap[:, h:h + 1], es.ap[:, ks], ALU.mult, ALU.mult, [Q0.b, es.b, nbeta.b], [p0.b])
                P.tt("dve", aq.ap, Q1.ap[:, 0:384], eit.ap, ALU.mult, [Q1.b, eit.b], [aq.b])
                yield
                for k in range(3):
                    ks = slice(k * 128, (k + 1) * 128)
                    P.tr(Q0.ap[:, ks], p0.ap[:, ks], ident, [p0.b, gmb], [Q0.b])
                yield
                P.copy("act", pt0.ap, Q0.ap[:, 0:384], [Q0.b], [pt0.b])
                for k in range(3):
                    ks = slice(k * 128, (k + 1) * 128)
                    P.tt("dve", tt_.ap[:, ks], Q0.ap[:, ks], ident, ALU.add, [Q0.b, gmb], [tt_.b])
                pc, ptc = p0, pt0
                yield
                for lvl in range(5):
                    pn, ptn = Pm[hb][(lvl + 1) % 2], PTm[hb][(lvl + 1) % 2]
                    for k in range(3):
                        ks = slice(k * 128, (k + 1) * 128)
                        P.mm(Q1.ap[:, ks], ptc.ap[:, ks], pc.ap[:, ks], True, True, [ptc.b, pc.b], [Q1.b])
                    if lvl < 4:
                        for k in range(3):
                            ks = slice(k * 128, (k + 1) * 128)
                            P.mm(Q2.ap[:, ks], pc.ap[:, ks], ptc.ap[:, ks], True, True, [ptc.b, pc.b], [Q2.b])
                    yield
                    P.copy("act", pn.ap, Q1.ap[:, 0:384], [Q1.b], [pn.b])
                    if lvl < 4:
                        P.copy("dve", ptn.ap, Q2.ap[:, 0:384], [Q2.b], [ptn.b])
                    yield
                    for k in range(3):
                        ks = slice(k * 128, (k + 1) * 128)
                        P.mm(Q3.ap[:, ks], pn.ap[:, ks], tt_.ap[:, ks], True, True, [pn.b, tt_.b], [Q3.b])
                    yield
                    P.tt("dve", tt_.ap, tt_.ap, Q3.ap[:, 0:384], ALU.add, [tt_.b, Q3.b], [tt_.b])
                    pc, ptc = pn, ptn
                    yield
                for k, h in enumerate(H):
                    hp, hr = h // 2, (h % 2) * 64
                    ks = slice(k * 128, (k + 1) * 128)
                    P.mm(Q0.ap[:, k * 64:(k + 1) * 64], tt_.ap[:, ks], vb.ap[:, h * 64:(h + 1) * 64], True, True, [tt_.b, vb.b], [Q0.b])
                    P.mm(Q1.ap[0:64, ks], kbg.ap[:, h * 64:(h + 1) * 64], tt_.ap[:, ks], True, True, [tt_.b, kbg.b], [Q1.b])
                    P.mm(Q2.ap[0:64, ks], self.ones.ap[:, 0:64], dgm.ap[:, ks], True, True, [self.ones.b, dgm.b], [Q2.b])
                yield
                P.copy("act", u[hb].ap, Q0.ap[:, 0:192], [Q0.b], [u[hb].b])
                P.copy("dve", w_.ap[0:64, :], Q1.ap[0:64, 0:384], [Q1.b], [w_.b])
                for k, h in enumerate(H):
                    hp, hr = h // 2, (h % 2) * 64
                    ks = slice(k * 128, (k + 1) * 128)
                    P.tt("dve", qh.ap[0:64, ks], fmv[hr:hr + 64, hp, :], Q2.ap[0:64, ks], ALU.mult, [fm.b, Q2.b], [qh.b])
                yield
                Sb = S[hb]
                vn = vnew[hb]
                for c in chunks:
                    cs = c * 64
                    for k, h in enumerate(H):
                        ks = slice(k * 128, (k + 1) * 128)
                        P.mm(Q3.ap[:, k * 64:(k + 1) * 64], w_.ap[0:64, ks], Sb.ap[0:64, k * 64:(k + 1) * 64], True, True, [w_.b, Sb.b], [Q3.b])
                    yield
                    P.tt("dve", vn.ap[cs:cs + 64, :], u[hb].ap[cs:cs + 64, :], Q3.ap[cs:cs + 64, 0:192], ALU.subtract, [u[hb].b, Q3.b], [vn.b])
                    yield
                    for k, h in enumerate(H):
                        ks = slice(k * 128, (k + 1) * 128)
                        k6 = slice(k * 64, (k + 1) * 64)
                        P.mm(Q0.ap[:, k6], qh.ap[0:64, ks], Sb.ap[0:64, k6], True, True, [qh.b, Sb.b], [Q0.b])
                        P.mm(Q1.ap[:, k6], aq.ap[cs:cs + 64, ks], vn.ap[cs:cs + 64, k6], True, True, [aq.b, vn.b], [Q1.b])
                        P.mm(Q2.ap[0:64, k6], ktail.ap[cs:cs + 64, h * 64:(h + 1) * 64], vn.ap[cs:cs + 64, k6], True, True, [ktail.b, vn.b], [Q2.b])
                    yield
                    P.copy("act", tmpo.ap[cs:cs + 64, :], Q1.ap[cs:cs + 64, 0:192], [Q1.b], [tmpo.b])
                    P.tt("dve", otb.ap[cs:cs + 64, :], Q0.ap[cs:cs + 64, 0:192], tmpo.ap[cs:cs + 64, :], ALU.add,
                         [Q0.b, tmpo.b], [otb.b])
                    for k, h in enumerate(H):
                        k6 = slice(k * 64, (k + 1) * 64)
                        P.stt("dve", Sb.ap[0:64, k6], Sb.ap[0:64, k6], egl.ap[0:64, c * 6 + h:c * 6 + h + 1], Q2.ap[0:64, k6], ALU.mult, ALU.add,
                              [Sb.b, egl.b, Q2.b], [Sb.b])
                    yield

            gens = [batch(0), batch(1)] + ([nxt] if nxt is not None else [])
            for _ in itertools.zip_longest(*gens):
                pass
            ot_r = [otbs[i % 2][0].b, otbs[i % 2][1].b]
            if d == 0:
                P.dma(self.of[tok0:tok0 + 128, :], ot.ap, r=ot_r)
            else:
                of_ = oft[i % 2]
                P.tt("pool", sg.ap, ot.ap, of_.ap, ALU.add, ot_r + [of_.b], [sg.b])
                P.tt("pool", sq.ap, sg.ap, sg.ap, ALU.mult, [sg.b], [sq.b])
                P.op("dve", lambda e: e.tensor_reduce(out=ss.ap, in_=v3(sq.ap, 6), axis=AX.X, op=ALU.add), [sq.b], [ss.b])
                P.act(rs.ap, ss.ap, AF.Ln, [ss.b], [rs.b], bias=EPS, scale=1.0 / 64)
                P.act(rs.ap, rs.ap, AF.Exp, [rs.b], [rs.b], scale=-0.5)
                P.tt("dve", v3(sq.ap, 6), v3(sg.ap, 6), bc3(rs.ap, 64), ALU.mult, [sg.b, rs.b], [sq.b])
                P.tt("pool", v3(ot.ap, 6), v3(sq.ap, 6), gw.unsqueeze(1).to_broadcast([128, 6, 64]), ALU.mult, [sq.b, self.gp.b], ot_r)
                P.act(sg.ap, ga.ap[:, 0:384], AF.Silu, [ga.b], [sg.b])
                P.tt("dve", sq.ap, ot.ap, sg.ap, ALU.mult, ot_r + [sg.b], [sq.b])
                for j in range(3):
                    P.tr(psK.ap[:, j * 128:(j + 1) * 128], sq.ap[:, j * 128:(j + 1) * 128], ident, [sq.b, gmb], [psK.b])
                P.copy("act", ybf.ap, psK.ap[:, 0:384], [psK.b], [ybf.b])
                P.dma(self.yT[384:768, tok0:tok0 + 128].rearrange("(j p) n -> p j n", p=128), v3(ybf.ap, 3), r=[ybf.b])

        for _ in prep(0):
            pass
        for i in range(len(order)):
            body(i, prep(i + 1) if i + 1 < len(order) else None)
        P.barrier()
        A.release(m0)

    def phase_out(self, l):
        P, A = self.P, self.A
        m0 = A.mark()
        last = (l == self.nlayers - 1) and self.nlayers == L
        Wo = A.bf16(8 * D)
        W1 = A.bf16(8 * 2 * DFF)
        W2 = A.bf16(22 * D)
        stg = [A.f32(1280), A.f32(1280)]
        self.load_w_bf16(Wo, self.w_out[l].rearrange("(kc p) n -> p kc n", p=128), 8, D, stg, piece=160)
        self.load_w_bf16(W1, self.w_f1[l].rearrange("(kc p) n -> p kc n", p=128), 8, 2 * DFF, stg, piece=160)
        self.load_w_bf16(W2, self.w_f2[l].rearrange("(kc p) n -> p kc n", p=128), 22, D, stg, piece=58)
        Wov, W1v, W2v = v3(Wo.ap, 8), v3(W1.ap, 8), v3(W2.ap, 22)
        N = 256
        xg = A.f32(8 * N)
        yg = [A.bf16(8 * N), A.bf16(8 * N)]
        hb = A.bf16(8 * N)
        actb = A.bf16(22 * N)
        sq = A.f32(N)
        rstd = A.f32(N)
        sgt = [A.f32(N)]
        xsrc3 = (self.xT if l == 0 else self.xs).rearrange("(kc p) n -> p kc n", p=128)
        xdst3 = self.xs.rearrange("(kc p) n -> p kc n", p=128)
        y3 = self.yT.rearrange("(kc p) n -> p kc n", p=128)
        o3 = self.outT.rearrange("(kc p) n -> p kc n", p=128)
        ps_ss = self.ps[0]
        psr = self.ps[1:4]
        psg = self.ps[4:6]
        psu = self.ps[6:8]
        ngr = NT // N
        xv = v3(xg.ap, 8)
        hv = v3(hb.ap, 8)
        av = v3(actb.ap, 22)
        nr = 0

        def loady(gi):
            P.dma(v3(yg[gi % 2].ap, 8), y3[:, :, gi * N:(gi + 1) * N], w=[yg[gi % 2].b])

        loady(0)
        for gi in range(ngr):
            n0 = gi * N
            s = 1 if gi == 0 else 0
            if gi + 1 < ngr:
                loady(gi + 1)
            P.dma(xv, xsrc3[:, :, n0:n0 + N], w=[xg.b])
            y = yg[gi % 2]
            yv = v3(y.ap, 8)
            for dt in range(8):
                ps = psr[nr % 3]
                nr += 1
                for kc in range(8):
                    P.mm(ps.ap[:, 0:N], Wov[:, kc, dt * 128:(dt + 1) * 128], yv[:, kc, :], kc == 0, kc == 7, [Wo.b, y.b], [ps.b])
                P.stt("dve", xv[:, dt, :], ps.ap[:, 0:N], self.modv(l, 2, dt, s), xv[:, dt, :], ALU.mult, ALU.add, [ps.b, self.mod.b, xg.b], [xg.b])
            self.norm_mod(xg, N, l, self.a2, 3, s, hb, sq, rstd, ps_ss)
            for ft in range(22):
                pg, pu = psg[ft % 2], psu[ft % 2]
                for kc in range(8):
                    P.mm(pg.ap[:, 0:N], W1v[:, kc, ft * 128:(ft + 1) * 128], hv[:, kc, :], kc == 0, kc == 7, [W1.b, hb.b], [pg.b])
                for kc in range(8):
                    P.mm(pu.ap[:, 0:N], W1v[:, kc, DFF + ft * 128:DFF + (ft + 1) * 128], hv[:, kc, :], kc == 0, kc == 7, [W1.b, hb.b], [pu.b])
                sg = sgt[0]
                P.act(sg.ap, pg.ap[:, 0:N], AF.Silu, [pg.b], [sg.b])
                P.tt("dve", av[:, ft, :], pu.ap[:, 0:N], sg.ap, ALU.mult, [pu.b, sg.b], [actb.b])
            for dt in range(8):
                ps = psr[nr % 3]
                nr += 1
                for ft in range(22):
                    P.mm(ps.ap[:, 0:N], W2v[:, ft, dt * 128:(dt + 1) * 128], av[:, ft, :], ft == 0, ft == 21, [W2.b, actb.b], [ps.b])
                P.stt("dve", xv[:, dt, :], ps.ap[:, 0:N], self.modv(l, 5, dt, s), xv[:, dt, :], ALU.mult, ALU.add, [ps.b, self.mod.b, xg.b], [xg.b])
            if not last:
                P.dma(xdst3[:, :, n0:n0 + N], xv, r=[xg.b])
            elif gi >= 1:
                for kc in range(8):
                    P.act(sq.ap, xv[:, kc, :], AF.Square, [xg.b], [sq.b])
                    P.mm(ps_ss.ap[:, 0:N], self.ones.ap, sq.ap, kc == 0, kc == 7, [self.ones.b, sq.b], [ps_ss.b])
                P.act(rstd.ap, ps_ss.ap[:, 0:N], AF.Ln, [ps_ss.b], [rstd.b], bias=EPS, scale=1.0 / D)
                P.act(rstd.ap, rstd.ap, AF.Exp, [rstd.b], [rstd.b], scale=-0.5)
                for kc in range(8):
                    P.stt("dve", xv[:, kc, :], xv[:, kc, :], self.fn.ap[:, kc:kc + 1], rstd.ap, ALU.mult, ALU.mult, [xg.b, self.fn.b, rstd.b], [xg.b])
                P.dma(o3[:, :, n0 - NCTX:n0 - NCTX + N], xv, r=[xg.b])
        P.barrier()
        A.release(m0)

    def build(self):
        self.consts()
        self.phase_mod()
        ph = self.phases
        for l in range(self.nlayers):
            if ph is None or "in" in ph:
                self.phase_in(l)
            if ph is None or "a" in ph:
                self.phase_a(l)
            if ph is None or "c" in ph:
                self.phase_c(l)
            if ph is None or "b1" in ph:
                self.phase_b1(l)
            if ph is None or "b2" in ph:
                self.phase_b2(l, 0)
                self.phase_b2(l, 1)
            if ph is None or "out" in ph:
                self.phase_out(l)
        self.P.finish()
        self.st.close()
        return self.nc


def rope_tables():
    half = 16
    inv_freq = (1.0 / (10000.0 ** (np.arange(0, half, 2, dtype=np.float32) / np.float32(half)))).astype(np.float32)
    t = np.arange(TL, dtype=np.int32)
    ang_r = (t // 64).astype(np.float32)[:, None] * inv_freq
    ang_c = (t % 64).astype(np.float32)[:, None] * inv_freq
    ang = np.concatenate([ang_r, ang_r, ang_c, ang_c], axis=-1)
    cos = np.cos(ang).astype(np.float32)
    sin = np.sin(ang).astype(np.float32)
    sign = np.array([-1.0] * 8 + [1.0] * 8 + [-1.0] * 8 + [1.0] * 8, np.float32)
    tab = np.stack([cos.T, (sin * sign).T], axis=1)
    return np.ascontiguousarray(np.tile(tab, (4, 1, 1)))


def na_bias_tiles(nb):
    out = np.full((L, 4, 5, 5, 128, 128), -30000.0, np.float32)
    qi = np.arange(128)
    ki = np.arange(128)
    for pat, (r0, base) in enumerate(((0, 0), (2, 0), (4, 0), (60, 54), (62, 54))):
        r = r0 + qi // 64
        c = qi % 64
        rs = np.clip(r - 4, 0, 56)
        cs = np.clip(c - 8, 0, 48)
        for j in range(5):
            kr = base + 2 * j + ki // 64
            kc = ki % 64
            valid = ((kr[:, None] >= rs[None, :]) & (kr[:, None] < rs[None, :] + 8) &
                     (kc[:, None] >= cs[None, :]) & (kc[:, None] < cs[None, :] + 16))
            dr = np.clip(kr[:, None] - r[None, :] + 7, 0, 14)
            dc = np.clip(kc[:, None] - c[None, :] + 15, 0, 30)
            g = nb[:, :, dr, dc]
            out[:, :, pat, j] = np.where(valid[None, None], g, np.float32(-30000.0))
    return np.ascontiguousarray(out.transpose(0, 1, 4, 2, 3, 5).reshape(L, 4, 128, 3200))


def gdn_masks():
    m = np.zeros((16, 128, 128), np.float32)
    i = np.arange(128)
    same = (i[:, None] // 64) == (i[None, :] // 64)
    r, c = i[:, None], i[None, :]
    m[0] = np.eye(128)
    m[1] = same
    m[2] = same & (r <= c)
    m[3] = same & (r >= c)
    m[4] = same & (r > c)
    m[5] = same & (r < c)
    m[6] = np.where(same & (r > c), 0.0, -30000.0)
    m[7] = np.where(same & (r < c), 0.0, -30000.0)
    m[8] = np.where(same & (c >= r), 0.0, -30000.0)
    m[9] = np.where(same & (c <= r), 0.0, -30000.0)
    endf = (i // 64) * 64 + 63
    endb = (i // 64) * 64
    m[10] = (r == endf[None, :])
    m[11] = (r == endb[None, :])
    for d_, ends in enumerate(((63, 127), (0, 64))):
        for c_ in range(2):
            m[12 + 2 * d_ + c_][ends[c_], :] = 1.0
    return np.ascontiguousarray(m.transpose(1, 0, 2).reshape(128, 2048))


def host_prep(inp):
    f = lambda a: np.ascontiguousarray(a, dtype=np.float32)
    w_in = inp["w_in"]
    sizes = (1152, 1152, 384, 12, 12, 768)
    offs = np.cumsum((0,) + sizes)
    a0, b0, g0, al0, be0, c0 = offs[:6]
    perm32 = np.concatenate([np.arange(8, 16), np.arange(0, 8), np.arange(24, 32), np.arange(16, 24)])
    pad = lambda m: np.concatenate([m.reshape(4, 96), m.reshape(4, 96)[:, :32]], axis=1).reshape(-1)
    idq = pad(np.arange(384))
    permq = pad((np.arange(384).reshape(12, 32)[:, perm32]).reshape(-1))
    cols = np.concatenate([
        a0 + idq, a0 + permq, a0 + 384 + idq, a0 + 384 + permq,
        b0 + np.arange(1152), c0 + np.arange(256), c0 + 256 + np.arange(256),
        a0 + 768 + np.arange(384), c0 + 512 + np.arange(256), g0 + np.arange(384), al0 + np.arange(12), be0 + np.arange(12)])
    assert cols.shape[0] == NW
    shared = {
        "w_mod": f(inp["w_mod"]),
        "bmodT": f(inp["b_mod"].reshape(L, 48, 128).transpose(2, 0, 1).reshape(128, L * 48)),
        "n1T": f(inp["norm1_w"].reshape(L, 8, 128).transpose(2, 0, 1).reshape(128, L * 8)),
        "n2T": f(inp["norm2_w"].reshape(L, 8, 128).transpose(2, 0, 1).reshape(128, L * 8)),
        "fnT": f(inp["final_norm_w"].reshape(8, 128).T),
        "w_in": f(w_in[:, :, cols]),
        "rope": rope_tables(),
        "lamp": f(np.stack([inp["lambda_q1"], inp["lambda_k1"], inp["lambda_q2"], inp["lambda_k2"]], axis=1).reshape(1, L * 128)),
        "dnT": f(np.tile(inp["diff_norm_w"].T, (2, 1))),
        "nab": na_bias_tiles(inp["na_bias"]),
        "convT": f(inp["conv_w"].reshape(L, 5, 9, 128).transpose(3, 0, 2, 1).reshape(128, L * 45)),
        "gpar": f(np.concatenate([inp["a_log"].reshape(-1), inp["dt_bias"].reshape(-1), inp["gdn_norm_w"].reshape(-1)])[None, :]),
        "gmask": gdn_masks(),
        "w_out": f(inp["w_out"]),
        "w_f1": f(inp["w_ffn_in"]),
        "w_f2": f(inp["w_ffn_out"]),
    }
    per = []
    for b in range(4):
        xT = f(np.concatenate([inp["ctx"][b], inp["x"][b]], axis=0).T)
        cT = f(np.stack([inp["c"][b].reshape(8, 128).T, inp["c_ctx"].reshape(8, 128).T], axis=2).reshape(128, 16))
        m = dict(shared)
        m["xT"] = xT
        m["cT"] = cT
        per.append(m)
    return per


def kernel(**inputs):
    inp = {k: np.asarray(v) for k, v in inputs.items()}
    per = host_prep(inp)
    nc = K().build()
    in_maps = [per[i % 4] for i in range(8)]
    res = run_bass_kernel_spmd(nc, in_maps, core_ids=list(range(8)))
    out = np.stack([np.ascontiguousarray(res.results[b]["outT"].T) for b in range(4)], axis=0)
    return out.astype(np.float32)
```

```python
import contextlib
import itertools
import math
import numpy as np
import concourse.bass as bass
import concourse.mybir as mybir
from concourse.bass_utils import run_bass_kernel_spmd

F32 = mybir.dt.float32
BF16 = mybir.dt.bfloat16
AF = mybir.ActivationFunctionType
ALU = mybir.AluOpType
AX = mybir.AxisListType

NT, NCTX, TL, D, L = 4352, 256, 4096, 1024, 4
NTILE = NT // 128
DFF = 2816
EPS = 1e-6
C_QA, C_QAP, C_KA, C_KAP, C_B, C_QC, C_KC = 0, 512, 1024, 1536, 2048, 3200, 3456
NFM = 3712
C_VA, C_VC, C_G = 3712, 4096, 4352
NW = 4760
GROUPS = [(0, 256)] + [(256 + 512 * i, 512) for i in range(8)]


class Buf:
    __slots__ = ("w", "r", "g")

    def __init__(self):
        self.w = None
        self.r = {}
        self.g = None


class T:
    __slots__ = ("ap", "b")

    def __init__(self, ap):
        self.ap = ap
        self.b = Buf()


class Prog:
    ENG = ("pe", "act", "dve", "pool", "sp")

    def __init__(self, nc, stack, n_dma=48, same=True):
        self.nc = nc
        self.ops = {e: [] for e in self.ENG}
        self.cnt = {e: 0 for e in self.ENG}
        self.seen = {e: {} for e in self.ENG}
        self.esem = {e: stack.enter_context(nc.semaphore("s_" + e)) for e in self.ENG}
        self.dsem = [stack.enter_context(nc.semaphore("d%d" % i)) for i in range(n_dma)]
        self.dval = [0] * n_dma
        self.dnext = 0
        self.same = same

    def _wait(self, eng, tok):
        if tok is None:
            return
        key, val = tok
        if key == eng and (not self.same or eng == "pe"):
            return
        if self.seen[eng].get(key, 0) >= val:
            return
        self.seen[eng][key] = val
        sem = self.esem[key] if isinstance(key, str) else self.dsem[key]
        self.ops[eng].append(lambda e: e.wait_ge(sem, val))

    def _deps(self, eng, reads, writes):
        for b in reads:
            self._wait(eng, b.w)
        for b in writes:
            self._wait(eng, b.w)
            for t in list(b.r.values()):
                self._wait(eng, t)

    def _commit(self, tok, reads, writes):
        for b in reads:
            b.r[tok[0]] = tok
        for b in writes:
            b.w = tok
            b.r = {}

    def op(self, eng, fn, r=(), w=()):
        self._deps(eng, r, w)
        guards = [b.g for b in r if b.g is not None] if eng in ("act", "dve") else ()
        for g in guards:
            if g[0] is not None and g[1] != eng:
                self._wait(eng, g[0])
        self.cnt[eng] += 1
        tok = (eng, self.cnt[eng])
        sem = self.esem[eng]
        self.ops[eng].append(lambda e: fn(e).then_inc(sem, 1))
        self._commit(tok, r, w)
        for g in guards:
            g[0], g[1] = tok, eng

    def dma(self, out, in_, r=(), w=(), eng="sp"):
        k = self.dnext
        self.dnext = (self.dnext + 1) % len(self.dsem)
        if self.dval[k]:
            self._wait(eng, (k, self.dval[k]))
        self._deps(eng, r, w)
        self.dval[k] += 16
        tok = (k, self.dval[k])
        sem = self.dsem[k]
        self.ops[eng].append(lambda e: e.dma_start(out=out, in_=in_).then_inc(sem, 16))
        self._commit(tok, r, w)

    def mm(self, out, lhsT, rhs, start, stop, r, w):
        self.op("pe", lambda e: e.matmul(out, lhsT=lhsT, rhs=rhs, start=start, stop=stop), r, w)

    def filler(self, n, out, lhsT, rhs):
        for _ in range(n):
            self.op("pe", lambda e: e.matmul(out, lhsT=lhsT, rhs=rhs, start=True, stop=True), (), ())

    def tr(self, out, in_, ident, r, w):
        self.op("pe", lambda e: e.transpose(out, in_, ident), r, w)

    def act(self, out, in_, func, r, w, bias=None, scale=None, accum=None):
        kw = {}
        if bias is not None:
            kw["bias"] = bias
        if scale is not None:
            kw["scale"] = scale
        if accum is not None:
            kw["accum_out"] = accum
        self.op("act", lambda e: e.activation(out=out, in_=in_, func=func, **kw), r, w)

    def tt(self, eng, out, in0, in1, op, r, w):
        self.op(eng, lambda e: e.tensor_tensor(out=out, in0=in0, in1=in1, op=op), r, w)

    def ts(self, eng, out, in0, s1, op0, r, w, s2=None, op1=None):
        if op1 is None:
            self.op(eng, lambda e: e.tensor_scalar(out=out, in0=in0, scalar1=s1, scalar2=None, op0=op0), r, w)
        else:
            self.op(eng, lambda e: e.tensor_scalar(out=out, in0=in0, scalar1=s1, scalar2=s2, op0=op0, op1=op1), r, w)

    def stt(self, eng, out, in0, scalar, in1, op0, op1, r, w):
        self.op(eng, lambda e: e.scalar_tensor_tensor(out=out, in0=in0, scalar=scalar, in1=in1, op0=op0, op1=op1), r, w)

    def copy(self, eng, out, in_, r, w):
        if eng == "act":
            self.op("act", lambda e: e.activation(out=out, in_=in_, func=AF.Copy), r, w)
        else:
            self.op(eng, lambda e: e.tensor_copy(out=out, in_=in_), r, w)

    def memset(self, eng, out, val, w):
        self.op(eng, lambda e: e.memset(out, val), (), w)

    def barrier(self):
        for e in self.ENG:
            for f in self.ENG:
                if f != e and self.cnt[f]:
                    self._wait(e, (f, self.cnt[f]))
            for k, v in enumerate(self.dval):
                if v:
                    self._wait(e, (k, v))

    def finish(self):
        self.barrier()
        ops = self.ops
        with self.nc.Block() as block:
            @block.tensor
            def _(e):
                for f in ops["pe"]:
                    f(e)

            @block.scalar
            def _(e):
                for f in ops["act"]:
                    f(e)

            @block.vector
            def _(e):
                for f in ops["dve"]:
                    f(e)

            @block.gpsimd
            def _(e):
                for f in ops["pool"]:
                    f(e)

            @block.sync
            def _(e):
                for f in ops["sp"]:
                    f(e)


class Arena:
    def __init__(self, ap, n):
        self.ap, self.n, self.top = ap, n, 0

    def f32(self, n, shape=None):
        a = self.top
        self.top += n
        assert self.top <= self.n, ("SBUF arena overflow", self.top, self.n)
        ap = self.ap[:, a:a + n]
        return T(ap)

    def bf16(self, n):
        t = self.f32((n + 1) // 2)
        t.ap = t.ap.bitcast(BF16)
        return t

    def mark(self):
        return self.top

    def release(self, m):
        self.top = m


def sub(bank, ap):
    t = T(ap)
    t.b = bank.b
    return t


def v3(ap, a):
    return ap.rearrange("p (a b) -> p a b", a=a)


class K:
    def __init__(self, debug=False, nlayers=L, phases=None):
        self.debug = debug
        self.nlayers = nlayers
        self.phases = phases
        nc = self.nc = bass.Bass("TRN2", target_bir_lowering=False)
        self.st = contextlib.ExitStack()
        self.P = Prog(nc, self.st)
        ein = lambda n, s, dt=F32: nc.dram_tensor(n, list(s), dt, kind="ExternalInput").ap()
        self.xT = ein("xT", (D, NT))
        self.cT = ein("cT", (128, 16))
        self.w_mod = ein("w_mod", (L, D, 6 * D))
        self.bmodT = ein("bmodT", (128, L * 48))
        self.n1T = ein("n1T", (128, L * 8))
        self.n2T = ein("n2T", (128, L * 8))
        self.fnT = ein("fnT", (128, 8))
        self.w_in = ein("w_in", (L, D, NW))
        self.rope = ein("rope", (128, 2, TL))
        self.lamp = ein("lamp", (1, L * 128))
        self.dnT = ein("dnT", (128, L))
        self.nab = ein("nab", (L, 4, 128, 3200))
        self.convT = ein("convT", (128, L * 45))
        self.gpar = ein("gpar", (1, L * 24 + L * 64))
        self.gmask = ein("gmask", (128, 2048))
        self.w_out = ein("w_out", (L, D, D))
        self.w_f1 = ein("w_f1", (L, D, 2 * DFF))
        self.w_f2 = ein("w_f2", (L, DFF, D))
        sk = "ExternalOutput" if debug else "Internal"
        scr = lambda n, s, dt=F32: nc.dram_tensor(n, list(s), dt, kind=sk).ap()
        self.xs = scr("xs", (D, NT))
        self.qa = scr("qa", (512, NT), BF16)
        self.ka = scr("ka", (512, NT), BF16)
        self.va = scr("va", (NT, 384), BF16)
        self.qkvb = scr("qkvb", (1152, NT))
        self.gab = scr("gab", (NT, 408))
        self.qc = scr("qc", (256, NT), BF16)
        self.kc = scr("kc", (256, NT), BF16)
        self.vc = scr("vc", (NT, 256), BF16)
        self.yT = scr("yT", (D, NT), BF16)
        self.qkn = scr("qkn", (1152, NT))
        self.of = scr("of", (NT, 384))
        self.outT = nc.dram_tensor("outT", [D, TL], F32, kind="ExternalOutput").ap()
        self.dbufs = {}
        arena_ap = self.st.enter_context(nc.sbuf_tensor("arena", [128, 53200], F32))
        self.A = Arena(arena_ap, 53200)
        self.ps = [T(self.st.enter_context(nc.psum_tensor("ps%d" % i, [128, 512], F32))[:]) for i in range(8)]
        for t in self.ps:
            t.b.g = [None, None]

    def db(self, name):
        return Buf()

    def consts(self):
        P, A = self.P, self.A
        self.ones = A.f32(128)
        P.memset("pool", self.ones.ap, 1.0, [self.ones.b])
        self.cTt = A.f32(16)
        P.dma(self.cTt.ap, self.cT, w=[self.cTt.b])
        self.sc = A.f32(16)
        P.act(self.sc.ap, self.cTt.ap, AF.Silu, [self.cTt.b], [self.sc.b])
        self.mod = A.f32(L * 48 * 2)
        self.bm = A.f32(L * 48)
        P.dma(self.bm.ap, self.bmodT, w=[self.bm.b])
        self.n1 = A.f32(L * 8)
        self.n2 = A.f32(L * 8)
        self.fn = A.f32(8)
        P.dma(self.n1.ap, self.n1T, w=[self.n1.b])
        P.dma(self.n2.ap, self.n2T, w=[self.n2.b])
        P.dma(self.fn.ap, self.fnT, w=[self.fn.b])
        self.cw = A.f32(L * 45)
        P.dma(self.cw.ap, self.convT, w=[self.cw.b])
        self.gm = A.f32(2048)
        P.dma(self.gm.ap, self.gmask, w=[self.gm.b])
        self.gm16 = A.bf16(640)
        P.copy("pool", self.gm16.ap[:, 0:128], self.gm.ap[:, 0:128], [self.gm.b], [self.gm16.b])
        P.copy("pool", self.gm16.ap[:, 128:640], self.gm.ap[:, 768:1280], [self.gm.b], [self.gm16.b])
        gp = self.gp = A.f32(L * 24 + L * 64)
        P.dma(gp.ap, self.gpar.partition_broadcast(128), w=[gp.b])
        self.nea = A.f32(L * 12)
        P.act(self.nea.ap, gp.ap[:, 0:L * 12], AF.Exp, [gp.b], [self.nea.b])
        P.ts("dve", self.nea.ap, self.nea.ap, -1.0, ALU.mult, [self.nea.b], [self.nea.b])
        self.nlam = A.f32(L)
        self.dnl = A.f32(L)
        self.a1 = A.f32(L * 16)
        self.a2 = A.f32(L * 16)
        mtmp = A.mark()
        lp = A.f32(L * 128)
        P.dma(lp.ap, self.lamp.partition_broadcast(128), w=[lp.b])
        lpv = lp.ap.rearrange("p (l f d) -> p l f d", l=L, f=4)
        pr_ = A.f32(L * 64)
        prv = pr_.ap.rearrange("p (l f d) -> p l f d", l=L, f=2)
        P.tt("dve", prv, lpv[:, :, 0:4:2, :], lpv[:, :, 1:4:2, :], ALU.mult, [lp.b], [pr_.b])
        ee = A.f32(L * 2)
        P.op("dve", lambda e: e.tensor_reduce(out=ee.ap, in_=pr_.ap.rearrange("p (g d) -> p g d", d=32), axis=AX.X, op=ALU.add), [pr_.b], [ee.b])
        P.act(ee.ap, ee.ap, AF.Exp, [ee.b], [ee.b])
        dn = A.f32(L)
        P.dma(dn.ap, self.dnT, w=[dn.b])
        for l in range(L):
            li = 0.8 - 0.6 * math.exp(-0.3 * l)
            P.stt("dve", self.nlam.ap[:, l:l + 1], ee.ap[:, 2 * l + 1:2 * l + 2], -li, ee.ap[:, 2 * l:2 * l + 1], ALU.add, ALU.subtract,
                  [ee.b], [self.nlam.b])
            P.ts("dve", self.dnl.ap[:, l:l + 1], dn.ap[:, l:l + 1], 1.0 - li, ALU.mult, [dn.b], [self.dnl.b])
        P.barrier()
        A.release(mtmp)

    def phase_mod(self):
        P, A = self.P, self.A
        m0 = A.mark()
        wm = [A.f32(8 * 768), A.f32(8 * 768)]
        ps = self.ps[0]
        it = 0
        for l in range(self.nlayers):
            wl = self.w_mod[l].rearrange("(kc p) n -> p kc n", p=128)
            for j in range(8):
                w = wm[it % 2]
                it += 1
                wv = v3(w.ap, 8)
                P.dma(wv, wl[:, :, j * 768:(j + 1) * 768], w=[w.b])
                for ct in range(6):
                    for kc in range(8):
                        P.mm(ps.ap[:, ct * 2:ct * 2 + 2], wv[:, kc, ct * 128:(ct + 1) * 128],
                             self.sc.ap[:, kc * 2:kc * 2 + 2], kc == 0, kc == 7, [w.b, self.sc.b], [ps.b])
                o = (l * 48 + j * 6) * 2
                P.tt("dve", v3(self.mod.ap[:, o:o + 12], 6), v3(ps.ap[:, 0:12], 6),
                     self.bm.ap[:, l * 48 + j * 6:l * 48 + j * 6 + 6].unsqueeze(2).to_broadcast([128, 6, 2]),
                     ALU.add, [ps.b, self.bm.b], [self.mod.b])
        for l in range(self.nlayers):
            for (a, n, which) in ((self.a1, self.n1, 1), (self.a2, self.n2, 4)):
                o = (l * 48 + which * 8) * 2
                P.stt("dve", v3(a.ap[:, l * 16:(l + 1) * 16], 8), v3(self.mod.ap[:, o:o + 16], 8), 1.0,
                      n.ap[:, l * 8:(l + 1) * 8].unsqueeze(2).to_broadcast([128, 8, 2]),
                      ALU.add, ALU.mult, [self.mod.b, n.b], [a.b])
        P.barrier()
        A.release(m0)

    def modv(self, l, which, kc, s):
        o = ((l * 48 + which * 8 + kc) * 2) + s
        return self.mod.ap[:, o:o + 1]

    def load_w_bf16(self, dst, src3, nk, ncols, stg, piece=512):
        P = self.P
        dv = v3(dst.ap, nk)
        i = 0
        for c0 in range(0, ncols, piece):
            c1 = min(ncols, c0 + piece)
            s = stg[i % 2]
            sv = v3(s.ap[:, 0:nk * (c1 - c0)], nk)
            P.dma(sv, src3[:, :, c0:c1], w=[s.b])
            P.copy("pool" if i % 2 else "act", dv[:, :, c0:c1], sv, [s.b], [dst.b])
            i += 1

    def norm_mod(self, xg, N, l, a, which_shift, s, hb, sq, rstd, ps_ss):
        P = self.P
        xv = v3(xg.ap, 8)
        hv = v3(hb.ap, 8)
        for kc in range(8):
            P.act(sq.ap[:, 0:N], xv[:, kc, :], AF.Square, [xg.b], [sq.b])
            P.mm(ps_ss.ap[:, 0:N], self.ones.ap, sq.ap[:, 0:N], kc == 0, kc == 7, [self.ones.b, sq.b], [ps_ss.b])
        P.act(rstd.ap[:, 0:N], ps_ss.ap[:, 0:N], AF.Ln, [ps_ss.b], [rstd.b], bias=EPS, scale=1.0 / D)
        P.act(rstd.ap[:, 0:N], rstd.ap[:, 0:N], AF.Exp, [rstd.b], [rstd.b], scale=-0.5)
        for kc in range(8):
            ao = l * 16 + kc * 2 + s
            P.stt("dve", sq.ap[:, 0:N], xv[:, kc, :], a.ap[:, ao:ao + 1], rstd.ap[:, 0:N], ALU.mult, ALU.mult,
                  [xg.b, a.b, rstd.b], [sq.b])
            P.act(hv[:, kc, :], sq.ap[:, 0:N], AF.Identity, [sq.b, self.mod.b], [hb.b], bias=self.modv(l, which_shift, kc, s), scale=1.0)

    def phase_in(self, l):
        P, A = self.P, self.A
        m0 = A.mark()
        Wb = A.bf16(8 * NW)
        stg = [A.f32(8 * 256), A.f32(8 * 256)]
        self.load_w_bf16(Wb, self.w_in[l].rearrange("(kc p) n -> p kc n", p=128), 8, NW, stg, piece=256)
        Wv = v3(Wb.ap, 8)
        xg = [A.f32(8 * 512), A.f32(8 * 512)]
        hb = [A.bf16(8 * 512), A.bf16(8 * 512)]
        sq = A.f32(512)
        rstd = A.f32(512)
        rp = [A.f32(1024), A.f32(1024)]
        ofm = [A.f32(512) for _ in range(3)]
        otm = [A.f32(408) for _ in range(3)]
        t1 = A.f32(512)
        t2 = A.f32(512)
        xsrc = self.xT if l == 0 else self.xs
        xsrc3 = xsrc.rearrange("(kc p) n -> p kc n", p=128)
        bx = self.db("xs")
        ps_ss = self.ps[0]
        psf = self.ps[1:5]
        pst = self.ps[5:8]
        nf = 0
        ntm = 0

        def load(gi):
            n0, N = GROUPS[gi]
            x = xg[gi % 2]
            P.dma(v3(x.ap[:, 0:8 * N], 8), xsrc3[:, :, n0:n0 + N], r=[bx], w=[x.b])
            if gi > 0:
                r_ = rp[gi % 2]
                P.dma(v3(r_.ap, 2), self.rope[:, :, n0 - NCTX:n0 - NCTX + 512], w=[r_.b])

        def norm(gi):
            n0, N = GROUPS[gi]
            x = xg[gi % 2]
            h = hb[gi % 2]
            xin = T(x.ap[:, 0:8 * N]); xin.b = x.b
            hin = T(h.ap[:, 0:8 * N]); hin.b = h.b
            self.norm_mod(xin, N, l, self.a1, 0, 1 if gi == 0 else 0, hin, sq, rstd, ps_ss)

        load(0)
        norm(0)
        for gi, (n0, N) in enumerate(GROUPS):
            if gi + 1 < len(GROUPS):
                load(gi + 1)
            x = xg[gi % 2]
            h = hb[gi % 2]
            s = 1 if gi == 0 else 0
            hv = v3(h.ap[:, 0:8 * N], 8)
            rv = v3(rp[gi % 2].ap, 2)

            def fm(ft, ps):
                for kc in range(8):
                    P.mm(ps.ap[:, 0:N], Wv[:, kc, ft * 128:(ft + 1) * 128], hv[:, kc, :], kc == 0, kc == 7, [Wb.b, h.b], [ps.b])

            for (c0, dst, nm) in ((C_QA, self.qa, "qa"), (C_KA, self.ka, "ka")):
                for j in range(4):
                    o = ofm[nf % 3]
                    ob = o.ap.bitcast(BF16)[:, 0:N]
                    p1 = psf[nf % 4]
                    nf += 1
                    fm(c0 // 128 + j, p1)
                    if gi == 0:
                        P.copy("dve", ob, p1.ap[:, 0:N], [p1.b], [o.b])
                    else:
                        p2 = psf[nf % 4]
                        nf += 1
                        fm(c0 // 128 + 4 + j, p2)
                        P.tt("dve", t1.ap, p1.ap, rv[:, 0, :], ALU.mult, [p1.b, rp[gi % 2].b], [t1.b])
                        P.tt("dve", t2.ap, p2.ap, rv[:, 1, :], ALU.mult, [p2.b, rp[gi % 2].b], [t2.b])
                        P.tt("pool", ob, t1.ap, t2.ap, ALU.add, [t1.b, t2.b], [o.b])
                    P.dma(dst[j * 128:(j + 1) * 128, n0:n0 + N], ob, r=[o.b], w=[self.db(nm)])
            for j in range(9):
                o = ofm[nf % 3]
                p1 = psf[nf % 4]
                nf += 1
                fm(C_B // 128 + j, p1)
                P.copy("act", o.ap[:, 0:N], p1.ap[:, 0:N], [p1.b], [o.b])
                P.dma(self.qkvb[j * 128:(j + 1) * 128, n0:n0 + N], o.ap[:, 0:N], r=[o.b], w=[self.db("qkvb")])
            for (c0, dst, nm) in ((C_QC, self.qc, "qc"), (C_KC, self.kc, "kc")):
                for j in range(2):
                    o = ofm[nf % 3]
                    ob = o.ap.bitcast(BF16)[:, 0:N]
                    p1 = psf[nf % 4]
                    nf += 1
                    fm(c0 // 128 + j, p1)
                    P.copy("dve", ob, p1.ap[:, 0:N], [p1.b], [o.b])
                    P.dma(dst[j * 128:(j + 1) * 128, n0:n0 + N], ob, r=[o.b], w=[self.db(nm)])
            if gi + 1 < len(GROUPS):
                norm(gi + 1)
            for tt in range(N // 128):
                tok0 = n0 + tt * 128
                for (c0, nc_, dst, nm, isbf) in ((C_VA, 384, self.va, "va", True), (C_VC, 256, self.vc, "vc", True),
                                                 (C_G, 408, self.gab, "gab", False)):
                    ps = pst[ntm % 3]
                    o = otm[ntm % 3]
                    ntm += 1
                    for kc in range(8):
                        P.mm(ps.ap[:, 0:nc_], hv[:, kc, tt * 128:(tt + 1) * 128], Wv[:, kc, c0:c0 + nc_], kc == 0, kc == 7,
                             [Wb.b, h.b], [ps.b])
                    oa = o.ap.bitcast(BF16)[:, 0:nc_] if isbf else o.ap[:, 0:nc_]
                    P.copy("act" if ntm % 2 else "dve", oa, ps.ap[:, 0:nc_], [ps.b], [o.b])
                    P.dma(dst[tok0:tok0 + 128, :], oa, r=[o.b], w=[self.db(nm)])
        P.barrier()
        A.release(m0)


    def phase_a(self, l):
        P, A = self.P, self.A
        m0 = A.mark()
        KAt = A.bf16(4 * NT)
        P.dma(v3(KAt.ap, 4), self.ka.rearrange("(t p) n -> p t n", p=128), w=[KAt.b])
        Kv = v3(KAt.ap, 4)
        VA = A.bf16(NTILE * 6 * 128)
        Vv = VA.ap.rearrange("p (t h d) -> p t h d", t=NTILE, h=6)
        P.memset("pool", VA.ap, 1.0, [VA.b])
        vsrc = self.va.rearrange("(t p) (h d) -> p t h d", p=128, h=6)
        for t0 in range(NTILE):
            P.dma(Vv[:, t0, :, 0:64], vsrc[:, t0, :, :], w=[VA.b])
        Qt = [A.bf16(4 * 512), A.bf16(4 * 512)]
        pT = [[A.bf16(512) for _ in range(3)] for _ in range(2)]
        o12 = [[A.f32(512), A.f32(512)], [A.f32(512), A.f32(512)]]
        rz = A.f32(512)
        sq = A.f32(512)
        rs = A.f32(512)
        yb = [A.bf16(512), A.bf16(512)]
        deferred = []
        nh = 0
        psS = [self.ps[0:2], self.ps[2:4]]
        psO = self.ps[4:6]
        psN = self.ps[6]
        psF = psN
        fl = A.bf16(128 + 512)
        P.memset("pool", fl.ap, 0.0, [fl.b])
        NFILL = getattr(self, "a_fill", 1)
        scale = 32.0 ** -0.5
        qsrc = self.qa.rearrange("(t p) n -> p t n", p=128)
        ny = 0

        def loadq(gi):
            n0, N = GROUPS[gi]
            q = Qt[gi % 2]
            P.dma(v3(q.ap, 4)[:, :, 0:N], qsrc[:, :, n0:n0 + N], w=[q.b])

        loadq(0)
        for gi, (n0, N) in enumerate(GROUPS):
            if gi + 1 < len(GROUPS):
                loadq(gi + 1)
            q = Qt[gi % 2]
            qv = v3(q.ap, 4)
            kts = [0, 1] if gi == 0 else list(range(NTILE))
            items = [(h, kt) for h in range(6) for kt in kts]

            def S(i):
                h, kt = items[i]
                for m in range(2):
                    hm = 2 * h + m
                    t, pr = hm // 3, (hm % 3) * 32
                    ps = psS[m][i % 2]
                    P.mm(ps.ap[:, 0:N], Kv[pr:pr + 32, t, kt * 128:(kt + 1) * 128], qv[pr:pr + 32, t, 0:N], True, True, [KAt.b, q.b], [ps.b])

            S(0)
            for i, (h, kt) in enumerate(items):
                if i + 1 < len(items):
                    S(i + 1)
                for _f in range(NFILL):
                    P.mm(psF.ap[:, 0:512], fl.ap[:, 0:128], fl.ap[:, 128:640], True, True, [fl.b], [psF.b])
                for m in range(2):
                    ps = psS[m][i % 2]
                    p = pT[m][i % 3]
                    P.act(p.ap[:, 0:N], ps.ap[:, 0:N], AF.Exp, [ps.b], [p.b], scale=scale)
                for m in range(2):
                    p = pT[m][i % 3]
                    po = psO[m]
                    P.mm(po.ap[:, 0:N], Vv[:, kt, h, :], p.ap[:, 0:N], kt == kts[0], kt == kts[-1], [VA.b, p.b], [po.b])
                for fn_ in [f for (due, f) in deferred if due <= i]:
                    fn_()
                deferred[:] = [(due, f) for (due, f) in deferred if due > i]
                if kt == kts[-1]:
                    oo = o12[nh % 2]
                    nh += 1
                    for m in range(2):
                        po = psO[m]
                        P.copy("dve", oo[m].ap[:, 0:N], po.ap[:, 0:N], [po.b], [oo[m].b])
                    for _f in range(getattr(self, "a_bfill", 4)):
                        P.mm(psF.ap[:, 0:512], fl.ap[:, 0:128], fl.ap[:, 128:640], True, True, [fl.b], [psF.b])

                    def fin1(oo=oo, h=h, N=N, n0=n0):
                        for m in range(2):
                            o = oo[m]
                            P.op("dve", lambda e, o=o: e.reciprocal(out=rz.ap[0:64, 0:N], in_=o.ap[64:128, 0:N]), [o.b], [rz.b])
                            P.tt("dve", o.ap[0:64, 0:N], o.ap[0:64, 0:N], rz.ap[0:64, 0:N], ALU.mult, [o.b, rz.b], [o.b])
                        o1, o2 = oo
                        P.stt("dve", o1.ap[0:64, 0:N], o2.ap[0:64, 0:N], self.nlam.ap[0:64, l:l + 1], o1.ap[0:64, 0:N], ALU.mult, ALU.add,
                              [o1.b, o2.b, self.nlam.b], [o1.b])
                        P.tt("pool", sq.ap[0:64, 0:N], o1.ap[0:64, 0:N], o1.ap[0:64, 0:N], ALU.mult, [o1.b], [sq.b])

                    def fin2(oo=oo, h=h, N=N, n0=n0):
                        o1 = oo[0]
                        P.mm(psN.ap[0:64, 0:N], self.ones.ap[0:64, 0:64], sq.ap[0:64, 0:N], True, True, [self.ones.b, sq.b], [psN.b])
                        P.act(rs.ap[0:64, 0:N], psN.ap[0:64, 0:N], AF.Ln, [psN.b], [rs.b], bias=EPS, scale=1.0 / 64)
                        P.act(rs.ap[0:64, 0:N], rs.ap[0:64, 0:N], AF.Exp, [rs.b], [rs.b], scale=-0.5)
                        y = yb[h % 2]
                        P.stt("dve", y.ap[0:64, 0:N], o1.ap[0:64, 0:N], self.dnl.ap[0:64, l:l + 1], rs.ap[0:64, 0:N], ALU.mult, ALU.mult,
                              [o1.b, rs.b, self.dnl.b], [y.b])
                        P.dma(self.yT[h * 64:(h + 1) * 64, n0:n0 + N], y.ap[0:64, 0:N], r=[y.b])

                    fin1()
                    if gi == 0:
                        fin2()
                    else:
                        deferred.append((i + 6, fin2))
                yield
            for (_, f) in deferred:
                f()
            deferred[:] = []
        P.barrier()
        A.release(m0)

    def phase_c(self, l):
        P, A = self.P, self.A
        m0 = A.mark()
        KCt = A.bf16(2 * NT)
        QCt = A.bf16(2 * NT)
        P.dma(v3(KCt.ap, 2), self.kc.rearrange("(t p) n -> p t n", p=128), w=[KCt.b])
        P.dma(v3(QCt.ap, 2), self.qc.rearrange("(t p) n -> p t n", p=128), w=[QCt.b])
        Kv, Qv = v3(KCt.ap, 2), v3(QCt.ap, 2)
        VC = A.bf16(NTILE * 4 * 128)
        Vv = VC.ap.rearrange("p (t h d) -> p t h d", t=NTILE, h=4)
        P.memset("pool", VC.ap, 1.0, [VC.b])
        vsrc = self.vc.rearrange("(t p) (h d) -> p t h d", p=128, h=4)
        for t0 in range(NTILE):
            P.dma(Vv[:, t0, :, 0:64], vsrc[:, t0, :, :], w=[VC.b])
        NB = A.f32(4 * 3200)
        NBv = NB.ap.rearrange("p (h a j q) -> p h a j q", h=4, a=5, j=5)
        for h in range(4):
            P.dma(NB.ap[:, h * 3200:(h + 1) * 3200], self.nab[l, h], w=[NB.b])
        sA = [A.f32(640), A.f32(640)]
        pA = [A.bf16(896), A.bf16(896)]
        rz = A.f32(128)
        ys = [A.bf16(256), A.bf16(256)]
        psA = self.ps[0:4]
        psO = self.ps[4:6]
        scale = 64.0 ** -0.5
        def geom(qt):
            if qt < 2:
                return [0, 1], 0, False
            r0 = 2 * (qt - 2)
            base = min(max(r0 - 4, 0), 54)
            return [2 + base // 2 + j for j in range(5)] + [0, 1], (r0 - base) // 2, True

        items = [(qt, h) for qt in range(NTILE) for h in range(4)]

        def stA(n):
            qt, h = items[n]
            kts, pat, lat = geom(qt)
            t, pr = h // 2, (h % 2) * 64
            pa, pb = psA[2 * (n % 2)], psA[2 * (n % 2) + 1]
            qop = Qv[pr:pr + 64, t, qt * 128:(qt + 1) * 128]
            for j, kt in enumerate(kts):
                dst = pa.ap[:, j * 128:(j + 1) * 128] if j < 4 else pb.ap[:, (j - 4) * 128:(j - 3) * 128]
                P.mm(dst, Kv[pr:pr + 64, t, kt * 128:(kt + 1) * 128], qop, True, True, [KCt.b, QCt.b], [pa.b if j < 4 else pb.b])

        def stB(n):
            qt, h = items[n]
            kts, pat, lat = geom(qt)
            t, pr = h // 2, (h % 2) * 64
            pa, pb = psA[2 * (n % 2)], psA[2 * (n % 2) + 1]
            s_, p_, po = sA[n % 2], pA[n % 2], psO[n % 2]
            yst = ys[qt % 2]
            ysv = v3(yst.ap, 2)
            nk = len(kts)
            if lat:
                P.stt("dve", s_.ap[:, 0:512], pa.ap[:, 0:512], scale, NBv[:, h, pat, 0:4, :].rearrange("p j q -> p (j q)"), ALU.mult, ALU.add,
                      [pa.b, NB.b], [s_.b])
                P.stt("dve", s_.ap[:, 512:640], pb.ap[:, 0:128], scale, NBv[:, h, pat, 4, :], ALU.mult, ALU.add,
                      [pb.b, NB.b], [s_.b])
                P.act(p_.ap[:, 0:640], s_.ap[:, 0:640], AF.Exp, [s_.b], [p_.b])
                P.act(p_.ap[:, 640:896], pb.ap[:, 128:384], AF.Exp, [pb.b], [p_.b], scale=scale)
            else:
                P.act(p_.ap[:, 0:256], pa.ap[:, 0:256], AF.Exp, [pa.b], [p_.b], scale=scale)
            for j, kt in enumerate(kts):
                P.mm(po.ap[:, 0:128], Vv[:, kt, h, :], p_.ap[:, j * 128:(j + 1) * 128], j == 0, j == nk - 1, [VC.b, p_.b], [po.b])
            P.op("dve", lambda e, po=po: e.reciprocal(out=rz.ap[0:64, :], in_=po.ap[64:128, 0:128]), [po.b], [rz.b])
            P.tt("dve", ysv[pr:pr + 64, t, :], po.ap[0:64, 0:128], rz.ap[0:64, :], ALU.mult, [po.b, rz.b], [yst.b])
            if h == 3:
                P.dma(self.yT[768:1024, qt * 128:(qt + 1) * 128].rearrange("(t p) n -> p t n", p=128), ysv, r=[yst.b])

        stA(0)
        for n in range(len(items)):
            if n + 1 < len(items):
                stA(n + 1)
            stB(n)
        P.barrier()
        A.release(m0)

    def gmk(self, i, ncol=128):
        return self.gm.ap[:, i * 128:i * 128 + ncol]

    def phase_b1(self, l, banks=None):
        P, A = self.P, self.A
        m0 = A.mark()
        xb = [A.f32(516) for _ in range(3)]
        acc = [A.f32(512) for _ in range(2)]
        sl = [A.f32(512) for _ in range(3)]
        sqs = [A.f32(512) for _ in range(2)]
        rs = A.f32(512)
        ob = [A.f32(512) for _ in range(2)]
        ps = banks if banks is not None else self.ps[0:2]
        nb_ = len(ps)
        onesblk = self.gmk(1)
        work = [(ft, n0, N) for ft in range(9) for (n0, N) in GROUPS]

        def stage1(it):
            ft, n0, N = work[it]
            s0, s1 = (0, NCTX) if n0 < NCTX else (NCTX, NT)
            x, a, sv, sq, p_ = xb[it % 3], acc[it % 2], sl[it % 3], sqs[it % 2], ps[it % nb_]
            lo, hi = max(n0 - 2, s0), min(n0 + N + 2, s1)
            if lo > n0 - 2:
                P.memset("pool", x.ap[:, 0:2], 0.0, [x.b])
            if hi < n0 + N + 2:
                P.memset("pool", x.ap[:, N + 2:N + 4], 0.0, [x.b])
            P.dma(x.ap[:, lo - (n0 - 2):hi - (n0 - 2)], self.qkvb[ft * 128:(ft + 1) * 128, lo:hi], w=[x.b])
            co = l * 45 + ft * 5
            P.ts("dve", a.ap[:, 0:N], x.ap[:, 0:N], self.cw.ap[:, co:co + 1], ALU.mult, [x.b, self.cw.b], [a.b])
            for j in range(1, 5):
                P.stt("dve", a.ap[:, 0:N], x.ap[:, j:j + N], self.cw.ap[:, co + j:co + j + 1], a.ap[:, 0:N], ALU.mult, ALU.add,
                      [x.b, self.cw.b, a.b], [a.b])
            P.act(sv.ap[:, 0:N], a.ap[:, 0:N], AF.Silu, [a.b], [sv.b])
            if ft < 6:
                P.tt("pool", sq.ap[:, 0:N], sv.ap[:, 0:N], sv.ap[:, 0:N], ALU.mult, [sv.b], [sq.b])
                P.mm(p_.ap[:, 0:N], onesblk, sq.ap[:, 0:N], True, True, [self.gm.b, sq.b], [p_.b])

        def stage2(it):
            ft, n0, N = work[it]
            sv, p_, o = sl[it % 3], ps[it % nb_], ob[it % 2]
            if ft < 6:
                P.act(rs.ap[:, 0:N], p_.ap[:, 0:N], AF.Ln, [p_.b], [rs.b], bias=EPS, scale=1.0)
                P.act(rs.ap[:, 0:N], rs.ap[:, 0:N], AF.Exp, [rs.b], [rs.b], scale=-0.5)
                P.stt("dve", o.ap[:, 0:N], sv.ap[:, 0:N], 0.125 if ft < 3 else 1.0, rs.ap[:, 0:N], ALU.mult, ALU.mult, [sv.b, rs.b], [o.b])
                P.dma(self.qkn[ft * 128:(ft + 1) * 128, n0:n0 + N], o.ap[:, 0:N], r=[o.b])
            else:
                P.dma(self.qkn[ft * 128:(ft + 1) * 128, n0:n0 + N], sv.ap[:, 0:N], r=[sv.b])

        stage1(0)
        for it in range(len(work)):
            if it + 1 < len(work) and nb_ > 1:
                stage1(it + 1)
            stage2(it)
            if it + 1 < len(work) and nb_ == 1:
                stage1(it + 1)
            yield
        P.barrier()
        A.release(m0)

    def phase_b2(self, l, d):
        P, A = self.P, self.A
        m0 = A.mark()
        ident = self.gmk(0)
        A_d, B_d, MS_d, MIT_d, SelEnd = self.gmk(2 + d), self.gmk(4 + d), self.gmk(6 + d), self.gmk(8 + d), self.gmk(10 + d)
        gmb = self.gm.b
        f = A.f32
        gabt = [f(408), f(408)]
        fmt = [f(9 * 128), f(9 * 128)]
        oft = [f(384), f(384)]
        SC = [[f(6) for _ in range(11)] + [f(12)] + [f(384) for _ in range(5)] for _ in range(2)]
        Ag, Bg, dg, Es, EiT, aqkT, TT, wT, qhT = [[f(384), f(384)] for _ in range(9)]
        Pm = [[f(384), f(384)], [f(384), f(384)]]
        PTm = [[f(384), f(384)], [f(384), f(384)]]
        u = [f(192), f(192)]
        vnew = [f(192), f(192)]
        S = [f(192), f(192)]
        otile = [f(384), f(384)]
        sq = f(384)
        ss = f(6)
        rs = f(6)
        sg = f(384)
        ybf = A.bf16(384)
        for hb in range(2):
            P.memset("pool", S[hb].ap[0:64, :], 0.0, [S[hb].b])
        QB = [self.ps[0:4], self.ps[4:8]]
        fmbs = [A.bf16(768), A.bf16(768)]
        tmpos = [f(192), f(192)]
        otbs = [[T(o_.ap[:, 0:192]), T(o_.ap[:, 192:384])] for o_ in otile]
        ident16, MS16, MIT16 = self.gm16.ap[:, 0:128], self.gm16.ap[:, (1 + d) * 128:(2 + d) * 128], self.gm16.ap[:, (3 + d) * 128:(4 + d) * 128]
        psK = self.ps[0]
        kcorn = [sub(self.ps[j_], self.ps[j_].ap[:, 384:512]) for j_ in (0, 2, 3)]
        vcorn = [sub(self.ps[j_], self.ps[j_].ap[:, 384:512]) for j_ in (4, 5, 6)]
        psSc = sub(self.ps[1], self.ps[1].ap[:, 384:512])
        dtb = self.gp.ap[:, L * 12 + l * 12 + d * 6:L * 12 + l * 12 + d * 6 + 6]
        nea = self.nea.ap[:, l * 12 + d * 6:l * 12 + d * 6 + 6]
        gw = self.gp.ap[:, L * 24 + l * 64:L * 24 + (l + 1) * 64]
        qkn3 = self.qkn.rearrange("(f p) n -> p f n", p=128)
        order = list(range(NTILE)) if d == 0 else [1, 0] + list(range(NTILE - 1, 1, -1))
        chunks = (0, 1) if d == 0 else (1, 0)
        if getattr(self, "b2_tiles", None):
            order = order[:self.b2_tiles]

        def load(i):
            tt = order[i]
            tok0 = tt * 128
            P.dma(gabt[i % 2].ap, self.gab[tok0:tok0 + 128, :], w=[gabt[i % 2].b])
            P.dma(v3(fmt[i % 2].ap, 9), qkn3[:, :, tok0:tok0 + 128], w=[fmt[i % 2].b])
            if d == 1:
                P.dma(oft[i % 2].ap, self.of[tok0:tok0 + 128, :], w=[oft[i % 2].b])

        bc3 = lambda ap, n: ap.unsqueeze(2).to_broadcast([128, ap.shape[1], n])
        mb3 = lambda ap, h: ap.unsqueeze(1).to_broadcast([128, h, 128])
        def common(i):
            par = i % 2
            z, e_, g, eb, beta, nbeta, gc, eg, dgl, ek, bg, egl, ktm, vtm, ktail, kbg, vb = SC[par]
            ga, fm, ot, fmb = gabt[par], fmt[par], otile[par], fmbs[par]
            return (z, e_, g, eb, beta, nbeta, gc, eg, dgl, ek, bg, egl, ktm, vtm, ktail, kbg, vb, ga, fm, v3(fm.ap, 9), ot, fmb, v3(fmb.ap, 6))

        def prep(i):
            load(i)
            z, e_, g, eb, beta, nbeta, gc, eg, dgl, ek, bg, egl, ktm, vtm, ktail, kbg, vb, ga, fm, fmv, ot, fmb, fbv = common(i)
            P.tt("dve", z.ap, ga.ap[:, 384 + 6 * d:390 + 6 * d], dtb, ALU.add, [ga.b, self.gp.b], [z.b])
            yield
            P.act(e_.ap, z.ap, AF.Exp, [z.b], [e_.b])
            yield
            P.act(e_.ap, e_.ap, AF.Ln, [e_.b], [e_.b], bias=1.0)
            yield
            P.tt("dve", g.ap, e_.ap, nea, ALU.mult, [e_.b, self.nea.b], [g.b])
            yield
            P.act(eb.ap, ga.ap[:, 396 + 6 * d:402 + 6 * d], AF.Exp, [ga.b], [eb.b], scale=-1.0)
            yield
            P.ts("dve", eb.ap, eb.ap, 1.0, ALU.add, [eb.b], [eb.b])
            yield
            P.op("dve", lambda e: e.reciprocal(out=beta.ap, in_=eb.ap), [eb.b], [beta.b])
            yield
            P.ts("dve", nbeta.ap, beta.ap, -1.0, ALU.mult, [beta.b], [nbeta.b])
            yield
            P.mm(psSc.ap[:, 0:6], A_d, g.ap, True, True, [gmb, g.b], [psSc.b])
            yield
            P.copy("dve", gc.ap, psSc.ap[:, 0:6], [psSc.b], [gc.b])
            yield
            P.act(eg.ap, gc.ap, AF.Exp, [gc.b], [eg.b])
            yield
            P.mm(psSc.ap[:, 8:14], SelEnd, gc.ap, True, True, [gmb, gc.b], [psSc.b])
            yield
            P.tt("dve", dgl.ap, psSc.ap[:, 8:14], gc.ap, ALU.subtract, [psSc.b, gc.b], [dgl.b])
            yield
            P.act(ek.ap, dgl.ap, AF.Exp, [dgl.b], [ek.b])
            yield
            P.tt("dve", bg.ap, beta.ap, eg.ap, ALU.mult, [beta.b, eg.b], [bg.b])
            yield
            for c in range(2):
                P.mm(psSc.ap[0:64, 16 + c * 6:22 + c * 6], self.gmk(12 + 2 * d + c, 64), gc.ap, True, True, [gmb, gc.b], [psSc.b])
                yield
            P.act(egl.ap[0:64, :], psSc.ap[0:64, 16:28], AF.Exp, [psSc.b], [egl.b])
            yield
            for j in range(3):
                cn = kcorn[j]
                P.tr(cn.ap, fmv[:, 3 + j, :], ident, [fm.b, gmb], [cn.b])
                yield
                P.copy("act", ktm.ap[:, j * 128:(j + 1) * 128], cn.ap, [cn.b], [ktm.b])
                yield
            for j in range(3):
                cn = vcorn[j]
                P.tr(cn.ap, fmv[:, 6 + j, :], ident, [fm.b, gmb], [cn.b])
                yield
                P.copy("dve", vtm.ap[:, j * 128:(j + 1) * 128], cn.ap, [cn.b], [vtm.b])
                yield
            k3, v3_ = v3(ktm.ap, 6), v3(vtm.ap, 6)
            P.tt("pool", v3(ktail.ap, 6), k3, bc3(ek.ap, 64), ALU.mult, [ktm.b, ek.b], [ktail.b])
            yield
            P.tt("pool", v3(kbg.ap, 6), k3, bc3(bg.ap, 64), ALU.mult, [ktm.b, bg.b], [kbg.b])
            yield
            P.tt("pool", v3(vb.ap, 6), v3_, bc3(beta.ap, 64), ALU.mult, [vtm.b, beta.b], [vb.b])
            yield
            P.copy("pool", fmb.ap, fm.ap[:, 0:768], [fm.b], [fmb.b])
            yield

        def body(i, nxt):
            tt = order[i]
            tok0 = tt * 128
            z, e_, g, eb, beta, nbeta, gc, eg, dgl, ek, bg, egl, ktm, vtm, ktail, kbg, vb, ga, fm, fmv, ot, fmb, fbv = common(i)
            def batch(hb, fm=fm, fmv=fmv, fbv=fbv, fmb=fmb, i=i):
                Q0, Q1, Q2, Q3 = QB[hb]
                H = [3 * hb, 3 * hb + 1, 3 * hb + 2]
                hs = slice(3 * hb, 3 * hb + 3)
                ag, bgm, dgm, es, eit, aq, tt_, w_, qh = Ag[hb], Bg[hb], dg[hb], Es[hb], EiT[hb], aqkT[hb], TT[hb], wT[hb], qhT[hb]
                otb = otbs[i % 2][hb]
                tmpo = tmpos[hb]
                P.tt("pool", v3(ag.ap, 3), mb3(A_d, 3), bc3(g.ap[:, hs], 128), ALU.mult, [gmb, g.b], [ag.b])
                P.tt("pool", v3(bgm.ap, 3), mb3(B_d, 3), bc3(g.ap[:, hs], 128), ALU.mult, [gmb, g.b], [bgm.b])
                P.tt("pool", v3(dgm.ap, 3), mb3(ident, 3), bc3(eg.ap[:, hs], 128), ALU.mult, [gmb, eg.b], [dgm.b])
                for k, h in enumerate(H):
                    hp, hr = h // 2, (h % 2) * 64
                    kTb = fbv[hr:hr + 64, 3 + hp, :]
                    qTb = fbv[hr:hr + 64, hp, :]
                    ks = slice(k * 128, (k + 1) * 128)
                    P.mm(Q0.ap[:, ks], kTb, kTb, True, True, [fmb.b], [Q0.b])
                    P.mm(Q1.ap[:, ks], kTb, qTb, True, True, [fmb.b], [Q1.b])
                    P.mm(Q2.ap[:, ks], ag.ap[:, ks], B_d, True, False, [ag.b, gmb], [Q2.b])
                    P.mm(Q2.ap[:, ks], ident16, MS16, False, True, [self.gm16.b], [Q2.b])
                    P.mm(Q3.ap[:, ks], bgm.ap[:, ks], A_d, True, False, [bgm.b, gmb], [Q3.b])
                    P.mm(Q3.ap[:, ks], ident16, MIT16, False, True, [self.gm16.b], [Q3.b])
                yield
                P.act(es.ap, Q2.ap[:, 0:384], AF.Exp, [Q2.b], [es.b])
                P.act(eit.ap, Q3.ap[:, 0:384], AF.Exp, [Q3.b], [eit.b])
                p0, pt0 = Pm[hb][0], PTm[hb][0]
                for k, h in enumerate(H):
                    ks = slice(k * 128, (k + 1) * 128)
                    P.stt("dve", p0.ap[:, ks], Q0.ap[:, ks], nbeta.ap[:, h:h + 1], es.ap[:, ks], ALU.mult, ALU.mult, [Q0.b, es.b, nbeta.b], [p0.b])
                P.tt("dve", aq.ap, Q1.ap[:, 0:384], eit.ap, ALU.mult, [Q1.b, eit.b], [aq.b])
                yield
                for k in range(3):
                    ks = slice(k * 128, (k + 1) * 128)
                    P.tr(Q0.ap[:, ks], p0.ap[:, ks], ident, [p0.b, gmb], [Q0.b])
                yield
                P.copy("act", pt0.ap, Q0.ap[:, 0:384], [Q0.b], [pt0.b])
                for k in range(3):
                    ks = slice(k * 128, (k + 1) * 128)
                    P.tt("dve", tt_.ap[:, ks], Q0.ap[:, ks], ident, ALU.add, [Q0.b, gmb], [tt_.b])
                pc, ptc = p0, pt0
                yield
                for lvl in range(5):
                    pn, ptn = Pm[hb][(lvl + 1) % 2], PTm[hb][(lvl + 1) % 2]
                    for k in range(3):
                        ks = slice(k * 128, (k + 1) * 128)
                        P.mm(Q1.ap[:, ks], ptc.ap[:, ks], pc.ap[:, ks], True, True, [ptc.b, pc.b], [Q1.b])
                    if lvl < 4:
                        for k in range(3):
                            ks = slice(k * 128, (k + 1) * 128)
                            P.mm(Q2.ap[:, ks], pc.ap[:, ks], ptc.ap[:, ks], True, True, [ptc.b, pc.b], [Q2.b])
                    yield
                    P.copy("act", pn.ap, Q1.ap[:, 0:384], [Q1.b], [pn.b])
                    if lvl < 4:
                        P.copy("dve", ptn.ap, Q2.ap[:, 0:384], [Q2.b], [ptn.b])
                    yield
                    for k in range(3):
                        ks = slice(k * 128, (k + 1) * 128)
                        P.mm(Q3.ap[:, ks], pn.ap[:, ks], tt_.ap[:, ks], True, True, [pn.b, tt_.b], [Q3.b])
                    yield
                    P.tt("dve", tt_.ap, tt_.ap, Q3.ap[:, 0:384], ALU.add, [tt_.b, Q3.b], [tt_.b])
                    pc, ptc = pn, ptn
                    yield
                for k, h in enumerate(H):
                    hp, hr = h // 2, (h % 2) * 64
                    ks = slice(k * 128, (k + 1) * 128)
                    P.mm(Q0.ap[:, k * 64:(k + 1) * 64], tt_.ap[:, ks], vb.ap[:, h * 64:(h + 1) * 64], True, True, [tt_.b, vb.b], [Q0.b])
                    P.mm(Q1.ap[0:64, ks], kbg.ap[:, h * 64:(h + 1) * 64], tt_.ap[:, ks], True, True, [tt_.b, kbg.b], [Q1.b])
                    P.mm(Q2.ap[0:64, ks], self.ones.ap[:, 0:64], dgm.ap[:, ks], True, True, [self.ones.b, dgm.b], [Q2.b])
                yield
                P.copy("act", u[hb].ap, Q0.ap[:, 0:192], [Q0.b], [u[hb].b])
                P.copy("dve", w_.ap[0:64, :], Q1.ap[0:64, 0:384], [Q1.b], [w_.b])
                for k, h in enumerate(H):
                    hp, hr = h // 2, (h % 2) * 64
                    ks = slice(k * 128, (k + 1) * 128)
                    P.tt("dve", qh.ap[0:64, ks], fmv[hr:hr + 64, hp, :], Q2.ap[0:64, ks], ALU.mult, [fm.b, Q2.b], [qh.b])
                yield
                Sb = S[hb]
                vn = vnew[hb]
                for c in chunks:
                    cs = c * 64
                    for k, h in enumerate(H):
                        ks = slice(k * 128, (k + 1) * 128)
                        P.mm(Q3.ap[:, k * 64:(k + 1) * 64], w_.ap[0:64, ks], Sb.ap[0:64, k * 64:(k + 1) * 64], True, True, [w_.b, Sb.b], [Q3.b])
                    yield
                    P.tt("dve", vn.ap[cs:cs + 64, :], u[hb].ap[cs:cs + 64, :], Q3.ap[cs:cs + 64, 0:192], ALU.subtract, [u[hb].b, Q3.b], [vn.b])
                    yield
                    for k, h in enumerate(H):
                        ks = slice(k * 128, (k + 1) * 128)
                        k6 = slice(k * 64, (k + 1) * 64)
                        P.mm(Q0.ap[:, k6], qh.ap[0:64, ks], Sb.ap[0:64, k6], True, True, [qh.b, Sb.b], [Q0.b])
                        P.mm(Q1.ap[:, k6], aq.ap[cs:cs + 64, ks], vn.ap[cs:cs + 64, k6], True, True, [aq.b, vn.b], [Q1.b])
                        P.mm(Q2.ap[0:64, k6], ktail.ap[cs:cs + 64, h * 64:(h + 1) * 64], vn.ap[cs:cs + 64, k6], True, True, [ktail.b, vn.b], [Q2.b])
                    yield
                    P.copy("act", tmpo.ap[cs:cs + 64, :], Q1.ap[cs:cs + 64, 0:192], [Q1.b], [tmpo.b])
                    P.tt("dve", otb.ap[cs:cs + 64, :], Q0.ap[cs:cs + 64, 0:192], tmpo.ap[cs:cs + 64, :], ALU.add,
                         [Q0.b, tmpo.b], [otb.b])
                    for k, h in enumerate(H):
                        k6 = slice(k * 64, (k + 1) * 64)
                        P.stt("dve", Sb.ap[0:64, k6], Sb.ap[0:64, k6], egl.ap[0:64, c * 6 + h:c * 6 + h + 1], Q2.ap[0:64, k6], ALU.mult, ALU.add,
                              [Sb.b, egl.b, Q2.b], [Sb.b])
                    yield

            gens = [batch(0), batch(1)] + ([nxt] if nxt is not None else [])
            for _ in itertools.zip_longest(*gens):
                pass
            ot_r = [otbs[i % 2][0].b, otbs[i % 2][1].b]
            if d == 0:
                P.dma(self.of[tok0:tok0 + 128, :], ot.ap, r=ot_r)
            else:
                of_ = oft[i % 2]
                P.tt("pool", sg.ap, ot.ap, of_.ap, ALU.add, ot_r + [of_.b], [sg.b])
                P.tt("pool", sq.ap, sg.ap, sg.ap, ALU.mult, [sg.b], [sq.b])
                P.op("dve", lambda e: e.tensor_reduce(out=ss.ap, in_=v3(sq.ap, 6), axis=AX.X, op=ALU.add), [sq.b], [ss.b])
                P.act(rs.ap, ss.ap, AF.Ln, [ss.b], [rs.b], bias=EPS, scale=1.0 / 64)
                P.act(rs.ap, rs.ap, AF.Exp, [rs.b], [rs.b], scale=-0.5)
                P.tt("dve", v3(sq.ap, 6), v3(sg.ap, 6), bc3(rs.ap, 64), ALU.mult, [sg.b, rs.b], [sq.b])
                P.tt("pool", v3(ot.ap, 6), v3(sq.ap, 6), gw.unsqueeze(1).to_broadcast([128, 6, 64]), ALU.mult, [sq.b, self.gp.b], ot_r)
                P.act(sg.ap, ga.ap[:, 0:384], AF.Silu, [ga.b], [sg.b])
                P.tt("dve", sq.ap, ot.ap, sg.ap, ALU.mult, ot_r + [sg.b], [sq.b])
                for j in range(3):
                    P.tr(psK.ap[:, j * 128:(j + 1) * 128], sq.ap[:, j * 128:(j + 1) * 128], ident, [sq.b, gmb], [psK.b])
                P.copy("act", ybf.ap, psK.ap[:, 0:384], [psK.b], [ybf.b])
                P.dma(self.yT[384:768, tok0:tok0 + 128].rearrange("(j p) n -> p j n", p=128), v3(ybf.ap, 3), r=[ybf.b])

        for _ in prep(0):
            pass
        for i in range(len(order)):
            body(i, prep(i + 1) if i + 1 < len(order) else None)
        P.barrier()
        A.release(m0)

    def phase_out(self, l):
        P, A = self.P, self.A
        m0 = A.mark()
        last = (l == self.nlayers - 1) and self.nlayers == L
        Wo = A.bf16(8 * D)
        W1 = A.bf16(8 * 2 * DFF)
        W2 = A.bf16(22 * D)
        N = 256
        actb = A.bf16(22 * N)
        stg = [T(actb.ap.bitcast(F32)[:, 0:1280]), T(actb.ap.bitcast(F32)[:, 1280:2560])]
        self.load_w_bf16(Wo, self.w_out[l].rearrange("(kc p) n -> p kc n", p=128), 8, D, stg, piece=160)
        self.load_w_bf16(W1, self.w_f1[l].rearrange("(kc p) n -> p kc n", p=128), 8, 2 * DFF, stg, piece=160)
        self.load_w_bf16(W2, self.w_f2[l].rearrange("(kc p) n -> p kc n", p=128), 22, D, stg, piece=58)
        P.barrier()
        Wov, W1v, W2v = v3(Wo.ap, 8), v3(W1.ap, 8), v3(W2.ap, 22)
        xgs = [A.f32(8 * N), A.f32(8 * N)]
        yg = [A.bf16(8 * N), A.bf16(8 * N)]
        hb = A.bf16(8 * N)
        sq = A.f32(N)
        rstd = A.f32(N)
        sg = A.f32(N)
        xsrc3 = (self.xT if l == 0 else self.xs).rearrange("(kc p) n -> p kc n", p=128)
        xdst3 = self.xs.rearrange("(kc p) n -> p kc n", p=128)
        y3 = self.yT.rearrange("(kc p) n -> p kc n", p=128)
        o3 = self.outT.rearrange("(kc p) n -> p kc n", p=128)
        ps_ss = self.ps[0]
        psr = self.ps[1:4]
        psg = self.ps[4:6]
        psu = self.ps[6:8]
        ngr = NT // N
        hv = v3(hb.ap, 8)
        av = v3(actb.ap, 22)
        cnt = {"nr": 0}

        def loady(gi):
            P.dma(v3(yg[gi % 2].ap, 8), y3[:, :, gi * N:(gi + 1) * N], w=[yg[gi % 2].b])

        def ffn_out(gi):
            n0 = gi * N
            s = 1 if gi == 0 else 0
            xg = xgs[gi % 2]
            xv = v3(xg.ap, 8)
            for dt in range(8):
                ps = psr[cnt["nr"] % 3]
                cnt["nr"] += 1
                for ft in range(22):
                    P.mm(ps.ap[:, 0:N], W2v[:, ft, dt * 128:(dt + 1) * 128], av[:, ft, :], ft == 0, ft == 21, [W2.b, actb.b], [ps.b])
                P.stt("dve", xv[:, dt, :], ps.ap[:, 0:N], self.modv(l, 5, dt, s), xv[:, dt, :], ALU.mult, ALU.add, [ps.b, self.mod.b, xg.b], [xg.b])
            if not last:
                P.dma(xdst3[:, :, n0:n0 + N], xv, r=[xg.b])
            elif gi >= 1:
                for kc in range(8):
                    P.act(sq.ap, xv[:, kc, :], AF.Square, [xg.b], [sq.b])
                    P.mm(ps_ss.ap[:, 0:N], self.ones.ap, sq.ap, kc == 0, kc == 7, [self.ones.b, sq.b], [ps_ss.b])
                P.act(rstd.ap, ps_ss.ap[:, 0:N], AF.Ln, [ps_ss.b], [rstd.b], bias=EPS, scale=1.0 / D)
                P.act(rstd.ap, rstd.ap, AF.Exp, [rstd.b], [rstd.b], scale=-0.5)
                for kc in range(8):
                    P.stt("dve", xv[:, kc, :], xv[:, kc, :], self.fn.ap[:, kc:kc + 1], rstd.ap, ALU.mult, ALU.mult, [xg.b, self.fn.b, rstd.b], [xg.b])
                P.dma(o3[:, :, n0 - NCTX:n0 - NCTX + N], xv, r=[xg.b])

        loady(0)
        for gi in range(ngr):
            n0 = gi * N
            s = 1 if gi == 0 else 0
            if gi + 1 < ngr:
                loady(gi + 1)
            xg = xgs[gi % 2]
            xv = v3(xg.ap, 8)
            P.dma(xv, xsrc3[:, :, n0:n0 + N], w=[xg.b])
            y = yg[gi % 2]
            yv = v3(y.ap, 8)
            for dt in range(8):
                ps = psr[cnt["nr"] % 3]
                cnt["nr"] += 1
                for kc in range(8):
                    P.mm(ps.ap[:, 0:N], Wov[:, kc, dt * 128:(dt + 1) * 128], yv[:, kc, :], kc == 0, kc == 7, [Wo.b, y.b], [ps.b])
                P.stt("dve", xv[:, dt, :], ps.ap[:, 0:N], self.modv(l, 2, dt, s), xv[:, dt, :], ALU.mult, ALU.add, [ps.b, self.mod.b, xg.b], [xg.b])
            if gi >= 1:
                ffn_out(gi - 1)
            self.norm_mod(xg, N, l, self.a2, 3, s, hb, sq, rstd, ps_ss)
            for ft in range(22):
                pg, pu = psg[ft % 2], psu[ft % 2]
                for kc in range(8):
                    P.mm(pg.ap[:, 0:N], W1v[:, kc, ft * 128:(ft + 1) * 128], hv[:, kc, :], kc == 0, kc == 7, [W1.b, hb.b], [pg.b])
                for kc in range(8):
                    P.mm(pu.ap[:, 0:N], W1v[:, kc, DFF + ft * 128:DFF + (ft + 1) * 128], hv[:, kc, :], kc == 0, kc == 7, [W1.b, hb.b], [pu.b])
                P.act(sg.ap, pg.ap[:, 0:N], AF.Silu, [pg.b], [sg.b])
                P.tt("dve", av[:, ft, :], pu.ap[:, 0:N], sg.ap, ALU.mult, [pu.b, sg.b], [actb.b])
        ffn_out(ngr - 1)
        P.barrier()
        A.release(m0)

    def build(self):
        self.consts()
        self.phase_mod()
        ph = self.phases
        for l in range(self.nlayers):
            if ph is None or "in" in ph:
                self.phase_in(l)
            if ph is None or "a" in ph:
                for _ in self.phase_a(l):
                    pass
            if ph is None or "b1" in ph:
                for _ in self.phase_b1(l):
                    pass
            if ph is None or "c" in ph:
                self.phase_c(l)
            if ph is None or "b2" in ph:
                self.phase_b2(l, 0)
                self.phase_b2(l, 1)
            if ph is None or "out" in ph:
                self.phase_out(l)
        self.P.finish()
        self.st.close()
        return self.nc


def rope_tables():
    half = 16
    inv_freq = (1.0 / (10000.0 ** (np.arange(0, half, 2, dtype=np.float32) / np.float32(half)))).astype(np.float32)
    t = np.arange(TL, dtype=np.int32)
    ang_r = (t // 64).astype(np.float32)[:, None] * inv_freq
    ang_c = (t % 64).astype(np.float32)[:, None] * inv_freq
    ang = np.concatenate([ang_r, ang_r, ang_c, ang_c], axis=-1)
    cos = np.cos(ang).astype(np.float32)
    sin = np.sin(ang).astype(np.float32)
    sign = np.array([-1.0] * 8 + [1.0] * 8 + [-1.0] * 8 + [1.0] * 8, np.float32)
    tab = np.stack([cos.T, (sin * sign).T], axis=1)
    return np.ascontiguousarray(np.tile(tab, (4, 1, 1)))


def na_bias_tiles(nb):
    out = np.full((L, 4, 5, 5, 128, 128), -30000.0, np.float32)
    qi = np.arange(128)
    ki = np.arange(128)
    for pat, (r0, base) in enumerate(((0, 0), (2, 0), (4, 0), (60, 54), (62, 54))):
        r = r0 + qi // 64
        c = qi % 64
        rs = np.clip(r - 4, 0, 56)
        cs = np.clip(c - 8, 0, 48)
        for j in range(5):
            kr = base + 2 * j + ki // 64
            kc = ki % 64
            valid = ((kr[:, None] >= rs[None, :]) & (kr[:, None] < rs[None, :] + 8) &
                     (kc[:, None] >= cs[None, :]) & (kc[:, None] < cs[None, :] + 16))
            dr = np.clip(kr[:, None] - r[None, :] + 7, 0, 14)
            dc = np.clip(kc[:, None] - c[None, :] + 15, 0, 30)
            g = nb[:, :, dr, dc]
            out[:, :, pat, j] = np.where(valid[None, None], g, np.float32(-30000.0))
    return np.ascontiguousarray(out.transpose(0, 1, 4, 2, 3, 5).reshape(L, 4, 128, 3200))


def gdn_masks():
    m = np.zeros((16, 128, 128), np.float32)
    i = np.arange(128)
    same = (i[:, None] // 64) == (i[None, :] // 64)
    r, c = i[:, None], i[None, :]
    m[0] = np.eye(128)
    m[1] = same
    m[2] = same & (r <= c)
    m[3] = same & (r >= c)
    m[4] = same & (r > c)
    m[5] = same & (r < c)
    m[6] = np.where(same & (r > c), 0.0, -30000.0)
    m[7] = np.where(same & (r < c), 0.0, -30000.0)
    m[8] = np.where(same & (c >= r), 0.0, -30000.0)
    m[9] = np.where(same & (c <= r), 0.0, -30000.0)
    endf = (i // 64) * 64 + 63
    endb = (i // 64) * 64
    m[10] = (r == endf[None, :])
    m[11] = (r == endb[None, :])
    for d_, ends in enumerate(((63, 127), (0, 64))):
        for c_ in range(2):
            m[12 + 2 * d_ + c_][ends[c_], :] = 1.0
    return np.ascontiguousarray(m.transpose(1, 0, 2).reshape(128, 2048))


def host_prep(inp):
    f = lambda a: np.ascontiguousarray(a, dtype=np.float32)
    w_in = inp["w_in"]
    sizes = (1152, 1152, 384, 12, 12, 768)
    offs = np.cumsum((0,) + sizes)
    a0, b0, g0, al0, be0, c0 = offs[:6]
    perm32 = np.concatenate([np.arange(8, 16), np.arange(0, 8), np.arange(24, 32), np.arange(16, 24)])
    pad = lambda m: np.concatenate([m.reshape(4, 96), m.reshape(4, 96)[:, :32]], axis=1).reshape(-1)
    idq = pad(np.arange(384))
    permq = pad((np.arange(384).reshape(12, 32)[:, perm32]).reshape(-1))
    cols = np.concatenate([
        a0 + idq, a0 + permq, a0 + 384 + idq, a0 + 384 + permq,
        b0 + np.arange(1152), c0 + np.arange(256), c0 + 256 + np.arange(256),
        a0 + 768 + np.arange(384), c0 + 512 + np.arange(256), g0 + np.arange(384), al0 + np.arange(12), be0 + np.arange(12)])
    assert cols.shape[0] == NW
    shared = {
        "w_mod": f(inp["w_mod"]),
        "bmodT": f(inp["b_mod"].reshape(L, 48, 128).transpose(2, 0, 1).reshape(128, L * 48)),
        "n1T": f(inp["norm1_w"].reshape(L, 8, 128).transpose(2, 0, 1).reshape(128, L * 8)),
        "n2T": f(inp["norm2_w"].reshape(L, 8, 128).transpose(2, 0, 1).reshape(128, L * 8)),
        "fnT": f(inp["final_norm_w"].reshape(8, 128).T),
        "w_in": f(w_in[:, :, cols]),
        "rope": rope_tables(),
        "lamp": f(np.stack([inp["lambda_q1"], inp["lambda_k1"], inp["lambda_q2"], inp["lambda_k2"]], axis=1).reshape(1, L * 128)),
        "dnT": f(np.tile(inp["diff_norm_w"].T, (2, 1))),
        "nab": na_bias_tiles(inp["na_bias"]),
        "convT": f(inp["conv_w"].reshape(L, 5, 9, 128).transpose(3, 0, 2, 1).reshape(128, L * 45)),
        "gpar": f(np.concatenate([inp["a_log"].reshape(-1), inp["dt_bias"].reshape(-1), inp["gdn_norm_w"].reshape(-1)])[None, :]),
        "gmask": gdn_masks(),
        "w_out": f(inp["w_out"]),
        "w_f1": f(inp["w_ffn_in"]),
        "w_f2": f(inp["w_ffn_out"]),
    }
    per = []
    for b in range(4):
        xT = f(np.concatenate([inp["ctx"][b], inp["x"][b]], axis=0).T)
        cT = f(np.stack([inp["c"][b].reshape(8, 128).T, inp["c_ctx"].reshape(8, 128).T], axis=2).reshape(128, 16))
        m = dict(shared)
        m["xT"] = xT
        m["cT"] = cT
        per.append(m)
    return per


def kernel(**inputs):
    inp = {k: np.asarray(v) for k, v in inputs.items()}
    per = host_prep(inp)
    nc = K().build()
    in_maps = [per[i % 4] for i in range(8)]
    res = run_bass_kernel_spmd(nc, in_maps, core_ids=list(range(8)))
    out = np.stack([np.ascontiguousarray(res.results[b]["outT"].T) for b in range(4)], axis=0)
    return out.astype(np.float32)
```

```python
import contextlib
import itertools
import math
import numpy as np
import concourse.bass as bass
import concourse.mybir as mybir
from concourse.bass_utils import run_bass_kernel_spmd

F32 = mybir.dt.float32
BF16 = mybir.dt.bfloat16
AF = mybir.ActivationFunctionType
ALU = mybir.AluOpType
AX = mybir.AxisListType

NT, NCTX, TL, D, L = 4352, 256, 4096, 1024, 4
NTILE = NT // 128
DFF = 2816
EPS = 1e-6
C_QA, C_QAP, C_KA, C_KAP, C_B, C_QC, C_KC = 0, 512, 1024, 1536, 2048, 3200, 3456
NFM = 3712
C_VA, C_VC, C_G = 3712, 4096, 4352
NW = 4760
GROUPS = [(0, 256)] + [(256 + 512 * i, 512) for i in range(8)]


class Buf:
    __slots__ = ("w", "r", "g")

    def __init__(self):
        self.w = None
        self.r = {}
        self.g = None


class T:
    __slots__ = ("ap", "b")

    def __init__(self, ap):
        self.ap = ap
        self.b = Buf()


class Prog:
    ENG = ("pe", "act", "dve", "pool", "sp")

    def __init__(self, nc, stack, n_dma=48, same=True):
        self.nc = nc
        self.ops = {e: [] for e in self.ENG}
        self.cnt = {e: 0 for e in self.ENG}
        self.seen = {e: {} for e in self.ENG}
        self.esem = {e: stack.enter_context(nc.semaphore("s_" + e)) for e in self.ENG}
        self.dsem = [stack.enter_context(nc.semaphore("d%d" % i)) for i in range(n_dma)]
        self.dval = [0] * n_dma
        self.dnext = 0
        self.same = same

    def _wait(self, eng, tok):
        if tok is None:
            return
        key, val = tok
        if key == eng and (not self.same or eng == "pe"):
            return
        if self.seen[eng].get(key, 0) >= val:
            return
        self.seen[eng][key] = val
        sem = self.esem[key] if isinstance(key, str) else self.dsem[key]
        self.ops[eng].append(lambda e: e.wait_ge(sem, val))

    def _deps(self, eng, reads, writes):
        for b in reads:
            self._wait(eng, b.w)
        for b in writes:
            self._wait(eng, b.w)
            for t in list(b.r.values()):
                self._wait(eng, t)

    def _commit(self, tok, reads, writes):
        for b in reads:
            b.r[tok[0]] = tok
        for b in writes:
            b.w = tok
            b.r = {}

    def op(self, eng, fn, r=(), w=()):
        self._deps(eng, r, w)
        guards = [b.g for b in r if b.g is not None] if eng in ("act", "dve") else ()
        for g in guards:
            if g[0] is not None and g[1] != eng:
                self._wait(eng, g[0])
        self.cnt[eng] += 1
        tok = (eng, self.cnt[eng])
        sem = self.esem[eng]
        self.ops[eng].append(lambda e: fn(e).then_inc(sem, 1))
        self._commit(tok, r, w)
        for g in guards:
            g[0], g[1] = tok, eng

    def dma(self, out, in_, r=(), w=(), eng="sp"):
        k = self.dnext
        self.dnext = (self.dnext + 1) % len(self.dsem)
        if self.dval[k]:
            self._wait(eng, (k, self.dval[k]))
        self._deps(eng, r, w)
        self.dval[k] += 16
        tok = (k, self.dval[k])
        sem = self.dsem[k]
        self.ops[eng].append(lambda e: e.dma_start(out=out, in_=in_).then_inc(sem, 16))
        self._commit(tok, r, w)

    def mm(self, out, lhsT, rhs, start, stop, r, w):
        self.op("pe", lambda e: e.matmul(out, lhsT=lhsT, rhs=rhs, start=start, stop=stop), r, w)

    def filler(self, n, out, lhsT, rhs):
        for _ in range(n):
            self.op("pe", lambda e: e.matmul(out, lhsT=lhsT, rhs=rhs, start=True, stop=True), (), ())

    def tr(self, out, in_, ident, r, w):
        self.op("pe", lambda e: e.transpose(out, in_, ident), r, w)

    def act(self, out, in_, func, r, w, bias=None, scale=None, accum=None):
        kw = {}
        if bias is not None:
            kw["bias"] = bias
        if scale is not None:
            kw["scale"] = scale
        if accum is not None:
            kw["accum_out"] = accum
        self.op("act", lambda e: e.activation(out=out, in_=in_, func=func, **kw), r, w)

    def tt(self, eng, out, in0, in1, op, r, w):
        self.op(eng, lambda e: e.tensor_tensor(out=out, in0=in0, in1=in1, op=op), r, w)

    def ts(self, eng, out, in0, s1, op0, r, w, s2=None, op1=None):
        if op1 is None:
            self.op(eng, lambda e: e.tensor_scalar(out=out, in0=in0, scalar1=s1, scalar2=None, op0=op0), r, w)
        else:
            self.op(eng, lambda e: e.tensor_scalar(out=out, in0=in0, scalar1=s1, scalar2=s2, op0=op0, op1=op1), r, w)

    def stt(self, eng, out, in0, scalar, in1, op0, op1, r, w):
        self.op(eng, lambda e: e.scalar_tensor_tensor(out=out, in0=in0, scalar=scalar, in1=in1, op0=op0, op1=op1), r, w)

    def copy(self, eng, out, in_, r, w):
        if eng == "act":
            self.op("act", lambda e: e.activation(out=out, in_=in_, func=AF.Copy), r, w)
        else:
            self.op(eng, lambda e: e.tensor_copy(out=out, in_=in_), r, w)

    def memset(self, eng, out, val, w):
        self.op(eng, lambda e: e.memset(out, val), (), w)

    def barrier(self):
        for e in self.ENG:
            for f in self.ENG:
                if f != e and self.cnt[f]:
                    self._wait(e, (f, self.cnt[f]))
            for k, v in enumerate(self.dval):
                if v:
                    self._wait(e, (k, v))

    def finish(self):
        self.barrier()
        ops = self.ops
        with self.nc.Block() as block:
            @block.tensor
            def _(e):
                for f in ops["pe"]:
                    f(e)

            @block.scalar
            def _(e):
                for f in ops["act"]:
                    f(e)

            @block.vector
            def _(e):
                for f in ops["dve"]:
                    f(e)

            @block.gpsimd
            def _(e):
                for f in ops["pool"]:
                    f(e)

            @block.sync
            def _(e):
                for f in ops["sp"]:
                    f(e)


class Arena:
    def __init__(self, ap, n):
        self.ap, self.n, self.top = ap, n, 0

    def f32(self, n, shape=None):
        a = self.top
        self.top += n
        assert self.top <= self.n, ("SBUF arena overflow", self.top, self.n)
        ap = self.ap[:, a:a + n]
        return T(ap)

    def bf16(self, n):
        t = self.f32((n + 1) // 2)
        t.ap = t.ap.bitcast(BF16)
        return t

    def mark(self):
        return self.top

    def release(self, m):
        self.top = m


def sub(bank, ap):
    t = T(ap)
    t.b = bank.b
    return t


def v3(ap, a):
    return ap.rearrange("p (a b) -> p a b", a=a)


class K:
    def __init__(self, debug=False, nlayers=L, phases=None):
        self.debug = debug
        self.nlayers = nlayers
        self.phases = phases
        nc = self.nc = bass.Bass("TRN2", target_bir_lowering=False)
        self.st = contextlib.ExitStack()
        self.P = Prog(nc, self.st)
        ein = lambda n, s, dt=F32: nc.dram_tensor(n, list(s), dt, kind="ExternalInput").ap()
        self.xT = ein("xT", (D, NT))
        self.cT = ein("cT", (128, 16))
        self.w_mod = ein("w_mod", (L, D, 6 * D))
        self.bmodT = ein("bmodT", (128, L * 48))
        self.n1T = ein("n1T", (128, L * 8))
        self.n2T = ein("n2T", (128, L * 8))
        self.fnT = ein("fnT", (128, 8))
        self.w_in = ein("w_in", (L, D, NW))
        self.rope = ein("rope", (128, 2, TL))
        self.lamp = ein("lamp", (1, L * 128))
        self.dnT = ein("dnT", (128, L))
        self.nab = ein("nab", (L, 4, 128, 3200))
        self.convT = ein("convT", (128, L * 45))
        self.gpar = ein("gpar", (1, L * 24 + L * 64))
        self.gmask = ein("gmask", (128, 2048))
        self.w_out = ein("w_out", (L, D, D))
        self.w_f1 = ein("w_f1", (L, D, 2 * DFF))
        self.w_f2 = ein("w_f2", (L, DFF, D))
        sk = "ExternalOutput" if debug else "Internal"
        scr = lambda n, s, dt=F32: nc.dram_tensor(n, list(s), dt, kind=sk).ap()
        self.xs = scr("xs", (D, NT))
        self.qa = scr("qa", (512, NT), BF16)
        self.ka = scr("ka", (512, NT), BF16)
        self.va = scr("va", (NT, 384), BF16)
        self.qkvb = scr("qkvb", (1152, NT))
        self.gab = scr("gab", (NT, 408))
        self.qc = scr("qc", (256, NT), BF16)
        self.kc = scr("kc", (256, NT), BF16)
        self.vc = scr("vc", (NT, 256), BF16)
        self.yT = scr("yT", (D, NT), BF16)
        self.qkn = scr("qkn", (1152, NT))
        self.of = scr("of", (NT, 384))
        self.outT = nc.dram_tensor("outT", [D, TL], F32, kind="ExternalOutput").ap()
        self.dbufs = {}
        arena_ap = self.st.enter_context(nc.sbuf_tensor("arena", [128, 53200], F32))
        self.A = Arena(arena_ap, 53200)
        self.ps = [T(self.st.enter_context(nc.psum_tensor("ps%d" % i, [128, 512], F32))[:]) for i in range(8)]
        for t in self.ps:
            t.b.g = [None, None]

    def db(self, name):
        return Buf()

    def consts(self):
        P, A = self.P, self.A
        self.ones = A.f32(128)
        P.memset("pool", self.ones.ap, 1.0, [self.ones.b])
        self.cTt = A.f32(16)
        P.dma(self.cTt.ap, self.cT, w=[self.cTt.b])
        self.sc = A.f32(16)
        P.act(self.sc.ap, self.cTt.ap, AF.Silu, [self.cTt.b], [self.sc.b])
        self.mod = A.f32(L * 48 * 2)
        self.bm = A.f32(L * 48)
        P.dma(self.bm.ap, self.bmodT, w=[self.bm.b])
        self.n1 = A.f32(L * 8)
        self.n2 = A.f32(L * 8)
        self.fn = A.f32(8)
        P.dma(self.n1.ap, self.n1T, w=[self.n1.b])
        P.dma(self.n2.ap, self.n2T, w=[self.n2.b])
        P.dma(self.fn.ap, self.fnT, w=[self.fn.b])
        self.cw = A.f32(L * 45)
        P.dma(self.cw.ap, self.convT, w=[self.cw.b])
        self.gm = A.f32(2048)
        P.dma(self.gm.ap, self.gmask, w=[self.gm.b])
        self.gm16 = A.bf16(640)
        P.copy("pool", self.gm16.ap[:, 0:128], self.gm.ap[:, 0:128], [self.gm.b], [self.gm16.b])
        P.copy("pool", self.gm16.ap[:, 128:640], self.gm.ap[:, 768:1280], [self.gm.b], [self.gm16.b])
        gp = self.gp = A.f32(L * 24 + L * 64)
        P.dma(gp.ap, self.gpar.partition_broadcast(128), w=[gp.b])
        self.nea = A.f32(L * 12)
        P.act(self.nea.ap, gp.ap[:, 0:L * 12], AF.Exp, [gp.b], [self.nea.b])
        P.ts("dve", self.nea.ap, self.nea.ap, -1.0, ALU.mult, [self.nea.b], [self.nea.b])
        self.nlam = A.f32(L)
        self.dnl = A.f32(L)
        self.a1 = A.f32(L * 16)
        self.a2 = A.f32(L * 16)
        mtmp = A.mark()
        lp = A.f32(L * 128)
        P.dma(lp.ap, self.lamp.partition_broadcast(128), w=[lp.b])
        lpv = lp.ap.rearrange("p (l f d) -> p l f d", l=L, f=4)
        pr_ = A.f32(L * 64)
        prv = pr_.ap.rearrange("p (l f d) -> p l f d", l=L, f=2)
        P.tt("dve", prv, lpv[:, :, 0:4:2, :], lpv[:, :, 1:4:2, :], ALU.mult, [lp.b], [pr_.b])
        ee = A.f32(L * 2)
        P.op("dve", lambda e: e.tensor_reduce(out=ee.ap, in_=pr_.ap.rearrange("p (g d) -> p g d", d=32), axis=AX.X, op=ALU.add), [pr_.b], [ee.b])
        P.act(ee.ap, ee.ap, AF.Exp, [ee.b], [ee.b])
        dn = A.f32(L)
        P.dma(dn.ap, self.dnT, w=[dn.b])
        for l in range(L):
            li = 0.8 - 0.6 * math.exp(-0.3 * l)
            P.stt("dve", self.nlam.ap[:, l:l + 1], ee.ap[:, 2 * l + 1:2 * l + 2], -li, ee.ap[:, 2 * l:2 * l + 1], ALU.add, ALU.subtract,
                  [ee.b], [self.nlam.b])
            P.ts("dve", self.dnl.ap[:, l:l + 1], dn.ap[:, l:l + 1], 1.0 - li, ALU.mult, [dn.b], [self.dnl.b])
        P.barrier()
        A.release(mtmp)

    def phase_mod(self):
        P, A = self.P, self.A
        m0 = A.mark()
        wm = [A.f32(8 * 768), A.f32(8 * 768)]
        ps = self.ps[0]
        it = 0
        for l in range(self.nlayers):
            wl = self.w_mod[l].rearrange("(kc p) n -> p kc n", p=128)
            for j in range(8):
                w = wm[it % 2]
                it += 1
                wv = v3(w.ap, 8)
                P.dma(wv, wl[:, :, j * 768:(j + 1) * 768], w=[w.b])
                for ct in range(6):
                    for kc in range(8):
                        P.mm(ps.ap[:, ct * 2:ct * 2 + 2], wv[:, kc, ct * 128:(ct + 1) * 128],
                             self.sc.ap[:, kc * 2:kc * 2 + 2], kc == 0, kc == 7, [w.b, self.sc.b], [ps.b])
                o = (l * 48 + j * 6) * 2
                P.tt("dve", v3(self.mod.ap[:, o:o + 12], 6), v3(ps.ap[:, 0:12], 6),
                     self.bm.ap[:, l * 48 + j * 6:l * 48 + j * 6 + 6].unsqueeze(2).to_broadcast([128, 6, 2]),
                     ALU.add, [ps.b, self.bm.b], [self.mod.b])
        for l in range(self.nlayers):
            for (a, n, which) in ((self.a1, self.n1, 1), (self.a2, self.n2, 4)):
                o = (l * 48 + which * 8) * 2
                P.stt("dve", v3(a.ap[:, l * 16:(l + 1) * 16], 8), v3(self.mod.ap[:, o:o + 16], 8), 1.0,
                      n.ap[:, l * 8:(l + 1) * 8].unsqueeze(2).to_broadcast([128, 8, 2]),
                      ALU.add, ALU.mult, [self.mod.b, n.b], [a.b])
        P.barrier()
        A.release(m0)

    def modv(self, l, which, kc, s):
        o = ((l * 48 + which * 8 + kc) * 2) + s
        return self.mod.ap[:, o:o + 1]

    def load_w_bf16(self, dst, src3, nk, ncols, stg, piece=512):
        P = self.P
        dv = v3(dst.ap, nk)
        i = 0
        for c0 in range(0, ncols, piece):
            c1 = min(ncols, c0 + piece)
            s = stg[i % 2]
            sv = v3(s.ap[:, 0:nk * (c1 - c0)], nk)
            P.dma(sv, src3[:, :, c0:c1], w=[s.b])
            P.copy("pool" if i % 2 else "act", dv[:, :, c0:c1], sv, [s.b], [dst.b])
            i += 1

    def norm_mod(self, xg, N, l, a, which_shift, s, hb, sq, rstd, ps_ss):
        P = self.P
        xv = v3(xg.ap, 8)
        hv = v3(hb.ap, 8)
        for kc in range(8):
            P.act(sq.ap[:, 0:N], xv[:, kc, :], AF.Square, [xg.b], [sq.b])
            P.mm(ps_ss.ap[:, 0:N], self.ones.ap, sq.ap[:, 0:N], kc == 0, kc == 7, [self.ones.b, sq.b], [ps_ss.b])
        P.act(rstd.ap[:, 0:N], ps_ss.ap[:, 0:N], AF.Ln, [ps_ss.b], [rstd.b], bias=EPS, scale=1.0 / D)
        P.act(rstd.ap[:, 0:N], rstd.ap[:, 0:N], AF.Exp, [rstd.b], [rstd.b], scale=-0.5)
        for kc in range(8):
            ao = l * 16 + kc * 2 + s
            P.stt("dve", sq.ap[:, 0:N], xv[:, kc, :], a.ap[:, ao:ao + 1], rstd.ap[:, 0:N], ALU.mult, ALU.mult,
                  [xg.b, a.b, rstd.b], [sq.b])
            P.act(hv[:, kc, :], sq.ap[:, 0:N], AF.Identity, [sq.b, self.mod.b], [hb.b], bias=self.modv(l, which_shift, kc, s), scale=1.0)

    def phase_in(self, l):
        P, A = self.P, self.A
        m0 = A.mark()
        Wb = A.bf16(8 * NW)
        stg = [A.f32(8 * 256), A.f32(8 * 256)]
        self.load_w_bf16(Wb, self.w_in[l].rearrange("(kc p) n -> p kc n", p=128), 8, NW, stg, piece=256)
        Wv = v3(Wb.ap, 8)
        xg = [A.f32(8 * 512), A.f32(8 * 512)]
        hb = [A.bf16(8 * 512), A.bf16(8 * 512)]
        sq = A.f32(512)
        rstd = A.f32(512)
        rp = [A.f32(1024), A.f32(1024)]
        ofm = [A.f32(512) for _ in range(3)]
        otm = [A.f32(408) for _ in range(3)]
        t1 = A.f32(512)
        t2 = A.f32(512)
        xsrc = self.xT if l == 0 else self.xs
        xsrc3 = xsrc.rearrange("(kc p) n -> p kc n", p=128)
        bx = self.db("xs")
        ps_ss = self.ps[0]
        psf = self.ps[1:5]
        pst = self.ps[5:8]
        nf = 0
        ntm = 0

        def load(gi):
            n0, N = GROUPS[gi]
            x = xg[gi % 2]
            P.dma(v3(x.ap[:, 0:8 * N], 8), xsrc3[:, :, n0:n0 + N], r=[bx], w=[x.b])
            if gi > 0:
                r_ = rp[gi % 2]
                P.dma(v3(r_.ap, 2), self.rope[:, :, n0 - NCTX:n0 - NCTX + 512], w=[r_.b])

        def norm(gi):
            n0, N = GROUPS[gi]
            x = xg[gi % 2]
            h = hb[gi % 2]
            xin = T(x.ap[:, 0:8 * N]); xin.b = x.b
            hin = T(h.ap[:, 0:8 * N]); hin.b = h.b
            self.norm_mod(xin, N, l, self.a1, 0, 1 if gi == 0 else 0, hin, sq, rstd, ps_ss)

        load(0)
        norm(0)
        for gi, (n0, N) in enumerate(GROUPS):
            if gi + 1 < len(GROUPS):
                load(gi + 1)
            x = xg[gi % 2]
            h = hb[gi % 2]
            s = 1 if gi == 0 else 0
            hv = v3(h.ap[:, 0:8 * N], 8)
            rv = v3(rp[gi % 2].ap, 2)

            def fm(ft, ps):
                for kc in range(8):
                    P.mm(ps.ap[:, 0:N], Wv[:, kc, ft * 128:(ft + 1) * 128], hv[:, kc, :], kc == 0, kc == 7, [Wb.b, h.b], [ps.b])

            for (c0, dst, nm) in ((C_QA, self.qa, "qa"), (C_KA, self.ka, "ka")):
                for j in range(4):
                    o = ofm[nf % 3]
                    ob = o.ap.bitcast(BF16)[:, 0:N]
                    p1 = psf[nf % 4]
                    nf += 1
                    fm(c0 // 128 + j, p1)
                    if gi == 0:
                        P.copy("dve", ob, p1.ap[:, 0:N], [p1.b], [o.b])
                    else:
                        p2 = psf[nf % 4]
                        nf += 1
                        fm(c0 // 128 + 4 + j, p2)
                        P.tt("dve", t1.ap, p1.ap, rv[:, 0, :], ALU.mult, [p1.b, rp[gi % 2].b], [t1.b])
                        P.tt("dve", t2.ap, p2.ap, rv[:, 1, :], ALU.mult, [p2.b, rp[gi % 2].b], [t2.b])
                        P.tt("pool", ob, t1.ap, t2.ap, ALU.add, [t1.b, t2.b], [o.b])
                    P.dma(dst[j * 128:(j + 1) * 128, n0:n0 + N], ob, r=[o.b], w=[self.db(nm)])
            for j in range(9):
                o = ofm[nf % 3]
                p1 = psf[nf % 4]
                nf += 1
                fm(C_B // 128 + j, p1)
                P.copy("act", o.ap[:, 0:N], p1.ap[:, 0:N], [p1.b], [o.b])
                P.dma(self.qkvb[j * 128:(j + 1) * 128, n0:n0 + N], o.ap[:, 0:N], r=[o.b], w=[self.db("qkvb")])
            for (c0, dst, nm) in ((C_QC, self.qc, "qc"), (C_KC, self.kc, "kc")):
                for j in range(2):
                    o = ofm[nf % 3]
                    ob = o.ap.bitcast(BF16)[:, 0:N]
                    p1 = psf[nf % 4]
                    nf += 1
                    fm(c0 // 128 + j, p1)
                    P.copy("dve", ob, p1.ap[:, 0:N], [p1.b], [o.b])
                    P.dma(dst[j * 128:(j + 1) * 128, n0:n0 + N], ob, r=[o.b], w=[self.db(nm)])
            if gi + 1 < len(GROUPS):
                norm(gi + 1)
            for tt in range(N // 128):
                tok0 = n0 + tt * 128
                for (c0, nc_, dst, nm, isbf) in ((C_VA, 384, self.va, "va", True), (C_VC, 256, self.vc, "vc", True),
                                                 (C_G, 408, self.gab, "gab", False)):
                    ps = pst[ntm % 3]
                    o = otm[ntm % 3]
                    ntm += 1
                    for kc in range(8):
                        P.mm(ps.ap[:, 0:nc_], hv[:, kc, tt * 128:(tt + 1) * 128], Wv[:, kc, c0:c0 + nc_], kc == 0, kc == 7,
                             [Wb.b, h.b], [ps.b])
                    oa = o.ap.bitcast(BF16)[:, 0:nc_] if isbf else o.ap[:, 0:nc_]
                    P.copy("act" if ntm % 2 else "dve", oa, ps.ap[:, 0:nc_], [ps.b], [o.b])
                    P.dma(dst[tok0:tok0 + 128, :], oa, r=[o.b], w=[self.db(nm)])
        P.barrier()
        A.release(m0)


    def phase_a(self, l):
        P, A = self.P, self.A
        m0 = A.mark()
        KAt = A.bf16(4 * NT)
        P.dma(v3(KAt.ap, 4), self.ka.rearrange("(t p) n -> p t n", p=128), w=[KAt.b])
        Kv = v3(KAt.ap, 4)
        VA = A.bf16(NTILE * 6 * 128)
        Vv = VA.ap.rearrange("p (t h d) -> p t h d", t=NTILE, h=6)
        P.memset("pool", VA.ap, 1.0, [VA.b])
        vsrc = self.va.rearrange("(t p) (h d) -> p t h d", p=128, h=6)
        for t0 in range(NTILE):
            P.dma(Vv[:, t0, :, 0:64], vsrc[:, t0, :, :], w=[VA.b])
        Qt = [A.bf16(4 * 512), A.bf16(4 * 512)]
        pT = [[A.bf16(512) for _ in range(3)] for _ in range(2)]
        o12 = [[A.f32(512), A.f32(512)], [A.f32(512), A.f32(512)]]
        rz = A.f32(512)
        sq = A.f32(512)
        rs = A.f32(512)
        yb = [A.bf16(512), A.bf16(512)]
        deferred = []
        nh = 0
        psS = [self.ps[0:2], self.ps[2:4]]
        psO = self.ps[4:6]
        psN = self.ps[6]
        psF = psN
        fl = A.bf16(128 + 512)
        P.memset("pool", fl.ap, 0.0, [fl.b])
        NFILL = getattr(self, "a_fill", 1)
        scale = 32.0 ** -0.5
        qsrc = self.qa.rearrange("(t p) n -> p t n", p=128)
        ny = 0

        def loadq(gi):
            n0, N = GROUPS[gi]
            q = Qt[gi % 2]
            P.dma(v3(q.ap, 4)[:, :, 0:N], qsrc[:, :, n0:n0 + N], w=[q.b])

        loadq(0)
        for gi, (n0, N) in enumerate(GROUPS):
            if gi + 1 < len(GROUPS):
                loadq(gi + 1)
            q = Qt[gi % 2]
            qv = v3(q.ap, 4)
            kts = [0, 1] if gi == 0 else list(range(NTILE))
            items = [(h, kt) for h in range(6) for kt in kts]

            def S(i):
                h, kt = items[i]
                for m in range(2):
                    hm = 2 * h + m
                    t, pr = hm // 3, (hm % 3) * 32
                    ps = psS[m][i % 2]
                    P.mm(ps.ap[:, 0:N], Kv[pr:pr + 32, t, kt * 128:(kt + 1) * 128], qv[pr:pr + 32, t, 0:N], True, True, [KAt.b, q.b], [ps.b])

            S(0)
            for i, (h, kt) in enumerate(items):
                if i + 1 < len(items):
                    S(i + 1)
                for _f in range(NFILL):
                    P.mm(psF.ap[:, 0:512], fl.ap[:, 0:128], fl.ap[:, 128:640], True, True, [fl.b], [psF.b])
                for m in range(2):
                    ps = psS[m][i % 2]
                    p = pT[m][i % 3]
                    P.act(p.ap[:, 0:N], ps.ap[:, 0:N], AF.Exp, [ps.b], [p.b], scale=scale)
                for m in range(2):
                    p = pT[m][i % 3]
                    po = psO[m]
                    P.mm(po.ap[:, 0:N], Vv[:, kt, h, :], p.ap[:, 0:N], kt == kts[0], kt == kts[-1], [VA.b, p.b], [po.b])
                for fn_ in [f for (due, f) in deferred if due <= i]:
                    fn_()
                deferred[:] = [(due, f) for (due, f) in deferred if due > i]
                if kt == kts[-1]:
                    oo = o12[nh % 2]
                    nh += 1
                    for m in range(2):
                        po = psO[m]
                        P.copy("dve", oo[m].ap[:, 0:N], po.ap[:, 0:N], [po.b], [oo[m].b])
                    for _f in range(getattr(self, "a_bfill", 4)):
                        P.mm(psF.ap[:, 0:512], fl.ap[:, 0:128], fl.ap[:, 128:640], True, True, [fl.b], [psF.b])

                    def fin1(oo=oo, h=h, N=N, n0=n0):
                        for m in range(2):
                            o = oo[m]
                            P.op("dve", lambda e, o=o: e.reciprocal(out=rz.ap[0:64, 0:N], in_=o.ap[64:128, 0:N]), [o.b], [rz.b])
                            P.tt("dve", o.ap[0:64, 0:N], o.ap[0:64, 0:N], rz.ap[0:64, 0:N], ALU.mult, [o.b, rz.b], [o.b])
                        o1, o2 = oo
                        P.stt("dve", o1.ap[0:64, 0:N], o2.ap[0:64, 0:N], self.nlam.ap[0:64, l:l + 1], o1.ap[0:64, 0:N], ALU.mult, ALU.add,
                              [o1.b, o2.b, self.nlam.b], [o1.b])
                        P.tt("pool", sq.ap[0:64, 0:N], o1.ap[0:64, 0:N], o1.ap[0:64, 0:N], ALU.mult, [o1.b], [sq.b])

                    def fin2(oo=oo, h=h, N=N, n0=n0):
                        o1 = oo[0]
                        P.mm(psN.ap[0:64, 0:N], self.ones.ap[0:64, 0:64], sq.ap[0:64, 0:N], True, True, [self.ones.b, sq.b], [psN.b])
                        P.act(rs.ap[0:64, 0:N], psN.ap[0:64, 0:N], AF.Ln, [psN.b], [rs.b], bias=EPS, scale=1.0 / 64)
                        P.act(rs.ap[0:64, 0:N], rs.ap[0:64, 0:N], AF.Exp, [rs.b], [rs.b], scale=-0.5)
                        y = yb[h % 2]
                        P.stt("dve", y.ap[0:64, 0:N], o1.ap[0:64, 0:N], self.dnl.ap[0:64, l:l + 1], rs.ap[0:64, 0:N], ALU.mult, ALU.mult,
                              [o1.b, rs.b, self.dnl.b], [y.b])
                        P.dma(self.yT[h * 64:(h + 1) * 64, n0:n0 + N], y.ap[0:64, 0:N], r=[y.b])

                    fin1()
                    if gi == 0:
                        fin2()
                    else:
                        deferred.append((i + 6, fin2))
                yield
            for (_, f) in deferred:
                f()
            deferred[:] = []
        P.barrier()
        A.release(m0)

    def phase_c(self, l):
        P, A = self.P, self.A
        m0 = A.mark()
        KCt = A.bf16(2 * NT)
        QCt = A.bf16(2 * NT)
        P.dma(v3(KCt.ap, 2), self.kc.rearrange("(t p) n -> p t n", p=128), w=[KCt.b])
        P.dma(v3(QCt.ap, 2), self.qc.rearrange("(t p) n -> p t n", p=128), w=[QCt.b])
        Kv, Qv = v3(KCt.ap, 2), v3(QCt.ap, 2)
        VC = A.bf16(NTILE * 4 * 128)
        Vv = VC.ap.rearrange("p (t h d) -> p t h d", t=NTILE, h=4)
        P.memset("pool", VC.ap, 1.0, [VC.b])
        vsrc = self.vc.rearrange("(t p) (h d) -> p t h d", p=128, h=4)
        for t0 in range(NTILE):
            P.dma(Vv[:, t0, :, 0:64], vsrc[:, t0, :, :], w=[VC.b])
        NB = A.f32(4 * 3200)
        NBv = NB.ap.rearrange("p (h a j q) -> p h a j q", h=4, a=5, j=5)
        for h in range(4):
            P.dma(NB.ap[:, h * 3200:(h + 1) * 3200], self.nab[l, h], w=[NB.b])
        sA = [A.f32(640), A.f32(640)]
        pA = [A.bf16(896), A.bf16(896)]
        rz = A.f32(128)
        ys = [A.bf16(256), A.bf16(256)]
        psA = self.ps[0:4]
        psO = self.ps[4:6]
        scale = 64.0 ** -0.5
        def geom(qt):
            if qt < 2:
                return [0, 1], 0, False
            r0 = 2 * (qt - 2)
            base = min(max(r0 - 4, 0), 54)
            return [2 + base // 2 + j for j in range(5)] + [0, 1], (r0 - base) // 2, True

        items = [(qt, h) for qt in range(NTILE) for h in range(4)]

        def stA(n):
            qt, h = items[n]
            kts, pat, lat = geom(qt)
            t, pr = h // 2, (h % 2) * 64
            pa, pb = psA[2 * (n % 2)], psA[2 * (n % 2) + 1]
            qop = Qv[pr:pr + 64, t, qt * 128:(qt + 1) * 128]
            for j, kt in enumerate(kts):
                dst = pa.ap[:, j * 128:(j + 1) * 128] if j < 4 else pb.ap[:, (j - 4) * 128:(j - 3) * 128]
                P.mm(dst, Kv[pr:pr + 64, t, kt * 128:(kt + 1) * 128], qop, True, True, [KCt.b, QCt.b], [pa.b if j < 4 else pb.b])

        def stB(n):
            qt, h = items[n]
            kts, pat, lat = geom(qt)
            t, pr = h // 2, (h % 2) * 64
            pa, pb = psA[2 * (n % 2)], psA[2 * (n % 2) + 1]
            s_, p_, po = sA[n % 2], pA[n % 2], psO[n % 2]
            yst = ys[qt % 2]
            ysv = v3(yst.ap, 2)
            nk = len(kts)
            if lat:
                P.stt("dve", s_.ap[:, 0:512], pa.ap[:, 0:512], scale, NBv[:, h, pat, 0:4, :].rearrange("p j q -> p (j q)"), ALU.mult, ALU.add,
                      [pa.b, NB.b], [s_.b])
                P.stt("dve", s_.ap[:, 512:640], pb.ap[:, 0:128], scale, NBv[:, h, pat, 4, :], ALU.mult, ALU.add,
                      [pb.b, NB.b], [s_.b])
                P.act(p_.ap[:, 0:640], s_.ap[:, 0:640], AF.Exp, [s_.b], [p_.b])
                P.act(p_.ap[:, 640:896], pb.ap[:, 128:384], AF.Exp, [pb.b], [p_.b], scale=scale)
            else:
                P.act(p_.ap[:, 0:256], pa.ap[:, 0:256], AF.Exp, [pa.b], [p_.b], scale=scale)
            for j, kt in enumerate(kts):
                P.mm(po.ap[:, 0:128], Vv[:, kt, h, :], p_.ap[:, j * 128:(j + 1) * 128], j == 0, j == nk - 1, [VC.b, p_.b], [po.b])
            P.op("dve", lambda e, po=po: e.reciprocal(out=rz.ap[0:64, :], in_=po.ap[64:128, 0:128]), [po.b], [rz.b])
            P.tt("dve", ysv[pr:pr + 64, t, :], po.ap[0:64, 0:128], rz.ap[0:64, :], ALU.mult, [po.b, rz.b], [yst.b])
            if h == 3:
                P.dma(self.yT[768:1024, qt * 128:(qt + 1) * 128].rearrange("(t p) n -> p t n", p=128), ysv, r=[yst.b])

        stA(0)
        for n in range(len(items)):
            if n + 1 < len(items):
                stA(n + 1)
            stB(n)
        P.barrier()
        A.release(m0)

    def gmk(self, i, ncol=128):
        return self.gm.ap[:, i * 128:i * 128 + ncol]

    def phase_b1(self, l, banks=None):
        P, A = self.P, self.A
        m0 = A.mark()
        xb = [A.f32(516) for _ in range(3)]
        acc = [A.f32(512) for _ in range(2)]
        sl = [A.f32(512) for _ in range(3)]
        sqs = [A.f32(512) for _ in range(2)]
        rs = A.f32(512)
        ob = [A.f32(512) for _ in range(2)]
        ps = banks if banks is not None else self.ps[0:2]
        nb_ = len(ps)
        onesblk = self.gmk(1)
        work = [(ft, n0, N) for ft in range(9) for (n0, N) in GROUPS]

        def stage1(it):
            ft, n0, N = work[it]
            s0, s1 = (0, NCTX) if n0 < NCTX else (NCTX, NT)
            x, a, sv, sq, p_ = xb[it % 3], acc[it % 2], sl[it % 3], sqs[it % 2], ps[it % nb_]
            lo, hi = max(n0 - 2, s0), min(n0 + N + 2, s1)
            if lo > n0 - 2:
                P.memset("pool", x.ap[:, 0:2], 0.0, [x.b])
            if hi < n0 + N + 2:
                P.memset("pool", x.ap[:, N + 2:N + 4], 0.0, [x.b])
            P.dma(x.ap[:, lo - (n0 - 2):hi - (n0 - 2)], self.qkvb[ft * 128:(ft + 1) * 128, lo:hi], w=[x.b])
            co = l * 45 + ft * 5
            P.ts("dve", a.ap[:, 0:N], x.ap[:, 0:N], self.cw.ap[:, co:co + 1], ALU.mult, [x.b, self.cw.b], [a.b])
            for j in range(1, 5):
                P.stt("dve", a.ap[:, 0:N], x.ap[:, j:j + N], self.cw.ap[:, co + j:co + j + 1], a.ap[:, 0:N], ALU.mult, ALU.add,
                      [x.b, self.cw.b, a.b], [a.b])
            P.act(sv.ap[:, 0:N], a.ap[:, 0:N], AF.Silu, [a.b], [sv.b])
            if ft < 6:
                P.tt("pool", sq.ap[:, 0:N], sv.ap[:, 0:N], sv.ap[:, 0:N], ALU.mult, [sv.b], [sq.b])
                P.mm(p_.ap[:, 0:N], onesblk, sq.ap[:, 0:N], True, True, [self.gm.b, sq.b], [p_.b])

        def stage2(it):
            ft, n0, N = work[it]
            sv, p_, o = sl[it % 3], ps[it % nb_], ob[it % 2]
            if ft < 6:
                P.act(rs.ap[:, 0:N], p_.ap[:, 0:N], AF.Ln, [p_.b], [rs.b], bias=EPS, scale=1.0)
                P.act(rs.ap[:, 0:N], rs.ap[:, 0:N], AF.Exp, [rs.b], [rs.b], scale=-0.5)
                P.stt("dve", o.ap[:, 0:N], sv.ap[:, 0:N], 0.125 if ft < 3 else 1.0, rs.ap[:, 0:N], ALU.mult, ALU.mult, [sv.b, rs.b], [o.b])
                P.dma(self.qkn[ft * 128:(ft + 1) * 128, n0:n0 + N], o.ap[:, 0:N], r=[o.b])
            else:
                P.dma(self.qkn[ft * 128:(ft + 1) * 128, n0:n0 + N], sv.ap[:, 0:N], r=[sv.b])

        stage1(0)
        for it in range(len(work)):
            if it + 1 < len(work) and nb_ > 1:
                stage1(it + 1)
            stage2(it)
            if it + 1 < len(work) and nb_ == 1:
                stage1(it + 1)
            yield
        P.barrier()
        A.release(m0)

    def phase_b2(self, l, d):
        P, A = self.P, self.A
        m0 = A.mark()
        ident = self.gmk(0)
        A_d, B_d, MS_d, MIT_d, SelEnd = self.gmk(2 + d), self.gmk(4 + d), self.gmk(6 + d), self.gmk(8 + d), self.gmk(10 + d)
        gmb = self.gm.b
        f = A.f32
        gabt = [f(408), f(408), f(408)]
        fmt = [f(9 * 128), f(9 * 128)]
        oft = [f(384), f(384), f(384)]
        SC = [[f(6) for _ in range(11)] + [f(12)] + [f(384) for _ in range(5)] for _ in range(2)]
        Ag, Bg, dg, Es, EiT, aqkT, TT, wT, qhT = [[f(384), f(384)] for _ in range(9)]
        Pm = [[f(384), f(384)], [f(384), f(384)]]
        PTm = [[f(384), f(384)], [f(384), f(384)]]
        u = [f(192), f(192)]
        vnew = [f(192), f(192)]
        S = [f(192), f(192)]
        otile = [f(384), f(384)]
        sq = f(384)
        ss = f(6)
        rs = f(6)
        sg = f(384)
        ybf = A.bf16(384)
        yo = f(384)
        for hb in range(2):
            P.memset("pool", S[hb].ap[0:64, :], 0.0, [S[hb].b])
        QB = [self.ps[0:4], self.ps[4:8]]
        fmbs = [A.bf16(768), A.bf16(768)]
        tmpos = [f(192), f(192)]
        otbs = [[T(o_.ap[:, 0:192]), T(o_.ap[:, 192:384])] for o_ in otile]
        ident16, MS16, MIT16 = self.gm16.ap[:, 0:128], self.gm16.ap[:, (1 + d) * 128:(2 + d) * 128], self.gm16.ap[:, (3 + d) * 128:(4 + d) * 128]
        psK = self.ps[0]
        kcorn = [sub(self.ps[j_], self.ps[j_].ap[:, 384:512]) for j_ in (0, 2, 3)]
        vcorn = [sub(self.ps[j_], self.ps[j_].ap[:, 384:512]) for j_ in (4, 5, 6)]
        ycorn = sub(self.ps[7], self.ps[7].ap[:, 384:512])
        psSc = sub(self.ps[1], self.ps[1].ap[:, 384:512])
        dtb = self.gp.ap[:, L * 12 + l * 12 + d * 6:L * 12 + l * 12 + d * 6 + 6]
        nea = self.nea.ap[:, l * 12 + d * 6:l * 12 + d * 6 + 6]
        gw = self.gp.ap[:, L * 24 + l * 64:L * 24 + (l + 1) * 64]
        qkn3 = self.qkn.rearrange("(f p) n -> p f n", p=128)
        order = list(range(NTILE)) if d == 0 else [1, 0] + list(range(NTILE - 1, 1, -1))
        chunks = (0, 1) if d == 0 else (1, 0)
        if getattr(self, "b2_tiles", None):
            order = order[:self.b2_tiles]

        def load(i):
            tt = order[i]
            tok0 = tt * 128
            P.dma(gabt[i % 3].ap, self.gab[tok0:tok0 + 128, :], w=[gabt[i % 3].b])
            P.dma(v3(fmt[i % 2].ap, 9), qkn3[:, :, tok0:tok0 + 128], w=[fmt[i % 2].b])
            if d == 1:
                P.dma(oft[i % 3].ap, self.of[tok0:tok0 + 128, :], w=[oft[i % 3].b])

        bc3 = lambda ap, n: ap.unsqueeze(2).to_broadcast([128, ap.shape[1], n])
        mb3 = lambda ap, h: ap.unsqueeze(1).to_broadcast([128, h, 128])
        def common(i):
            par = i % 2
            z, e_, g, eb, beta, nbeta, gc, eg, dgl, ek, bg, egl, ktm, vtm, ktail, kbg, vb = SC[par]
            ga, fm, ot, fmb = gabt[i % 3], fmt[par], otile[par], fmbs[par]
            return (z, e_, g, eb, beta, nbeta, gc, eg, dgl, ek, bg, egl, ktm, vtm, ktail, kbg, vb, ga, fm, v3(fm.ap, 9), ot, fmb, v3(fmb.ap, 6))

        def prep(i):
            load(i)
            z, e_, g, eb, beta, nbeta, gc, eg, dgl, ek, bg, egl, ktm, vtm, ktail, kbg, vb, ga, fm, fmv, ot, fmb, fbv = common(i)
            P.tt("dve", z.ap, ga.ap[:, 384 + 6 * d:390 + 6 * d], dtb, ALU.add, [ga.b, self.gp.b], [z.b])
            yield
            P.act(e_.ap, z.ap, AF.Exp, [z.b], [e_.b])
            yield
            P.act(e_.ap, e_.ap, AF.Ln, [e_.b], [e_.b], bias=1.0)
            yield
            P.tt("dve", g.ap, e_.ap, nea, ALU.mult, [e_.b, self.nea.b], [g.b])
            yield
            P.act(eb.ap, ga.ap[:, 396 + 6 * d:402 + 6 * d], AF.Exp, [ga.b], [eb.b], scale=-1.0)
            yield
            P.ts("dve", eb.ap, eb.ap, 1.0, ALU.add, [eb.b], [eb.b])
            yield
            P.op("dve", lambda e: e.reciprocal(out=beta.ap, in_=eb.ap), [eb.b], [beta.b])
            yield
            P.ts("dve", nbeta.ap, beta.ap, -1.0, ALU.mult, [beta.b], [nbeta.b])
            yield
            P.mm(psSc.ap[:, 0:6], A_d, g.ap, True, True, [gmb, g.b], [psSc.b])
            yield
            P.copy("dve", gc.ap, psSc.ap[:, 0:6], [psSc.b], [gc.b])
            yield
            P.act(eg.ap, gc.ap, AF.Exp, [gc.b], [eg.b])
            yield
            P.mm(psSc.ap[:, 8:14], SelEnd, gc.ap, True, True, [gmb, gc.b], [psSc.b])
            yield
            P.tt("dve", dgl.ap, psSc.ap[:, 8:14], gc.ap, ALU.subtract, [psSc.b, gc.b], [dgl.b])
            yield
            P.act(ek.ap, dgl.ap, AF.Exp, [dgl.b], [ek.b])
            yield
            P.tt("dve", bg.ap, beta.ap, eg.ap, ALU.mult, [beta.b, eg.b], [bg.b])
            yield
            for c in range(2):
                P.mm(psSc.ap[0:64, 16 + c * 6:22 + c * 6], self.gmk(12 + 2 * d + c, 64), gc.ap, True, True, [gmb, gc.b], [psSc.b])
                yield
            P.act(egl.ap[0:64, :], psSc.ap[0:64, 16:28], AF.Exp, [psSc.b], [egl.b])
            yield
            for j in range(3):
                cn = kcorn[j]
                P.tr(cn.ap, fmv[:, 3 + j, :], ident, [fm.b, gmb], [cn.b])
                yield
                P.copy("act", ktm.ap[:, j * 128:(j + 1) * 128], cn.ap, [cn.b], [ktm.b])
                yield
            for j in range(3):
                cn = vcorn[j]
                P.tr(cn.ap, fmv[:, 6 + j, :], ident, [fm.b, gmb], [cn.b])
                yield
                P.copy("dve", vtm.ap[:, j * 128:(j + 1) * 128], cn.ap, [cn.b], [vtm.b])
                yield
            k3, v3_ = v3(ktm.ap, 6), v3(vtm.ap, 6)
            P.tt("pool", v3(ktail.ap, 6), k3, bc3(ek.ap, 64), ALU.mult, [ktm.b, ek.b], [ktail.b])
            yield
            P.tt("pool", v3(kbg.ap, 6), k3, bc3(bg.ap, 64), ALU.mult, [ktm.b, bg.b], [kbg.b])
            yield
            P.tt("pool", v3(vb.ap, 6), v3_, bc3(beta.ap, 64), ALU.mult, [vtm.b, beta.b], [vb.b])
            yield
            P.copy("pool", fmb.ap, fm.ap[:, 0:768], [fm.b], [fmb.b])
            yield

        def body(i, nxt, prevout):
            tt = order[i]
            tok0 = tt * 128
            z, e_, g, eb, beta, nbeta, gc, eg, dgl, ek, bg, egl, ktm, vtm, ktail, kbg, vb, ga, fm, fmv, ot, fmb, fbv = common(i)
            def batch(hb, fm=fm, fmv=fmv, fbv=fbv, fmb=fmb, i=i):
                Q0, Q1, Q2, Q3 = QB[hb]
                H = [3 * hb, 3 * hb + 1, 3 * hb + 2]
                hs = slice(3 * hb, 3 * hb + 3)
                ag, bgm, dgm, es, eit, aq, tt_, w_, qh = Ag[hb], Bg[hb], dg[hb], Es[hb], EiT[hb], aqkT[hb], TT[hb], wT[hb], qhT[hb]
                otb = otbs[i % 2][hb]
                tmpo = tmpos[hb]
                P.tt("pool", v3(ag.ap, 3), mb3(A_d, 3), bc3(g.ap[:, hs], 128), ALU.mult, [gmb, g.b], [ag.b])
                P.tt("pool", v3(bgm.ap, 3), mb3(B_d, 3), bc3(g.ap[:, hs], 128), ALU.mult, [gmb, g.b], [bgm.b])
                P.tt("pool", v3(dgm.ap, 3), mb3(ident, 3), bc3(eg.ap[:, hs], 128), ALU.mult, [gmb, eg.b], [dgm.b])
                for k, h in enumerate(H):
                    hp, hr = h // 2, (h % 2) * 64
                    kTb = fbv[hr:hr + 64, 3 + hp, :]
                    qTb = fbv[hr:hr + 64, hp, :]
                    ks = slice(k * 128, (k + 1) * 128)
                    P.mm(Q0.ap[:, ks], kTb, kTb, True, True, [fmb.b], [Q0.b])
                    P.mm(Q1.ap[:, ks], kTb, qTb, True, True, [fmb.b], [Q1.b])
                    P.mm(Q2.ap[:, ks], ag.ap[:, ks], B_d, True, False, [ag.b, gmb], [Q2.b])
                    P.mm(Q2.ap[:, ks], ident16, MS16, False, True, [self.gm16.b], [Q2.b])
                    P.mm(Q3.ap[:, ks], bgm.ap[:, ks], A_d, True, False, [bgm.b, gmb], [Q3.b])
                    P.mm(Q3.ap[:, ks], ident16, MIT16, False, True, [self.gm16.b], [Q3.b])
                yield
                P.act(es.ap, Q2.ap[:, 0:384], AF.Exp, [Q2.b], [es.b])
                P.act(eit.ap, Q3.ap[:, 0:384], AF.Exp, [Q3.b], [eit.b])
                p0, pt0 = Pm[hb][0], PTm[hb][0]
                for k, h in enumerate(H):
                    ks = slice(k * 128, (k + 1) * 128)
                    P.stt("dve", p0.ap[:, ks], Q0.ap[:, ks], nbeta.ap[:, h:h + 1], es.ap[:, ks], ALU.mult, ALU.mult, [Q0.b, es.b, nbeta.b], [p0.b])
                P.tt("dve", aq.ap, Q1.ap[:, 0:384], eit.ap, ALU.mult, [Q1.b, eit.b], [aq.b])
                yield
                for k in range(3):
                    ks = slice(k * 128, (k + 1) * 128)
                    P.tr(Q0.ap[:, ks], p0.ap[:, ks], ident, [p0.b, gmb], [Q0.b])
                yield
                P.copy("act", pt0.ap, Q0.ap[:, 0:384], [Q0.b], [pt0.b])
                for k in range(3):
                    ks = slice(k * 128, (k + 1) * 128)
                    P.tt("dve", tt_.ap[:, ks], Q0.ap[:, ks], ident, ALU.add, [Q0.b, gmb], [tt_.b])
                pc, ptc = p0, pt0
                yield
                for lvl in range(5):
                    pn, ptn = Pm[hb][(lvl + 1) % 2], PTm[hb][(lvl + 1) % 2]
                    for k in range(3):
                        ks = slice(k * 128, (k + 1) * 128)
                        P.mm(Q1.ap[:, ks], ptc.ap[:, ks], pc.ap[:, ks], True, True, [ptc.b, pc.b], [Q1.b])
                    if lvl < 4:
                        for k in range(3):
                            ks = slice(k * 128, (k + 1) * 128)
                            P.mm(Q2.ap[:, ks], pc.ap[:, ks], ptc.ap[:, ks], True, True, [ptc.b, pc.b], [Q2.b])
                    yield
                    P.copy("act", pn.ap, Q1.ap[:, 0:384], [Q1.b], [pn.b])
                    if lvl < 4:
                        P.copy("dve", ptn.ap, Q2.ap[:, 0:384], [Q2.b], [ptn.b])
                    yield
                    for k in range(3):
                        ks = slice(k * 128, (k + 1) * 128)
                        P.mm(Q3.ap[:, ks], pn.ap[:, ks], tt_.ap[:, ks], True, True, [pn.b, tt_.b], [Q3.b])
                    yield
                    P.tt("dve", tt_.ap, tt_.ap, Q3.ap[:, 0:384], ALU.add, [tt_.b, Q3.b], [tt_.b])
                    pc, ptc = pn, ptn
                    yield
                for k, h in enumerate(H):
                    hp, hr = h // 2, (h % 2) * 64
                    ks = slice(k * 128, (k + 1) * 128)
                    P.mm(Q0.ap[:, k * 64:(k + 1) * 64], tt_.ap[:, ks], vb.ap[:, h * 64:(h + 1) * 64], True, True, [tt_.b, vb.b], [Q0.b])
                    P.mm(Q1.ap[0:64, ks], kbg.ap[:, h * 64:(h + 1) * 64], tt_.ap[:, ks], True, True, [tt_.b, kbg.b], [Q1.b])
                    P.mm(Q2.ap[0:64, ks], self.ones.ap[:, 0:64], dgm.ap[:, ks], True, True, [self.ones.b, dgm.b], [Q2.b])
                yield
                P.copy("act", u[hb].ap, Q0.ap[:, 0:192], [Q0.b], [u[hb].b])
                P.copy("dve", w_.ap[0:64, :], Q1.ap[0:64, 0:384], [Q1.b], [w_.b])
                for k, h in enumerate(H):
                    hp, hr = h // 2, (h % 2) * 64
                    ks = slice(k * 128, (k + 1) * 128)
                    P.tt("dve", qh.ap[0:64, ks], fmv[hr:hr + 64, hp, :], Q2.ap[0:64, ks], ALU.mult, [fm.b, Q2.b], [qh.b])
                yield
                Sb = S[hb]
                vn = vnew[hb]
                for c in chunks:
                    cs = c * 64
                    for k, h in enumerate(H):
                        ks = slice(k * 128, (k + 1) * 128)
                        P.mm(Q3.ap[:, k * 64:(k + 1) * 64], w_.ap[0:64, ks], Sb.ap[0:64, k * 64:(k + 1) * 64], True, True, [w_.b, Sb.b], [Q3.b])
                    yield
                    P.tt("dve", vn.ap[cs:cs + 64, :], u[hb].ap[cs:cs + 64, :], Q3.ap[cs:cs + 64, 0:192], ALU.subtract, [u[hb].b, Q3.b], [vn.b])
                    yield
                    for k, h in enumerate(H):
                        ks = slice(k * 128, (k + 1) * 128)
                        k6 = slice(k * 64, (k + 1) * 64)
                        P.mm(Q0.ap[:, k6], qh.ap[0:64, ks], Sb.ap[0:64, k6], True, True, [qh.b, Sb.b], [Q0.b])
                        P.mm(Q1.ap[:, k6], aq.ap[cs:cs + 64, ks], vn.ap[cs:cs + 64, k6], True, True, [aq.b, vn.b], [Q1.b])
                        P.mm(Q2.ap[0:64, k6], ktail.ap[cs:cs + 64, h * 64:(h + 1) * 64], vn.ap[cs:cs + 64, k6], True, True, [ktail.b, vn.b], [Q2.b])
                    yield
                    P.copy("act", tmpo.ap[cs:cs + 64, :], Q1.ap[cs:cs + 64, 0:192], [Q1.b], [tmpo.b])
                    P.tt("dve", otb.ap[cs:cs + 64, :], Q0.ap[cs:cs + 64, 0:192], tmpo.ap[cs:cs + 64, :], ALU.add,
                         [Q0.b, tmpo.b], [otb.b])
                    for k, h in enumerate(H):
                        k6 = slice(k * 64, (k + 1) * 64)
                        P.stt("dve", Sb.ap[0:64, k6], Sb.ap[0:64, k6], egl.ap[0:64, c * 6 + h:c * 6 + h + 1], Q2.ap[0:64, k6], ALU.mult, ALU.add,
                              [Sb.b, egl.b, Q2.b], [Sb.b])
                    yield

            gens = [batch(0), batch(1)] + ([nxt] if nxt is not None else []) + ([prevout] if prevout is not None else [])
            for _ in itertools.zip_longest(*gens):
                pass
            return

        def outgen(i):
            tt = order[i]
            tok0 = tt * 128
            ga, ot = gabt[i % 3], otile[i % 2]
            ot_r = [otbs[i % 2][0].b, otbs[i % 2][1].b]
            if d == 0:
                P.dma(self.of[tok0:tok0 + 128, :], ot.ap, r=ot_r)
                yield
                return
            of_ = oft[i % 3]
            P.tt("pool", sg.ap, ot.ap, of_.ap, ALU.add, ot_r + [of_.b], [sg.b])
            yield
            P.tt("pool", sq.ap, sg.ap, sg.ap, ALU.mult, [sg.b], [sq.b])
            yield
            P.op("dve", lambda e: e.tensor_reduce(out=ss.ap, in_=v3(sq.ap, 6), axis=AX.X, op=ALU.add), [sq.b], [ss.b])
            yield
            P.act(rs.ap, ss.ap, AF.Ln, [ss.b], [rs.b], bias=EPS, scale=1.0 / 64)
            yield
            P.act(rs.ap, rs.ap, AF.Exp, [rs.b], [rs.b], scale=-0.5)
            yield
            P.tt("dve", v3(sq.ap, 6), v3(sg.ap, 6), bc3(rs.ap, 64), ALU.mult, [sg.b, rs.b], [sq.b])
            yield
            P.tt("pool", v3(yo.ap, 6), v3(sq.ap, 6), gw.unsqueeze(1).to_broadcast([128, 6, 64]), ALU.mult, [sq.b, self.gp.b], [yo.b])
            yield
            P.act(sg.ap, ga.ap[:, 0:384], AF.Silu, [ga.b], [sg.b])
            yield
            P.tt("dve", sq.ap, yo.ap, sg.ap, ALU.mult, [yo.b, sg.b], [sq.b])
            yield
            for j in range(3):
                P.tr(ycorn.ap, sq.ap[:, j * 128:(j + 1) * 128], ident, [sq.b, gmb], [ycorn.b])
                yield
                P.copy("act", ybf.ap[:, j * 128:(j + 1) * 128], ycorn.ap, [ycorn.b], [ybf.b])
                yield
            P.dma(self.yT[384:768, tok0:tok0 + 128].rearrange("(j p) n -> p j n", p=128), v3(ybf.ap, 3), r=[ybf.b])
            yield

        for _ in prep(0):
            pass
        for i in range(len(order)):
            body(i, prep(i + 1) if i + 1 < len(order) else None, outgen(i - 1) if i >= 1 else None)
        for _ in outgen(len(order) - 1):
            pass
        P.barrier()
        A.release(m0)

    def phase_out(self, l):
        P, A = self.P, self.A
        m0 = A.mark()
        last = (l == self.nlayers - 1) and self.nlayers == L
        Wo = A.bf16(8 * D)
        W1 = A.bf16(8 * 2 * DFF)
        W2 = A.bf16(22 * D)
        N = 256
        actb = A.bf16(22 * N)
        stg = [T(actb.ap.bitcast(F32)[:, 0:1280]), T(actb.ap.bitcast(F32)[:, 1280:2560])]
        self.load_w_bf16(Wo, self.w_out[l].rearrange("(kc p) n -> p kc n", p=128), 8, D, stg, piece=160)
        self.load_w_bf16(W1, self.w_f1[l].rearrange("(kc p) n -> p kc n", p=128), 8, 2 * DFF, stg, piece=160)
        self.load_w_bf16(W2, self.w_f2[l].rearrange("(kc p) n -> p kc n", p=128), 22, D, stg, piece=58)
        P.barrier()
        Wov, W1v, W2v = v3(Wo.ap, 8), v3(W1.ap, 8), v3(W2.ap, 22)
        xgs = [A.f32(8 * N), A.f32(8 * N)]
        yg = [A.bf16(8 * N), A.bf16(8 * N)]
        hb = A.bf16(8 * N)
        sq = A.f32(N)
        rstd = A.f32(N)
        sg = A.f32(N)
        xsrc3 = (self.xT if l == 0 else self.xs).rearrange("(kc p) n -> p kc n", p=128)
        xdst3 = self.xs.rearrange("(kc p) n -> p kc n", p=128)
        y3 = self.yT.rearrange("(kc p) n -> p kc n", p=128)
        o3 = self.outT.rearrange("(kc p) n -> p kc n", p=128)
        ps_ss = self.ps[0]
        psr = self.ps[1:4]
        psg = self.ps[4:6]
        psu = self.ps[6:8]
        ngr = NT // N
        hv = v3(hb.ap, 8)
        av = v3(actb.ap, 22)
        cnt = {"nr": 0}

        def loady(gi):
            P.dma(v3(yg[gi % 2].ap, 8), y3[:, :, gi * N:(gi + 1) * N], w=[yg[gi % 2].b])

        def ffn_out(gi):
            n0 = gi * N
            s = 1 if gi == 0 else 0
            xg = xgs[gi % 2]
            xv = v3(xg.ap, 8)
            for dt in range(8):
                ps = psr[cnt["nr"] % 3]
                cnt["nr"] += 1
                for ft in range(22):
                    P.mm(ps.ap[:, 0:N], W2v[:, ft, dt * 128:(dt + 1) * 128], av[:, ft, :], ft == 0, ft == 21, [W2.b, actb.b], [ps.b])
                P.stt("dve", xv[:, dt, :], ps.ap[:, 0:N], self.modv(l, 5, dt, s), xv[:, dt, :], ALU.mult, ALU.add, [ps.b, self.mod.b, xg.b], [xg.b])
            if not last:
                P.dma(xdst3[:, :, n0:n0 + N], xv, r=[xg.b])
            elif gi >= 1:
                for kc in range(8):
                    P.act(sq.ap, xv[:, kc, :], AF.Square, [xg.b], [sq.b])
                    P.mm(ps_ss.ap[:, 0:N], self.ones.ap, sq.ap, kc == 0, kc == 7, [self.ones.b, sq.b], [ps_ss.b])
                P.act(rstd.ap, ps_ss.ap[:, 0:N], AF.Ln, [ps_ss.b], [rstd.b], bias=EPS, scale=1.0 / D)
                P.act(rstd.ap, rstd.ap, AF.Exp, [rstd.b], [rstd.b], scale=-0.5)
                for kc in range(8):
                    P.stt("dve", xv[:, kc, :], xv[:, kc, :], self.fn.ap[:, kc:kc + 1], rstd.ap, ALU.mult, ALU.mult, [xg.b, self.fn.b, rstd.b], [xg.b])
                P.dma(o3[:, :, n0 - NCTX:n0 - NCTX + N], xv, r=[xg.b])

        loady(0)
        for gi in range(ngr):
            n0 = gi * N
            s = 1 if gi == 0 else 0
            if gi + 1 < ngr:
                loady(gi + 1)
            xg = xgs[gi % 2]
            xv = v3(xg.ap, 8)
            P.dma(xv, xsrc3[:, :, n0:n0 + N], w=[xg.b])
            y = yg[gi % 2]
            yv = v3(y.ap, 8)
            for dt in range(8):
                ps = psr[cnt["nr"] % 3]
                cnt["nr"] += 1
                for kc in range(8):
                    P.mm(ps.ap[:, 0:N], Wov[:, kc, dt * 128:(dt + 1) * 128], yv[:, kc, :], kc == 0, kc == 7, [Wo.b, y.b], [ps.b])
                P.stt("dve", xv[:, dt, :], ps.ap[:, 0:N], self.modv(l, 2, dt, s), xv[:, dt, :], ALU.mult, ALU.add, [ps.b, self.mod.b, xg.b], [xg.b])
            if gi >= 1:
                ffn_out(gi - 1)
            self.norm_mod(xg, N, l, self.a2, 3, s, hb, sq, rstd, ps_ss)
            for ft in range(22):
                pg, pu = psg[ft % 2], psu[ft % 2]
                for kc in range(8):
                    P.mm(pg.ap[:, 0:N], W1v[:, kc, ft * 128:(ft + 1) * 128], hv[:, kc, :], kc == 0, kc == 7, [W1.b, hb.b], [pg.b])
                for kc in range(8):
                    P.mm(pu.ap[:, 0:N], W1v[:, kc, DFF + ft * 128:DFF + (ft + 1) * 128], hv[:, kc, :], kc == 0, kc == 7, [W1.b, hb.b], [pu.b])
                P.act(sg.ap, pg.ap[:, 0:N], AF.Silu, [pg.b], [sg.b])
                P.tt("dve", av[:, ft, :], pu.ap[:, 0:N], sg.ap, ALU.mult, [pu.b, sg.b], [actb.b])
        ffn_out(ngr - 1)
        P.barrier()
        A.release(m0)

    def build(self):
        self.consts()
        self.phase_mod()
        ph = self.phases
        for l in range(self.nlayers):
            if ph is None or "in" in ph:
                self.phase_in(l)
            if ph is None or "a" in ph:
                for _ in self.phase_a(l):
                    pass
            if ph is None or "b1" in ph:
                for _ in self.phase_b1(l):
                    pass
            if ph is None or "c" in ph:
                self.phase_c(l)
            if ph is None or "b2" in ph:
                self.phase_b2(l, 0)
                self.phase_b2(l, 1)
            if ph is None or "out" in ph:
                self.phase_out(l)
        self.P.finish()
        self.st.close()
        return self.nc


def rope_tables():
    half = 16
    inv_freq = (1.0 / (10000.0 ** (np.arange(0, half, 2, dtype=np.float32) / np.float32(half)))).astype(np.float32)
    t = np.arange(TL, dtype=np.int32)
    ang_r = (t // 64).astype(np.float32)[:, None] * inv_freq
    ang_c = (t % 64).astype(np.float32)[:, None] * inv_freq
    ang = np.concatenate([ang_r, ang_r, ang_c, ang_c], axis=-1)
    cos = np.cos(ang).astype(np.float32)
    sin = np.sin(ang).astype(np.float32)
    sign = np.array([-1.0] * 8 + [1.0] * 8 + [-1.0] * 8 + [1.0] * 8, np.float32)
    tab = np.stack([cos.T, (sin * sign).T], axis=1)
    return np.ascontiguousarray(np.tile(tab, (4, 1, 1)))


def na_bias_tiles(nb):
    out = np.full((L, 4, 5, 5, 128, 128), -30000.0, np.float32)
    qi = np.arange(128)
    ki = np.arange(128)
    for pat, (r0, base) in enumerate(((0, 0), (2, 0), (4, 0), (60, 54), (62, 54))):
        r = r0 + qi // 64
        c = qi % 64
        rs = np.clip(r - 4, 0, 56)
        cs = np.clip(c - 8, 0, 48)
        for j in range(5):
            kr = base + 2 * j + ki // 64
            kc = ki % 64
            valid = ((kr[:, None] >= rs[None, :]) & (kr[:, None] < rs[None, :] + 8) &
                     (kc[:, None] >= cs[None, :]) & (kc[:, None] < cs[None, :] + 16))
            dr = np.clip(kr[:, None] - r[None, :] + 7, 0, 14)
            dc = np.clip(kc[:, None] - c[None, :] + 15, 0, 30)
            g = nb[:, :, dr, dc]
            out[:, :, pat, j] = np.where(valid[None, None], g, np.float32(-30000.0))
    return np.ascontiguousarray(out.transpose(0, 1, 4, 2, 3, 5).reshape(L, 4, 128, 3200))


def gdn_masks():
    m = np.zeros((16, 128, 128), np.float32)
    i = np.arange(128)
    same = (i[:, None] // 64) == (i[None, :] // 64)
    r, c = i[:, None], i[None, :]
    m[0] = np.eye(128)
    m[1] = same
    m[2] = same & (r <= c)
    m[3] = same & (r >= c)
    m[4] = same & (r > c)
    m[5] = same & (r < c)
    m[6] = np.where(same & (r > c), 0.0, -30000.0)
    m[7] = np.where(same & (r < c), 0.0, -30000.0)
    m[8] = np.where(same & (c >= r), 0.0, -30000.0)
    m[9] = np.where(same & (c <= r), 0.0, -30000.0)
    endf = (i // 64) * 64 + 63
    endb = (i // 64) * 64
    m[10] = (r == endf[None, :])
    m[11] = (r == endb[None, :])
    for d_, ends in enumerate(((63, 127), (0, 64))):
        for c_ in range(2):
            m[12 + 2 * d_ + c_][ends[c_], :] = 1.0
    return np.ascontiguousarray(m.transpose(1, 0, 2).reshape(128, 2048))


def host_prep(inp):
    f = lambda a: np.ascontiguousarray(a, dtype=np.float32)
    w_in = inp["w_in"]
    sizes = (1152, 1152, 384, 12, 12, 768)
    offs = np.cumsum((0,) + sizes)
    a0, b0, g0, al0, be0, c0 = offs[:6]
    perm32 = np.concatenate([np.arange(8, 16), np.arange(0, 8), np.arange(24, 32), np.arange(16, 24)])
    pad = lambda m: np.concatenate([m.reshape(4, 96), m.reshape(4, 96)[:, :32]], axis=1).reshape(-1)
    idq = pad(np.arange(384))
    permq = pad((np.arange(384).reshape(12, 32)[:, perm32]).reshape(-1))
    cols = np.concatenate([
        a0 + idq, a0 + permq, a0 + 384 + idq, a0 + 384 + permq,
        b0 + np.arange(1152), c0 + np.arange(256), c0 + 256 + np.arange(256),
        a0 + 768 + np.arange(384), c0 + 512 + np.arange(256), g0 + np.arange(384), al0 + np.arange(12), be0 + np.arange(12)])
    assert cols.shape[0] == NW
    shared = {
        "w_mod": f(inp["w_mod"]),
        "bmodT": f(inp["b_mod"].reshape(L, 48, 128).transpose(2, 0, 1).reshape(128, L * 48)),
        "n1T": f(inp["norm1_w"].reshape(L, 8, 128).transpose(2, 0, 1).reshape(128, L * 8)),
        "n2T": f(inp["norm2_w"].reshape(L, 8, 128).transpose(2, 0, 1).reshape(128, L * 8)),
        "fnT": f(inp["final_norm_w"].reshape(8, 128).T),
        "w_in": f(w_in[:, :, cols]),
        "rope": rope_tables(),
        "lamp": f(np.stack([inp["lambda_q1"], inp["lambda_k1"], inp["lambda_q2"], inp["lambda_k2"]], axis=1).reshape(1, L * 128)),
        "dnT": f(np.tile(inp["diff_norm_w"].T, (2, 1))),
        "nab": na_bias_tiles(inp["na_bias"]),
        "convT": f(inp["conv_w"].reshape(L, 5, 9, 128).transpose(3, 0, 2, 1).reshape(128, L * 45)),
        "gpar": f(np.concatenate([inp["a_log"].reshape(-1), inp["dt_bias"].reshape(-1), inp["gdn_norm_w"].reshape(-1)])[None, :]),
        "gmask": gdn_masks(),
        "w_out": f(inp["w_out"]),
        "w_f1": f(inp["w_ffn_in"]),
        "w_f2": f(inp["w_ffn_out"]),
    }
    per = []
    for b in range(4):
        xT = f(np.concatenate([inp["ctx"][b], inp["x"][b]], axis=0).T)
        cT = f(np.stack([inp["c"][b].reshape(8, 128).T, inp["c_ctx"].reshape(8, 128).T], axis=2).reshape(128, 16))
        m = dict(shared)
        m["xT"] = xT
        m["cT"] = cT
        per.append(m)
    return per


def kernel(**inputs):
    inp = {k: np.asarray(v) for k, v in inputs.items()}
    per = host_prep(inp)
    nc = K().build()
    in_maps = [per[i % 4] for i in range(8)]
    res = run_bass_kernel_spmd(nc, in_maps, core_ids=list(range(8)))
    out = np.stack([np.ascontiguousarray(res.results[b]["outT"].T) for b in range(4)], axis=0)
    return out.astype(np.float32)
```

```python
import contextlib
import itertools
import math
import numpy as np
import concourse.bass as bass
import concourse.mybir as mybir
from concourse.bass_utils import run_bass_kernel_spmd

F32 = mybir.dt.float32
BF16 = mybir.dt.bfloat16
AF = mybir.ActivationFunctionType
ALU = mybir.AluOpType
AX = mybir.AxisListType

NT, NCTX, TL, D, L = 4352, 256, 4096, 1024, 4
NTILE = NT // 128
DFF = 2816
EPS = 1e-6
C_QA, C_QAP, C_KA, C_KAP, C_B, C_QC, C_KC = 0, 512, 1024, 1536, 2048, 3200, 3456
NFM = 3712
C_VA, C_VC, C_G = 3712, 4096, 4352
NW = 4760
GROUPS = [(0, 256)] + [(256 + 512 * i, 512) for i in range(8)]


class Buf:
    __slots__ = ("w", "r", "g")

    def __init__(self):
        self.w = None
        self.r = {}
        self.g = None


class T:
    __slots__ = ("ap", "b")

    def __init__(self, ap):
        self.ap = ap
        self.b = Buf()


class Prog:
    ENG = ("pe", "act", "dve", "pool", "sp")

    def __init__(self, nc, stack, n_dma=48, same=True):
        self.nc = nc
        self.ops = {e: [] for e in self.ENG}
        self.cnt = {e: 0 for e in self.ENG}
        self.seen = {e: {} for e in self.ENG}
        self.esem = {e: stack.enter_context(nc.semaphore("s_" + e)) for e in self.ENG}
        self.dsem = [stack.enter_context(nc.semaphore("d%d" % i)) for i in range(n_dma)]
        self.dval = [0] * n_dma
        self.dnext = 0
        self.same = same

    def _wait(self, eng, tok):
        if tok is None:
            return
        key, val = tok
        if key == eng and (not self.same or eng == "pe"):
            return
        if self.seen[eng].get(key, 0) >= val:
            return
        self.seen[eng][key] = val
        sem = self.esem[key] if isinstance(key, str) else self.dsem[key]
        self.ops[eng].append(lambda e: e.wait_ge(sem, val))

    def _deps(self, eng, reads, writes):
        for b in reads:
            self._wait(eng, b.w)
        for b in writes:
            self._wait(eng, b.w)
            for t in list(b.r.values()):
                self._wait(eng, t)

    def _commit(self, tok, reads, writes):
        for b in reads:
            b.r[tok[0]] = tok
        for b in writes:
            b.w = tok
            b.r = {}

    def op(self, eng, fn, r=(), w=()):
        self._deps(eng, r, w)
        guards = [b.g for b in r if b.g is not None] if eng in ("act", "dve") else ()
        for g in guards:
            if g[0] is not None and g[1] != eng:
                self._wait(eng, g[0])
        self.cnt[eng] += 1
        tok = (eng, self.cnt[eng])
        sem = self.esem[eng]
        self.ops[eng].append(lambda e: fn(e).then_inc(sem, 1))
        self._commit(tok, r, w)
        for g in guards:
            g[0], g[1] = tok, eng

    def dma(self, out, in_, r=(), w=(), eng="sp"):
        k = self.dnext
        self.dnext = (self.dnext + 1) % len(self.dsem)
        if self.dval[k]:
            self._wait(eng, (k, self.dval[k]))
        self._deps(eng, r, w)
        self.dval[k] += 16
        tok = (k, self.dval[k])
        sem = self.dsem[k]
        self.ops[eng].append(lambda e: e.dma_start(out=out, in_=in_).then_inc(sem, 16))
        self._commit(tok, r, w)

    def mm(self, out, lhsT, rhs, start, stop, r, w):
        self.op("pe", lambda e: e.matmul(out, lhsT=lhsT, rhs=rhs, start=start, stop=stop), r, w)

    def filler(self, n, out, lhsT, rhs):
        for _ in range(n):
            self.op("pe", lambda e: e.matmul(out, lhsT=lhsT, rhs=rhs, start=True, stop=True), (), ())

    def tr(self, out, in_, ident, r, w):
        self.op("pe", lambda e: e.transpose(out, in_, ident), r, w)

    def act(self, out, in_, func, r, w, bias=None, scale=None, accum=None):
        kw = {}
        if bias is not None:
            kw["bias"] = bias
        if scale is not None:
            kw["scale"] = scale
        if accum is not None:
            kw["accum_out"] = accum
        self.op("act", lambda e: e.activation(out=out, in_=in_, func=func, **kw), r, w)

    def tt(self, eng, out, in0, in1, op, r, w):
        self.op(eng, lambda e: e.tensor_tensor(out=out, in0=in0, in1=in1, op=op), r, w)

    def ts(self, eng, out, in0, s1, op0, r, w, s2=None, op1=None):
        if op1 is None:
            self.op(eng, lambda e: e.tensor_scalar(out=out, in0=in0, scalar1=s1, scalar2=None, op0=op0), r, w)
        else:
            self.op(eng, lambda e: e.tensor_scalar(out=out, in0=in0, scalar1=s1, scalar2=s2, op0=op0, op1=op1), r, w)

    def stt(self, eng, out, in0, scalar, in1, op0, op1, r, w):
        self.op(eng, lambda e: e.scalar_tensor_tensor(out=out, in0=in0, scalar=scalar, in1=in1, op0=op0, op1=op1), r, w)

    def copy(self, eng, out, in_, r, w):
        if eng == "act":
            self.op("act", lambda e: e.activation(out=out, in_=in_, func=AF.Copy), r, w)
        else:
            self.op(eng, lambda e: e.tensor_copy(out=out, in_=in_), r, w)

    def memset(self, eng, out, val, w):
        self.op(eng, lambda e: e.memset(out, val), (), w)

    def barrier(self):
        for e in self.ENG:
            for f in self.ENG:
                if f != e and self.cnt[f]:
                    self._wait(e, (f, self.cnt[f]))
            for k, v in enumerate(self.dval):
                if v:
                    self._wait(e, (k, v))

    def finish(self):
        self.barrier()
        ops = self.ops
        with self.nc.Block() as block:
            @block.tensor
            def _(e):
                for f in ops["pe"]:
                    f(e)

            @block.scalar
            def _(e):
                for f in ops["act"]:
                    f(e)

            @block.vector
            def _(e):
                for f in ops["dve"]:
                    f(e)

            @block.gpsimd
            def _(e):
                for f in ops["pool"]:
                    f(e)

            @block.sync
            def _(e):
                for f in ops["sp"]:
                    f(e)


class Arena:
    def __init__(self, ap, n):
        self.ap, self.n, self.top = ap, n, 0

    def f32(self, n, shape=None):
        a = self.top
        self.top += n
        assert self.top <= self.n, ("SBUF arena overflow", self.top, self.n)
        ap = self.ap[:, a:a + n]
        return T(ap)

    def bf16(self, n):
        t = self.f32((n + 1) // 2)
        t.ap = t.ap.bitcast(BF16)
        return t

    def mark(self):
        return self.top

    def release(self, m):
        self.top = m


def sub(bank, ap):
    t = T(ap)
    t.b = bank.b
    return t


def v3(ap, a):
    return ap.rearrange("p (a b) -> p a b", a=a)


class K:
    def __init__(self, debug=False, nlayers=L, phases=None):
        self.debug = debug
        self.nlayers = nlayers
        self.phases = phases
        nc = self.nc = bass.Bass("TRN2", target_bir_lowering=False)
        self.st = contextlib.ExitStack()
        self.P = Prog(nc, self.st)
        ein = lambda n, s, dt=F32: nc.dram_tensor(n, list(s), dt, kind="ExternalInput").ap()
        self.xT = ein("xT", (D, NT))
        self.cT = ein("cT", (128, 16))
        self.w_mod = ein("w_mod", (L, D, 6 * D))
        self.bmodT = ein("bmodT", (128, L * 48))
        self.n1T = ein("n1T", (128, L * 8))
        self.n2T = ein("n2T", (128, L * 8))
        self.fnT = ein("fnT", (128, 8))
        self.w_in = ein("w_in", (L, D, NW))
        self.rope = ein("rope", (128, 2, TL))
        self.lamp = ein("lamp", (1, L * 128))
        self.dnT = ein("dnT", (128, L))
        self.nab = ein("nab", (L, 4, 128, 3200))
        self.convT = ein("convT", (128, L * 45))
        self.gpar = ein("gpar", (1, L * 24 + L * 64))
        self.gmask = ein("gmask", (128, 2048))
        self.w_out = ein("w_out", (L, D, D))
        self.w_f1 = ein("w_f1", (L, D, 2 * DFF))
        self.w_f2 = ein("w_f2", (L, DFF, D))
        sk = "ExternalOutput" if debug else "Internal"
        scr = lambda n, s, dt=F32: nc.dram_tensor(n, list(s), dt, kind=sk).ap()
        self.xs = scr("xs", (D, NT))
        self.qa = scr("qa", (512, NT), BF16)
        self.ka = scr("ka", (512, NT), BF16)
        self.va = scr("va", (NT, 384), BF16)
        self.qkvb = scr("qkvb", (1152, NT))
        self.gab = scr("gab", (NT, 408))
        self.qc = scr("qc", (256, NT), BF16)
        self.kc = scr("kc", (256, NT), BF16)
        self.vc = scr("vc", (NT, 256), BF16)
        self.yT = scr("yT", (D, NT), BF16)
        self.qkn = scr("qkn", (1152, NT))
        self.of = scr("of", (NT, 384))
        self.outT = nc.dram_tensor("outT", [D, TL], F32, kind="ExternalOutput").ap()
        self.dbufs = {}
        arena_ap = self.st.enter_context(nc.sbuf_tensor("arena", [128, 53200], F32))
        self.A = Arena(arena_ap, 53200)
        self.ps = [T(self.st.enter_context(nc.psum_tensor("ps%d" % i, [128, 512], F32))[:]) for i in range(8)]
        for t in self.ps:
            t.b.g = [None, None]

    def db(self, name):
        return Buf()

    def consts(self):
        P, A = self.P, self.A
        self.ones = A.f32(128)
        P.memset("pool", self.ones.ap, 1.0, [self.ones.b])
        self.cTt = A.f32(16)
        P.dma(self.cTt.ap, self.cT, w=[self.cTt.b])
        self.sc = A.f32(16)
        P.act(self.sc.ap, self.cTt.ap, AF.Silu, [self.cTt.b], [self.sc.b])
        self.mod = A.f32(L * 48 * 2)
        self.bm = A.f32(L * 48)
        P.dma(self.bm.ap, self.bmodT, w=[self.bm.b])
        self.n1 = A.f32(L * 8)
        self.n2 = A.f32(L * 8)
        self.fn = A.f32(8)
        P.dma(self.n1.ap, self.n1T, w=[self.n1.b])
        P.dma(self.n2.ap, self.n2T, w=[self.n2.b])
        P.dma(self.fn.ap, self.fnT, w=[self.fn.b])
        self.cw = A.f32(L * 45)
        P.dma(self.cw.ap, self.convT, w=[self.cw.b])
        self.gm = A.f32(2048)
        P.dma(self.gm.ap, self.gmask, w=[self.gm.b])
        self.gm16 = A.bf16(640)
        P.copy("pool", self.gm16.ap[:, 0:128], self.gm.ap[:, 0:128], [self.gm.b], [self.gm16.b])
        P.copy("pool", self.gm16.ap[:, 128:640], self.gm.ap[:, 768:1280], [self.gm.b], [self.gm16.b])
        gp = self.gp = A.f32(L * 24 + L * 64)
        P.dma(gp.ap, self.gpar.partition_broadcast(128), w=[gp.b])
        self.nea = A.f32(L * 12)
        P.act(self.nea.ap, gp.ap[:, 0:L * 12], AF.Exp, [gp.b], [self.nea.b])
        P.ts("dve", self.nea.ap, self.nea.ap, -1.0, ALU.mult, [self.nea.b], [self.nea.b])
        self.nlam = A.f32(L)
        self.dnl = A.f32(L)
        self.a1 = A.f32(L * 16)
        self.a2 = A.f32(L * 16)
        mtmp = A.mark()
        lp = A.f32(L * 128)
        P.dma(lp.ap, self.lamp.partition_broadcast(128), w=[lp.b])
        lpv = lp.ap.rearrange("p (l f d) -> p l f d", l=L, f=4)
        pr_ = A.f32(L * 64)
        prv = pr_.ap.rearrange("p (l f d) -> p l f d", l=L, f=2)
        P.tt("dve", prv, lpv[:, :, 0:4:2, :], lpv[:, :, 1:4:2, :], ALU.mult, [lp.b], [pr_.b])
        ee = A.f32(L * 2)
        P.op("dve", lambda e: e.tensor_reduce(out=ee.ap, in_=pr_.ap.rearrange("p (g d) -> p g d", d=32), axis=AX.X, op=ALU.add), [pr_.b], [ee.b])
        P.act(ee.ap, ee.ap, AF.Exp, [ee.b], [ee.b])
        dn = A.f32(L)
        P.dma(dn.ap, self.dnT, w=[dn.b])
        for l in range(L):
            li = 0.8 - 0.6 * math.exp(-0.3 * l)
            P.stt("dve", self.nlam.ap[:, l:l + 1], ee.ap[:, 2 * l + 1:2 * l + 2], -li, ee.ap[:, 2 * l:2 * l + 1], ALU.add, ALU.subtract,
                  [ee.b], [self.nlam.b])
            P.ts("dve", self.dnl.ap[:, l:l + 1], dn.ap[:, l:l + 1], 1.0 - li, ALU.mult, [dn.b], [self.dnl.b])
        P.barrier()
        A.release(mtmp)

    def phase_mod(self):
        P, A = self.P, self.A
        m0 = A.mark()
        wm = [A.f32(8 * 768), A.f32(8 * 768)]
        ps = self.ps[0]
        it = 0
        for l in range(self.nlayers):
            wl = self.w_mod[l].rearrange("(kc p) n -> p kc n", p=128)
            for j in range(8):
                w = wm[it % 2]
                it += 1
                wv = v3(w.ap, 8)
                P.dma(wv, wl[:, :, j * 768:(j + 1) * 768], w=[w.b])
                for ct in range(6):
                    for kc in range(8):
                        P.mm(ps.ap[:, ct * 2:ct * 2 + 2], wv[:, kc, ct * 128:(ct + 1) * 128],
                             self.sc.ap[:, kc * 2:kc * 2 + 2], kc == 0, kc == 7, [w.b, self.sc.b], [ps.b])
                o = (l * 48 + j * 6) * 2
                P.tt("dve", v3(self.mod.ap[:, o:o + 12], 6), v3(ps.ap[:, 0:12], 6),
                     self.bm.ap[:, l * 48 + j * 6:l * 48 + j * 6 + 6].unsqueeze(2).to_broadcast([128, 6, 2]),
                     ALU.add, [ps.b, self.bm.b], [self.mod.b])
        for l in range(self.nlayers):
            for (a, n, which) in ((self.a1, self.n1, 1), (self.a2, self.n2, 4)):
                o = (l * 48 + which * 8) * 2
                P.stt("dve", v3(a.ap[:, l * 16:(l + 1) * 16], 8), v3(self.mod.ap[:, o:o + 16], 8), 1.0,
                      n.ap[:, l * 8:(l + 1) * 8].unsqueeze(2).to_broadcast([128, 8, 2]),
                      ALU.add, ALU.mult, [self.mod.b, n.b], [a.b])
        P.barrier()
        A.release(m0)

    def modv(self, l, which, kc, s):
        o = ((l * 48 + which * 8 + kc) * 2) + s
        return self.mod.ap[:, o:o + 1]

    def load_w_bf16(self, dst, src3, nk, ncols, stg, piece=512):
        P = self.P
        dv = v3(dst.ap, nk)
        i = 0
        for c0 in range(0, ncols, piece):
            c1 = min(ncols, c0 + piece)
            s = stg[i % 2]
            sv = v3(s.ap[:, 0:nk * (c1 - c0)], nk)
            P.dma(sv, src3[:, :, c0:c1], w=[s.b])
            P.copy("pool" if i % 2 else "act", dv[:, :, c0:c1], sv, [s.b], [dst.b])
            i += 1

    def norm_mod(self, xg, N, l, a, which_shift, s, hb, sq, rstd, ps_ss):
        P = self.P
        xv = v3(xg.ap, 8)
        hv = v3(hb.ap, 8)
        for kc in range(8):
            P.act(sq.ap[:, 0:N], xv[:, kc, :], AF.Square, [xg.b], [sq.b])
            P.mm(ps_ss.ap[:, 0:N], self.ones.ap, sq.ap[:, 0:N], kc == 0, kc == 7, [self.ones.b, sq.b], [ps_ss.b])
        P.act(rstd.ap[:, 0:N], ps_ss.ap[:, 0:N], AF.Ln, [ps_ss.b], [rstd.b], bias=EPS, scale=1.0 / D)
        P.act(rstd.ap[:, 0:N], rstd.ap[:, 0:N], AF.Exp, [rstd.b], [rstd.b], scale=-0.5)
        for kc in range(8):
            ao = l * 16 + kc * 2 + s
            P.stt("dve", sq.ap[:, 0:N], xv[:, kc, :], a.ap[:, ao:ao + 1], rstd.ap[:, 0:N], ALU.mult, ALU.mult,
                  [xg.b, a.b, rstd.b], [sq.b])
            P.act(hv[:, kc, :], sq.ap[:, 0:N], AF.Identity, [sq.b, self.mod.b], [hb.b], bias=self.modv(l, which_shift, kc, s), scale=1.0)

    def phase_in(self, l):
        P, A = self.P, self.A
        m0 = A.mark()
        Wb = A.bf16(8 * NW)
        stg = [A.f32(8 * 256), A.f32(8 * 256)]
        self.load_w_bf16(Wb, self.w_in[l].rearrange("(kc p) n -> p kc n", p=128), 8, NW, stg, piece=256)
        Wv = v3(Wb.ap, 8)
        xg = [A.f32(8 * 512), A.f32(8 * 512)]
        hb = [A.bf16(8 * 512), A.bf16(8 * 512)]
        sq = A.f32(512)
        rstd = A.f32(512)
        rp = [A.f32(1024), A.f32(1024)]
        ofm = [A.f32(512) for _ in range(3)]
        otm = [A.f32(408) for _ in range(3)]
        t1 = A.f32(512)
        t2 = A.f32(512)
        xsrc = self.xT if l == 0 else self.xs
        xsrc3 = xsrc.rearrange("(kc p) n -> p kc n", p=128)
        bx = self.db("xs")
        ps_ss = self.ps[0]
        psf = self.ps[1:5]
        pst = self.ps[5:8]
        nf = 0
        ntm = 0

        def load(gi):
            n0, N = GROUPS[gi]
            x = xg[gi % 2]
            P.dma(v3(x.ap[:, 0:8 * N], 8), xsrc3[:, :, n0:n0 + N], r=[bx], w=[x.b])
            if gi > 0:
                r_ = rp[gi % 2]
                P.dma(v3(r_.ap, 2), self.rope[:, :, n0 - NCTX:n0 - NCTX + 512], w=[r_.b])

        def norm(gi):
            n0, N = GROUPS[gi]
            x = xg[gi % 2]
            h = hb[gi % 2]
            xin = T(x.ap[:, 0:8 * N]); xin.b = x.b
            hin = T(h.ap[:, 0:8 * N]); hin.b = h.b
            self.norm_mod(xin, N, l, self.a1, 0, 1 if gi == 0 else 0, hin, sq, rstd, ps_ss)

        load(0)
        norm(0)
        for gi, (n0, N) in enumerate(GROUPS):
            if gi + 1 < len(GROUPS):
                load(gi + 1)
            x = xg[gi % 2]
            h = hb[gi % 2]
            s = 1 if gi == 0 else 0
            hv = v3(h.ap[:, 0:8 * N], 8)
            rv = v3(rp[gi % 2].ap, 2)

            def fm(ft, ps):
                for kc in range(8):
                    P.mm(ps.ap[:, 0:N], Wv[:, kc, ft * 128:(ft + 1) * 128], hv[:, kc, :], kc == 0, kc == 7, [Wb.b, h.b], [ps.b])

            for (c0, dst, nm) in ((C_QA, self.qa, "qa"), (C_KA, self.ka, "ka")):
                for j in range(4):
                    o = ofm[nf % 3]
                    ob = o.ap.bitcast(BF16)[:, 0:N]
                    p1 = psf[nf % 4]
                    nf += 1
                    fm(c0 // 128 + j, p1)
                    if gi == 0:
                        P.copy("dve", ob, p1.ap[:, 0:N], [p1.b], [o.b])
                    else:
                        p2 = psf[nf % 4]
                        nf += 1
                        fm(c0 // 128 + 4 + j, p2)
                        P.tt("dve", t1.ap, p1.ap, rv[:, 0, :], ALU.mult, [p1.b, rp[gi % 2].b], [t1.b])
                        P.tt("dve", t2.ap, p2.ap, rv[:, 1, :], ALU.mult, [p2.b, rp[gi % 2].b], [t2.b])
                        P.tt("pool", ob, t1.ap, t2.ap, ALU.add, [t1.b, t2.b], [o.b])
                    P.dma(dst[j * 128:(j + 1) * 128, n0:n0 + N], ob, r=[o.b], w=[self.db(nm)])
            for j in range(9):
                o = ofm[nf % 3]
                p1 = psf[nf % 4]
                nf += 1
                fm(C_B // 128 + j, p1)
                P.copy("act", o.ap[:, 0:N], p1.ap[:, 0:N], [p1.b], [o.b])
                P.dma(self.qkvb[j * 128:(j + 1) * 128, n0:n0 + N], o.ap[:, 0:N], r=[o.b], w=[self.db("qkvb")])
            for (c0, dst, nm) in ((C_QC, self.qc, "qc"), (C_KC, self.kc, "kc")):
                for j in range(2):
                    o = ofm[nf % 3]
                    ob = o.ap.bitcast(BF16)[:, 0:N]
                    p1 = psf[nf % 4]
                    nf += 1
                    fm(c0 // 128 + j, p1)
                    P.copy("dve", ob, p1.ap[:, 0:N], [p1.b], [o.b])
                    P.dma(dst[j * 128:(j + 1) * 128, n0:n0 + N], ob, r=[o.b], w=[self.db(nm)])
            if gi + 1 < len(GROUPS):
                norm(gi + 1)
            for tt in range(N // 128):
                tok0 = n0 + tt * 128
                for (c0, nc_, dst, nm, isbf) in ((C_VA, 384, self.va, "va", True), (C_VC, 256, self.vc, "vc", True),
                                                 (C_G, 408, self.gab, "gab", False)):
                    ps = pst[ntm % 3]
                    o = otm[ntm % 3]
                    ntm += 1
                    for kc in range(8):
                        P.mm(ps.ap[:, 0:nc_], hv[:, kc, tt * 128:(tt + 1) * 128], Wv[:, kc, c0:c0 + nc_], kc == 0, kc == 7,
                             [Wb.b, h.b], [ps.b])
                    oa = o.ap.bitcast(BF16)[:, 0:nc_] if isbf else o.ap[:, 0:nc_]
                    P.copy("act" if ntm % 2 else "dve", oa, ps.ap[:, 0:nc_], [ps.b], [o.b])
                    P.dma(dst[tok0:tok0 + 128, :], oa, r=[o.b], w=[self.db(nm)])
        P.barrier()
        A.release(m0)


    def phase_a(self, l):
        P, A = self.P, self.A
        m0 = A.mark()
        KAt = A.bf16(4 * NT)
        P.dma(v3(KAt.ap, 4), self.ka.rearrange("(t p) n -> p t n", p=128), w=[KAt.b])
        Kv = v3(KAt.ap, 4)
        VA = A.bf16(NTILE * 6 * 128)
        Vv = VA.ap.rearrange("p (t h d) -> p t h d", t=NTILE, h=6)
        P.memset("pool", VA.ap, 1.0, [VA.b])
        vsrc = self.va.rearrange("(t p) (h d) -> p t h d", p=128, h=6)
        for t0 in range(NTILE):
            P.dma(Vv[:, t0, :, 0:64], vsrc[:, t0, :, :], w=[VA.b])
        Qt = [A.bf16(4 * 512), A.bf16(4 * 512)]
        pT = [[A.bf16(512) for _ in range(3)] for _ in range(2)]
        o12 = [[A.f32(512), A.f32(512)], [A.f32(512), A.f32(512)]]
        rz = A.f32(512)
        sq = A.f32(512)
        rs = A.f32(512)
        yb = [A.bf16(512), A.bf16(512)]
        deferred = []
        git = [0]
        nh = 0
        psS = [self.ps[0:2], self.ps[2:4]]
        psO = self.ps[4:6]
        psN = self.ps[6]
        psF = psN
        fl = A.bf16(128 + 512)
        P.memset("pool", fl.ap, 0.0, [fl.b])
        NFILL = getattr(self, "a_fill", 1)
        scale = 32.0 ** -0.5
        qsrc = self.qa.rearrange("(t p) n -> p t n", p=128)
        ny = 0

        def loadq(gi):
            n0, N = GROUPS[gi]
            q = Qt[gi % 2]
            P.dma(v3(q.ap, 4)[:, :, 0:N], qsrc[:, :, n0:n0 + N], w=[q.b])

        loadq(0)
        for gi, (n0, N) in enumerate(GROUPS):
            if gi + 1 < len(GROUPS):
                loadq(gi + 1)
            q = Qt[gi % 2]
            qv = v3(q.ap, 4)
            kts = [0, 1] if gi == 0 else list(range(NTILE))
            items = [(h, kt) for h in range(6) for kt in kts]

            def S(i):
                h, kt = items[i]
                for m in range(2):
                    hm = 2 * h + m
                    t, pr = hm // 3, (hm % 3) * 32
                    ps = psS[m][i % 2]
                    P.mm(ps.ap[:, 0:N], Kv[pr:pr + 32, t, kt * 128:(kt + 1) * 128], qv[pr:pr + 32, t, 0:N], True, True, [KAt.b, q.b], [ps.b])

            S(0)
            for i, (h, kt) in enumerate(items):
                if i + 1 < len(items):
                    S(i + 1)
                for _f in range(NFILL):
                    P.mm(psF.ap[:, 0:512], fl.ap[:, 0:128], fl.ap[:, 128:640], True, True, [fl.b], [psF.b])
                for m in range(2):
                    ps = psS[m][i % 2]
                    p = pT[m][i % 3]
                    P.act(p.ap[:, 0:N], ps.ap[:, 0:N], AF.Exp, [ps.b], [p.b], scale=scale)
                for m in range(2):
                    p = pT[m][i % 3]
                    po = psO[m]
                    P.mm(po.ap[:, 0:N], Vv[:, kt, h, :], p.ap[:, 0:N], kt == kts[0], kt == kts[-1], [VA.b, p.b], [po.b])
                git[0] += 1
                for fn_ in [f for (due, f) in deferred if due <= git[0]]:
                    fn_()
                deferred[:] = [(due, f) for (due, f) in deferred if due > git[0]]
                if kt == kts[-1]:
                    oo = o12[nh % 2]
                    nh += 1
                    for m in range(2):
                        po = psO[m]
                        P.copy("dve", oo[m].ap[:, 0:N], po.ap[:, 0:N], [po.b], [oo[m].b])
                    for _f in range(getattr(self, "a_bfill", 4)):
                        P.mm(psF.ap[:, 0:512], fl.ap[:, 0:128], fl.ap[:, 128:640], True, True, [fl.b], [psF.b])

                    def fin1(oo=oo, h=h, N=N, n0=n0):
                        for m in range(2):
                            o = oo[m]
                            P.op("dve", lambda e, o=o: e.reciprocal(out=rz.ap[0:64, 0:N], in_=o.ap[64:128, 0:N]), [o.b], [rz.b])
                            P.tt("dve", o.ap[0:64, 0:N], o.ap[0:64, 0:N], rz.ap[0:64, 0:N], ALU.mult, [o.b, rz.b], [o.b])
                        o1, o2 = oo
                        P.stt("dve", o1.ap[0:64, 0:N], o2.ap[0:64, 0:N], self.nlam.ap[0:64, l:l + 1], o1.ap[0:64, 0:N], ALU.mult, ALU.add,
                              [o1.b, o2.b, self.nlam.b], [o1.b])
                        P.tt("pool", sq.ap[0:64, 0:N], o1.ap[0:64, 0:N], o1.ap[0:64, 0:N], ALU.mult, [o1.b], [sq.b])

                    def fin2(oo=oo, h=h, N=N, n0=n0):
                        o1 = oo[0]
                        P.mm(psN.ap[0:64, 0:N], self.ones.ap[0:64, 0:64], sq.ap[0:64, 0:N], True, True, [self.ones.b, sq.b], [psN.b])
                        P.act(rs.ap[0:64, 0:N], psN.ap[0:64, 0:N], AF.Ln, [psN.b], [rs.b], bias=EPS, scale=1.0 / 64)
                        P.act(rs.ap[0:64, 0:N], rs.ap[0:64, 0:N], AF.Exp, [rs.b], [rs.b], scale=-0.5)
                        y = yb[h % 2]
                        P.stt("dve", y.ap[0:64, 0:N], o1.ap[0:64, 0:N], self.dnl.ap[0:64, l:l + 1], rs.ap[0:64, 0:N], ALU.mult, ALU.mult,
                              [o1.b, rs.b, self.dnl.b], [y.b])
                        P.dma(self.yT[h * 64:(h + 1) * 64, n0:n0 + N], y.ap[0:64, 0:N], r=[y.b])

                    fin1()
                    if gi == 0:
                        fin2()
                    else:
                        deferred.append((git[0] + 6, fin2))
                yield
        for (_, f) in deferred:
            f()
        deferred[:] = []
        P.barrier()
        A.release(m0)

    def phase_c(self, l):
        P, A = self.P, self.A
        m0 = A.mark()
        KCt = A.bf16(2 * NT)
        QCt = A.bf16(2 * NT)
        P.dma(v3(KCt.ap, 2), self.kc.rearrange("(t p) n -> p t n", p=128), w=[KCt.b])
        P.dma(v3(QCt.ap, 2), self.qc.rearrange("(t p) n -> p t n", p=128), w=[QCt.b])
        Kv, Qv = v3(KCt.ap, 2), v3(QCt.ap, 2)
        VC = A.bf16(NTILE * 4 * 128)
        Vv = VC.ap.rearrange("p (t h d) -> p t h d", t=NTILE, h=4)
        P.memset("pool", VC.ap, 1.0, [VC.b])
        vsrc = self.vc.rearrange("(t p) (h d) -> p t h d", p=128, h=4)
        for t0 in range(NTILE):
            P.dma(Vv[:, t0, :, 0:64], vsrc[:, t0, :, :], w=[VC.b])
        NB = A.f32(4 * 3200)
        NBv = NB.ap.rearrange("p (h a j q) -> p h a j q", h=4, a=5, j=5)
        for h in range(4):
            P.dma(NB.ap[:, h * 3200:(h + 1) * 3200], self.nab[l, h], w=[NB.b])
        sA = [A.f32(640), A.f32(640)]
        pA = [A.bf16(896), A.bf16(896)]
        rz = A.f32(128)
        ys = [A.bf16(256), A.bf16(256)]
        psA = self.ps[0:4]
        psO = self.ps[4:6]
        scale = 64.0 ** -0.5
        def geom(qt):
            if qt < 2:
                return [0, 1], 0, False
            r0 = 2 * (qt - 2)
            base = min(max(r0 - 4, 0), 54)
            return [2 + base // 2 + j for j in range(5)] + [0, 1], (r0 - base) // 2, True

        items = [(qt, h) for qt in range(NTILE) for h in range(4)]

        def stA(n):
            qt, h = items[n]
            kts, pat, lat = geom(qt)
            t, pr = h // 2, (h % 2) * 64
            pa, pb = psA[2 * (n % 2)], psA[2 * (n % 2) + 1]
            qop = Qv[pr:pr + 64, t, qt * 128:(qt + 1) * 128]
            for j, kt in enumerate(kts):
                dst = pa.ap[:, j * 128:(j + 1) * 128] if j < 4 else pb.ap[:, (j - 4) * 128:(j - 3) * 128]
                P.mm(dst, Kv[pr:pr + 64, t, kt * 128:(kt + 1) * 128], qop, True, True, [KCt.b, QCt.b], [pa.b if j < 4 else pb.b])

        def stB(n):
            qt, h = items[n]
            kts, pat, lat = geom(qt)
            t, pr = h // 2, (h % 2) * 64
            pa, pb = psA[2 * (n % 2)], psA[2 * (n % 2) + 1]
            s_, p_, po = sA[n % 2], pA[n % 2], psO[n % 2]
            yst = ys[qt % 2]
            ysv = v3(yst.ap, 2)
            nk = len(kts)
            if lat:
                P.stt("dve", s_.ap[:, 0:512], pa.ap[:, 0:512], scale, NBv[:, h, pat, 0:4, :].rearrange("p j q -> p (j q)"), ALU.mult, ALU.add,
                      [pa.b, NB.b], [s_.b])
                P.stt("dve", s_.ap[:, 512:640], pb.ap[:, 0:128], scale, NBv[:, h, pat, 4, :], ALU.mult, ALU.add,
                      [pb.b, NB.b], [s_.b])
                P.act(p_.ap[:, 0:640], s_.ap[:, 0:640], AF.Exp, [s_.b], [p_.b])
                P.act(p_.ap[:, 640:896], pb.ap[:, 128:384], AF.Exp, [pb.b], [p_.b], scale=scale)
            else:
                P.act(p_.ap[:, 0:256], pa.ap[:, 0:256], AF.Exp, [pa.b], [p_.b], scale=scale)
            for j, kt in enumerate(kts):
                P.mm(po.ap[:, 0:128], Vv[:, kt, h, :], p_.ap[:, j * 128:(j + 1) * 128], j == 0, j == nk - 1, [VC.b, p_.b], [po.b])
            P.op("dve", lambda e, po=po: e.reciprocal(out=rz.ap[0:64, :], in_=po.ap[64:128, 0:128]), [po.b], [rz.b])
            P.tt("dve", ysv[pr:pr + 64, t, :], po.ap[0:64, 0:128], rz.ap[0:64, :], ALU.mult, [po.b, rz.b], [yst.b])
            if h == 3:
                P.dma(self.yT[768:1024, qt * 128:(qt + 1) * 128].rearrange("(t p) n -> p t n", p=128), ysv, r=[yst.b])

        stA(0)
        for n in range(len(items)):
            if n + 1 < len(items):
                stA(n + 1)
            stB(n)
        P.barrier()
        A.release(m0)

    def gmk(self, i, ncol=128):
        return self.gm.ap[:, i * 128:i * 128 + ncol]

    def phase_b1(self, l, banks=None):
        P, A = self.P, self.A
        m0 = A.mark()
        xb = [A.f32(516) for _ in range(3)]
        acc = [A.f32(512) for _ in range(2)]
        sl = [A.f32(512) for _ in range(3)]
        sqs = [A.f32(512) for _ in range(2)]
        rs = A.f32(512)
        ob = [A.f32(512) for _ in range(2)]
        ps = banks if banks is not None else self.ps[0:2]
        nb_ = len(ps)
        onesblk = self.gmk(1)
        work = [(ft, n0, N) for ft in range(9) for (n0, N) in GROUPS]

        def stage1(it):
            ft, n0, N = work[it]
            s0, s1 = (0, NCTX) if n0 < NCTX else (NCTX, NT)
            x, a, sv, sq, p_ = xb[it % 3], acc[it % 2], sl[it % 3], sqs[it % 2], ps[it % nb_]
            lo, hi = max(n0 - 2, s0), min(n0 + N + 2, s1)
            if lo > n0 - 2:
                P.memset("pool", x.ap[:, 0:2], 0.0, [x.b])
            if hi < n0 + N + 2:
                P.memset("pool", x.ap[:, N + 2:N + 4], 0.0, [x.b])
            P.dma(x.ap[:, lo - (n0 - 2):hi - (n0 - 2)], self.qkvb[ft * 128:(ft + 1) * 128, lo:hi], w=[x.b])
            co = l * 45 + ft * 5
            P.ts("dve", a.ap[:, 0:N], x.ap[:, 0:N], self.cw.ap[:, co:co + 1], ALU.mult, [x.b, self.cw.b], [a.b])
            for j in range(1, 5):
                P.stt("dve", a.ap[:, 0:N], x.ap[:, j:j + N], self.cw.ap[:, co + j:co + j + 1], a.ap[:, 0:N], ALU.mult, ALU.add,
                      [x.b, self.cw.b, a.b], [a.b])
            P.act(sv.ap[:, 0:N], a.ap[:, 0:N], AF.Silu, [a.b], [sv.b])
            if ft < 6:
                P.tt("pool", sq.ap[:, 0:N], sv.ap[:, 0:N], sv.ap[:, 0:N], ALU.mult, [sv.b], [sq.b])
                P.mm(p_.ap[:, 0:N], onesblk, sq.ap[:, 0:N], True, True, [self.gm.b, sq.b], [p_.b])

        def stage2(it):
            ft, n0, N = work[it]
            sv, p_, o = sl[it % 3], ps[it % nb_], ob[it % 2]
            if ft < 6:
                P.act(rs.ap[:, 0:N], p_.ap[:, 0:N], AF.Ln, [p_.b], [rs.b], bias=EPS, scale=1.0)
                P.act(rs.ap[:, 0:N], rs.ap[:, 0:N], AF.Exp, [rs.b], [rs.b], scale=-0.5)
                P.stt("dve", o.ap[:, 0:N], sv.ap[:, 0:N], 0.125 if ft < 3 else 1.0, rs.ap[:, 0:N], ALU.mult, ALU.mult, [sv.b, rs.b], [o.b])
                P.dma(self.qkn[ft * 128:(ft + 1) * 128, n0:n0 + N], o.ap[:, 0:N], r=[o.b])
            else:
                P.dma(self.qkn[ft * 128:(ft + 1) * 128, n0:n0 + N], sv.ap[:, 0:N], r=[sv.b])

        stage1(0)
        for it in range(len(work)):
            if it + 1 < len(work) and nb_ > 1:
                stage1(it + 1)
            stage2(it)
            if it + 1 < len(work) and nb_ == 1:
                stage1(it + 1)
            yield
        P.barrier()
        A.release(m0)

    def phase_b2(self, l, d):
        P, A = self.P, self.A
        m0 = A.mark()
        ident = self.gmk(0)
        A_d, B_d, MS_d, MIT_d, SelEnd = self.gmk(2 + d), self.gmk(4 + d), self.gmk(6 + d), self.gmk(8 + d), self.gmk(10 + d)
        gmb = self.gm.b
        f = A.f32
        gabt = [f(408), f(408), f(408)]
        fmt = [f(9 * 128), f(9 * 128)]
        oft = [f(384), f(384), f(384)]
        SC = [[f(6) for _ in range(11)] + [f(12)] + [f(384) for _ in range(5)] for _ in range(2)]
        Ag, Bg, dg, Es, EiT, aqkT, TT, wT, qhT = [[f(384), f(384)] for _ in range(9)]
        Pm = [[f(384), f(384)], [f(384), f(384)]]
        PTm = [[f(384), f(384)], [f(384), f(384)]]
        u = [f(192), f(192)]
        vnew = [f(192), f(192)]
        S = [f(192), f(192)]
        otile = [f(384), f(384)]
        sq = f(384)
        ss = f(6)
        rs = f(6)
        sg = f(384)
        ybf = A.bf16(384)
        yo = f(384)
        for hb in range(2):
            P.memset("pool", S[hb].ap[0:64, :], 0.0, [S[hb].b])
        QB = [self.ps[0:4], self.ps[4:8]]
        fmbs = [A.bf16(768), A.bf16(768)]
        tmpos = [f(192), f(192)]
        otbs = [[T(o_.ap[:, 0:192]), T(o_.ap[:, 192:384])] for o_ in otile]
        ident16, MS16, MIT16 = self.gm16.ap[:, 0:128], self.gm16.ap[:, (1 + d) * 128:(2 + d) * 128], self.gm16.ap[:, (3 + d) * 128:(4 + d) * 128]
        psK = self.ps[0]
        kcorn = [sub(self.ps[j_], self.ps[j_].ap[:, 384:512]) for j_ in (0, 2, 3)]
        vcorn = [sub(self.ps[j_], self.ps[j_].ap[:, 384:512]) for j_ in (4, 5, 6)]
        ycorn = sub(self.ps[7], self.ps[7].ap[:, 384:512])
        psSc = sub(self.ps[1], self.ps[1].ap[:, 384:512])
        dtb = self.gp.ap[:, L * 12 + l * 12 + d * 6:L * 12 + l * 12 + d * 6 + 6]
        nea = self.nea.ap[:, l * 12 + d * 6:l * 12 + d * 6 + 6]
        gw = self.gp.ap[:, L * 24 + l * 64:L * 24 + (l + 1) * 64]
        qkn3 = self.qkn.rearrange("(f p) n -> p f n", p=128)
        order = list(range(NTILE)) if d == 0 else [1, 0] + list(range(NTILE - 1, 1, -1))
        chunks = (0, 1) if d == 0 else (1, 0)
        if getattr(self, "b2_tiles", None):
            order = order[:self.b2_tiles]

        def load(i):
            tt = order[i]
            tok0 = tt * 128
            P.dma(gabt[i % 3].ap, self.gab[tok0:tok0 + 128, :], w=[gabt[i % 3].b])
            P.dma(v3(fmt[i % 2].ap, 9), qkn3[:, :, tok0:tok0 + 128], w=[fmt[i % 2].b])
            if d == 1:
                P.dma(oft[i % 3].ap, self.of[tok0:tok0 + 128, :], w=[oft[i % 3].b])

        bc3 = lambda ap, n: ap.unsqueeze(2).to_broadcast([128, ap.shape[1], n])
        mb3 = lambda ap, h: ap.unsqueeze(1).to_broadcast([128, h, 128])
        def common(i):
            par = i % 2
            z, e_, g, eb, beta, nbeta, gc, eg, dgl, ek, bg, egl, ktm, vtm, ktail, kbg, vb = SC[par]
            ga, fm, ot, fmb = gabt[i % 3], fmt[par], otile[par], fmbs[par]
            return (z, e_, g, eb, beta, nbeta, gc, eg, dgl, ek, bg, egl, ktm, vtm, ktail, kbg, vb, ga, fm, v3(fm.ap, 9), ot, fmb, v3(fmb.ap, 6))

        def prep(i):
            load(i)
            z, e_, g, eb, beta, nbeta, gc, eg, dgl, ek, bg, egl, ktm, vtm, ktail, kbg, vb, ga, fm, fmv, ot, fmb, fbv = common(i)
            P.tt("dve", z.ap, ga.ap[:, 384 + 6 * d:390 + 6 * d], dtb, ALU.add, [ga.b, self.gp.b], [z.b])
            yield
            P.act(e_.ap, z.ap, AF.Exp, [z.b], [e_.b])
            yield
            P.act(e_.ap, e_.ap, AF.Ln, [e_.b], [e_.b], bias=1.0)
            yield
            P.tt("dve", g.ap, e_.ap, nea, ALU.mult, [e_.b, self.nea.b], [g.b])
            yield
            P.act(eb.ap, ga.ap[:, 396 + 6 * d:402 + 6 * d], AF.Exp, [ga.b], [eb.b], scale=-1.0)
            yield
            P.ts("dve", eb.ap, eb.ap, 1.0, ALU.add, [eb.b], [eb.b])
            yield
            P.op("dve", lambda e: e.reciprocal(out=beta.ap, in_=eb.ap), [eb.b], [beta.b])
            yield
            P.ts("dve", nbeta.ap, beta.ap, -1.0, ALU.mult, [beta.b], [nbeta.b])
            yield
            P.mm(psSc.ap[:, 0:6], A_d, g.ap, True, True, [gmb, g.b], [psSc.b])
            yield
            P.copy("dve", gc.ap, psSc.ap[:, 0:6], [psSc.b], [gc.b])
            yield
            P.act(eg.ap, gc.ap, AF.Exp, [gc.b], [eg.b])
            yield
            P.mm(psSc.ap[:, 8:14], SelEnd, gc.ap, True, True, [gmb, gc.b], [psSc.b])
            yield
            P.tt("dve", dgl.ap, psSc.ap[:, 8:14], gc.ap, ALU.subtract, [psSc.b, gc.b], [dgl.b])
            yield
            P.act(ek.ap, dgl.ap, AF.Exp, [dgl.b], [ek.b])
            yield
            P.tt("dve", bg.ap, beta.ap, eg.ap, ALU.mult, [beta.b, eg.b], [bg.b])
            yield
            for c in range(2):
                P.mm(psSc.ap[0:64, 16 + c * 6:22 + c * 6], self.gmk(12 + 2 * d + c, 64), gc.ap, True, True, [gmb, gc.b], [psSc.b])
                yield
            P.act(egl.ap[0:64, :], psSc.ap[0:64, 16:28], AF.Exp, [psSc.b], [egl.b])
            yield
            for j in range(3):
                cn = kcorn[j]
                P.tr(cn.ap, fmv[:, 3 + j, :], ident, [fm.b, gmb], [cn.b])
                yield
                P.copy("act", ktm.ap[:, j * 128:(j + 1) * 128], cn.ap, [cn.b], [ktm.b])
                yield
            for j in range(3):
                cn = vcorn[j]
                P.tr(cn.ap, fmv[:, 6 + j, :], ident, [fm.b, gmb], [cn.b])
                yield
                P.copy("dve", vtm.ap[:, j * 128:(j + 1) * 128], cn.ap, [cn.b], [vtm.b])
                yield
            k3, v3_ = v3(ktm.ap, 6), v3(vtm.ap, 6)
            P.tt("pool", v3(ktail.ap, 6), k3, bc3(ek.ap, 64), ALU.mult, [ktm.b, ek.b], [ktail.b])
            yield
            P.tt("pool", v3(kbg.ap, 6), k3, bc3(bg.ap, 64), ALU.mult, [ktm.b, bg.b], [kbg.b])
            yield
            P.tt("pool", v3(vb.ap, 6), v3_, bc3(beta.ap, 64), ALU.mult, [vtm.b, beta.b], [vb.b])
            yield
            P.copy("pool", fmb.ap, fm.ap[:, 0:768], [fm.b], [fmb.b])
            yield

        def body(i, nxt, prevout):
            tt = order[i]
            tok0 = tt * 128
            z, e_, g, eb, beta, nbeta, gc, eg, dgl, ek, bg, egl, ktm, vtm, ktail, kbg, vb, ga, fm, fmv, ot, fmb, fbv = common(i)
            def batch(hb, fm=fm, fmv=fmv, fbv=fbv, fmb=fmb, i=i):
                Q0, Q1, Q2, Q3 = QB[hb]
                H = [3 * hb, 3 * hb + 1, 3 * hb + 2]
                hs = slice(3 * hb, 3 * hb + 3)
                ag, bgm, dgm, es, eit, aq, tt_, w_, qh = Ag[hb], Bg[hb], dg[hb], Es[hb], EiT[hb], aqkT[hb], TT[hb], wT[hb], qhT[hb]
                otb = otbs[i % 2][hb]
                tmpo = tmpos[hb]
                P.tt("pool", v3(ag.ap, 3), mb3(A_d, 3), bc3(g.ap[:, hs], 128), ALU.mult, [gmb, g.b], [ag.b])
                P.tt("pool", v3(bgm.ap, 3), mb3(B_d, 3), bc3(g.ap[:, hs], 128), ALU.mult, [gmb, g.b], [bgm.b])
                P.tt("pool", v3(dgm.ap, 3), mb3(ident, 3), bc3(eg.ap[:, hs], 128), ALU.mult, [gmb, eg.b], [dgm.b])
                for k, h in enumerate(H):
                    hp, hr = h // 2, (h % 2) * 64
                    kTb = fbv[hr:hr + 64, 3 + hp, :]
                    qTb = fbv[hr:hr + 64, hp, :]
                    ks = slice(k * 128, (k + 1) * 128)
                    P.mm(Q0.ap[:, ks], kTb, kTb, True, True, [fmb.b], [Q0.b])
                    P.mm(Q1.ap[:, ks], kTb, qTb, True, True, [fmb.b], [Q1.b])
                    P.mm(Q2.ap[:, ks], ag.ap[:, ks], B_d, True, False, [ag.b, gmb], [Q2.b])
                    P.mm(Q2.ap[:, ks], ident16, MS16, False, True, [self.gm16.b], [Q2.b])
                    P.mm(Q3.ap[:, ks], bgm.ap[:, ks], A_d, True, False, [bgm.b, gmb], [Q3.b])
                    P.mm(Q3.ap[:, ks], ident16, MIT16, False, True, [self.gm16.b], [Q3.b])
                yield
                P.act(es.ap, Q2.ap[:, 0:384], AF.Exp, [Q2.b], [es.b])
                P.act(eit.ap, Q3.ap[:, 0:384], AF.Exp, [Q3.b], [eit.b])
                p0, pt0 = Pm[hb][0], PTm[hb][0]
                for k, h in enumerate(H):
                    ks = slice(k * 128, (k + 1) * 128)
                    P.stt("dve", p0.ap[:, ks], Q0.ap[:, ks], nbeta.ap[:, h:h + 1], es.ap[:, ks], ALU.mult, ALU.mult, [Q0.b, es.b, nbeta.b], [p0.b])
                P.tt("dve", aq.ap, Q1.ap[:, 0:384], eit.ap, ALU.mult, [Q1.b, eit.b], [aq.b])
                yield
                for k in range(3):
                    ks = slice(k * 128, (k + 1) * 128)
                    P.tr(Q0.ap[:, ks], p0.ap[:, ks], ident, [p0.b, gmb], [Q0.b])
                yield
                P.copy("act", pt0.ap, Q0.ap[:, 0:384], [Q0.b], [pt0.b])
                for k in range(3):
                    ks = slice(k * 128, (k + 1) * 128)
                    P.tt("dve", tt_.ap[:, ks], Q0.ap[:, ks], ident, ALU.add, [Q0.b, gmb], [tt_.b])
                pc, ptc = p0, pt0
                yield
                for lvl in range(5):
                    pn, ptn = Pm[hb][(lvl + 1) % 2], PTm[hb][(lvl + 1) % 2]
                    for k in range(3):
                        ks = slice(k * 128, (k + 1) * 128)
                        P.mm(Q1.ap[:, ks], ptc.ap[:, ks], pc.ap[:, ks], True, True, [ptc.b, pc.b], [Q1.b])
                    if lvl < 4:
                        for k in range(3):
                            ks = slice(k * 128, (k + 1) * 128)
                            P.mm(Q2.ap[:, ks], pc.ap[:, ks], ptc.ap[:, ks], True, True, [ptc.b, pc.b], [Q2.b])
                    yield
                    P.copy("act", pn.ap, Q1.ap[:, 0:384], [Q1.b], [pn.b])
                    if lvl < 4:
                        P.copy("dve", ptn.ap, Q2.ap[:, 0:384], [Q2.b], [ptn.b])
                    yield
                    for k in range(3):
                        ks = slice(k * 128, (k + 1) * 128)
                        P.mm(Q3.ap[:, ks], pn.ap[:, ks], tt_.ap[:, ks], True, True, [pn.b, tt_.b], [Q3.b])
                    yield
                    P.tt("dve", tt_.ap, tt_.ap, Q3.ap[:, 0:384], ALU.add, [tt_.b, Q3.b], [tt_.b])
                    pc, ptc = pn, ptn
                    yield
                for k, h in enumerate(H):
                    hp, hr = h // 2, (h % 2) * 64
                    ks = slice(k * 128, (k + 1) * 128)
                    P.mm(Q0.ap[:, k * 64:(k + 1) * 64], tt_.ap[:, ks], vb.ap[:, h * 64:(h + 1) * 64], True, True, [tt_.b, vb.b], [Q0.b])
                    P.mm(Q1.ap[0:64, ks], kbg.ap[:, h * 64:(h + 1) * 64], tt_.ap[:, ks], True, True, [tt_.b, kbg.b], [Q1.b])
                    P.mm(Q2.ap[0:64, ks], self.ones.ap[:, 0:64], dgm.ap[:, ks], True, True, [self.ones.b, dgm.b], [Q2.b])
                yield
                P.copy("act", u[hb].ap, Q0.ap[:, 0:192], [Q0.b], [u[hb].b])
                P.copy("dve", w_.ap[0:64, :], Q1.ap[0:64, 0:384], [Q1.b], [w_.b])
                for k, h in enumerate(H):
                    hp, hr = h // 2, (h % 2) * 64
                    ks = slice(k * 128, (k + 1) * 128)
                    P.tt("dve", qh.ap[0:64, ks], fmv[hr:hr + 64, hp, :], Q2.ap[0:64, ks], ALU.mult, [fm.b, Q2.b], [qh.b])
                yield
                Sb = S[hb]
                vn = vnew[hb]
                for c in chunks:
                    cs = c * 64
                    for k, h in enumerate(H):
                        ks = slice(k * 128, (k + 1) * 128)
                        P.mm(Q3.ap[:, k * 64:(k + 1) * 64], w_.ap[0:64, ks], Sb.ap[0:64, k * 64:(k + 1) * 64], True, True, [w_.b, Sb.b], [Q3.b])
                    yield
                    P.tt("dve", vn.ap[cs:cs + 64, :], u[hb].ap[cs:cs + 64, :], Q3.ap[cs:cs + 64, 0:192], ALU.subtract, [u[hb].b, Q3.b], [vn.b])
                    yield
                    for k, h in enumerate(H):
                        ks = slice(k * 128, (k + 1) * 128)
                        k6 = slice(k * 64, (k + 1) * 64)
                        P.mm(Q0.ap[:, k6], qh.ap[0:64, ks], Sb.ap[0:64, k6], True, True, [qh.b, Sb.b], [Q0.b])
                        P.mm(Q1.ap[:, k6], aq.ap[cs:cs + 64, ks], vn.ap[cs:cs + 64, k6], True, True, [aq.b, vn.b], [Q1.b])
                        P.mm(Q2.ap[0:64, k6], ktail.ap[cs:cs + 64, h * 64:(h + 1) * 64], vn.ap[cs:cs + 64, k6], True, True, [ktail.b, vn.b], [Q2.b])
                    yield
                    P.copy("act", tmpo.ap[cs:cs + 64, :], Q1.ap[cs:cs + 64, 0:192], [Q1.b], [tmpo.b])
                    P.tt("dve", otb.ap[cs:cs + 64, :], Q0.ap[cs:cs + 64, 0:192], tmpo.ap[cs:cs + 64, :], ALU.add,
                         [Q0.b, tmpo.b], [otb.b])
                    for k, h in enumerate(H):
                        k6 = slice(k * 64, (k + 1) * 64)
                        P.stt("dve", Sb.ap[0:64, k6], Sb.ap[0:64, k6], egl.ap[0:64, c * 6 + h:c * 6 + h + 1], Q2.ap[0:64, k6], ALU.mult, ALU.add,
                              [Sb.b, egl.b, Q2.b], [Sb.b])
                    yield

            gens = [batch(0), batch(1)] + ([nxt] if nxt is not None else []) + ([prevout] if prevout is not None else [])
            for _ in itertools.zip_longest(*gens):
                pass
            return

        def outgen(i):
            tt = order[i]
            tok0 = tt * 128
            ga, ot = gabt[i % 3], otile[i % 2]
            ot_r = [otbs[i % 2][0].b, otbs[i % 2][1].b]
            if d == 0:
                P.dma(self.of[tok0:tok0 + 128, :], ot.ap, r=ot_r)
                yield
                return
            of_ = oft[i % 3]
            P.tt("pool", sg.ap, ot.ap, of_.ap, ALU.add, ot_r + [of_.b], [sg.b])
            yield
            P.tt("pool", sq.ap, sg.ap, sg.ap, ALU.mult, [sg.b], [sq.b])
            yield
            P.op("dve", lambda e: e.tensor_reduce(out=ss.ap, in_=v3(sq.ap, 6), axis=AX.X, op=ALU.add), [sq.b], [ss.b])
            yield
            P.act(rs.ap, ss.ap, AF.Ln, [ss.b], [rs.b], bias=EPS, scale=1.0 / 64)
            yield
            P.act(rs.ap, rs.ap, AF.Exp, [rs.b], [rs.b], scale=-0.5)
            yield
            P.tt("dve", v3(sq.ap, 6), v3(sg.ap, 6), bc3(rs.ap, 64), ALU.mult, [sg.b, rs.b], [sq.b])
            yield
            P.tt("pool", v3(yo.ap, 6), v3(sq.ap, 6), gw.unsqueeze(1).to_broadcast([128, 6, 64]), ALU.mult, [sq.b, self.gp.b], [yo.b])
            yield
            P.act(sg.ap, ga.ap[:, 0:384], AF.Silu, [ga.b], [sg.b])
            yield
            P.tt("dve", sq.ap, yo.ap, sg.ap, ALU.mult, [yo.b, sg.b], [sq.b])
            yield
            for j in range(3):
                P.tr(ycorn.ap, sq.ap[:, j * 128:(j + 1) * 128], ident, [sq.b, gmb], [ycorn.b])
                yield
                P.copy("act", ybf.ap[:, j * 128:(j + 1) * 128], ycorn.ap, [ycorn.b], [ybf.b])
                yield
            P.dma(self.yT[384:768, tok0:tok0 + 128].rearrange("(j p) n -> p j n", p=128), v3(ybf.ap, 3), r=[ybf.b])
            yield

        for _ in prep(0):
            pass
        for i in range(len(order)):
            body(i, prep(i + 1) if i + 1 < len(order) else None, outgen(i - 1) if i >= 1 else None)
        for _ in outgen(len(order) - 1):
            pass
        P.barrier()
        A.release(m0)

    def phase_out(self, l):
        P, A = self.P, self.A
        m0 = A.mark()
        last = (l == self.nlayers - 1) and self.nlayers == L
        Wo = A.bf16(8 * D)
        W1 = A.bf16(8 * 2 * DFF)
        W2 = A.bf16(22 * D)
        N = 256
        actb = A.bf16(22 * N)
        stg = [T(actb.ap.bitcast(F32)[:, 0:1280]), T(actb.ap.bitcast(F32)[:, 1280:2560])]
        self.load_w_bf16(Wo, self.w_out[l].rearrange("(kc p) n -> p kc n", p=128), 8, D, stg, piece=160)
        self.load_w_bf16(W1, self.w_f1[l].rearrange("(kc p) n -> p kc n", p=128), 8, 2 * DFF, stg, piece=160)
        self.load_w_bf16(W2, self.w_f2[l].rearrange("(kc p) n -> p kc n", p=128), 22, D, stg, piece=58)
        P.barrier()
        Wov, W1v, W2v = v3(Wo.ap, 8), v3(W1.ap, 8), v3(W2.ap, 22)
        xgs = [A.f32(8 * N), A.f32(8 * N)]
        yg = [A.bf16(8 * N), A.bf16(8 * N)]
        hb = A.bf16(8 * N)
        sq = A.f32(N)
        rstd = A.f32(N)
        sg = A.f32(N)
        xsrc3 = (self.xT if l == 0 else self.xs).rearrange("(kc p) n -> p kc n", p=128)
        xdst3 = self.xs.rearrange("(kc p) n -> p kc n", p=128)
        y3 = self.yT.rearrange("(kc p) n -> p kc n", p=128)
        o3 = self.outT.rearrange("(kc p) n -> p kc n", p=128)
        ps_ss = self.ps[0]
        psr = self.ps[1:4]
        psg = self.ps[4:6]
        psu = self.ps[6:8]
        ngr = NT // N
        hv = v3(hb.ap, 8)
        av = v3(actb.ap, 22)
        cnt = {"nr": 0}

        def loady(gi):
            P.dma(v3(yg[gi % 2].ap, 8), y3[:, :, gi * N:(gi + 1) * N], w=[yg[gi % 2].b])

        def ffn_out(gi):
            n0 = gi * N
            s = 1 if gi == 0 else 0
            xg = xgs[gi % 2]
            xv = v3(xg.ap, 8)
            for dt in range(8):
                ps = psr[cnt["nr"] % 3]
                cnt["nr"] += 1
                for ft in range(22):
                    P.mm(ps.ap[:, 0:N], W2v[:, ft, dt * 128:(dt + 1) * 128], av[:, ft, :], ft == 0, ft == 21, [W2.b, actb.b], [ps.b])
                P.stt("dve", xv[:, dt, :], ps.ap[:, 0:N], self.modv(l, 5, dt, s), xv[:, dt, :], ALU.mult, ALU.add, [ps.b, self.mod.b, xg.b], [xg.b])
            if not last:
                P.dma(xdst3[:, :, n0:n0 + N], xv, r=[xg.b])
            elif gi >= 1:
                for kc in range(8):
                    P.act(sq.ap, xv[:, kc, :], AF.Square, [xg.b], [sq.b])
                    P.mm(ps_ss.ap[:, 0:N], self.ones.ap, sq.ap, kc == 0, kc == 7, [self.ones.b, sq.b], [ps_ss.b])
                P.act(rstd.ap, ps_ss.ap[:, 0:N], AF.Ln, [ps_ss.b], [rstd.b], bias=EPS, scale=1.0 / D)
                P.act(rstd.ap, rstd.ap, AF.Exp, [rstd.b], [rstd.b], scale=-0.5)
                for kc in range(8):
                    P.stt("dve", xv[:, kc, :], xv[:, kc, :], self.fn.ap[:, kc:kc + 1], rstd.ap, ALU.mult, ALU.mult, [xg.b, self.fn.b, rstd.b], [xg.b])
                P.dma(o3[:, :, n0 - NCTX:n0 - NCTX + N], xv, r=[xg.b])

        loady(0)
        for gi in range(ngr):
            n0 = gi * N
            s = 1 if gi == 0 else 0
            if gi + 1 < ngr:
                loady(gi + 1)
            xg = xgs[gi % 2]
            xv = v3(xg.ap, 8)
            P.dma(xv, xsrc3[:, :, n0:n0 + N], w=[xg.b])
            y = yg[gi % 2]
            yv = v3(y.ap, 8)
            for dt in range(8):
                ps = psr[cnt["nr"] % 3]
                cnt["nr"] += 1
                for kc in range(8):
                    P.mm(ps.ap[:, 0:N], Wov[:, kc, dt * 128:(dt + 1) * 128], yv[:, kc, :], kc == 0, kc == 7, [Wo.b, y.b], [ps.b])
                P.stt("dve", xv[:, dt, :], ps.ap[:, 0:N], self.modv(l, 2, dt, s), xv[:, dt, :], ALU.mult, ALU.add, [ps.b, self.mod.b, xg.b], [xg.b])
            if gi >= 1:
                ffn_out(gi - 1)
            self.norm_mod(xg, N, l, self.a2, 3, s, hb, sq, rstd, ps_ss)
            for ft in range(22):
                pg, pu = psg[ft % 2], psu[ft % 2]
                for kc in range(8):
                    P.mm(pg.ap[:, 0:N], W1v[:, kc, ft * 128:(ft + 1) * 128], hv[:, kc, :], kc == 0, kc == 7, [W1.b, hb.b], [pg.b])
                for kc in range(8):
                    P.mm(pu.ap[:, 0:N], W1v[:, kc, DFF + ft * 128:DFF + (ft + 1) * 128], hv[:, kc, :], kc == 0, kc == 7, [W1.b, hb.b], [pu.b])
                P.act(sg.ap, pg.ap[:, 0:N], AF.Silu, [pg.b], [sg.b])
                P.tt("dve", av[:, ft, :], pu.ap[:, 0:N], sg.ap, ALU.mult, [pu.b, sg.b], [actb.b])
        ffn_out(ngr - 1)
        P.barrier()
        A.release(m0)

    def build(self):
        self.consts()
        self.phase_mod()
        ph = self.phases
        for l in range(self.nlayers):
            if ph is None or "in" in ph:
                self.phase_in(l)
            if ph is None or "a" in ph:
                for _ in self.phase_a(l):
                    pass
            if ph is None or "b1" in ph:
                for _ in self.phase_b1(l):
                    pass
            if ph is None or "c" in ph:
                self.phase_c(l)
            if ph is None or "b2" in ph:
                self.phase_b2(l, 0)
                self.phase_b2(l, 1)
            if ph is None or "out" in ph:
                self.phase_out(l)
        self.P.finish()
        self.st.close()
        return self.nc


def rope_tables():
    half = 16
    inv_freq = (1.0 / (10000.0 ** (np.arange(0, half, 2, dtype=np.float32) / np.float32(half)))).astype(np.float32)
    t = np.arange(TL, dtype=np.int32)
    ang_r = (t // 64).astype(np.float32)[:, None] * inv_freq
    ang_c = (t % 64).astype(np.float32)[:, None] * inv_freq
    ang = np.concatenate([ang_r, ang_r, ang_c, ang_c], axis=-1)
    cos = np.cos(ang).astype(np.float32)
    sin = np.sin(ang).astype(np.float32)
    sign = np.array([-1.0] * 8 + [1.0] * 8 + [-1.0] * 8 + [1.0] * 8, np.float32)
    tab = np.stack([cos.T, (sin * sign).T], axis=1)
    return np.ascontiguousarray(np.tile(tab, (4, 1, 1)))


def na_bias_tiles(nb):
    out = np.full((L, 4, 5, 5, 128, 128), -30000.0, np.float32)
    qi = np.arange(128)
    ki = np.arange(128)
    for pat, (r0, base) in enumerate(((0, 0), (2, 0), (4, 0), (60, 54), (62, 54))):
        r = r0 + qi // 64
        c = qi % 64
        rs = np.clip(r - 4, 0, 56)
        cs = np.clip(c - 8, 0, 48)
        for j in range(5):
            kr = base + 2 * j + ki // 64
            kc = ki % 64
            valid = ((kr[:, None] >= rs[None, :]) & (kr[:, None] < rs[None, :] + 8) &
                     (kc[:, None] >= cs[None, :]) & (kc[:, None] < cs[None, :] + 16))
            dr = np.clip(kr[:, None] - r[None, :] + 7, 0, 14)
            dc = np.clip(kc[:, None] - c[None, :] + 15, 0, 30)
            g = nb[:, :, dr, dc]
            out[:, :, pat, j] = np.where(valid[None, None], g, np.float32(-30000.0))
    return np.ascontiguousarray(out.transpose(0, 1, 4, 2, 3, 5).reshape(L, 4, 128, 3200))


def gdn_masks():
    m = np.zeros((16, 128, 128), np.float32)
    i = np.arange(128)
    same = (i[:, None] // 64) == (i[None, :] // 64)
    r, c = i[:, None], i[None, :]
    m[0] = np.eye(128)
    m[1] = same
    m[2] = same & (r <= c)
    m[3] = same & (r >= c)
    m[4] = same & (r > c)
    m[5] = same & (r < c)
    m[6] = np.where(same & (r > c), 0.0, -30000.0)
    m[7] = np.where(same & (r < c), 0.0, -30000.0)
    m[8] = np.where(same & (c >= r), 0.0, -30000.0)
    m[9] = np.where(same & (c <= r), 0.0, -30000.0)
    endf = (i // 64) * 64 + 63
    endb = (i // 64) * 64
    m[10] = (r == endf[None, :])
    m[11] = (r == endb[None, :])
    for d_, ends in enumerate(((63, 127), (0, 64))):
        for c_ in range(2):
            m[12 + 2 * d_ + c_][ends[c_], :] = 1.0
    return np.ascontiguousarray(m.transpose(1, 0, 2).reshape(128, 2048))


def host_prep(inp):
    f = lambda a: np.ascontiguousarray(a, dtype=np.float32)
    w_in = inp["w_in"]
    sizes = (1152, 1152, 384, 12, 12, 768)
    offs = np.cumsum((0,) + sizes)
    a0, b0, g0, al0, be0, c0 = offs[:6]
    perm32 = np.concatenate([np.arange(8, 16), np.arange(0, 8), np.arange(24, 32), np.arange(16, 24)])
    pad = lambda m: np.concatenate([m.reshape(4, 96), m.reshape(4, 96)[:, :32]], axis=1).reshape(-1)
    idq = pad(np.arange(384))
    permq = pad((np.arange(384).reshape(12, 32)[:, perm32]).reshape(-1))
    cols = np.concatenate([
        a0 + idq, a0 + permq, a0 + 384 + idq, a0 + 384 + permq,
        b0 + np.arange(1152), c0 + np.arange(256), c0 + 256 + np.arange(256),
        a0 + 768 + np.arange(384), c0 + 512 + np.arange(256), g0 + np.arange(384), al0 + np.arange(12), be0 + np.arange(12)])
    assert cols.shape[0] == NW
    shared = {
        "w_mod": f(inp["w_mod"]),
        "bmodT": f(inp["b_mod"].reshape(L, 48, 128).transpose(2, 0, 1).reshape(128, L * 48)),
        "n1T": f(inp["norm1_w"].reshape(L, 8, 128).transpose(2, 0, 1).reshape(128, L * 8)),
        "n2T": f(inp["norm2_w"].reshape(L, 8, 128).transpose(2, 0, 1).reshape(128, L * 8)),
        "fnT": f(inp["final_norm_w"].reshape(8, 128).T),
        "w_in": f(w_in[:, :, cols]),
        "rope": rope_tables(),
        "lamp": f(np.stack([inp["lambda_q1"], inp["lambda_k1"], inp["lambda_q2"], inp["lambda_k2"]], axis=1).reshape(1, L * 128)),
        "dnT": f(np.tile(inp["diff_norm_w"].T, (2, 1))),
        "nab": na_bias_tiles(inp["na_bias"]),
        "convT": f(inp["conv_w"].reshape(L, 5, 9, 128).transpose(3, 0, 2, 1).reshape(128, L * 45)),
        "gpar": f(np.concatenate([inp["a_log"].reshape(-1), inp["dt_bias"].reshape(-1), inp["gdn_norm_w"].reshape(-1)])[None, :]),
        "gmask": gdn_masks(),
        "w_out": f(inp["w_out"]),
        "w_f1": f(inp["w_ffn_in"]),
        "w_f2": f(inp["w_ffn_out"]),
    }
    per = []
    for b in range(4):
        xT = f(np.concatenate([inp["ctx"][b], inp["x"][b]], axis=0).T)
        cT = f(np.stack([inp["c"][b].reshape(8, 128).T, inp["c_ctx"].reshape(8, 128).T], axis=2).reshape(128, 16))
        m = dict(shared)
        m["xT"] = xT
        m["cT"] = cT
        per.append(m)
    return per


def kernel(**inputs):
    inp = {k: np.asarray(v) for k, v in inputs.items()}
    per = host_prep(inp)
    nc = K().build()
    in_maps = [per[i % 4] for i in range(8)]
    res = run_bass_kernel_spmd(nc, in_maps, core_ids=list(range(8)))
    out = np.stack([np.ascontiguousarray(res.results[b]["outT"].T) for b in range(4)], axis=0)
    return out.astype(np.float32)
```
